# Optimizing a Trainium2 kernel written in Bass

```python
import math
import jax
import jax.numpy as jnp
from jax import lax
import numpy as np

D_MODEL = 1024
BATCH = 32
SEQ = 256
DEPTH = 4
DEC_BATCH = 2
DEC_SEQ = 1024
PAST_LEN = 512

GRID_W = 64
EPS = 1e-6
NEG_INF = -1e30
ROPE_BASE = 10000.0
BLOCK = 128
WINDOW = 128
HEAD_DIM = 64
RET_HEADS = 4
RET_WIDTH = RET_HEADS * HEAD_DIM
HY_WIDTH = 256
HY_ORDER = 2
HY_BANDS = 8
HY_POS_DIM = 1 + 2 * HY_BANDS
HY_FILTER_HIDDEN = 64
HY_DECAY_SLOW = 3.07
HY_DECAY_FAST = 15.35
GQA_Q_HEADS = 4
GQA_KV_HEADS = 2
GQA_GROUPS = GQA_Q_HEADS // GQA_KV_HEADS
MLA_HEADS = 4
MLA_Q_RANK = 256
MLA_KV_RANK = 128
MLA_NOPE = 64
MLA_ROPE = 32
MLA_V = 64
D_FF = 2816
RET_IN = 4 * RET_WIDTH
HY_IN = (HY_ORDER + 1) * HY_WIDTH
GQA_IN = (GQA_Q_HEADS + 2 * GQA_KV_HEADS) * HEAD_DIM
MLA_IN = MLA_Q_RANK + MLA_KV_RANK + MLA_ROPE
IN_WIDTH = RET_IN + HY_IN + GQA_IN + MLA_IN
MIX_WIDTH = RET_WIDTH + HY_WIDTH + GQA_Q_HEADS * HEAD_DIM + MLA_HEADS * MLA_V

kernel_name = 'hybrid_flow_trunk_ctx_prefix_step'


def rms_norm(x, g):
    xf = x.astype(jnp.float32)
    y = xf * lax.rsqrt(jnp.mean(xf * xf, axis=-1, keepdims=True) + EPS)
    return (y * g.astype(jnp.float32)).astype(x.dtype)


def modulation(cond, w, b):
    m = jax.nn.silu(cond) @ w + b
    return jnp.split(m[..., None, :], 6, axis=-1)


def dwconv3(x, w, b):
    xp = jnp.pad(x, ((0, 0), (1, 1), (0, 0)))
    return xp[:, :-2] * w[0] + xp[:, 1:-1] * w[1] + xp[:, 2:] * w[2] + b


def axial_rope_tables(n_tokens, dim, dtype):
    n_rows = n_tokens // GRID_W
    rows = jnp.repeat(jnp.arange(n_rows, dtype=jnp.float32), GRID_W)
    cols = jnp.tile(jnp.arange(GRID_W, dtype=jnp.float32), n_rows)
    quarter = dim // 4
    inv = ROPE_BASE ** (-jnp.arange(quarter, dtype=jnp.float32) / quarter)
    ang = jnp.concatenate([rows[:, None] * inv, cols[:, None] * inv], axis=-1)
    return jnp.cos(ang).astype(dtype), jnp.sin(ang).astype(dtype)


def apply_rope(x, cos, sin):
    x1, x2 = jnp.split(x, 2, axis=-1)
    return jnp.concatenate([x1 * cos - x2 * sin, x1 * sin + x2 * cos], axis=-1)


def retention_scan(q, k, v, log_gamma, s0):
    b, nh, seq_len, dk = q.shape
    dv = v.shape[-1]
    n = seq_len // BLOCK
    qc = q.reshape(b, nh, n, BLOCK, dk)
    kc = k.reshape(b, nh, n, BLOCK, dk)
    vc = v.reshape(b, nh, n, BLOCK, dv)
    pos = jnp.arange(BLOCK, dtype=jnp.float32)
    lag = pos[:, None] - pos[None, :]
    decay = jnp.where(lag >= 0, jnp.exp(jnp.maximum(lag, 0.0) * log_gamma[:, None, None]), 0.0)
    scores = jnp.einsum('bhnid,bhnjd->bhnij', qc, kc) * decay[None, :, None]
    o_intra = jnp.einsum('bhnij,bhnje->bhnie', scores, vc)
    q_dec = jnp.exp((pos + 1.0) * log_gamma[:, None])
    k_dec = jnp.exp((BLOCK - 1.0 - pos) * log_gamma[:, None])
    chunk_dec = jnp.exp(BLOCK * log_gamma)[None, :, None, None]
    inc = jnp.einsum('bhnjd,bhnje->nbhde', kc * k_dec[None, :, None, :, None], vc)

    def step(s, inc_n):
        return s * chunk_dec + inc_n, s

    s_final, s_before = lax.scan(step, s0, inc)
    o_cross = jnp.einsum('bhnid,nbhde->bhnie', qc * q_dec[None, :, None, :, None], s_before)
    return (o_intra + o_cross).reshape(b, nh, seq_len, dv), s_final


def retention_mixer(x_in, decay_logit, gn_g, s0):
    b, seq_len, _ = x_in.shape
    q, k, v, gate = jnp.split(x_in, 4, axis=-1)

    def heads(t):
        return t.astype(jnp.float32).reshape(b, seq_len, RET_HEADS, HEAD_DIM).transpose(0, 2, 1, 3)

    log_gamma = jax.nn.log_sigmoid(decay_logit.astype(jnp.float32))
    qh, kh, vh = heads(q), heads(k) * HEAD_DIM ** -0.5, heads(v)
    o_f, s_f = retention_scan(qh, kh, vh, log_gamma[0], s0[:, 0])
    o_b, s_b = retention_scan(jnp.flip(qh, 2), jnp.flip(kh, 2), jnp.flip(vh, 2), log_gamma[1], s0[:, 1])
    o = o_f + jnp.flip(o_b, 2)
    mu = jnp.mean(o, axis=-1, keepdims=True)
    var = jnp.var(o, axis=-1, keepdims=True)
    o = ((o - mu) * lax.rsqrt(var + EPS)).transpose(0, 2, 1, 3).reshape(b, seq_len, RET_WIDTH)
    y = jax.nn.silu(gate.astype(jnp.float32)) * (o * gn_g.astype(jnp.float32))
    return y.astype(x_in.dtype), jnp.stack([s_f, s_b], axis=1)


def hyena_filters(seq_len, w1, b1, w2, b2, w3, decay):
    t = jnp.arange(seq_len, dtype=jnp.float32) / seq_len
    bands = jnp.arange(1, HY_BANDS + 1, dtype=jnp.float32)
    ang = (2.0 * math.pi) * t[:, None] * bands
    z = jnp.concatenate([t[:, None], jnp.cos(ang), jnp.sin(ang)], axis=-1)
    h = jnp.sin(z @ w1 + b1)
    h = jnp.sin(h @ w2 + b2)
    h = (h @ w3).astype(jnp.float32).reshape(seq_len, HY_ORDER, 2, HY_WIDTH)
    window = jnp.exp(-t[:, None, None, None] * jnp.abs(decay.astype(jnp.float32))[None, None])
    return h * window


def bidir_fftconv(u, h_fwd, h_bwd, bias):
    seq_len, ch = h_fwd.shape
    g = jnp.concatenate([h_fwd, jnp.zeros((1, ch), jnp.float32), h_bwd[:seq_len - 1][::-1]], axis=0)
    spec = jnp.fft.rfft(u, n=2 * seq_len, axis=1) * jnp.fft.rfft(g, axis=0)[None]
    y = jnp.fft.irfft(spec, n=2 * seq_len, axis=1)[:, :seq_len]
    return y + u * bias.astype(jnp.float32)


def hyena_mixer(x_in, p):
    seq_len = x_in.shape[1]
    u = dwconv3(x_in, p['hy_short_w'], p['hy_short_b']).astype(jnp.float32)
    x1, x2, v = jnp.split(u, 3, axis=-1)
    filt = hyena_filters(seq_len, p['hy_w1'], p['hy_b1'], p['hy_w2'], p['hy_b2'], p['hy_w3'], p['hy_decay'])
    z = x1 * bidir_fftconv(v, filt[:, 0, 0], filt[:, 0, 1], p['hy_bias'][0])
    z = x2 * bidir_fftconv(z, filt[:, 1, 0], filt[:, 1, 1], p['hy_bias'][1])
    return z.astype(x_in.dtype)


def dense_attention(q, k, v, scale, sink):
    b, nk, g, lq, dq = q.shape
    dv = v.shape[-1]
    nq = lq // BLOCK
    qb = jnp.moveaxis(q.reshape(b, nk, g, nq, BLOCK, dq), 3, 0)

    def one_block(qblk):
        s = jnp.einsum('bkgqd,bkjd->bkgqj', qblk, k).astype(jnp.float32) * scale
        if sink is None:
            pr = jax.nn.softmax(s, axis=-1)
        else:
            s_sink = jnp.broadcast_to(sink.astype(jnp.float32)[None, :, :, None, None], s.shape[:-1] + (1,))
            pr = jax.nn.softmax(jnp.concatenate([s, s_sink], axis=-1), axis=-1)[..., :-1]
        return jnp.einsum('bkgqj,bkje->bkgqe', pr.astype(v.dtype), v)

    o = lax.map(one_block, qb)
    return jnp.moveaxis(o, 0, 3).reshape(b, nk, g, lq, dv)


def window_attention(q, k, v, k_ctx, v_ctx, sink):
    b, nkv, g, seq_len, d = q.shape
    nb = seq_len // BLOCK
    qb = q.reshape(b, nkv, g, nb, BLOCK, d)

    def bands(t):
        tp = jnp.pad(t, ((0, 0), (0, 0), (BLOCK, BLOCK), (0, 0))).reshape(b, nkv, nb + 2, BLOCK, d)
        return jnp.concatenate([tp[:, :, :nb], tp[:, :, 1:nb + 1], tp[:, :, 2:]], axis=3)

    kw, vw = bands(k), bands(v)
    qpos = jnp.arange(seq_len).reshape(nb, BLOCK, 1)
    kpos = (jnp.arange(nb)[:, None, None] - 1) * BLOCK + jnp.arange(3 * BLOCK)[None, None, :]
    valid = (jnp.abs(kpos - qpos) <= WINDOW) & (kpos >= 0) & (kpos < seq_len)
    scale = d ** -0.5
    s_win = jnp.einsum('bkgnqd,bknjd->bkgnqj', qb, kw).astype(jnp.float32) * scale
    s_win = jnp.where(valid, s_win, NEG_INF)
    s_ctx = jnp.einsum('bkgnqd,bkcd->bkgnqc', qb, k_ctx).astype(jnp.float32) * scale
    s_sink = jnp.broadcast_to(sink.astype(jnp.float32)[None, :, :, None, None, None], s_win.shape[:-1] + (1,))
    pr = jax.nn.softmax(jnp.concatenate([s_win, s_ctx, s_sink], axis=-1), axis=-1).astype(v.dtype)
    n_win = 3 * BLOCK
    n_ctx = k_ctx.shape[2]
    o = (jnp.einsum('bkgnqj,bknjd->bkgnqd', pr[..., :n_win], vw)
         + jnp.einsum('bkgnqc,bkcd->bkgnqd', pr[..., n_win:n_win + n_ctx], v_ctx))
    return o.reshape(b, nkv, g, seq_len, d)


def mla_keys_values(ckv, k_rope, w_uk, w_uv):
    b, seq_len, _ = ckv.shape
    k_nope = (ckv @ w_uk).reshape(b, seq_len, MLA_HEADS, MLA_NOPE).transpose(0, 2, 1, 3)
    v = (ckv @ w_uv).reshape(b, seq_len, MLA_HEADS, MLA_V).transpose(0, 2, 1, 3)
    k_r = jnp.broadcast_to(k_rope[:, None], (b, MLA_HEADS, seq_len, MLA_ROPE))
    return jnp.concatenate([k_nope, k_r], axis=-1), v


def token_mixer(h, p, cache):
    b, seq_len, _ = h.shape
    latent = cache is not None
    proj = h @ p['w_in']
    ret_in, hy_in, gqa_in, mla_in = jnp.split(proj, [RET_IN, RET_IN + HY_IN, RET_IN + HY_IN + GQA_IN], axis=-1)
    if latent:
        s0 = cache['state_ret'].astype(jnp.float32)
    else:
        s0 = jnp.zeros((b, 2, RET_HEADS, HEAD_DIM, HEAD_DIM), jnp.float32)
    y_ret, ret_state = retention_mixer(ret_in, p['ret_decay_logit'], p['ret_gn_g'], s0)
    y_hy = hyena_mixer(hy_in, p)
    nq = GQA_Q_HEADS * HEAD_DIM
    nkv = GQA_KV_HEADS * HEAD_DIM
    gq = gqa_in[..., :nq].reshape(b, seq_len, GQA_KV_HEADS, GQA_GROUPS, HEAD_DIM).transpose(0, 2, 3, 1, 4)
    gk = gqa_in[..., nq:nq + nkv].reshape(b, seq_len, GQA_KV_HEADS, HEAD_DIM).transpose(0, 2, 1, 3)
    gv = gqa_in[..., nq + nkv:].reshape(b, seq_len, GQA_KV_HEADS, HEAD_DIM).transpose(0, 2, 1, 3)
    sink = p['gqa_sink'].reshape(GQA_KV_HEADS, GQA_GROUPS)
    q_lat, kv_lat, k_rope = jnp.split(mla_in, [MLA_Q_RANK, MLA_Q_RANK + MLA_KV_RANK], axis=-1)
    mq = (rms_norm(q_lat, p['mla_q_norm']) @ p['mla_w_uq']).reshape(
        b, seq_len, MLA_HEADS, MLA_NOPE + MLA_ROPE).transpose(0, 2, 1, 3)
    q_nope, q_rope = mq[..., :MLA_NOPE], mq[..., MLA_NOPE:]
    ckv = rms_norm(kv_lat, p['mla_kv_norm'])
    if latent:
        cos_g, sin_g = axial_rope_tables(seq_len, HEAD_DIM, h.dtype)
        o_gqa = window_attention(apply_rope(gq, cos_g, sin_g), apply_rope(gk, cos_g, sin_g), gv,
                                 cache['gqa_k'], cache['gqa_v'], sink)
        cos_m, sin_m = axial_rope_tables(seq_len, MLA_ROPE, h.dtype)
        q_rope = apply_rope(q_rope, cos_m, sin_m)
        k_lat, v_lat = mla_keys_values(ckv, apply_rope(k_rope, cos_m, sin_m), p['mla_w_uk'], p['mla_w_uv'])
        k_ctx, v_ctx = mla_keys_values(cache['mla_ckv'], cache['mla_krope'], p['mla_w_uk'], p['mla_w_uv'])
        mk = jnp.concatenate([k_ctx, k_lat], axis=2)
        mv = jnp.concatenate([v_ctx, v_lat], axis=2)
        ctx_state = None
    else:
        o_gqa = dense_attention(gq, gk, gv, HEAD_DIM ** -0.5, sink)
        mk, mv = mla_keys_values(ckv, k_rope, p['mla_w_uk'], p['mla_w_uv'])
        ctx_state = (ret_state, gk, gv, ckv, k_rope)
    mq = jnp.concatenate([q_nope, q_rope], axis=-1)[:, :, None]
    o_mla = dense_attention(mq, mk, mv, (MLA_NOPE + MLA_ROPE) ** -0.5, None)[:, :, 0]
    y_gqa = o_gqa.transpose(0, 3, 1, 2, 4).reshape(b, seq_len, nq)
    y_mla = o_mla.transpose(0, 2, 1, 3).reshape(b, seq_len, MLA_HEADS * MLA_V)
    y = jnp.concatenate([y_ret, y_hy, y_gqa, y_mla], axis=-1) @ p['w_out']
    return y, ctx_state


def conv_ffn(h, p):
    gate, up = jnp.split(h @ p['ffn_w_up'], 2, axis=-1)
    gate = dwconv3(gate, p['ffn_conv_w'], p['ffn_conv_b'])
    return (jax.nn.silu(gate) * up) @ p['ffn_w_down']


def trunk_layer(x, mod, p, cache):
    shift1, scale1, gate1, shift2, scale2, gate2 = mod
    h = rms_norm(x, p['norm_g'][0]) * (1.0 + scale1) + shift1
    y, ctx_state = token_mixer(h, p, cache)
    x = x + gate1 * rms_norm(y, p['norm_g'][1])
    h = rms_norm(x, p['norm_g'][2]) * (1.0 + scale2) + shift2
    x = x + gate2 * rms_norm(conv_ffn(h, p), p['norm_g'][3])
    return x, ctx_state


def setup_inputs(seed: int = 0) -> dict:
    key = jax.random.key(seed)
    keys = jax.random.split(key, 48)
    counter = [0]

    def nrm(shape, scale):
        k = keys[counter[0]]
        counter[0] += 1
        return jax.random.normal(k, shape, jnp.float32) * scale

    ret_logit0 = jnp.log(2.0 ** (5.0 + jnp.arange(RET_HEADS, dtype=jnp.float32)) - 1.0)
    hy_decay0 = jnp.linspace(HY_DECAY_SLOW, HY_DECAY_FAST, HY_WIDTH, dtype=jnp.float32)
    return {
        'x_prompt': nrm((BATCH, SEQ, D_MODEL), 1.0),
        'x_sample': nrm((DEC_BATCH, DEC_SEQ, D_MODEL), 1.0),
        'state_ret': nrm((DEC_BATCH, DEPTH, 2, RET_HEADS, HEAD_DIM, HEAD_DIM), 1.0),
        'cache_gqa_k': nrm((DEC_BATCH, DEPTH, GQA_KV_HEADS, PAST_LEN, HEAD_DIM), 1.0),
        'cache_gqa_v': nrm((DEC_BATCH, DEPTH, GQA_KV_HEADS, PAST_LEN, HEAD_DIM), 1.0),
        'cache_mla_ckv': nrm((DEC_BATCH, DEPTH, PAST_LEN, MLA_KV_RANK), 1.0),
        'cache_mla_krope': nrm((DEC_BATCH, DEPTH, PAST_LEN, MLA_ROPE), 1.0),
        'c': nrm((DEC_BATCH, D_MODEL), 1.0),
        'c_ctx': nrm((D_MODEL,), 1.0),
        'ada_w': nrm((DEPTH, D_MODEL, 6 * D_MODEL), 0.5 * D_MODEL ** -0.5),
        'ada_b': nrm((DEPTH, 6 * D_MODEL), 0.02),
        'norm_g': 1.0 + nrm((DEPTH, 4, D_MODEL), 0.02),
        'w_in': nrm((DEPTH, D_MODEL, IN_WIDTH), D_MODEL ** -0.5),
        'ret_decay_logit': ret_logit0[None, None] + nrm((DEPTH, 2, RET_HEADS), 0.1),
        'ret_gn_g': 1.0 + nrm((DEPTH, RET_WIDTH), 0.02),
        'hy_short_w': nrm((DEPTH, 3, HY_IN), 3 ** -0.5),
        'hy_short_b': nrm((DEPTH, HY_IN), 0.02),
        'hy_w1': nrm((DEPTH, HY_POS_DIM, HY_FILTER_HIDDEN), 1.0),
        'hy_b1': nrm((DEPTH, HY_FILTER_HIDDEN), 0.1),
        'hy_w2': nrm((DEPTH, HY_FILTER_HIDDEN, HY_FILTER_HIDDEN), HY_FILTER_HIDDEN ** -0.5),
        'hy_b2': nrm((DEPTH, HY_FILTER_HIDDEN), 0.1),
        'hy_w3': nrm((DEPTH, HY_FILTER_HIDDEN, HY_ORDER * 2 * HY_WIDTH), 0.3 * HY_FILTER_HIDDEN ** -0.5),
        'hy_decay': hy_decay0[None, None] + nrm((DEPTH, 2, HY_WIDTH), 0.1),
        'hy_bias': nrm((DEPTH, HY_ORDER, HY_WIDTH), 0.1),
        'gqa_sink': nrm((DEPTH, GQA_Q_HEADS), 0.5),
        'mla_q_norm': 1.0 + nrm((DEPTH, MLA_Q_RANK), 0.02),
        'mla_kv_norm': 1.0 + nrm((DEPTH, MLA_KV_RANK), 0.02),
        'mla_w_uq': nrm((DEPTH, MLA_Q_RANK, MLA_HEADS * (MLA_NOPE + MLA_ROPE)), MLA_Q_RANK ** -0.5),
        'mla_w_uk': nrm((DEPTH, MLA_KV_RANK, MLA_HEADS * MLA_NOPE), MLA_KV_RANK ** -0.5),
        'mla_w_uv': nrm((DEPTH, MLA_KV_RANK, MLA_HEADS * MLA_V), MLA_KV_RANK ** -0.5),
        'w_out': nrm((DEPTH, MIX_WIDTH, D_MODEL), MIX_WIDTH ** -0.5),
        'ffn_w_up': nrm((DEPTH, D_MODEL, 2 * D_FF), D_MODEL ** -0.5),
        'ffn_conv_w': nrm((DEPTH, 3, D_FF), 3 ** -0.5),
        'ffn_conv_b': nrm((DEPTH, D_FF), 0.02),
        'ffn_w_down': nrm((DEPTH, D_FF, D_MODEL), D_FF ** -0.5),
    }


def reference(x_prompt, x_sample, state_ret, cache_gqa_k, cache_gqa_v, cache_mla_ckv, cache_mla_krope,
              c, c_ctx, ada_w, ada_b, norm_g, w_in, ret_decay_logit, ret_gn_g, hy_short_w, hy_short_b,
              hy_w1, hy_b1, hy_w2, hy_b2, hy_w3, hy_decay, hy_bias, gqa_sink, mla_q_norm, mla_kv_norm,
              mla_w_uq, mla_w_uk, mla_w_uv, w_out, ffn_w_up, ffn_conv_w, ffn_conv_b, ffn_w_down):
    xp = x_prompt
    xs = x_sample
    ret_states, gqa_ks, gqa_vs, mla_ckvs, mla_krs = [], [], [], [], []
    for l in range(DEPTH):
        p = {
            'norm_g': norm_g[l], 'w_in': w_in[l], 'ret_decay_logit': ret_decay_logit[l],
            'ret_gn_g': ret_gn_g[l], 'hy_short_w': hy_short_w[l], 'hy_short_b': hy_short_b[l],
            'hy_w1': hy_w1[l], 'hy_b1': hy_b1[l], 'hy_w2': hy_w2[l], 'hy_b2': hy_b2[l],
            'hy_w3': hy_w3[l], 'hy_decay': hy_decay[l], 'hy_bias': hy_bias[l], 'gqa_sink': gqa_sink[l],
            'mla_q_norm': mla_q_norm[l], 'mla_kv_norm': mla_kv_norm[l], 'mla_w_uq': mla_w_uq[l],
            'mla_w_uk': mla_w_uk[l], 'mla_w_uv': mla_w_uv[l], 'w_out': w_out[l],
            'ffn_w_up': ffn_w_up[l], 'ffn_conv_w': ffn_conv_w[l], 'ffn_conv_b': ffn_conv_b[l],
            'ffn_w_down': ffn_w_down[l],
        }
        xp, ctx_state = trunk_layer(xp, modulation(c_ctx, ada_w[l], ada_b[l]), p, None)
        ret_states.append(ctx_state[0])
        gqa_ks.append(ctx_state[1])
        gqa_vs.append(ctx_state[2])
        mla_ckvs.append(ctx_state[3])
        mla_krs.append(ctx_state[4])
        cache = {
            'state_ret': state_ret[:, l], 'gqa_k': cache_gqa_k[:, l], 'gqa_v': cache_gqa_v[:, l],
            'mla_ckv': cache_mla_ckv[:, l], 'mla_krope': cache_mla_krope[:, l],
        }
        xs, _ = trunk_layer(xs, modulation(c, ada_w[l], ada_b[l]), p, cache)
    new_state_ret = jnp.stack(ret_states, axis=1)
    new_cache_gqa_k = jnp.stack(gqa_ks, axis=1)
    new_cache_gqa_v = jnp.stack(gqa_vs, axis=1)
    new_cache_mla_ckv = jnp.stack(mla_ckvs, axis=1)
    new_cache_mla_krope = jnp.stack(mla_krs, axis=1)
    return (xp, xs, new_state_ret, new_cache_gqa_k, new_cache_gqa_v, new_cache_mla_ckv, new_cache_mla_krope)
```

```python
import numpy as np
import concourse.bass as bass
import concourse.mybir as mybir
from concourse.bass_utils import run_bass_kernel_spmd

F32 = mybir.dt.float32
BF16 = mybir.dt.bfloat16
I32 = mybir.dt.int32
ALU = mybir.AluOpType
AF = mybir.ActivationFunctionType
AX = mybir.AxisListType

COMPUTE = ("pe", "act", "dve", "pool")
NDMASEM = 6


class T:
    def __init__(self, t, nsub=1, name=""):
        self.t = t
        self.nsub = nsub
        self.name = name
        self.lw = [None] * nsub
        self.rd = [[] for _ in range(nsub)]
        self.psum = False

    def __getitem__(self, idx):
        return self.t[idx]


class Prog:
    def __init__(self, nc, ctx):
        self.nc = nc
        self.ctx = ctx
        self.ops = {e: [] for e in COMPUTE + ("sp",)}
        self.cnt = {e: 0 for e in COMPUTE}
        self.semobj = {}
        self.sem = {}
        for e in COMPUTE:
            self.semobj["s_" + e] = ctx.enter_context(nc.semaphore("s_" + e))
            self.sem[e] = "s_" + e
        self.dsem = {}
        self.dcnt = {}
        for q in ("sp", "act", "pool"):
            self.dsem[q] = []
            for i in range(NDMASEM):
                nm = "d_%s%d" % (q, i)
                self.semobj[nm] = ctx.enter_context(nc.semaphore(nm))
                self.dsem[q].append(nm)
            self.dcnt[q] = 0
        self.known = {e: {} for e in COMPUTE + ("sp",)}
        self.out_deps = []
        self.nalloc = 0
        self.free_deps = {}
        self.phase = ""
        self.pe_phase = []
        self.scopes = []
        self.nps = 0

    def sb(self, shape, dt=F32, nsub=1, name=None):
        self.nalloc += 1
        name = (name or "sb") + ("_%d" % self.nalloc)
        if self.scopes:
            st, lst = self.scopes[-1]
        else:
            st, lst = self.ctx, None
        t = st.enter_context(self.nc.sbuf_tensor(name, list(shape), dt))
        tt = T(t, nsub, name)
        if self.free_deps:
            fd = [(k, v[0], v[1]) for k, v in self.free_deps.items()]
            for i in range(nsub):
                tt.rd[i] = list(fd)
        if lst is not None:
            lst.append(tt)
        return tt

    def push(self):
        import contextlib as _cl
        st = _cl.ExitStack()
        self.scopes.append((st, []))

    def pop(self):
        st, lst = self.scopes.pop()
        for tt in lst:
            for i in range(tt.nsub):
                for d in ([tt.lw[i]] if tt.lw[i] is not None else []) + tt.rd[i]:
                    k, v, src = d
                    if self.free_deps.get(k, (0, None))[0] < v:
                        self.free_deps[k] = (v, src)
        st.close()

    def ps(self, shape, dt=F32, nsub=1, name=None):
        self.nalloc += 1
        name = name or ("ps%d" % self.nalloc)
        t = self.ctx.enter_context(self.nc.psum_tensor(name, list(shape), dt))
        tt = T(t, nsub, name)
        tt.psum = True
        return tt

    @staticmethod
    def _norm(lst):
        out = []
        for x in lst or []:
            if isinstance(x, T):
                out.extend((x, i) for i in range(x.nsub))
            else:
                t, s = x
                if s is None:
                    out.extend((t, i) for i in range(t.nsub))
                elif isinstance(s, (list, tuple, range)):
                    out.extend((t, i) for i in s)
                else:
                    out.append((t, s))
        return out

    def _collect(self, eng, reads, writes):
        deps = {}

        def add(d):
            if d is None:
                return
            key, val, src = d
            if src == eng and eng == "pe":
                return
            if deps.get(key, 0) < val:
                deps[key] = val

        for (t, s) in reads:
            add(t.lw[s])
            if t.psum:
                for d in t.rd[s]:
                    if d[2] != eng:
                        add(d)
        for (t, s) in writes:
            add(t.lw[s])
            for d in t.rd[s]:
                add(d)
        waits = []
        kn = self.known[eng]
        for key, val in deps.items():
            if kn.get(key, 0) >= val:
                continue
            kn[key] = val
            waits.append((key, val))
        return waits

    def _commit(self, dep, reads, writes):
        for (t, s) in reads:
            t.rd[s].append(dep)
        for (t, s) in writes:
            t.lw[s] = dep
            t.rd[s] = []

    def op(self, eng, fn, reads=None, writes=None):
        reads = self._norm(reads)
        writes = self._norm(writes)
        waits = self._collect(eng, reads, writes)
        self.cnt[eng] += 1
        n = self.cnt[eng]
        sem = self.sem[eng]
        dep = (sem, n, eng)
        self.ops[eng].append((waits, fn, (sem, 1)))
        if eng == "pe":
            self.pe_phase.append(self.phase)
        self._commit(dep, reads, writes)
        return dep

    def dma(self, q, out, in_, reads=None, writes=None, is_output=False):
        reads = self._norm(reads)
        writes = self._norm(writes)
        waits = self._collect(q, reads, writes)
        i = self.dcnt[q]
        self.dcnt[q] += 1
        sem = self.dsem[q][i % NDMASEM]
        prev = 16 * (i // NDMASEM)
        val = prev + 16
        kn = self.known[q]
        if prev > 0 and kn.get(sem, 0) < prev:
            kn[sem] = prev
            waits.append((sem, prev))
        dep = (sem, val, "dma")

        def fn(e, out=out, in_=in_):
            return e.dma_start(out=out, in_=in_)

        self.ops[q].append((waits, fn, (sem, 16)))
        self._commit(dep, reads, writes)
        if is_output:
            self.out_deps.append(dep)
        return dep

    def emit(self):
        nc = self.nc
        fin = []
        for (sem, val, _) in self.out_deps:
            fin.append((sem, val))
        ops = self.ops
        so = self.semobj
        with nc.Block() as block:
            def run(e, lst, extra=None):
                for (waits, fn, inc) in lst:
                    for (s, v) in waits:
                        e.wait_ge(so[s], v)
                    ins = fn(e)
                    if inc is not None:
                        ins.then_inc(so[inc[0]], inc[1])
                if extra:
                    best = {}
                    for (s, v) in extra:
                        if best.get(s, 0) < v:
                            best[s] = v
                    for s, v in best.items():
                        e.wait_ge(so[s], v)

            @block.sync
            def _(e):
                run(e, ops["sp"], fin)

            @block.tensor
            def _(e):
                run(e, ops["pe"])

            @block.scalar
            def _(e):
                run(e, ops["act"])

            @block.vector
            def _(e):
                run(e, ops["dve"])

            @block.gpsimd
            def _(e):
                run(e, ops["pool"])


import math
import contextlib
import ml_dtypes

D = 1024
NTOK = 1024
DEPTH = 4
EPS = 1e-6
DFF = 2816
NHC = 22
PI = math.pi
TWO_PI = 2.0 * math.pi
BF = ml_dtypes.bfloat16

C_ADAB = 0
C_NORM = 48
C_HSW = 80
C_HSB = 98
C_HBIAS = 104
C_GN = 108
C_QN = 110
C_KVN = 112
C_FCW = 113
C_FCB = 179
C_HB1 = 201
C_HB2 = 202
C_LG = 203
C_SINK = 215
NCOLP = 224

K_IOTA1 = 0
K_REV = 128
K_LAGP = 256
K_LAGN = 384
K_U = 512
K_LO = 640
K_REVC = 768
K_POSC = 832
NCST = 896


def kc_tile(W):
    K, N = W.shape
    return np.ascontiguousarray(W.reshape(K // 128, 128, N).transpose(1, 0, 2)).reshape(128, -1)


def col_tile(v):
    v = np.asarray(v)
    lead = v.shape[:-1]
    n = v.shape[-1] // 128
    a = v.reshape(lead + (n, 128))
    a = np.moveaxis(a, -1, 0)
    return np.ascontiguousarray(a).reshape(128, -1)


def host_consts():
    c = {}
    cst = np.zeros((128, NCST), np.float32)
    i = np.arange(128, dtype=np.float32)
    j = np.arange(128, dtype=np.float32)[:, None]
    cst[:, K_IOTA1:K_IOTA1 + 128] = (i + 1)[None, :]
    cst[:, K_REV:K_REV + 128] = (128 - i)[None, :]
    cst[:, K_LAGP:K_LAGP + 128] = np.maximum(i[None, :] - j, 0)
    cst[:, K_LAGN:K_LAGN + 128] = np.maximum(j - i[None, :], 0)
    cst[:, K_U:K_U + 128] = (i[None, :] >= j)
    cst[:, K_LO:K_LO + 128] = (j >= i[None, :])
    cst[:, K_REVC:K_REVC + 64] = (127 - j)
    cst[:, K_POSC:K_POSC + 64] = j
    c["cst"] = cst
    for nm, L in (("P", 256), ("S", 1024)):
        N = 2 * L
        jj = np.arange(L, dtype=np.float64)[:, None]
        ff = np.arange(L, dtype=np.float64)[None, :]
        ang = PI * (2 * ff + 1) * jj / N
        Cc = np.cos(ang)
        Sc = np.sin(ang)
        mats = [Cc, Sc, Cc.T * (2.0 / N), Sc.T * (2.0 / N)]
        nh = 2 if L > 512 else 1
        hwid = L // nh
        c["dft" + nm] = np.stack([np.stack([kc_tile(np.ascontiguousarray(m[:, hv * hwid:(hv + 1) * hwid]).astype(np.float32)) for hv in range(nh)]) for m in mats]).astype(BF)
        zg = np.zeros((17, 2, L), np.float32)
        tn = np.zeros((128, 2, L // 128), np.float32)
        for g in range(2):
            t = ((np.arange(L) - g) / L).astype(np.float32)
            bands = np.arange(1, 9, dtype=np.float32)
            a = (2.0 * PI) * t[:, None] * bands
            z = np.concatenate([t[:, None], np.cos(a), np.sin(a)], axis=-1).astype(np.float32)
            zg[:, g, :] = z.T
            tn[:, g, :] = -(t.reshape(L // 128, 128).T)
        c["zg" + nm] = zg
        c["tn" + nm] = tn
    Ls = 1024
    rows = (np.arange(Ls) // 64).astype(np.float32)
    cols = (np.arange(Ls) % 64).astype(np.float32)

    def tables(dim):
        q = dim // 4
        inv = (10000.0 ** (-np.arange(q, dtype=np.float32) / q)).astype(np.float32)
        ang = np.concatenate([rows[:, None] * inv, cols[:, None] * inv], axis=-1)
        return np.cos(ang).astype(np.float32), np.sin(ang).astype(np.float32)

    cg, sg = tables(64)
    rope = np.zeros((128, 4, Ls), np.float32)
    for p in range(128):
        d = p % 64
        r = d % 32
        rope[p, 0] = cg[:, r]
        rope[p, 1] = sg[:, r] * (-1.0 if d < 32 else 1.0)
    cm, sm = tables(32)
    rope[:, 2] = 1.0
    for p in range(64, 96):
        d = p - 64
        r = d % 16
        rope[p, 2] = cm[:, r]
        rope[p, 3] = sm[:, r] * (-1.0 if d < 16 else 1.0)
    c["rope"] = rope
    R = np.zeros((128, 2, 128), np.float32)
    for m in range(128):
        d = m % 64
        base = m - d
        if d < 32:
            R[base + d + 32, 0, m] = 1.0
        else:
            R[base + d - 32, 0, m] = 1.0
    for m in range(64, 96):
        d = m - 64
        if d < 16:
            R[64 + d + 16, 1, m] = 1.0
        else:
            R[64 + d - 16, 1, m] = 1.0
    c["rmat"] = R.astype(BF)
    return c


def host_weights(inp):
    w = {}
    L = DEPTH
    ada_w = inp["ada_w"]
    w["adaw"] = np.stack([np.stack([kc_tile(ada_w[l][:, b * 512:(b + 1) * 512]) for b in range(12)]) for l in range(L)])
    w_in = inp["w_in"]
    retA = np.r_[0:256, 256:512, 768:1024]
    retB = np.r_[256:768]
    hy = np.r_[1024:1792]
    gb = 1792
    mb = 2304
    qperm = np.concatenate([gb + h * 64 + np.arange(64) for h in (0, 2, 1, 3)])
    gq = np.concatenate([qperm, gb + np.r_[256:384], gb + np.r_[256:384], gb + np.r_[384:512], mb + np.r_[384:416]])
    ml = np.concatenate([mb + np.r_[0:384], mb + np.r_[320:416]])
    w["wretA"] = np.stack([kc_tile(w_in[l][:, retA]) for l in range(L)])
    w["wretB"] = np.stack([kc_tile(w_in[l][:, retB]) for l in range(L)])
    w["why"] = np.stack([kc_tile(w_in[l][:, hy]) for l in range(L)])
    w["wgqa"] = np.stack([kc_tile(w_in[l][:, gq]) for l in range(L)])
    w["wmla"] = np.stack([kc_tile(w_in[l][:, ml]) for l in range(L)])
    w["mlaw"] = np.stack([np.concatenate([kc_tile(inp["mla_w_uq"][l]), inp["mla_w_uk"][l], inp["mla_w_uv"][l]], axis=1) for l in range(L)])
    perm = np.concatenate([np.r_[0:512], 512 + np.concatenate([h * 64 + np.arange(64) for h in (0, 2, 1, 3)]), np.r_[768:1024]])
    w["wout"] = np.stack([kc_tile(inp["w_out"][l][perm, :]) for l in range(L)])
    up = inp["ffn_w_up"]
    w["wup"] = np.stack([np.stack([kc_tile(np.concatenate([up[l][:, 256 * b:256 * b + 256], up[l][:, DFF + 256 * b:DFF + 256 * b + 256]], axis=1)) for b in range(11)]) for l in range(L)])
    dn = inp["ffn_w_down"]
    w["wdn"] = np.stack([np.stack([kc_tile(dn[l][:, 128 * b:128 * b + 128]) for b in range(8)]) for l in range(L)])
    colp = np.zeros((L, 128, NCOLP), np.float32)
    rowp = np.zeros((L, 128, 512), np.float32)
    hyw = np.zeros((L, 64, 1152), np.float32)
    for l in range(L):
        colp[l, :, C_ADAB:C_ADAB + 48] = col_tile(inp["ada_b"][l])
        colp[l, :, C_NORM:C_NORM + 32] = col_tile(inp["norm_g"][l])
        colp[l, :, C_HSW:C_HSW + 18] = col_tile(inp["hy_short_w"][l])
        colp[l, :, C_HSB:C_HSB + 6] = col_tile(inp["hy_short_b"][l])
        colp[l, :, C_HBIAS:C_HBIAS + 4] = col_tile(inp["hy_bias"][l])
        colp[l, :, C_GN:C_GN + 2] = col_tile(inp["ret_gn_g"][l])
        colp[l, :, C_QN:C_QN + 2] = col_tile(inp["mla_q_norm"][l])
        colp[l, :, C_KVN:C_KVN + 1] = col_tile(inp["mla_kv_norm"][l])
        colp[l, :, C_FCW:C_FCW + 66] = col_tile(inp["ffn_conv_w"][l])
        colp[l, :, C_FCB:C_FCB + 22] = col_tile(inp["ffn_conv_b"][l])
        colp[l, 0:64, C_HB1] = inp["hy_b1"][l]
        colp[l, 0:64, C_HB2] = inp["hy_b2"][l]
        lg = inp["ret_decay_logit"][l]
        for d in range(2):
            for c in range(2):
                colp[l, 0:64, C_LG + d * 2 + c] = lg[d, 2 * c]
                colp[l, 64:128, C_LG + d * 2 + c] = lg[d, 2 * c + 1]
            for h in range(4):
                colp[l, :, C_LG + 4 + d * 4 + h] = lg[d, h]
        for h in range(4):
            colp[l, :, C_SINK + h] = inp["gqa_sink"][l, h]
        rowp[l, :, :] = inp["hy_decay"][l].reshape(1, 512)
        hyw[l, 0:17, 0:64] = inp["hy_w1"][l]
        hyw[l, :, 64:128] = inp["hy_w2"][l]
        hyw[l, :, 128:1152] = inp["hy_w3"][l]
    w["colp"] = colp
    w["rowp"] = rowp
    w["hyw"] = hyw
    return w


class Grp:
    pass


def build(dbg=False):
    nc = bass.Bass("TRN2", target_bir_lowering=False)

    def din(name, shape, dt=F32):
        return nc.dram_tensor(name, list(shape), dt, kind="ExternalInput").ap()

    def dout(name, shape, dt=F32):
        return nc.dram_tensor(name, list(shape), dt, kind="ExternalOutput").ap()

    I = {}
    I["xp"] = din("xp", [NTOK, D])
    I["xs"] = din("xs", [NTOK, D])
    I["sret"] = din("sret", [DEPTH, 2, 4, 64, 64])
    I["cgk"] = din("cgk", [DEPTH, 2, 512, 64])
    I["cgv"] = din("cgv", [DEPTH, 2, 512, 64])
    I["cckv"] = din("cckv", [DEPTH, 512, 128])
    I["ckr"] = din("ckr", [DEPTH, 512, 32])
    I["cond"] = din("cond", [2, 128, 8])
    I["adaw"] = din("adaw", [DEPTH, 12, 128, 4096])
    I["wretA"] = din("wretA", [DEPTH, 128, 8 * 768])
    I["wretB"] = din("wretB", [DEPTH, 128, 8 * 512])
    I["why"] = din("why", [DEPTH, 128, 8 * 768])
    I["wgqa"] = din("wgqa", [DEPTH, 128, 8 * 672])
    I["wmla"] = din("wmla", [DEPTH, 128, 8 * 480])
    I["mlaw"] = din("mlaw", [DEPTH, 128, 1280])
    I["wout"] = din("wout", [DEPTH, 128, 8192])
    I["wup"] = din("wup", [DEPTH, 11, 128, 4096])
    I["wdn"] = din("wdn", [DEPTH, 8, 128, 22 * 128])
    I["colp"] = din("colp", [DEPTH, 128, NCOLP])
    I["rowp"] = din("rowp", [DEPTH, 128, 512])
    I["hyw"] = din("hyw", [DEPTH, 64, 1152])
    I["cst"] = din("cst", [128, NCST])
    I["dftP"] = din("dftP", [4, 1, 128, 2 * 256], BF16)
    I["dftS"] = din("dftS", [4, 2, 128, 8 * 512], BF16)
    I["zgP"] = din("zgP", [17, 2, 256])
    I["zgS"] = din("zgS", [17, 2, 1024])
    I["tnP"] = din("tnP", [128, 2, 2])
    I["tnS"] = din("tnS", [128, 2, 8])
    I["rope"] = din("rope", [128, 4, 1024])
    I["rmat"] = din("rmat", [128, 2, 128], BF16)
    O = {}
    O["yp"] = dout("yp", [NTOK, D])
    O["ys"] = dout("ys", [NTOK, D])
    O["nsr"] = dout("nsr", [4, DEPTH, 2, 4, 64, 64])
    O["ngk"] = dout("ngk", [4, DEPTH, 2, 256, 64])
    O["ngv"] = dout("ngv", [4, DEPTH, 2, 256, 64])
    O["nckv"] = dout("nckv", [4, DEPTH, 256, 128])
    O["nkr"] = dout("nkr", [4, DEPTH, 256, 32])
    DBG = {}

    with contextlib.ExitStack() as ctx:
        P = Prog(nc, ctx)
        ident = P.sb([128, 128], F32, name="ident")
        ones_bf = P.sb([128, 128], BF16, name="ones")
        BO = P.sb([128, 128], F32, name="BO")
        cc = P.sb([128, 4], F32, name="cc")
        cst = P.sb([128, NCST], F32, name="cst")
        rmat = P.sb([128, 2, 128], BF16, name="rmat")
        NW = 2
        wring = [P.sb([128, 8192], BF16, nsub=2, name="wr%d" % i) for i in range(NW)]
        wstate = {"i": 0, "h": 0}
        colp = P.sb([128, NCOLP], F32, name="colp")
        modt = P.sb([128, 48], F32, name="modt")
        mods = P.sb([128, 6, 8], F32, name="mods")
        scond2 = P.sb([128, 8, 33], BF16, name="scond2")
        modrow = [P.sb([33, 512], F32, name="modrow0")] * 2
        condt2 = P.sb([128, 2, 8], F32, name="condt2")
        modS = P.sb([128, DEPTH, 48], F32, name="modS")
        mstate = {"ready": False}
        pss = [P.ps([128, 512], F32, name="psr%d" % i) for i in range(8)]
        acc = [pss[6], pss[7]]
        pstate = {"i": 0, "n": 8}
        evs = {"i": 0}

        def pst():
            t = pss[pstate["i"] % pstate["n"]]
            pstate["i"] += 1
            return t

        def wslot():
            t = wring[wstate["i"] % NW]
            wstate["i"] += 1
            return t

        def wload(src2d, n, q="pool"):
            t = wslot()
            P.dma(q, t[:, 0:n], src2d, writes=[t])
            return t

        def wload_h(src2d, n):
            k = wstate["h"] % (2 * NW)
            wstate["h"] += 1
            t, hf = wring[k // 2], k % 2
            P.dma("pool", t[:, hf * 4096:hf * 4096 + n], src2d, writes=[(t, hf)])
            return t, hf, hf * 4096

        def mm(out, lhsT, rhs, start, stop, rd, wr):
            P.op("pe", lambda e: e.matmul(out, lhsT=lhsT, rhs=rhs, start=start, stop=stop, skip_group_check=True), reads=rd, writes=wr)

        def tr(out, in_, k, rd, wr):
            P.op("pe", lambda e: e.transpose(out=out, in_=in_, identity=ident[0:k, 0:k]), reads=rd + [ident], writes=wr)

        def cp(eng, out, in_, rd, wr):
            if eng == "act":
                P.op("act", lambda e: e.activation(out=out, in_=in_, func=AF.Copy), reads=rd, writes=wr)
            else:
                P.op(eng, lambda e: e.tensor_copy(out=out, in_=in_), reads=rd, writes=wr)

        def evac(out, in_, rd, wr):
            evs["i"] += 1
            cp("act" if evs["i"] % 2 else "dve", out, in_, rd, wr)

        def act(out, in_, func, rd, wr, scale=None, bias=None):
            kw = {}
            if scale is not None:
                kw["scale"] = scale
            if bias is not None:
                kw["bias"] = bias
            P.op("act", lambda e: e.activation(out=out, in_=in_, func=func, **kw), reads=rd, writes=wr)

        def tt(eng, out, in0, in1, op, rd, wr):
            P.op(eng, lambda e: e.tensor_tensor(out=out, in0=in0, in1=in1, op=op), reads=rd, writes=wr)

        def ts(eng, out, in0, s1, s2, op0, op1, rd, wr):
            if op1 is None:
                P.op(eng, lambda e: e.tensor_scalar(out=out, in0=in0, scalar1=s1, scalar2=None, op0=op0), reads=rd, writes=wr)
            else:
                P.op(eng, lambda e: e.tensor_scalar(out=out, in0=in0, scalar1=s1, scalar2=s2, op0=op0, op1=op1), reads=rd, writes=wr)

        def stt(out, in0, scalar, in1, op0, op1, rd, wr):
            P.op("dve", lambda e: e.scalar_tensor_tensor(out=out, in0=in0, scalar=scalar, in1=in1, op0=op0, op1=op1), reads=rd, writes=wr)

        def recip(out, in_, rd, wr):
            P.op("dve", lambda e: e.reciprocal(out=out, in_=in_), reads=rd, writes=wr)

        def memset(eng, ap, val, wr):
            P.op(eng, lambda e: e.memset(ap, val), writes=wr)

        def dump(name, tile, ap, shape, dt=F32):
            if not dbg:
                return
            d = dout("dbg_" + name, shape, F32)
            DBG[name] = d
            P.dma("pool" if dt != F32 else "sp", d, ap, reads=[tile], is_output=True)

        P.dma("sp", cst[:], I["cst"], writes=[cst])
        P.dma("sp", rmat[:], I["rmat"], writes=[rmat])
        memset("pool", ident[:], 1.0, [ident])
        P.op("pool", lambda e: e.affine_select(out=ident[:], in_=ident[:], pattern=[[-1, 128]], compare_op=ALU.is_equal, fill=0.0, base=0, channel_multiplier=1), reads=[ident], writes=[ident])
        memset("dve", ones_bf[:], 1.0, [ones_bf])
        memset("dve", BO[:], 0.0, [BO])
        memset("dve", BO[0:64, 0:64], 1.0 / 64, [BO])
        memset("dve", BO[64:128, 64:128], 1.0 / 64, [BO])
        BD = P.sb([128, 128], F32, name="BD")
        memset("dve", BD[:], 0.0, [BD])
        memset("dve", BD[0:64, 0:64], 1.0, [BD])
        memset("dve", BD[64:128, 64:128], 1.0, [BD])
        memset("dve", cc[:, 0:1], EPS, [cc])
        memset("dve", cc[:, 1:2], 1.0, [cc])
        memset("dve", cc[:, 2:3], 0.0, [cc])
        epsc = cc[:, 0:1]

        def s2(t, c, tti):
            return (t, c * 2 + tti)

        def sc(t, c):
            return (t, [c * 2, c * 2 + 1])

        def rms_rstd(srcs, tti, dim, rstd, sqb):
            ps = pst()
            n = len(srcs)
            for i, (t, ap, sub) in enumerate(srcs):
                sq = sqb[i % 2]
                if i % 2 == 0:
                    act(sq[:], ap, AF.Square, [(t, sub)], [sq])
                else:
                    tt("pool", sq[:], ap, ap, ALU.mult, [(t, sub)], [sq])
                mm(ps[:, :], ones_bf[:, :], sq[:], i == 0, i == n - 1, [ones_bf, sq], [ps])
            act(rstd[:], ps[:, :], AF.Sqrt, [ps, cc], [rstd], scale=1.0 / dim, bias=epsc)
            recip(rstd[:], rstd[:], [rstd], [rstd])

        def run_group(G):
            L, NSEQ, nch = G.L, G.NSEQ, G.L // 128
            SAMPLE = G.sample
            P.push()
            x = P.sb([128, 8, NTOK], F32, nsub=16, name="x")
            ymix = P.sb([128, 8, NTOK], BF16, nsub=16, name="ymix")
            rstd2 = [P.sb([128, 512], F32, name="rstdb%d" % i) for i in range(2)]
            rstd = rstd2[0]
            sqbig = P.sb([128, 8, 512], BF16, nsub=2, name="sqbig")
            xr = P.sb([128, 4, 512], F32, nsub=4, name="xr")
            tmpf = [P.sb([128, 512], F32, name="tmpf%d" % i) for i in range(2)]
            tst = {"i": 0}

            def tmp():
                t = tmpf[tst["i"] % 2]
                tst["i"] += 1
                return t

            P.phase = "io"
            P.push()
            xt = [P.sb([128, D], F32, name="xt%d" % i) for i in range(2)]
            for blk in range(8):
                xb_ = xt[blk % 2]
                P.dma("sp", xb_[:], G.x_in[blk * 128:(blk + 1) * 128, :], writes=[xb_])
                for half in range(2):
                    ps = pst()
                    for c4 in range(4):
                        c = half * 4 + c4
                        tr(ps[:, c4 * 128:(c4 + 1) * 128], xb_[:, c * 128:(c + 1) * 128], 128, [xb_], [ps])
                    evac(x.t[:, half * 4:half * 4 + 4, blk * 128:(blk + 1) * 128],
                         ps[:, :].rearrange("p (a b) -> p a b", a=4), [ps],
                         [(x, [(half * 4 + c4) * 2 + blk // 4 for c4 in range(4)])])
            P.pop()
            first_group = not mstate["ready"]
            if first_group:
                P.dma("sp", condt2[:], I["cond"].rearrange("g p c -> p g c"), writes=[condt2])
                memset("dve", scond2[:], 0.0, [scond2])
                act(scond2.t[:, :, 0], condt2.t[:, G.gi, :], AF.Silu, [condt2], [scond2])
                act(scond2.t[:, :, 32], condt2.t[:, 1 - G.gi, :], AF.Silu, [condt2], [scond2])

            import os as _os
            for l in range(int(_os.environ.get('MK_DEPTH', DEPTH))):
                P.phase = "mod"
                P.dma("sp", colp[:], I["colp"][l], writes=[colp])
                if first_group:
                    psm, psm2 = pss[7], pss[6]
                    pstate["n"] = 6
                    for b in range(12):
                        w, whf, wo = wload_h(I["adaw"][l, b], 4096)
                        wv = w.t[:, wo:wo + 4096].rearrange("p (k n) -> p k n", k=8)
                        ps = pst()
                        for kc in range(8):
                            mm(ps[0:33, :], scond2.t[:, kc, :], wv[:, kc, :], kc == 0, kc == 7, [(w, whf), scond2], [ps])
                        mr = modrow[b % 2]
                        evac(mr[0:33, :], ps[0:33, :], [ps], [mr])
                        for j4 in range(4):
                            j = b * 4 + j4
                            mm(psm[:, j:j + 1], mr[0:1, j4 * 128:(j4 + 1) * 128], cc[0:1, 1:2], True, True, [mr, cc], [psm])
                            mm(psm2[:, j:j + 1], mr[32:33, j4 * 128:(j4 + 1) * 128], cc[32:33, 1:2], True, True, [mr, cc], [psm2])
                    pstate["n"] = 8
                    tt("dve", modt[:], psm[:, 0:48], colp[:, C_ADAB:C_ADAB + 48], ALU.add, [psm, colp], [modt])
                    tt("dve", modS.t[:, l, :], psm2[:, 0:48], colp[:, C_ADAB:C_ADAB + 48], ALU.add, [psm2, colp], [modS])
                else:
                    cp("dve", modt[:], modS.t[:, l, :], [modS], [modt])
                stt(mods.t[:, 0, :], modt[:, 8:16], 1.0, colp[:, C_NORM:C_NORM + 8], ALU.add, ALU.mult, [modt, colp], [mods])
                cp("dve", mods.t[:, 1, :], modt[:, 0:8], [modt], [mods])
                tt("dve", mods.t[:, 2, :], modt[:, 16:24], colp[:, C_NORM + 8:C_NORM + 16], ALU.mult, [modt, colp], [mods])
                stt(mods.t[:, 3, :], modt[:, 32:40], 1.0, colp[:, C_NORM + 16:C_NORM + 24], ALU.add, ALU.mult, [modt, colp], [mods])
                cp("dve", mods.t[:, 4, :], modt[:, 24:32], [modt], [mods])
                tt("dve", mods.t[:, 5, :], modt[:, 40:48], colp[:, C_NORM + 24:C_NORM + 32], ALU.mult, [modt, colp], [mods])

                def big_stats(src, tti):
                    cs = slice(tti * 512, (tti + 1) * 512)
                    act(sqbig.t[:, 0:4, :], src.t[:, 0:4, cs], AF.Square, [(src, [c * 2 + tti for c in range(4)])], [(sqbig, 0)])
                    tt("pool", sqbig.t[:, 4:8, :], src.t[:, 4:8, cs], src.t[:, 4:8, cs], ALU.mult, [(src, [c * 2 + tti for c in range(4, 8)])], [(sqbig, 1)])
                    ps = pst()
                    for c in range(8):
                        mm(ps[:, :], ones_bf[:, :], sqbig.t[:, c, :], c == 0, c == 7, [ones_bf, (sqbig, c // 4)], [ps])
                    r_ = rstd2[tti]
                    act(r_[:], ps[:, :], AF.Sqrt, [ps, cc], [r_], scale=1.0 / D, bias=epsc)
                    recip(r_[:], r_[:], [r_], [r_])

                def norm_mod(src, ai, bi, dst):
                    for tti in range(2):
                        big_stats(src, tti)
                    for tti in range(2):
                        cs = slice(tti * 512, (tti + 1) * 512)
                        for hv in range(2):
                            tt("dve", xr[:], src.t[:, hv * 4:hv * 4 + 4, cs], rstd2[tti].t[:, None, :].to_broadcast([128, 4, 512]), ALU.mult,
                               [(src, [c * 2 + tti for c in range(hv * 4, hv * 4 + 4)]), rstd2[tti]], [xr])
                            for c4 in range(4):
                                c = hv * 4 + c4
                                if c % 2 == 0:
                                    act(dst.t[:, c, cs], xr.t[:, c4, :], AF.Identity, [(xr, c4), mods], [s2(dst, c, tti)], scale=mods.t[:, ai, c:c + 1], bias=mods.t[:, bi, c:c + 1])
                                else:
                                    ts("pool", dst.t[:, c, cs], xr.t[:, c4, :], mods.t[:, ai, c:c + 1], mods.t[:, bi, c:c + 1], ALU.mult, ALU.add, [(xr, c4), mods], [s2(dst, c, tti)])

                P.push()
                h = P.sb([128, 8, NTOK], BF16, nsub=16, name="h")
                P.phase = "norm1"
                norm_mod(x, 0, 1, h)

                def proj_fm(w, wv, col0, M, evf):
                    for tti in range(2):
                        ps = pst()
                        for kc in range(8):
                            mm(ps[0:M, :], wv[:, kc, col0:col0 + M], h.t[:, kc, tti * 512:(tti + 1) * 512], kc == 0, kc == 7, [w, s2(h, kc, tti)], [ps])
                        evf(ps, tti)

                def proj_tm(w, wv, col0, n, evf):
                    for blk in range(8):
                        ps = pst()
                        for kc in range(8):
                            mm(ps[:, 0:n], h.t[:, kc, blk * 128:(blk + 1) * 128], wv[:, kc, col0:col0 + n], kc == 0, kc == 7, [w, s2(h, kc, blk // 4)], [ps])
                        evf(ps, blk)

                def retention():
                    P.push()
                    qf = P.sb([128, 2, NTOK], BF16, name="qf")
                    qb = P.sb([128, 2, NTOK], BF16, name="qb")
                    sg = P.sb([128, 2, NTOK], BF16, name="sg")
                    ktm = P.sb([128, 8, 256], BF16, nsub=8, name="ktm")
                    vtm = P.sb([128, 8, 256], BF16, nsub=8, name="vtm")
                    vf = P.sb([128, 8, 256], BF16, nsub=8, name="vf")
                    vb = P.sb([128, 8, 256], BF16, nsub=8, name="vb")
                    lg = P.sb([128, 12], F32, name="lg")
                    patf = P.sb([128, 2, 128], F32, name="patf")
                    patb = P.sb([128, 2, 128], F32, name="patb")
                    cdc = P.sb([128, 4], F32, name="cdc")
                    DM = P.sb([128, 512], F32, name="DM")
                    kdp = P.sb([128, 2, 256], F32, name="kdp")
                    tfb = P.sb([128, 2, 128], F32, name="tfb")
                    S = P.sb([128, NSEQ * 2, 2, 128], F32, nsub=NSEQ * 2, name="S")
                    Sbf = P.sb([128, 8, 4, 128], BF16, nsub=16, name="Sbf")
                    SD = P.sb([128, 8, 512], BF16, nsub=8, name="SD")
                    P.push()
                    qTz = P.sb([128, 4, NTOK], BF16, name="qTz")
                    memset("pool", qTz[:], 0.0, [qTz])
                    kT = P.sb([128, 2, NTOK], BF16, name="kT")
                    act(lg[:], colp[:, C_LG:C_LG + 12], AF.Exp, [colp], [lg], scale=-1.0)
                    ts("dve", lg[:], lg[:], 1.0, None, ALU.add, None, [lg], [lg])
                    act(lg[:], lg[:], AF.Ln, [lg], [lg])
                    ts("dve", lg[:], lg[:], -1.0, None, ALU.mult, None, [lg], [lg])
                    for c in range(2):
                        act(patf.t[:, c, :], cst[:, K_IOTA1:K_IOTA1 + 128], AF.Exp, [cst, lg], [patf], scale=lg[:, c:c + 1])
                        act(patb.t[:, c, :], cst[:, K_REV:K_REV + 128], AF.Exp, [cst, lg], [patb], scale=lg[:, 2 + c:3 + c])
                    act(cdc[:], lg[:, 0:4], AF.Exp, [lg], [cdc], scale=128.0)
                    for hh in range(4):
                        act(tfb.t[:, 0, :], cst[:, K_LAGP:K_LAGP + 128], AF.Exp, [cst, lg], [tfb], scale=lg[:, 4 + hh:5 + hh])
                        act(tfb.t[:, 1, :], cst[:, K_LAGN:K_LAGN + 128], AF.Exp, [cst, lg], [tfb], scale=lg[:, 8 + hh:9 + hh])
                        stt(tfb.t[:, 0, :], tfb.t[:, 0, :], 0.125, cst[:, K_U:K_U + 128], ALU.mult, ALU.mult, [tfb, cst], [tfb])
                        stt(tfb.t[:, 1, :], tfb.t[:, 1, :], 0.125, cst[:, K_LO:K_LO + 128], ALU.mult, ALU.mult, [tfb, cst], [tfb])
                        tt("dve", DM[:, hh * 128:(hh + 1) * 128], tfb.t[:, 0, :], tfb.t[:, 1, :], ALU.add, [tfb], [DM])
                        act(kdp.t[:, 0, hh * 64:(hh + 1) * 64], cst[:, K_REVC:K_REVC + 64], AF.Exp, [cst, lg], [kdp], scale=lg[:, 4 + hh:5 + hh])
                        act(kdp.t[:, 1, hh * 64:(hh + 1) * 64], cst[:, K_POSC:K_POSC + 64], AF.Exp, [cst, lg], [kdp], scale=lg[:, 8 + hh:9 + hh])
                    ts("dve", kdp[:], kdp[:], 0.125, None, ALU.mult, None, [kdp], [kdp])
                    RS = 99
                    if RS <= 1:
                        P.pop()
                        return
                    w = wload(I["wretA"][l], 6144)
                    wv = w.t[:, 0:6144].rearrange("p (k n) -> p k n", k=8)
                    for c in range(2):
                        def ev_q(ps, tti, c=c):
                            cs = slice(tti * 512, (tti + 1) * 512)
                            for hf in range(2):
                                cp("act", qTz.t[hf * 64:(hf + 1) * 64, 2 * c + hf, cs], ps[hf * 64:(hf + 1) * 64, :], [ps], [qTz])
                            p3 = ps[:, :].rearrange("p (a b) -> p a b", a=4)
                            tt("dve", qf.t[:, c, cs].rearrange("p (a b) -> p a b", a=4), p3, patf.t[:, c:c + 1, :].to_broadcast([128, 4, 128]), ALU.mult, [ps, patf], [qf])
                            tt("dve", qb.t[:, c, cs].rearrange("p (a b) -> p a b", a=4), p3, patb.t[:, c:c + 1, :].to_broadcast([128, 4, 128]), ALU.mult, [ps, patb], [qb])
                        SUB = _os.environ.get("MK_RET_SUB", "qkg")
                        if "q" in SUB:
                            proj_fm(w, wv, c * 128, 128, ev_q)

                        def ev_k(ps, tti, c=c):
                            evac(kT.t[:, c, tti * 512:(tti + 1) * 512], ps[:, :], [ps], [kT])
                        if "k" in SUB:
                            proj_fm(w, wv, 256 + c * 128, 128, ev_k)

                        def ev_g(ps, tti, c=c):
                            act(sg.t[:, c, tti * 512:(tti + 1) * 512], ps[:, :], AF.Silu, [ps], [sg])
                        if "g" in SUB:
                            proj_fm(w, wv, 512 + c * 128, 128, ev_g)
                    if RS <= 2:
                        P.pop()
                        return
                    w2 = wload(I["wretB"][l], 4096)
                    wv2 = w2.t[:, 0:4096].rearrange("p (k n) -> p k n", k=8)

                    def ev_tm(ps, blk):
                        cp("act", ktm.t[:, blk, :], ps[:, 0:256], [ps], [(ktm, blk)])
                        cp("act", vtm.t[:, blk, :], ps[:, 256:512], [ps], [(vtm, blk)])
                        tt("dve", vf.t[:, blk, :], ps[:, 256:512], kdp.t[:, 0, :], ALU.mult, [ps, kdp], [(vf, blk)])
                        tt("dve", vb.t[:, blk, :], ps[:, 256:512], kdp.t[:, 1, :], ALU.mult, [ps, kdp], [(vb, blk)])
                    proj_tm(w2, wv2, 0, 512, ev_tm)
                    if RS <= 3:
                        P.pop()
                        return
                    for blk in range(8):
                        bc = slice(blk * 128, (blk + 1) * 128)
                        ps = pst()
                        for hh in range(4):
                            c, po = hh // 2, (hh % 2) * 64
                            mm(ps[:, hh * 128:(hh + 1) * 128], kT.t[:, c, bc], qTz.t[:, hh, bc], True, True, [kT, qTz], [ps])
                        tt("dve", SD.t[:, blk, :], ps[:, :], DM[:], ALU.mult, [ps, DM], [(SD, blk)])
                    if RS <= 4:
                        P.pop()
                        return
                    P.pop()
                    osb = P.sb([128, 2, NTOK], F32, nsub=4, name="osb")
                    memset("pool", S[:], 0.0, [S])
                    if SAMPLE:
                        for dr in range(2):
                            for c in range(2):
                                for hf in range(2):
                                    hh = 2 * c + hf
                                    P.dma("sp", S.t[hf * 64:(hf + 1) * 64, dr, c, hf * 64:(hf + 1) * 64], I["sret"][l, dr, hh], writes=[(S, dr)])
                    for step in range(nch):
                        for s in range(NSEQ):
                            for dr in range(2):
                                n = step if dr == 0 else nch - 1 - step
                                ch = s * 2 + dr
                                blk = s * nch + n
                                tt("pool", Sbf.t[:, blk, dr * 2:dr * 2 + 2, :], S.t[:, ch, :, :], BD.t[:, None, :].to_broadcast([128, 2, 128]), ALU.mult, [(S, ch), BD], [(Sbf, blk * 2 + dr)])
                                ps = pst()
                                vv = vf if dr == 0 else vb
                                for c in range(2):
                                    mm(ps[:, c * 128:(c + 1) * 128], ktm.t[:, blk, c * 128:(c + 1) * 128], vv.t[:, blk, c * 128:(c + 1) * 128], True, True, [(ktm, blk), (vv, blk)], [ps])
                                for c in range(2):
                                    stt(S.t[:, ch, c, :], S.t[:, ch, c, :], cdc[:, dr * 2 + c:dr * 2 + c + 1], ps[:, c * 128:(c + 1) * 128], ALU.mult, ALU.add, [(S, ch), cdc, ps], [(S, ch)])
                    if not SAMPLE:
                        for s in range(NSEQ):
                            for dr in range(2):
                                for c in range(2):
                                    for hf in range(2):
                                        hh = 2 * c + hf
                                        P.dma("sp", O["nsr"][s, l, dr, hh], S.t[hf * 64:(hf + 1) * 64, s * 2 + dr, c, hf * 64:(hf + 1) * 64], reads=[(S, s * 2 + dr)], is_output=True)
                    if RS <= 5:
                        P.pop()
                        return
                    for blk in range(8):
                        bc = slice(blk * 128, (blk + 1) * 128)
                        ps = pst()
                        for c in range(2):
                            for hf in range(2):
                                hh = 2 * c + hf
                                po = hf * 64
                                o_ = ps[:, (c * 2 + hf) * 128:(c * 2 + hf + 1) * 128]
                                mm(o_, vtm.t[:, blk, c * 128:(c + 1) * 128], SD.t[:, blk, hh * 128:(hh + 1) * 128], True, False, [(vtm, blk), (SD, blk)], [ps])
                                mm(o_, Sbf.t[:, blk, c, :], qf.t[:, c, bc], False, False, [(Sbf, blk * 2), qf], [ps])
                                mm(o_, Sbf.t[:, blk, 2 + c, :], qb.t[:, c, bc], False, True, [(Sbf, blk * 2 + 1), qb], [ps])
                        for c in range(2):
                            for hf in range(2):
                                po = hf * 64
                                evac(osb.t[po:po + 64, c, bc], ps[po:po + 64, (c * 2 + hf) * 128:(c * 2 + hf + 1) * 128], [ps], [(osb, c * 2 + blk // 4)])
                    if RS <= 6:
                        P.pop()
                        return
                    cen = tmpf[0]
                    sq = tmpf[1]
                    rs = rstd2[1]
                    for c in range(2):
                        for tti in range(2):
                            cs = slice(tti * 512, (tti + 1) * 512)
                            ps = pst()
                            mm(ps[:, :], BO[:, :], osb.t[:, c, cs], True, True, [BO, (osb, c * 2 + tti)], [ps])
                            tt("dve", cen[:], osb.t[:, c, cs], ps[:, :], ALU.subtract, [(osb, c * 2 + tti), ps], [cen])
                            act(sq[:], cen[:], AF.Square, [cen], [sq])
                            ps2 = pst()
                            mm(ps2[:, :], BO[:, :], sq[:], True, True, [BO, sq], [ps2])
                            act(rs[:], ps2[:, :], AF.Sqrt, [ps2, cc], [rs], scale=1.0, bias=epsc)
                            recip(rs[:], rs[:], [rs], [rs])
                            tt("dve", cen[:], cen[:], rs[:], ALU.mult, [cen, rs], [cen])
                            stt(ymix.t[:, c, cs], cen[:], colp[:, C_GN + c:C_GN + c + 1], sg.t[:, c, cs], ALU.mult, ALU.mult, [cen, colp, sg], [s2(ymix, c, tti)])
                    P.pop()

                def gqa():
                    P.push()
                    NCTX = 512 if SAMPLE else 0
                    NBK = 8 + (4 if SAMPLE else 0)
                    qT = P.sb([128, 2, NTOK], BF16, name="gqT")
                    kT = P.sb([128, 2, NCTX + NTOK], BF16, name="gkT")
                    memset("pool", kT[:], 0.0, [kT])
                    vaug = P.sb([128, NBK, 2, 192], BF16, name="gva")
                    tmo = P.sb([128, 8, 288], F32, name="tmo")
                    PT = [P.sb([128, 512], BF16, name="PT%d" % i) for i in range(3)]
                    pti = {"i": 0}
                    den = tmpf[0]

                    def nPT():
                        t = PT[pti["i"] % 3]
                        pti["i"] += 1
                        return t
                    memset("pool", vaug[:], 1.0, [vaug])
                    w = wload(I["wgqa"][l], 8 * 672)
                    wv = w.t[:, 0:8 * 672].rearrange("p (k n) -> p k n", k=8)
                    if SAMPLE:
                        ropt = P.sb([128, 2, NTOK], F32, name="ropt")
                        P.dma("sp", ropt[:], I["rope"][:, 0:2, :], writes=[ropt])
                        xfs = [P.sb([128, 512], F32, name="xf%d" % i) for i in range(2)]
                        xbs = [P.sb([128, 512], BF16, name="xb%d" % i) for i in range(2)]
                        t1s = [P.sb([128, 512], F32, name="t1%d" % i) for i in range(2)]
                        t2s = [P.sb([128, 512], F32, name="t2%d" % i) for i in range(2)]
                        rst = {"i": 0}

                        def ev_rope(dst_fn):
                            def f(ps, tti):
                                k_ = rst["i"] % 2
                                rst["i"] += 1
                                xf, xb, t1, t2 = xfs[k_], xbs[k_], t1s[k_], t2s[k_]
                                cs = slice(tti * 512, (tti + 1) * 512)
                                cp("act", xf[:], ps[:, :], [ps], [xf])
                                cp("dve", xb[:], ps[:, :], [ps], [xb])
                                ps2 = pst()
                                mm(ps2[:, :], rmat.t[:, 0, :], xb[:], True, True, [rmat, xb], [ps2])
                                tt("pool", t1[:], xf[:], ropt.t[:, 0, cs], ALU.mult, [xf, ropt], [t1])
                                tt("dve", t2[:], ps2[:, :], ropt.t[:, 1, cs], ALU.mult, [ps2, ropt], [t2])
                                for (dst, ap, r0, r1) in dst_fn(cs):
                                    tt("pool", ap, t1[r0:r1, :], t2[r0:r1, :], ALU.add, [t1, t2], [dst])
                            return f
                        for c in range(2):
                            proj_fm(w, wv, c * 128, 128, ev_rope(lambda cs, c=c: [(qT, qT.t[:, c, cs], 0, 128)]))
                        proj_fm(w, wv, 256, 128, ev_rope(lambda cs: [(kT, kT.t[kv_ * 64:(kv_ + 1) * 64, kv_, NCTX + cs.start:NCTX + cs.stop], kv_ * 64, (kv_ + 1) * 64) for kv_ in range(2)]))
                    else:
                        for c in range(2):
                            proj_fm(w, wv, c * 128, 128, lambda ps, tti, c=c: evac(qT.t[:, c, tti * 512:(tti + 1) * 512], ps[:, :], [ps], [qT]))
                        def ev_gk(ps, tti):
                            for kv_ in range(2):
                                evac(kT.t[kv_ * 64:(kv_ + 1) * 64, kv_, tti * 512:(tti + 1) * 512], ps[kv_ * 64:(kv_ + 1) * 64, :], [ps], [kT])
                        proj_fm(w, wv, 256, 128, ev_gk)
                    nb0 = 4 if SAMPLE else 0

                    def ev_tm(ps, blk):
                        if not SAMPLE:
                            cp("act", tmo.t[:, blk, :], ps[:, 0:288], [ps], [tmo])
                        cp("dve", vaug.t[:, nb0 + blk, :, 64:128], ps[:, 128:256].rearrange("p (k d) -> p k d", k=2), [ps], [vaug])
                    proj_tm(w, wv, 384, 288, ev_tm)
                    if not SAMPLE:
                        for s in range(NSEQ):
                            for k_ in range(2):
                                P.dma("sp", O["ngk"][s, l, k_].rearrange("(b p) d -> p b d", p=128),
                                      tmo.t[:, 2 * s:2 * s + 2, k_ * 64:(k_ + 1) * 64], reads=[tmo], is_output=True)
                                P.dma("sp", O["ngv"][s, l, k_].rearrange("(b p) d -> p b d", p=128),
                                      tmo.t[:, 2 * s:2 * s + 2, 128 + k_ * 64:128 + (k_ + 1) * 64], reads=[tmo], is_output=True)
                            P.dma("sp", O["nkr"][s, l].rearrange("(b p) d -> p b d", p=128),
                                  tmo.t[:, 2 * s:2 * s + 2, 256:288], reads=[tmo], is_output=True)
                    if SAMPLE:
                        ctm = P.sb([128, 4, 2, 64], F32, name="ctm")
                        cvm = P.sb([128, 4, 2, 64], F32, name="cvm")
                        for k_ in range(2):
                            P.dma("sp", ctm.t[:, :, k_, :], I["cgk"][l, k_].rearrange("(c p) d -> p c d", p=128), writes=[ctm])
                            P.dma("sp", cvm.t[:, :, k_, :], I["cgv"][l, k_].rearrange("(c p) d -> p c d", p=128), writes=[cvm])
                        ps = pst()
                        for cb in range(4):
                            tr(ps[:, cb * 128:(cb + 1) * 128], ctm.t[:, cb, :, :].rearrange("p k d -> p (k d)"), 128, [ctm], [ps])
                        for kv_ in range(2):
                            evac(kT.t[kv_ * 64:(kv_ + 1) * 64, kv_, 0:512], ps[kv_ * 64:(kv_ + 1) * 64, :], [ps], [kT])
                        cp("pool", vaug.t[:, 0:4, :, 64:128], cvm[:], [cvm], [vaug])

                    def normalize(pso, po, ncols, hh, cq, c0):
                        nr = slice(po, po + 64)
                        dr_ = slice(64 - po, 128 - po)
                        ts("dve", den[nr, 0:ncols], pso[dr_, 0:ncols], colp[nr, C_SINK + hh:C_SINK + hh + 1], None, ALU.add, None, [pso, colp], [den])
                        recip(den[nr, 0:ncols], den[nr, 0:ncols], [den], [den])
                        tt("dve", ymix.t[nr, 4 + cq, c0:c0 + ncols], pso[nr, 0:ncols], den[nr, 0:ncols], ALU.mult, [pso, den], [sc(ymix, 4 + cq)])

                    act(colp[:, C_SINK:C_SINK + 4], colp[:, C_SINK:C_SINK + 4], AF.Exp, [colp], [colp])
                    if not SAMPLE:
                        for s in range(NSEQ):
                            for hh in range(4):
                                cq, kv = hh % 2, hh // 2
                                po = kv * 64
                                vs = slice(64, 192) if po == 0 else slice(0, 128)
                                ps = pst()
                                for kb in range(2):
                                    kc_ = slice(s * 256 + kb * 128, s * 256 + (kb + 1) * 128)
                                    mm(ps[:, kb * 256:(kb + 1) * 256], kT.t[:, kv, kc_], qT.t[:, cq, s * 256:(s + 1) * 256], True, True, [kT, qT], [ps])
                                pt = nPT()
                                act(pt[:], ps[:, :], AF.Exp, [ps], [pt], scale=0.125)
                                pso = pst()
                                for kb in range(2):
                                    mm(pso[:, 0:256], vaug.t[:, s * 2 + kb, kv, vs], pt[:, kb * 256:(kb + 1) * 256], kb == 0, kb == 1, [vaug, pt], [pso])
                                normalize(pso, po, 256, hh, cq, s * 256)
                    else:
                        for hh in range(4):
                            cq, kv = hh % 2, hh // 2
                            po = kv * 64
                            vs = slice(64, 192) if po == 0 else slice(0, 128)
                            acc = [pss[4 + 2 * (hh % 2)], pss[5 + 2 * (hh % 2)]]
                            for cb in range(4):
                                for tti in range(2):
                                    ps = pst()
                                    mm(ps[:, :], kT.t[:, kv, cb * 128:(cb + 1) * 128], qT.t[:, cq, tti * 512:(tti + 1) * 512], True, True, [kT, qT], [ps])
                                    pt = nPT()
                                    act(pt[:], ps[:, :], AF.Exp, [ps], [pt], scale=0.125)
                                    mm(acc[tti][:, :], vaug.t[:, cb, kv, vs], pt[:], cb == 0, False, [vaug, pt], [acc[tti]])
                            for m in range(8):
                                qlo, qhi = max(m - 1, 0), min(m + 1, 7)
                                n = (qhi - qlo + 1) * 128
                                ps = pst()
                                mm(ps[:, 0:n], kT.t[:, kv, 512 + m * 128:512 + (m + 1) * 128], qT.t[:, cq, qlo * 128:qlo * 128 + n], True, True, [kT, qT], [ps])
                                pt = nPT()
                                act(pt[:, 0:n], ps[:, 0:n], AF.Exp, [ps], [pt], scale=0.125)
                                if m - 1 >= 0:
                                    o0 = (m - 1 - qlo) * 128
                                    P.op("pool", lambda e, pt=pt, o0=o0: e.affine_select(out=pt[:, o0:o0 + 128], in_=pt[:, o0:o0 + 128], pattern=[[1, 128]], compare_op=ALU.is_ge, fill=0.0, base=0, channel_multiplier=-1), reads=[pt], writes=[pt])
                                if m + 1 <= 7:
                                    o0 = (m + 1 - qlo) * 128
                                    P.op("pool", lambda e, pt=pt, o0=o0: e.affine_select(out=pt[:, o0:o0 + 128], in_=pt[:, o0:o0 + 128], pattern=[[-1, 128]], compare_op=ALU.is_ge, fill=0.0, base=0, channel_multiplier=1), reads=[pt], writes=[pt])
                                for nq in range(qlo, qhi + 1):
                                    a = acc[nq // 4]
                                    mm(a[:, (nq % 4) * 128:(nq % 4 + 1) * 128], vaug.t[:, 4 + m, kv, vs], pt[:, (nq - qlo) * 128:(nq - qlo + 1) * 128], False, True, [vaug, pt], [a])
                            for tti in range(2):
                                normalize(acc[tti], po, 512, hh, cq, tti * 512)
                    P.pop()

                def mla():
                    P.push()
                    NCTX = 512 if SAMPLE else 0
                    NK = NCTX + NTOK
                    NBK = NK // 128
                    qn = P.sb([128, 2, NTOK], BF16, nsub=4, name="qn")
                    ckvT = P.sb([128, NTOK], F32, nsub=2, name="ckvT")
                    ckb = P.sb([128, NK], BF16, name="ckb")
                    krT = P.sb([128, NK], BF16, name="krT")
                    mw = P.sb([128, 1280], BF16, name="mw")
                    pti = {"i": 0}

                    def nPT():
                        t = PT[pti["i"] % 3]
                        pti["i"] += 1
                        return t
                    P.dma("pool", mw[:], I["mlaw"][l], writes=[mw])
                    uq = mw.t[:, 0:768].rearrange("p (k n) -> p k n", k=2)
                    w = wload(I["wmla"][l], 8 * 480)
                    wv = w.t[:, 0:8 * 480].rearrange("p (k n) -> p k n", k=8)
                    if SAMPLE:
                        ropm = P.sb([128, 2, NTOK], F32, name="ropm")
                        P.dma("sp", ropm[:], I["rope"][:, 2:4, :], writes=[ropm])
                        mxbs = [P.sb([128, 512], BF16, name="mxb%d" % i) for i in range(2)]
                        mt1s = [P.sb([128, 512], F32, name="mt10"), tmpf[1]]
                        mt2s = [P.sb([128, 512], F32, name="mt2%d" % i) for i in range(2)]
                        mrst = {"i": 0}

                        def rope96(ps, cs, lo, dst, dap):
                            k_ = mrst["i"] % 2
                            mrst["i"] += 1
                            xb, t1, t2 = mxbs[k_], mt1s[k_], mt2s[k_]
                            cp("act", xb[0:96, :], ps[0:96, :], [ps], [xb])
                            ps2 = pst()
                            mm(ps2[0:96, :], rmat.t[0:96, 1, 0:96], xb[0:96, :], True, True, [rmat, xb], [ps2])
                            tt("dve", t1[lo:96, :], ps[lo:96, :], ropm.t[lo:96, 0, cs], ALU.mult, [ps, ropm], [t1])
                            tt("dve", t2[lo:96, :], ps2[lo:96, :], ropm.t[lo:96, 1, cs], ALU.mult, [ps2, ropm], [t2])
                            tt("pool", dap, t1[lo:96, :], t2[lo:96, :], ALU.add, [t1, t2], [dst])
                    P.push()
                    ql = P.sb([128, 2, NTOK], F32, nsub=4, name="ql")
                    kvl = P.sb([128, NTOK], F32, nsub=2, name="kvl")
                    sqb = [P.sb([128, 512], BF16, name="sqb%d" % i) for i in range(2)]
                    for c in range(2):
                        proj_fm(w, wv, c * 128, 128, lambda ps, tti, c=c: evac(ql.t[:, c, tti * 512:(tti + 1) * 512], ps[:, :], [ps], [(ql, c * 2 + tti)]))
                    proj_fm(w, wv, 256, 128, lambda ps, tti: evac(kvl[:, tti * 512:(tti + 1) * 512], ps[:, :], [ps], [(kvl, tti)]))

                    def ev_kr(ps, tti):
                        cs = slice(tti * 512, (tti + 1) * 512)
                        if SAMPLE:
                            rope96(ps, cs, 64, krT, krT[64:96, NCTX + cs.start:NCTX + cs.stop])
                        else:
                            evac(krT[64:96, cs], ps[64:96, :], [ps], [krT])
                    proj_fm(w, wv, 384, 96, ev_kr)
                    for tti in range(2):
                        cs = slice(tti * 512, (tti + 1) * 512)
                        rms_rstd([(ql, ql.t[:, c, cs], c * 2 + tti) for c in range(2)], tti, 256, rstd, sqb)
                        for c in range(2):
                            stt(qn.t[:, c, cs], ql.t[:, c, cs], colp[:, C_QN + c:C_QN + c + 1], rstd[:], ALU.mult, ALU.mult, [(ql, c * 2 + tti), colp, rstd], [(qn, c * 2 + tti)])
                        rms_rstd([(kvl, kvl[:, cs], tti)], tti, 128, rstd, sqb)
                        stt(ckvT[:, cs], kvl[:, cs], colp[:, C_KVN:C_KVN + 1], rstd[:], ALU.mult, ALU.mult, [(kvl, tti), colp, rstd], [(ckvT, tti)])
                        cp("pool", ckb[:, NCTX + cs.start:NCTX + cs.stop], ckvT[:, cs], [(ckvT, tti)], [ckb])
                    if not SAMPLE:
                        otm = P.sb([128, 8, 128], F32, name="otm")
                        for half in range(2):
                            ps = pst()
                            for b4 in range(4):
                                blk = half * 4 + b4
                                tr(ps[:, b4 * 128:(b4 + 1) * 128], ckvT[:, blk * 128:(blk + 1) * 128], 128, [(ckvT, half)], [ps])
                            evac(otm.t[:, half * 4:half * 4 + 4, :], ps[:, :].rearrange("p (a b) -> p a b", a=4), [ps], [otm])
                        for s in range(NSEQ):
                            P.dma("sp", O["nckv"][s, l].rearrange("(b p) d -> p b d", p=128), otm.t[:, 2 * s:2 * s + 2, :], reads=[otm], is_output=True)
                    else:
                        ctm = P.sb([128, 4, 128], F32, name="mctm")
                        krm = P.sb([128, 4, 96], F32, name="krm")
                        P.dma("sp", ctm[:], I["cckv"][l].rearrange("(c p) d -> p c d", p=128), writes=[ctm])
                        memset("pool", krm[:], 0.0, [krm])
                        P.dma("sp", krm.t[:, :, 64:96], I["ckr"][l].rearrange("(c p) d -> p c d", p=128), writes=[krm])
                        ps = pst()
                        for cb in range(4):
                            tr(ps[:, cb * 128:(cb + 1) * 128], ctm.t[:, cb, :], 128, [ctm], [ps])
                        evac(ckb[:, 0:512], ps[:, :], [ps], [ckb])
                        ps = pst()
                        for cb in range(4):
                            tr(ps[0:96, cb * 128:(cb + 1) * 128], krm.t[:, cb, :], 128, [krm], [ps])
                        evac(krT[64:96, 0:512], ps[64:96, :], [ps], [krT])
                    P.pop()
                    qh = P.sb([128, 4, NTOK], BF16, name="qh")
                    kTh = P.sb([128, 4, NK], BF16, name="kTh")
                    vaug = P.sb([128, NBK, 4, 192], BF16, name="mva")
                    PT = [P.sb([128, 512], BF16, name="mPT%d" % i) for i in range(3)]
                    den = tmpf[0]
                    memset("pool", vaug[:], 1.0, [vaug])
                    for hh in range(4):
                        for tti in range(2):
                            cs = slice(tti * 512, (tti + 1) * 512)
                            ps = pst()
                            for kc in range(2):
                                mm(ps[0:96, :], uq[:, kc, hh * 96:(hh + 1) * 96], qn.t[:, kc, cs], kc == 0, kc == 1, [mw, (qn, kc * 2 + tti)], [ps])
                            if SAMPLE:
                                rope96(ps, cs, 0, qh, qh.t[0:96, hh, cs])
                            else:
                                evac(qh.t[0:96, hh, cs], ps[0:96, :], [ps], [qh])
                    for hh in range(4):
                        for kt in range(NK // 512):
                            ks = slice(kt * 512, (kt + 1) * 512)
                            ps = pst()
                            mm(ps[0:64, :], mw[:, 768 + hh * 64:768 + (hh + 1) * 64], ckb[:, ks], True, True, [mw, ckb], [ps])
                            evac(kTh.t[0:64, hh, ks], ps[0:64, :], [ps], [kTh])
                        cp("pool", kTh.t[64:96, hh, :], krT[64:96, :], [krT], [kTh])
                    for b in range(NBK):
                        ps = pst()
                        mm(ps[:, 0:256], ckb[:, b * 128:(b + 1) * 128], mw[:, 1024:1280], True, True, [ckb, mw], [ps])
                        evac(vaug.t[:, b, :, 64:128], ps[:, 0:256].rearrange("p (k d) -> p k d", k=4), [ps], [vaug])
                    SCL = float(96 ** -0.5)

                    def normalize(pso, po, ncols, cm, c0):
                        nr = slice(po, po + 64)
                        dr_ = slice(64 - po, 128 - po)
                        recip(den[nr, 0:ncols], pso[dr_, 0:ncols], [pso], [den])
                        tt("dve", ymix.t[nr, 6 + cm, c0:c0 + ncols], pso[nr, 0:ncols], den[nr, 0:ncols], ALU.mult, [pso, den], [sc(ymix, 6 + cm)])
                    if not SAMPLE:
                        for s in range(NSEQ):
                            for hh in range(4):
                                cm, po = hh // 2, (hh % 2) * 64
                                vs = slice(64, 192) if po == 0 else slice(0, 128)
                                ps = pst()
                                for kb in range(2):
                                    kc_ = slice(s * 256 + kb * 128, s * 256 + (kb + 1) * 128)
                                    mm(ps[:, kb * 256:(kb + 1) * 256], kTh.t[0:96, hh, kc_], qh.t[0:96, hh, s * 256:(s + 1) * 256], True, True, [kTh, qh], [ps])
                                pt = nPT()
                                act(pt[:], ps[:, :], AF.Exp, [ps], [pt], scale=SCL)
                                pso = pst()
                                for kb in range(2):
                                    mm(pso[:, 0:256], vaug.t[:, s * 2 + kb, hh, vs], pt[:, kb * 256:(kb + 1) * 256], kb == 0, kb == 1, [vaug, pt], [pso])
                                normalize(pso, po, 256, cm, s * 256)
                    else:
                        for hh in range(4):
                            cm, po = hh // 2, (hh % 2) * 64
                            vs = slice(64, 192) if po == 0 else slice(0, 128)
                            acc = [pss[4 + 2 * (hh % 2)], pss[5 + 2 * (hh % 2)]]
                            for kb in range(NBK):
                                for tti in range(2):
                                    ps = pst()
                                    mm(ps[:, :], kTh.t[0:96, hh, kb * 128:(kb + 1) * 128], qh.t[0:96, hh, tti * 512:(tti + 1) * 512], True, True, [kTh, qh], [ps])
                                    pt = nPT()
                                    act(pt[:], ps[:, :], AF.Exp, [ps], [pt], scale=SCL)
                                    mm(acc[tti][:, :], vaug.t[:, kb, hh, vs], pt[:], kb == 0, kb == NBK - 1, [vaug, pt], [acc[tti]])
                            for tti in range(2):
                                normalize(acc[tti], po, 512, cm, tti * 512)
                    P.pop()

                def hyena():
                    P.push()
                    h2 = P.sb([64, 2, L], F32, name="h2")
                    hw = P.sb([64, 1152], F32, name="hw")
                    absd = P.sb([128, 512], F32, name="absd")
                    tn = P.sb([128, 2, nch], F32, name="tn")
                    P.dma("sp", hw[:], I["hyw"][l], writes=[hw])
                    P.dma("sp", absd[:], I["rowp"][l], writes=[absd])
                    P.dma("sp", tn[:], G.tn, writes=[tn])
                    act(absd[:], absd[:], AF.Abs, [absd], [absd])
                    PH = P.phase
                    P.phase = PH + ".mlp"
                    P.push()
                    zg = P.sb([17, 2, L], F32, name="zg")
                    h1 = P.sb([64, 512], F32, name="h1")
                    ri = P.sb([64, 512], I32, name="ri")
                    rf = P.sb([64, 512], F32, name="rf")
                    ra = P.sb([64, 512], F32, name="ra")
                    P.dma("sp", zg[:], G.zg, writes=[zg])
                    nt = min(L, 512)

                    def sin_layer(ps, bcol, out_ap, out_t):
                        ts("dve", ra[:, 0:nt], ps[0:64, 0:nt], colp[0:64, bcol:bcol + 1], float(PI + 16 * TWO_PI), ALU.add, ALU.add, [ps, colp], [ra])
                        ts("dve", ri[:, 0:nt], ra[:, 0:nt], float(1.0 / TWO_PI), None, ALU.mult, None, [ra], [ri])
                        cp("dve", rf[:, 0:nt], ri[:, 0:nt], [ri], [rf])
                        stt(ra[:, 0:nt], rf[:, 0:nt], float(-TWO_PI), ra[:, 0:nt], ALU.mult, ALU.add, [rf, ra], [ra])
                        ts("dve", rf[:, 0:nt], ra[:, 0:nt], 0.0, float(TWO_PI), ALU.is_lt, ALU.mult, [ra], [rf])
                        tt("dve", ra[:, 0:nt], ra[:, 0:nt], rf[:, 0:nt], ALU.add, [ra, rf], [ra])
                        ts("dve", ra[:, 0:nt], ra[:, 0:nt], float(PI), 3.1415925, ALU.subtract, ALU.min, [ra], [ra])
                        ts("dve", ra[:, 0:nt], ra[:, 0:nt], -3.1415925, None, ALU.max, None, [ra], [ra])
                        act(out_ap, ra[:, 0:nt], AF.Sin, [ra], [out_t])
                    for g in range(2):
                        for ti in range(L // nt):
                            cs = slice(ti * nt, (ti + 1) * nt)
                            ps = pst()
                            mm(ps[0:64, 0:nt], hw[0:17, 0:64], zg.t[0:17, g, cs], True, True, [hw, zg], [ps])
                            sin_layer(ps, C_HB1, h1[:, 0:nt], h1)
                            ps2 = pst()
                            mm(ps2[0:64, 0:nt], hw[0:64, 64:128], h1[:, 0:nt], True, True, [hw, h1], [ps2])
                            sin_layer(ps2, C_HB2, h2.t[:, g, cs], h2)
                    P.pop()
                    hpad = P.sb([128, NSEQ, L + 2], F32, name="hpad")
                    memset("pool", hpad[:], 0.0, [hpad])
                    u = P.sb([128, 3, NTOK], F32, nsub=6, name="u")
                    z1 = P.sb([128, NTOK], F32, nsub=2, name="z1")
                    utm = P.sb([128, nch, NSEQ, 128], BF16, name="utm")
                    AB = P.sb([128, nch, 2, 2, 128], BF16, name="AB")
                    Gt = P.sb([128, nch, 2, 128], F32, name="Gt")
                    Ur = P.sb([128, nch, NSEQ, 128], F32, name="Ur")
                    Y = P.sb([128, nch, 2, NSEQ, 128], BF16, name="Y")
                    ffs = [P.sb([128, 2, 128], F32, name="ff%d" % i) for i in range(2)]
                    fbs = [P.sb([128, 2, 128], F32, name="fb%d" % i) for i in range(2)]
                    wins = [P.sb([128, 2, 128], F32, name="win%d" % i) for i in range(2)]
                    m1s = [P.sb([128, 2, 128], F32, name="m1%d" % i) for i in range(2)]
                    m2s = [P.sb([128, 2, 128], F32, name="m2%d" % i) for i in range(2)]
                    nsg = 2 if NSEQ > 1 else 1
                    sgs = [(a, a + nsg) for a in range(0, NSEQ, nsg)]
                    dft = G.dft
                    nel = nch * L

                    NH = 2 if L > 512 else 1
                    HW_ = L // NH
                    FPH = HW_ // 128

                    def dload(i):
                        res = []
                        for hv in range(NH):
                            k = wstate["h"] % (2 * NW)
                            wstate["h"] += 1
                            t, hf = wring[k // 2], k % 2
                            P.dma("sp", t[:, hf * 4096:hf * 4096 + nch * HW_], dft[i, hv], writes=[(t, hf)])
                            res.append((t, hf, t.t[:, hf * 4096:hf * 4096 + nch * HW_].rearrange("p (k n) -> p k n", k=nch)))
                        return res

                    def mcol(res, fch):
                        t, hf, v = res[fch // FPH]
                        c0 = (fch % FPH) * 128
                        return (t, hf), v, c0
                    for cc_ in range(2):
                        P.phase = PH + ".proj"
                        w = wload(I["why"][l], 6144)
                        wv = w.t[:, 0:6144].rearrange("p (k n) -> p k n", k=8)
                        for part in range(3):
                            ci = part * 2 + cc_

                            def ev_h(ps, tti):
                                if NSEQ > 1:
                                    evac(hpad.t[:, 2 * tti:2 * tti + 2, 1:L + 1], ps[:, :].rearrange("p (s t) -> p s t", s=2), [ps], [hpad])
                                else:
                                    evac(hpad.t[:, 0, 1 + tti * 512:1 + (tti + 1) * 512], ps[:, :], [ps], [hpad])
                            proj_fm(w, wv, ci * 128, 128, ev_h)
                            uv = u.t[:, part, :].rearrange("p (s t) -> p s t", s=NSEQ)
                            act(uv, hpad.t[:, :, 1:L + 1], AF.Identity, [hpad, colp], [sc(u, part)], scale=colp[:, C_HSW + 6 + ci:C_HSW + 7 + ci], bias=colp[:, C_HSB + ci:C_HSB + ci + 1])
                            stt(uv, hpad.t[:, :, 0:L], colp[:, C_HSW + ci:C_HSW + ci + 1], uv, ALU.mult, ALU.add, [hpad, colp, sc(u, part)], [sc(u, part)])
                            stt(uv, hpad.t[:, :, 2:L + 2], colp[:, C_HSW + 12 + ci:C_HSW + 13 + ci], uv, ALU.mult, ALU.add, [hpad, colp, sc(u, part)], [sc(u, part)])
                        P.phase = PH + ".filt"
                        for tch in range(nch):
                            tcs = slice(tch * 128, (tch + 1) * 128)
                            ps = pst()
                            for o_ in range(2):
                                cf = 128 + o_ * 512 + cc_ * 128
                                mm(ps[:, o_ * 128:(o_ + 1) * 128], h2.t[:, 0, tcs], hw[0:64, cf:cf + 128], True, True, [h2, hw], [ps])
                                mm(ps[:, 256 + o_ * 128:256 + (o_ + 1) * 128], h2.t[:, 1, tcs], hw[0:64, cf + 256:cf + 384], True, True, [h2, hw], [ps])
                            win, ff, fb = wins[tch % 2], ffs[tch % 2], fbs[tch % 2]
                            act(win.t[:, 0, :], absd[:, cc_ * 128:(cc_ + 1) * 128], AF.Exp, [absd, tn], [win], scale=tn.t[:, 0, tch:tch + 1])
                            act(win.t[:, 1, :], absd[:, 256 + cc_ * 128:256 + (cc_ + 1) * 128], AF.Exp, [absd, tn], [win], scale=tn.t[:, 1, tch:tch + 1])
                            tt("dve", ff[:], ps[:, 0:256].rearrange("p (o c) -> p o c", o=2), win.t[:, 0:1, :].to_broadcast([128, 2, 128]), ALU.mult, [ps, win], [ff])
                            tt("dve", fb[:], ps[:, 256:512].rearrange("p (o c) -> p o c", o=2), win.t[:, 1:2, :].to_broadcast([128, 2, 128]), ALU.mult, [ps, win], [fb])
                            if tch == 0:
                                memset("dve", fb[0:1, :, :], 0.0, [fb])
                            tt("pool", AB.t[:, tch, :, 0, :], ff[:], fb[:], ALU.add, [ff, fb], [AB])
                            tt("pool", AB.t[:, tch, :, 1, :], ff[:], fb[:], ALU.subtract, [ff, fb], [AB])
                        for o in range(2):
                            P.phase = PH + ".tr"
                            for blk in range(8):
                                s_, jc = blk // nch, blk % nch
                                ps = pst()
                                if o == 0:
                                    tr(ps[:, 0:128], u.t[:, 2, blk * 128:(blk + 1) * 128], 128, [(u, 4 + blk // 4)], [ps])
                                else:
                                    tr(ps[:, 0:128], z1[:, blk * 128:(blk + 1) * 128], 128, [(z1, blk // 4)], [ps])
                                evac(utm.t[:, jc, s_, :], ps[:, 0:128], [ps], [utm])
                            P.phase = PH + ".A"
                            mres = dload(0)
                            for fch in range(nch):
                                ps = pst()
                                mt, mv, c0 = mcol(mres, fch)
                                for tch in range(nch):
                                    mm(ps[:, 0:128], mv[:, tch, c0:c0 + 128], AB.t[:, tch, o, 0, :], tch == 0, tch == nch - 1, [mt, AB], [ps])
                                evac(Gt.t[:, fch, 0, :], ps[:, 0:128], [ps], [Gt])
                            for (sa, sb_) in sgs:
                                ns = sb_ - sa
                                for fch in range(nch):
                                    ps = pst()
                                    mt, mv, c0 = mcol(mres, fch)
                                    for jc in range(nch):
                                        mm(ps[:, 0:ns * 128], mv[:, jc, c0:c0 + 128], utm.t[:, jc, sa:sb_, :], jc == 0, jc == nch - 1, [mt, utm], [ps])
                                    evac(Ur.t[:, fch, sa:sb_, :], ps[:, 0:ns * 128].rearrange("p (s c) -> p s c", s=ns), [ps], [Ur])
                            P.phase = PH + ".B"
                            mres = dload(1)
                            for fch in range(nch):
                                ps = pst()
                                mt, mv, c0 = mcol(mres, fch)
                                for tch in range(nch):
                                    mm(ps[:, 0:128], mv[:, tch, c0:c0 + 128], AB.t[:, tch, o, 1, :], tch == 0, tch == nch - 1, [mt, AB], [ps])
                                evac(Gt.t[:, fch, 1, :], ps[:, 0:128], [ps], [Gt])
                            for (sa, sb_) in sgs:
                                ns = sb_ - sa
                                for fch in range(nch):
                                    ps = pst()
                                    mt, mv, c0 = mcol(mres, fch)
                                    for jc in range(nch):
                                        mm(ps[:, 0:ns * 128], mv[:, jc, c0:c0 + 128], utm.t[:, jc, sa:sb_, :], jc == 0, jc == nch - 1, [mt, utm], [ps])
                                    p3 = ps[:, 0:ns * 128].rearrange("p (s c) -> p s c", s=ns)
                                    gr = Gt.t[:, fch, 0:1, :].to_broadcast([128, ns, 128])
                                    gi = Gt.t[:, fch, 1:2, :].to_broadcast([128, ns, 128])
                                    ur = Ur.t[:, fch, sa:sb_, :]
                                    m1, m2 = m1s[fch % 2], m2s[fch % 2]
                                    tt("pool", m1.t[:, 0:ns, :], ur, gr, ALU.mult, [Ur, Gt], [m1])
                                    tt("dve", m2.t[:, 0:ns, :], p3, gi, ALU.mult, [ps, Gt], [m2])
                                    tt("pool", Y.t[:, fch, 0, sa:sb_, :], m1.t[:, 0:ns, :], m2.t[:, 0:ns, :], ALU.subtract, [m1, m2], [Y])
                                    tt("pool", m1.t[:, 0:ns, :], ur, gi, ALU.mult, [Ur, Gt], [m1])
                                    tt("dve", m2.t[:, 0:ns, :], p3, gr, ALU.mult, [ps, Gt], [m2])
                                    tt("pool", Y.t[:, fch, 1, sa:sb_, :], m1.t[:, 0:ns, :], m2.t[:, 0:ns, :], ALU.add, [m1, m2], [Y])
                            P.phase = PH + ".inv"
                            mrc = dload(2)
                            mrs = dload(3)
                            nt = min(L, 512)
                            tiles = [(s_, it, pst()) for s_ in range(NSEQ) for it in range(L // nt)]
                            for (s_, it, ps) in tiles:
                                t_, hf_, v_ = mrc[it]
                                for fch in range(nch):
                                    mm(ps[:, 0:nt], Y.t[:, fch, 0, s_, :], v_[:, fch, 0:nt], fch == 0, False, [Y, (t_, hf_)], [ps])
                            for (s_, it, ps) in tiles:
                                t_, hf_, v_ = mrs[it]
                                for fch in range(nch):
                                    mm(ps[:, 0:nt], Y.t[:, fch, 1, s_, :], v_[:, fch, 0:nt], False, fch == nch - 1, [Y, (t_, hf_)], [ps])
                            for (s_, it, ps) in tiles:
                                if True:
                                    t0 = s_ * L + it * nt
                                    tcs = slice(t0, t0 + nt)
                                    tix = t0 // 512
                                    t_ = tmp()
                                    bcol = colp[:, C_HBIAS + o * 2 + cc_:C_HBIAS + o * 2 + cc_ + 1]
                                    if o == 0:
                                        stt(t_[:, 0:nt], u.t[:, 2, tcs], bcol, ps[:, 0:nt], ALU.mult, ALU.add, [(u, 4 + tix), colp, ps], [t_])
                                        tt("pool", z1[:, tcs], t_[:, 0:nt], u.t[:, 0, tcs], ALU.mult, [t_, (u, 0 + tix)], [(z1, tix)])
                                    else:
                                        stt(t_[:, 0:nt], z1[:, tcs], bcol, ps[:, 0:nt], ALU.mult, ALU.add, [(z1, tix), colp, ps], [t_])
                                        tt("pool", ymix.t[:, 2 + cc_, tcs], t_[:, 0:nt], u.t[:, 1, tcs], ALU.mult, [t_, (u, 2 + tix)], [s2(ymix, 2 + cc_, tix)])
                    P.pop()

                skip = _os.environ.get("MK_SKIP", "")
                if "ret" not in skip:
                    P.phase = "ret"
                    retention()
                pstate["n"] = 4 if SAMPLE else 8
                if "gqa" not in skip:
                    P.phase = "gqa" + G.name
                    gqa()
                if "mla" not in skip:
                    P.phase = "mla" + G.name
                    mla()
                pstate["n"] = 8
                if "hy" not in skip:
                    P.phase = "hy" + G.name
                    hyena()
                P.phase = "wout"
                P.pop()
                if dbg and l == 0:
                    dump("ymix_" + G.name, ymix, ymix[:], [128, 8, NTOK], BF16)

                P.push()
                y = P.sb([128, 8, NTOK], F32, nsub=16, name="y")

                def post_norm_add(ci):
                    for tti in range(2):
                        big_stats(y, tti)
                    for tti in range(2):
                        cs = slice(tti * 512, (tti + 1) * 512)
                        for hv in range(2):
                            tt("pool", xr[:], y.t[:, hv * 4:hv * 4 + 4, cs], rstd2[tti].t[:, None, :].to_broadcast([128, 4, 512]), ALU.mult,
                               [(y, [c * 2 + tti for c in range(hv * 4, hv * 4 + 4)]), rstd2[tti]], [xr])
                            for c4 in range(4):
                                c = hv * 4 + c4
                                stt(x.t[:, c, cs], xr.t[:, c4, :], mods.t[:, ci, c:c + 1], x.t[:, c, cs], ALU.mult, ALU.add, [(xr, c4), mods, s2(x, c, tti)], [s2(x, c, tti)])
                w = wload(I["wout"][l], 8192)
                wv = w.t[:, :].rearrange("p (k n) -> p k n", k=8)
                for m in range(8):
                    for tti in range(2):
                        ps = pst()
                        for kc in range(8):
                            mm(ps[:, :], wv[:, kc, m * 128:(m + 1) * 128], ymix.t[:, kc, tti * 512:(tti + 1) * 512], kc == 0, kc == 7, [w, s2(ymix, kc, tti)], [ps])
                        evac(y.t[:, m, tti * 512:(tti + 1) * 512], ps[:, :], [ps], [s2(y, m, tti)])
                P.phase = "wout.post"
                post_norm_add(2)
                if dbg and l == 0:
                    dump("x1_" + G.name, x, x[:], [128, 8, NTOK])
                if "ffn" in skip:
                    P.pop()
                    continue
                P.phase = "ffn"
                h2_ = ymix
                norm_mod(x, 3, 4, h2_)
                P.phase = "ffn.up"
                actb = P.sb([128, NHC, NTOK], BF16, name="actb")
                gps = [P.sb([128, NSEQ, L + 2], F32, name="gp%d" % i) for i in range(2)]
                gts = [P.sb([128, NTOK], F32, name="gt%d" % i) for i in range(2)]
                for gp in gps:
                    memset("pool", gp[:], 0.0, [gp])
                psus = {}
                wcur = {}

                def ffn_tail(hc):
                    gt = gts[hc % 2]
                    act(gt[:], gt[:], AF.Silu, [gt], [gt])
                    for tti in range(2):
                        cs = slice(tti * 512, (tti + 1) * 512)
                        tt("dve", actb.t[:, hc, cs], gt[:, cs], psus[hc][tti][:, :], ALU.mult, [gt, psus[hc][tti]], [actb])
                    del psus[hc]

                for hc in range(NHC + 1):
                    if hc < NHC:
                        b_, j = hc // 2, hc % 2
                        if j == 0:
                            wcur["w"] = wload_h(I["wup"][l, b_], 4096)
                        w, whf, wo = wcur["w"]
                        wv = w.t[:, wo:wo + 4096].rearrange("p (k n) -> p k n", k=8)
                        gp, gt = gps[hc % 2], gts[hc % 2]
                        psg = []
                        for tti in range(2):
                            ps = pst()
                            for kc in range(8):
                                mm(ps[:, :], wv[:, kc, j * 128:(j + 1) * 128], h2_.t[:, kc, tti * 512:(tti + 1) * 512], kc == 0, kc == 7, [(w, whf), s2(h2_, kc, tti)], [ps])
                            psg.append(ps)
                        pu = []
                        for tti in range(2):
                            ps = pst()
                            for kc in range(8):
                                mm(ps[:, :], wv[:, kc, 256 + j * 128:256 + (j + 1) * 128], h2_.t[:, kc, tti * 512:(tti + 1) * 512], kc == 0, kc == 7, [(w, whf), s2(h2_, kc, tti)], [ps])
                            pu.append(ps)
                        psus[hc] = pu
                    if hc >= 1:
                        ffn_tail(hc - 1)
                    if hc < NHC:
                        for tti in range(2):
                            ps = psg[tti]
                            if NSEQ > 1:
                                cp("act", gp.t[:, 2 * tti:2 * tti + 2, 1:L + 1], ps[:, :].rearrange("p (s t) -> p s t", s=2), [ps], [gp])
                            else:
                                cp("act", gp.t[:, 0, 1 + tti * 512:1 + (tti + 1) * 512], ps[:, :], [ps], [gp])
                        gv = gt[:, :].rearrange("p (s t) -> p s t", s=NSEQ)
                        act(gv, gp.t[:, :, 1:L + 1], AF.Identity, [gp, colp], [gt], scale=colp[:, C_FCW + 22 + hc:C_FCW + 23 + hc], bias=colp[:, C_FCB + hc:C_FCB + hc + 1])
                        stt(gv, gp.t[:, :, 0:L], colp[:, C_FCW + hc:C_FCW + hc + 1], gv, ALU.mult, ALU.add, [gp, colp, gt], [gt])
                        stt(gv, gp.t[:, :, 2:L + 2], colp[:, C_FCW + 44 + hc:C_FCW + 45 + hc], gv, ALU.mult, ALU.add, [gp, colp, gt], [gt])
                P.phase = "ffn.dn"
                for m in range(8):
                    w, whf, wo = wload_h(I["wdn"][l, m], 22 * 128)
                    wv = w.t[:, wo:wo + 22 * 128].rearrange("p (k n) -> p k n", k=22)
                    for tti in range(2):
                        ps = pst()
                        for kc in range(NHC):
                            mm(ps[:, :], wv[:, kc, :], actb.t[:, kc, tti * 512:(tti + 1) * 512], kc == 0, kc == NHC - 1, [(w, whf), actb], [ps])
                        evac(y.t[:, m, tti * 512:(tti + 1) * 512], ps[:, :], [ps], [s2(y, m, tti)])
                P.phase = "ffn.post"
                post_norm_add(5)
                P.pop()
                if dbg and l == 0:
                    dump("x2_" + G.name, x, x[:], [128, 8, NTOK])

            P.phase = "io"
            P.push()
            ot = [P.sb([128, D], F32, name="ot%d" % i) for i in range(2)]
            for blk in range(8):
                o_ = ot[blk % 2]
                for half in range(2):
                    ps = pst()
                    for c4 in range(4):
                        c = half * 4 + c4
                        tr(ps[:, c4 * 128:(c4 + 1) * 128], x.t[:, c, blk * 128:(blk + 1) * 128], 128, [s2(x, c, blk // 4)], [ps])
                    evac(o_[:, half * 512:(half + 1) * 512], ps[:, :], [ps], [o_])
                P.dma("sp", G.y_out[blk * 128:(blk + 1) * 128, :], o_[:], reads=[o_], is_output=True)
            P.pop()
            P.pop()
            mstate["ready"] = True

        GP = Grp()
        GP.name, GP.L, GP.NSEQ, GP.sample, GP.gi = "P", 256, 4, False, 0
        GP.x_in, GP.y_out, GP.dft, GP.zg, GP.tn = I["xp"], O["yp"], I["dftP"], I["zgP"], I["tnP"]
        GS = Grp()
        GS.name, GS.L, GS.NSEQ, GS.sample, GS.gi = "S", 1024, 1, True, 1
        GS.x_in, GS.y_out, GS.dft, GS.zg, GS.tn = I["xs"], O["ys"], I["dftS"], I["zgS"], I["tnS"]
        import os as _os_mod
        which = _os_mod.environ.get("MK_GROUPS", "PS")
        if "P" in which:
            run_group(GP)
        if "S" in which:
            run_group(GS)
        if _os_mod.environ.get("MK_PHASES"):
            import json as _json
            _json.dump(P.pe_phase, open(_os_mod.environ["MK_PHASES"], "w"))
        P.emit()
    return nc, DBG


_CONST = {}


def prep_inputs(inp):
    if "c" not in _CONST:
        _CONST["c"] = host_consts()
    cst = _CONST["c"]
    w = host_weights(inp)
    shared = dict(w)
    shared.update({"cst": cst["cst"], "dftP": cst["dftP"], "dftS": cst["dftS"], "zgP": cst["zgP"], "zgS": cst["zgS"],
                   "tnP": cst["tnP"], "tnS": cst["tnS"], "rope": cst["rope"], "rmat": cst["rmat"]})
    in_maps = []
    for core in range(8):
        b = core % 2
        m = dict(shared)
        m["xp"] = np.ascontiguousarray(inp["x_prompt"][core * 4:(core + 1) * 4].reshape(NTOK, D))
        m["xs"] = np.ascontiguousarray(inp["x_sample"][b])
        m["sret"] = np.ascontiguousarray(inp["state_ret"][b])
        m["cgk"] = np.ascontiguousarray(inp["cache_gqa_k"][b])
        m["cgv"] = np.ascontiguousarray(inp["cache_gqa_v"][b])
        m["cckv"] = np.ascontiguousarray(inp["cache_mla_ckv"][b])
        m["ckr"] = np.ascontiguousarray(inp["cache_mla_krope"][b])
        cond = np.stack([col_tile(inp["c_ctx"]), col_tile(inp["c"][b])]).astype(np.float32)
        m["cond"] = np.ascontiguousarray(cond)
        in_maps.append(m)
    return in_maps


def kernel(**inputs):
    inp = {k: np.asarray(v) for k, v in inputs.items()}
    in_maps = prep_inputs(inp)
    nc, _ = build(False)
    res = run_bass_kernel_spmd(nc, in_maps, core_ids=list(range(8)))
    rs = res.results
    yp = np.concatenate([rs[c]["yp"].reshape(4, 256, D) for c in range(8)], axis=0)
    ys = np.stack([rs[0]["ys"], rs[1]["ys"]], axis=0)
    nsr = np.concatenate([rs[c]["nsr"] for c in range(8)], axis=0)
    ngk = np.concatenate([rs[c]["ngk"] for c in range(8)], axis=0)
    ngv = np.concatenate([rs[c]["ngv"] for c in range(8)], axis=0)
    nckv = np.concatenate([rs[c]["nckv"] for c in range(8)], axis=0)
    nkr = np.concatenate([rs[c]["nkr"] for c in range(8)], axis=0)
    f = lambda a: np.ascontiguousarray(a, dtype=np.float32)
    return (f(yp), f(ys), f(nsr), f(ngk), f(ngv), f(nckv), f(nkr))
```

```python
import numpy as np
import concourse.bass as bass
import concourse.mybir as mybir
from concourse.bass_utils import run_bass_kernel_spmd

F32 = mybir.dt.float32
BF16 = mybir.dt.bfloat16
I32 = mybir.dt.int32
ALU = mybir.AluOpType
AF = mybir.ActivationFunctionType
AX = mybir.AxisListType

COMPUTE = ("pe", "act", "dve", "pool")
NDMASEM = 6


class T:
    def __init__(self, t, nsub=1, name=""):
        self.t = t
        self.nsub = nsub
        self.name = name
        self.lw = [None] * nsub
        self.rd = [[] for _ in range(nsub)]
        self.psum = False

    def __getitem__(self, idx):
        return self.t[idx]


class Prog:
    def __init__(self, nc, ctx):
        self.nc = nc
        self.ctx = ctx
        self.ops = {e: [] for e in COMPUTE + ("sp",)}
        self.cnt = {e: 0 for e in COMPUTE}
        self.semobj = {}
        self.sem = {}
        for e in COMPUTE:
            self.semobj["s_" + e] = ctx.enter_context(nc.semaphore("s_" + e))
            self.sem[e] = "s_" + e
        self.dsem = {}
        self.dcnt = {}
        for q in ("sp", "act", "pool"):
            self.dsem[q] = []
            for i in range(NDMASEM):
                nm = "d_%s%d" % (q, i)
                self.semobj[nm] = ctx.enter_context(nc.semaphore(nm))
                self.dsem[q].append(nm)
            self.dcnt[q] = 0
        self.known = {e: {} for e in COMPUTE + ("sp",)}
        self.out_deps = []
        self.nalloc = 0
        self.free_deps = {}
        self.phase = ""
        self.pe_phase = []
        self.scopes = []
        self.nps = 0

    def sb(self, shape, dt=F32, nsub=1, name=None):
        self.nalloc += 1
        name = (name or "sb") + ("_%d" % self.nalloc)
        if self.scopes:
            st, lst = self.scopes[-1]
        else:
            st, lst = self.ctx, None
        t = st.enter_context(self.nc.sbuf_tensor(name, list(shape), dt))
        tt = T(t, nsub, name)
        if self.free_deps:
            fd = [(k, v[0], v[1]) for k, v in self.free_deps.items()]
            for i in range(nsub):
                tt.rd[i] = list(fd)
        if lst is not None:
            lst.append(tt)
        return tt

    def push(self):
        import contextlib as _cl
        st = _cl.ExitStack()
        self.scopes.append((st, []))

    def pop(self):
        st, lst = self.scopes.pop()
        for tt in lst:
            for i in range(tt.nsub):
                for d in ([tt.lw[i]] if tt.lw[i] is not None else []) + tt.rd[i]:
                    k, v, src = d
                    if self.free_deps.get(k, (0, None))[0] < v:
                        self.free_deps[k] = (v, src)
        st.close()

    def ps(self, shape, dt=F32, nsub=1, name=None):
        self.nalloc += 1
        name = name or ("ps%d" % self.nalloc)
        t = self.ctx.enter_context(self.nc.psum_tensor(name, list(shape), dt))
        tt = T(t, nsub, name)
        tt.psum = True
        return tt

    @staticmethod
    def _norm(lst):
        out = []
        for x in lst or []:
            if isinstance(x, T):
                out.extend((x, i) for i in range(x.nsub))
            else:
                t, s = x
                if s is None:
                    out.extend((t, i) for i in range(t.nsub))
                elif isinstance(s, (list, tuple, range)):
                    out.extend((t, i) for i in s)
                else:
                    out.append((t, s))
        return out

    def _collect(self, eng, reads, writes):
        deps = {}

        def add(d):
            if d is None:
                return
            key, val, src = d
            if src == eng and eng == "pe":
                return
            if deps.get(key, 0) < val:
                deps[key] = val

        for (t, s) in reads:
            add(t.lw[s])
            if t.psum:
                for d in t.rd[s]:
                    if d[2] != eng:
                        add(d)
        for (t, s) in writes:
            add(t.lw[s])
            for d in t.rd[s]:
                add(d)
        waits = []
        kn = self.known[eng]
        for key, val in deps.items():
            if kn.get(key, 0) >= val:
                continue
            kn[key] = val
            waits.append((key, val))
        return waits

    def _commit(self, dep, reads, writes):
        for (t, s) in reads:
            t.rd[s].append(dep)
        for (t, s) in writes:
            t.lw[s] = dep
            t.rd[s] = []

    def op(self, eng, fn, reads=None, writes=None):
        reads = self._norm(reads)
        writes = self._norm(writes)
        waits = self._collect(eng, reads, writes)
        self.cnt[eng] += 1
        n = self.cnt[eng]
        sem = self.sem[eng]
        dep = (sem, n, eng)
        self.ops[eng].append((waits, fn, (sem, 1)))
        if eng == "pe":
            self.pe_phase.append(self.phase)
        self._commit(dep, reads, writes)
        return dep

    def dma(self, q, out, in_, reads=None, writes=None, is_output=False):
        reads = self._norm(reads)
        writes = self._norm(writes)
        waits = self._collect(q, reads, writes)
        i = self.dcnt[q]
        self.dcnt[q] += 1
        sem = self.dsem[q][i % NDMASEM]
        prev = 16 * (i // NDMASEM)
        val = prev + 16
        kn = self.known[q]
        if prev > 0 and kn.get(sem, 0) < prev:
            kn[sem] = prev
            waits.append((sem, prev))
        dep = (sem, val, "dma")

        def fn(e, out=out, in_=in_):
            return e.dma_start(out=out, in_=in_)

        self.ops[q].append((waits, fn, (sem, 16)))
        self._commit(dep, reads, writes)
        if is_output:
            self.out_deps.append(dep)
        return dep

    def emit(self):
        nc = self.nc
        fin = []
        for (sem, val, _) in self.out_deps:
            fin.append((sem, val))
        ops = self.ops
        so = self.semobj
        with nc.Block() as block:
            def run(e, lst, extra=None):
                for (waits, fn, inc) in lst:
                    for (s, v) in waits:
                        e.wait_ge(so[s], v)
                    ins = fn(e)
                    if inc is not None:
                        ins.then_inc(so[inc[0]], inc[1])
                if extra:
                    best = {}
                    for (s, v) in extra:
                        if best.get(s, 0) < v:
                            best[s] = v
                    for s, v in best.items():
                        e.wait_ge(so[s], v)

            @block.sync
            def _(e):
                run(e, ops["sp"], fin)

            @block.tensor
            def _(e):
                run(e, ops["pe"])

            @block.scalar
            def _(e):
                run(e, ops["act"])

            @block.vector
            def _(e):
                run(e, ops["dve"])

            @block.gpsimd
            def _(e):
                run(e, ops["pool"])


import math
import contextlib
import ml_dtypes

D = 1024
NTOK = 1024
DEPTH = 4
EPS = 1e-6
DFF = 2816
NHC = 22
PI = math.pi
TWO_PI = 2.0 * math.pi
BF = ml_dtypes.bfloat16

C_ADAB = 0
C_NORM = 48
C_HSW = 80
C_HSB = 98
C_HBIAS = 104
C_GN = 108
C_QN = 110
C_KVN = 112
C_FCW = 113
C_FCB = 179
C_HB1 = 201
C_HB2 = 202
C_LG = 203
C_SINK = 215
NCOLP = 224

K_IOTA1 = 0
K_REV = 128
K_LAGP = 256
K_LAGN = 384
K_U = 512
K_LO = 640
K_REVC = 768
K_POSC = 832
NCST = 896


def kc_tile(W):
    K, N = W.shape
    return np.ascontiguousarray(W.reshape(K // 128, 128, N).transpose(1, 0, 2)).reshape(128, -1)


def col_tile(v):
    v = np.asarray(v)
    lead = v.shape[:-1]
    n = v.shape[-1] // 128
    a = v.reshape(lead + (n, 128))
    a = np.moveaxis(a, -1, 0)
    return np.ascontiguousarray(a).reshape(128, -1)


def host_consts():
    c = {}
    cst = np.zeros((128, NCST), np.float32)
    i = np.arange(128, dtype=np.float32)
    j = np.arange(128, dtype=np.float32)[:, None]
    cst[:, K_IOTA1:K_IOTA1 + 128] = (i + 1)[None, :]
    cst[:, K_REV:K_REV + 128] = (128 - i)[None, :]
    cst[:, K_LAGP:K_LAGP + 128] = np.maximum(i[None, :] - j, 0)
    cst[:, K_LAGN:K_LAGN + 128] = np.maximum(j - i[None, :], 0)
    cst[:, K_U:K_U + 128] = (i[None, :] >= j)
    cst[:, K_LO:K_LO + 128] = (j >= i[None, :])
    cst[:, K_REVC:K_REVC + 64] = (127 - j)
    cst[:, K_POSC:K_POSC + 64] = j
    c["cst"] = cst
    for nm, L in (("P", 256), ("S", 1024)):
        N = 2 * L
        jj = np.arange(L, dtype=np.float64)[:, None]
        ff = np.arange(L, dtype=np.float64)[None, :]
        ang = PI * (2 * ff + 1) * jj / N
        Cc = np.cos(ang)
        Sc = np.sin(ang)
        mats = [Cc, Sc, Cc.T * (2.0 / N), Sc.T * (2.0 / N)]
        nh = 2 if L > 512 else 1
        hwid = L // nh
        c["dft" + nm] = np.stack([np.stack([kc_tile(np.ascontiguousarray(m[:, hv * hwid:(hv + 1) * hwid]).astype(np.float32)) for hv in range(nh)]) for m in mats]).astype(BF)
        zg = np.zeros((17, 2, L), np.float32)
        tn = np.zeros((128, 2, L // 128), np.float32)
        for g in range(2):
            t = ((np.arange(L) - g) / L).astype(np.float32)
            bands = np.arange(1, 9, dtype=np.float32)
            a = (2.0 * PI) * t[:, None] * bands
            z = np.concatenate([t[:, None], np.cos(a), np.sin(a)], axis=-1).astype(np.float32)
            zg[:, g, :] = z.T
            tn[:, g, :] = -(t.reshape(L // 128, 128).T)
        c["zg" + nm] = zg
        c["tn" + nm] = tn
    Ls = 1024
    rows = (np.arange(Ls) // 64).astype(np.float32)
    cols = (np.arange(Ls) % 64).astype(np.float32)

    def tables(dim):
        q = dim // 4
        inv = (10000.0 ** (-np.arange(q, dtype=np.float32) / q)).astype(np.float32)
        ang = np.concatenate([rows[:, None] * inv, cols[:, None] * inv], axis=-1)
        return np.cos(ang).astype(np.float32), np.sin(ang).astype(np.float32)

    cg, sg = tables(64)
    rope = np.zeros((128, 4, Ls), np.float32)
    for p in range(128):
        d = p % 64
        r = d % 32
        rope[p, 0] = cg[:, r]
        rope[p, 1] = sg[:, r] * (-1.0 if d < 32 else 1.0)
    cm, sm = tables(32)
    rope[:, 2] = 1.0
    for p in range(64, 96):
        d = p - 64
        r = d % 16
        rope[p, 2] = cm[:, r]
        rope[p, 3] = sm[:, r] * (-1.0 if d < 16 else 1.0)
    c["rope"] = rope
    R = np.zeros((128, 2, 128), np.float32)
    for m in range(128):
        d = m % 64
        base = m - d
        if d < 32:
            R[base + d + 32, 0, m] = 1.0
        else:
            R[base + d - 32, 0, m] = 1.0
    for m in range(64, 96):
        d = m - 64
        if d < 16:
            R[64 + d + 16, 1, m] = 1.0
        else:
            R[64 + d - 16, 1, m] = 1.0
    c["rmat"] = R.astype(BF)
    return c


def host_weights(inp):
    w = {}
    L = DEPTH
    ada_w = inp["ada_w"]
    w["adaw"] = np.stack([np.stack([kc_tile(ada_w[l][:, b * 512:(b + 1) * 512]) for b in range(12)]) for l in range(L)])
    w_in = inp["w_in"]
    retA = np.r_[0:256, 256:512, 768:1024]
    retB = np.r_[256:768]
    hy = np.r_[1024:1792]
    gb = 1792
    mb = 2304
    qperm = np.concatenate([gb + h * 64 + np.arange(64) for h in (0, 2, 1, 3)])
    gq = np.concatenate([qperm, gb + np.r_[256:384], gb + np.r_[256:384], gb + np.r_[384:512], mb + np.r_[384:416]])
    ml = np.concatenate([mb + np.r_[0:384], mb + np.r_[320:416]])
    w["wretA"] = np.stack([kc_tile(w_in[l][:, retA]) for l in range(L)])
    w["wretB"] = np.stack([kc_tile(w_in[l][:, retB]) for l in range(L)])
    w["why"] = np.stack([kc_tile(w_in[l][:, hy]) for l in range(L)])
    w["wgqa"] = np.stack([kc_tile(w_in[l][:, gq]) for l in range(L)])
    w["wmla"] = np.stack([kc_tile(w_in[l][:, ml]) for l in range(L)])
    w["mlaw"] = np.stack([np.concatenate([kc_tile(inp["mla_w_uq"][l]), inp["mla_w_uk"][l], inp["mla_w_uv"][l]], axis=1) for l in range(L)])
    perm = np.concatenate([np.r_[0:512], 512 + np.concatenate([h * 64 + np.arange(64) for h in (0, 2, 1, 3)]), np.r_[768:1024]])
    w["wout"] = np.stack([kc_tile(inp["w_out"][l][perm, :]) for l in range(L)])
    up = inp["ffn_w_up"]
    w["wup"] = np.stack([np.stack([kc_tile(np.concatenate([up[l][:, 256 * b:256 * b + 256], up[l][:, DFF + 256 * b:DFF + 256 * b + 256]], axis=1)) for b in range(11)]) for l in range(L)])
    dn = inp["ffn_w_down"]
    w["wdn"] = np.stack([np.stack([kc_tile(dn[l][:, 128 * b:128 * b + 128]) for b in range(8)]) for l in range(L)])
    colp = np.zeros((L, 128, NCOLP), np.float32)
    rowp = np.zeros((L, 128, 512), np.float32)
    hyw = np.zeros((L, 64, 1152), np.float32)
    for l in range(L):
        colp[l, :, C_ADAB:C_ADAB + 48] = col_tile(inp["ada_b"][l])
        colp[l, :, C_NORM:C_NORM + 32] = col_tile(inp["norm_g"][l])
        colp[l, :, C_HSW:C_HSW + 18] = col_tile(inp["hy_short_w"][l])
        colp[l, :, C_HSB:C_HSB + 6] = col_tile(inp["hy_short_b"][l])
        colp[l, :, C_HBIAS:C_HBIAS + 4] = col_tile(inp["hy_bias"][l])
        colp[l, :, C_GN:C_GN + 2] = col_tile(inp["ret_gn_g"][l])
        colp[l, :, C_QN:C_QN + 2] = col_tile(inp["mla_q_norm"][l])
        colp[l, :, C_KVN:C_KVN + 1] = col_tile(inp["mla_kv_norm"][l])
        colp[l, :, C_FCW:C_FCW + 66] = col_tile(inp["ffn_conv_w"][l])
        colp[l, :, C_FCB:C_FCB + 22] = col_tile(inp["ffn_conv_b"][l])
        colp[l, 0:64, C_HB1] = inp["hy_b1"][l]
        colp[l, 0:64, C_HB2] = inp["hy_b2"][l]
        lg = inp["ret_decay_logit"][l]
        for d in range(2):
            for c in range(2):
                colp[l, 0:64, C_LG + d * 2 + c] = lg[d, 2 * c]
                colp[l, 64:128, C_LG + d * 2 + c] = lg[d, 2 * c + 1]
            for h in range(4):
                colp[l, :, C_LG + 4 + d * 4 + h] = lg[d, h]
        for h in range(4):
            colp[l, :, C_SINK + h] = inp["gqa_sink"][l, h]
        rowp[l, :, :] = inp["hy_decay"][l].reshape(1, 512)
        hyw[l, 0:17, 0:64] = inp["hy_w1"][l]
        hyw[l, :, 64:128] = inp["hy_w2"][l]
        hyw[l, :, 128:1152] = inp["hy_w3"][l]
    w["colp"] = colp
    w["rowp"] = rowp
    w["hyw"] = hyw
    return w


class Grp:
    pass


def build(dbg=False):
    nc = bass.Bass("TRN2", target_bir_lowering=False)

    def din(name, shape, dt=F32):
        return nc.dram_tensor(name, list(shape), dt, kind="ExternalInput").ap()

    def dout(name, shape, dt=F32):
        return nc.dram_tensor(name, list(shape), dt, kind="ExternalOutput").ap()

    I = {}
    I["xp"] = din("xp", [NTOK, D])
    I["xs"] = din("xs", [NTOK, D])
    I["sret"] = din("sret", [DEPTH, 2, 4, 64, 64])
    I["cgk"] = din("cgk", [DEPTH, 2, 512, 64])
    I["cgv"] = din("cgv", [DEPTH, 2, 512, 64])
    I["cckv"] = din("cckv", [DEPTH, 512, 128])
    I["ckr"] = din("ckr", [DEPTH, 512, 32])
    I["cond"] = din("cond", [2, 128, 8])
    I["adaw"] = din("adaw", [DEPTH, 12, 128, 4096])
    I["wretA"] = din("wretA", [DEPTH, 128, 8 * 768])
    I["wretB"] = din("wretB", [DEPTH, 128, 8 * 512])
    I["why"] = din("why", [DEPTH, 128, 8 * 768])
    I["wgqa"] = din("wgqa", [DEPTH, 128, 8 * 672])
    I["wmla"] = din("wmla", [DEPTH, 128, 8 * 480])
    I["mlaw"] = din("mlaw", [DEPTH, 128, 1280])
    I["wout"] = din("wout", [DEPTH, 128, 8192])
    I["wup"] = din("wup", [DEPTH, 11, 128, 4096])
    I["wdn"] = din("wdn", [DEPTH, 8, 128, 22 * 128])
    I["colp"] = din("colp", [DEPTH, 128, NCOLP])
    I["rowp"] = din("rowp", [DEPTH, 128, 512])
    I["hyw"] = din("hyw", [DEPTH, 64, 1152])
    I["cst"] = din("cst", [128, NCST])
    I["dftP"] = din("dftP", [4, 1, 128, 2 * 256], BF16)
    I["dftS"] = din("dftS", [4, 2, 128, 8 * 512], BF16)
    I["zgP"] = din("zgP", [17, 2, 256])
    I["zgS"] = din("zgS", [17, 2, 1024])
    I["tnP"] = din("tnP", [128, 2, 2])
    I["tnS"] = din("tnS", [128, 2, 8])
    I["rope"] = din("rope", [128, 4, 1024])
    I["rmat"] = din("rmat", [128, 2, 128], BF16)
    O = {}
    O["yp"] = dout("yp", [NTOK, D])
    O["ys"] = dout("ys", [NTOK, D])
    O["nsr"] = dout("nsr", [4, DEPTH, 2, 4, 64, 64])
    O["ngk"] = dout("ngk", [4, DEPTH, 2, 256, 64])
    O["ngv"] = dout("ngv", [4, DEPTH, 2, 256, 64])
    O["nckv"] = dout("nckv", [4, DEPTH, 256, 128])
    O["nkr"] = dout("nkr", [4, DEPTH, 256, 32])
    DBG = {}

    with contextlib.ExitStack() as ctx:
        P = Prog(nc, ctx)
        ident = P.sb([128, 128], F32, name="ident")
        ones_bf = P.sb([128, 128], BF16, name="ones")
        BO = P.sb([128, 128], F32, name="BO")
        cc = P.sb([128, 4], F32, name="cc")
        cst = P.sb([128, NCST], F32, name="cst")
        rmat = P.sb([128, 2, 128], BF16, name="rmat")
        NW = 2
        wring = [P.sb([128, 8192], BF16, nsub=2, name="wr%d" % i) for i in range(NW)]
        wstate = {"i": 0, "h": 0}
        colp = P.sb([128, NCOLP], F32, name="colp")
        modt = P.sb([128, 48], F32, name="modt")
        mods = P.sb([128, 6, 8], F32, name="mods")
        scond2 = P.sb([128, 8, 33], BF16, name="scond2")
        modrow = [P.sb([33, 512], F32, name="modrow0")] * 2
        condt2 = P.sb([128, 2, 8], F32, name="condt2")
        modS = P.sb([128, DEPTH, 48], F32, name="modS")
        mstate = {"ready": False}
        pss = [P.ps([128, 512], F32, name="psr%d" % i) for i in range(8)]
        acc = [pss[6], pss[7]]
        pstate = {"i": 0, "n": 8}
        evs = {"i": 0}

        def pst():
            t = pss[pstate["i"] % pstate["n"]]
            pstate["i"] += 1
            return t

        def wslot():
            t = wring[wstate["i"] % NW]
            wstate["i"] += 1
            return t

        def wload(src2d, n, q="pool"):
            t = wslot()
            P.dma(q, t[:, 0:n], src2d, writes=[t])
            return t

        def wload_h(src2d, n):
            k = wstate["h"] % (2 * NW)
            wstate["h"] += 1
            t, hf = wring[k // 2], k % 2
            P.dma("pool", t[:, hf * 4096:hf * 4096 + n], src2d, writes=[(t, hf)])
            return t, hf, hf * 4096

        def mm(out, lhsT, rhs, start, stop, rd, wr):
            P.op("pe", lambda e: e.matmul(out, lhsT=lhsT, rhs=rhs, start=start, stop=stop, skip_group_check=True), reads=rd, writes=wr)

        def tr(out, in_, k, rd, wr):
            P.op("pe", lambda e: e.transpose(out=out, in_=in_, identity=ident[0:k, 0:k]), reads=rd + [ident], writes=wr)

        def cp(eng, out, in_, rd, wr):
            if eng == "act":
                P.op("act", lambda e: e.activation(out=out, in_=in_, func=AF.Copy), reads=rd, writes=wr)
            else:
                P.op(eng, lambda e: e.tensor_copy(out=out, in_=in_), reads=rd, writes=wr)

        def evac(out, in_, rd, wr):
            evs["i"] += 1
            cp("act" if evs["i"] % 2 else "dve", out, in_, rd, wr)

        def act(out, in_, func, rd, wr, scale=None, bias=None):
            kw = {}
            if scale is not None:
                kw["scale"] = scale
            if bias is not None:
                kw["bias"] = bias
            P.op("act", lambda e: e.activation(out=out, in_=in_, func=func, **kw), reads=rd, writes=wr)

        def tt(eng, out, in0, in1, op, rd, wr):
            P.op(eng, lambda e: e.tensor_tensor(out=out, in0=in0, in1=in1, op=op), reads=rd, writes=wr)

        def ts(eng, out, in0, s1, s2, op0, op1, rd, wr):
            if op1 is None:
                P.op(eng, lambda e: e.tensor_scalar(out=out, in0=in0, scalar1=s1, scalar2=None, op0=op0), reads=rd, writes=wr)
            else:
                P.op(eng, lambda e: e.tensor_scalar(out=out, in0=in0, scalar1=s1, scalar2=s2, op0=op0, op1=op1), reads=rd, writes=wr)

        def stt(out, in0, scalar, in1, op0, op1, rd, wr):
            P.op("dve", lambda e: e.scalar_tensor_tensor(out=out, in0=in0, scalar=scalar, in1=in1, op0=op0, op1=op1), reads=rd, writes=wr)

        def recip(out, in_, rd, wr):
            P.op("dve", lambda e: e.reciprocal(out=out, in_=in_), reads=rd, writes=wr)

        def memset(eng, ap, val, wr):
            P.op(eng, lambda e: e.memset(ap, val), writes=wr)

        def dump(name, tile, ap, shape, dt=F32):
            if not dbg:
                return
            d = dout("dbg_" + name, shape, F32)
            DBG[name] = d
            P.dma("pool" if dt != F32 else "sp", d, ap, reads=[tile], is_output=True)

        P.dma("sp", cst[:], I["cst"], writes=[cst])
        P.dma("sp", rmat[:], I["rmat"], writes=[rmat])
        memset("pool", ident[:], 1.0, [ident])
        P.op("pool", lambda e: e.affine_select(out=ident[:], in_=ident[:], pattern=[[-1, 128]], compare_op=ALU.is_equal, fill=0.0, base=0, channel_multiplier=1), reads=[ident], writes=[ident])
        memset("dve", ones_bf[:], 1.0, [ones_bf])
        memset("dve", BO[:], 0.0, [BO])
        memset("dve", BO[0:64, 0:64], 1.0 / 64, [BO])
        memset("dve", BO[64:128, 64:128], 1.0 / 64, [BO])
        BD = P.sb([128, 128], F32, name="BD")
        memset("dve", BD[:], 0.0, [BD])
        memset("dve", BD[0:64, 0:64], 1.0, [BD])
        memset("dve", BD[64:128, 64:128], 1.0, [BD])
        memset("dve", cc[:, 0:1], EPS, [cc])
        memset("dve", cc[:, 1:2], 1.0, [cc])
        memset("dve", cc[:, 2:3], 0.0, [cc])
        epsc = cc[:, 0:1]

        def s2(t, c, tti):
            return (t, c * 2 + tti)

        def sc(t, c):
            return (t, [c * 2, c * 2 + 1])

        def rms_rstd(srcs, tti, dim, rstd, sqb):
            ps = pst()
            n = len(srcs)
            for i, (t, ap, sub) in enumerate(srcs):
                sq = sqb[i % 2]
                if i % 2 == 0:
                    act(sq[:], ap, AF.Square, [(t, sub)], [sq])
                else:
                    tt("pool", sq[:], ap, ap, ALU.mult, [(t, sub)], [sq])
                mm(ps[:, :], ones_bf[:, :], sq[:], i == 0, i == n - 1, [ones_bf, sq], [ps])
            act(rstd[:], ps[:, :], AF.Sqrt, [ps, cc], [rstd], scale=1.0 / dim, bias=epsc)
            recip(rstd[:], rstd[:], [rstd], [rstd])

        def run_group(G):
            L, NSEQ, nch = G.L, G.NSEQ, G.L // 128
            SAMPLE = G.sample
            P.push()
            x = P.sb([128, 8, NTOK], F32, nsub=16, name="x")
            ymix = P.sb([128, 8, NTOK], BF16, nsub=16, name="ymix")
            rstd2 = [P.sb([128, 512], F32, name="rstdb%d" % i) for i in range(2)]
            rstd = rstd2[0]
            sqbig = P.sb([128, 8, 512], BF16, nsub=2, name="sqbig")
            xr = P.sb([128, 4, 512], F32, nsub=4, name="xr")
            tmpf = [P.sb([128, 512], F32, name="tmpf%d" % i) for i in range(2)]
            tst = {"i": 0}

            def tmp():
                t = tmpf[tst["i"] % 2]
                tst["i"] += 1
                return t

            P.phase = "io"
            P.push()
            xt = [P.sb([128, D], F32, name="xt%d" % i) for i in range(2)]
            for blk in range(8):
                xb_ = xt[blk % 2]
                P.dma("sp", xb_[:], G.x_in[blk * 128:(blk + 1) * 128, :], writes=[xb_])
                for half in range(2):
                    ps = pst()
                    for c4 in range(4):
                        c = half * 4 + c4
                        tr(ps[:, c4 * 128:(c4 + 1) * 128], xb_[:, c * 128:(c + 1) * 128], 128, [xb_], [ps])
                    evac(x.t[:, half * 4:half * 4 + 4, blk * 128:(blk + 1) * 128],
                         ps[:, :].rearrange("p (a b) -> p a b", a=4), [ps],
                         [(x, [(half * 4 + c4) * 2 + blk // 4 for c4 in range(4)])])
            P.pop()
            first_group = not mstate["ready"]
            if first_group:
                P.dma("sp", condt2[:], I["cond"].rearrange("g p c -> p g c"), writes=[condt2])
                memset("dve", scond2[:], 0.0, [scond2])
                act(scond2.t[:, :, 0], condt2.t[:, G.gi, :], AF.Silu, [condt2], [scond2])
                act(scond2.t[:, :, 32], condt2.t[:, 1 - G.gi, :], AF.Silu, [condt2], [scond2])

            import os as _os
            for l in range(int(_os.environ.get('MK_DEPTH', DEPTH))):
                P.phase = "mod"
                P.dma("sp", colp[:], I["colp"][l], writes=[colp])
                if first_group:
                    psm, psm2 = pss[7], pss[6]
                    pstate["n"] = 6
                    for b in range(12):
                        w, whf, wo = wload_h(I["adaw"][l, b], 4096)
                        wv = w.t[:, wo:wo + 4096].rearrange("p (k n) -> p k n", k=8)
                        ps = pst()
                        for kc in range(8):
                            mm(ps[0:33, :], scond2.t[:, kc, :], wv[:, kc, :], kc == 0, kc == 7, [(w, whf), scond2], [ps])
                        mr = modrow[b % 2]
                        evac(mr[0:33, :], ps[0:33, :], [ps], [mr])
                        for j4 in range(4):
                            j = b * 4 + j4
                            mm(psm[:, j:j + 1], mr[0:1, j4 * 128:(j4 + 1) * 128], cc[0:1, 1:2], True, True, [mr, cc], [psm])
                            mm(psm2[:, j:j + 1], mr[32:33, j4 * 128:(j4 + 1) * 128], cc[32:33, 1:2], True, True, [mr, cc], [psm2])
                    pstate["n"] = 8
                    tt("dve", modt[:], psm[:, 0:48], colp[:, C_ADAB:C_ADAB + 48], ALU.add, [psm, colp], [modt])
                    tt("dve", modS.t[:, l, :], psm2[:, 0:48], colp[:, C_ADAB:C_ADAB + 48], ALU.add, [psm2, colp], [modS])
                else:
                    cp("dve", modt[:], modS.t[:, l, :], [modS], [modt])
                stt(mods.t[:, 0, :], modt[:, 8:16], 1.0, colp[:, C_NORM:C_NORM + 8], ALU.add, ALU.mult, [modt, colp], [mods])
                cp("dve", mods.t[:, 1, :], modt[:, 0:8], [modt], [mods])
                tt("dve", mods.t[:, 2, :], modt[:, 16:24], colp[:, C_NORM + 8:C_NORM + 16], ALU.mult, [modt, colp], [mods])
                stt(mods.t[:, 3, :], modt[:, 32:40], 1.0, colp[:, C_NORM + 16:C_NORM + 24], ALU.add, ALU.mult, [modt, colp], [mods])
                cp("dve", mods.t[:, 4, :], modt[:, 24:32], [modt], [mods])
                tt("dve", mods.t[:, 5, :], modt[:, 40:48], colp[:, C_NORM + 24:C_NORM + 32], ALU.mult, [modt, colp], [mods])

                def big_stats(src, tti):
                    cs = slice(tti * 512, (tti + 1) * 512)
                    act(sqbig.t[:, 0:4, :], src.t[:, 0:4, cs], AF.Square, [(src, [c * 2 + tti for c in range(4)])], [(sqbig, 0)])
                    tt("pool", sqbig.t[:, 4:8, :], src.t[:, 4:8, cs], src.t[:, 4:8, cs], ALU.mult, [(src, [c * 2 + tti for c in range(4, 8)])], [(sqbig, 1)])
                    ps = pst()
                    for c in range(8):
                        mm(ps[:, :], ones_bf[:, :], sqbig.t[:, c, :], c == 0, c == 7, [ones_bf, (sqbig, c // 4)], [ps])
                    r_ = rstd2[tti]
                    act(r_[:], ps[:, :], AF.Sqrt, [ps, cc], [r_], scale=1.0 / D, bias=epsc)
                    recip(r_[:], r_[:], [r_], [r_])

                def norm_mod(src, ai, bi, dst):
                    for tti in range(2):
                        big_stats(src, tti)
                    for tti in range(2):
                        cs = slice(tti * 512, (tti + 1) * 512)
                        for hv in range(2):
                            tt("dve", xr[:], src.t[:, hv * 4:hv * 4 + 4, cs], rstd2[tti].t[:, None, :].to_broadcast([128, 4, 512]), ALU.mult,
                               [(src, [c * 2 + tti for c in range(hv * 4, hv * 4 + 4)]), rstd2[tti]], [xr])
                            for c4 in range(4):
                                c = hv * 4 + c4
                                if c % 2 == 0:
                                    act(dst.t[:, c, cs], xr.t[:, c4, :], AF.Identity, [(xr, c4), mods], [s2(dst, c, tti)], scale=mods.t[:, ai, c:c + 1], bias=mods.t[:, bi, c:c + 1])
                                else:
                                    ts("pool", dst.t[:, c, cs], xr.t[:, c4, :], mods.t[:, ai, c:c + 1], mods.t[:, bi, c:c + 1], ALU.mult, ALU.add, [(xr, c4), mods], [s2(dst, c, tti)])

                P.push()
                h = P.sb([128, 8, NTOK], BF16, nsub=16, name="h")
                P.phase = "norm1"
                norm_mod(x, 0, 1, h)

                def proj_fm(w, wv, col0, M, evf):
                    for tti in range(2):
                        ps = pst()
                        for kc in range(8):
                            mm(ps[0:M, :], wv[:, kc, col0:col0 + M], h.t[:, kc, tti * 512:(tti + 1) * 512], kc == 0, kc == 7, [w, s2(h, kc, tti)], [ps])
                        evf(ps, tti)

                def proj_tm(w, wv, col0, n, evf):
                    for blk in range(8):
                        ps = pst()
                        for kc in range(8):
                            mm(ps[:, 0:n], h.t[:, kc, blk * 128:(blk + 1) * 128], wv[:, kc, col0:col0 + n], kc == 0, kc == 7, [w, s2(h, kc, blk // 4)], [ps])
                        evf(ps, blk)

                def retention():
                    P.push()
                    qf = P.sb([128, 2, NTOK], BF16, name="qf")
                    qb = P.sb([128, 2, NTOK], BF16, name="qb")
                    sg = P.sb([128, 2, NTOK], BF16, name="sg")
                    ktm = P.sb([128, 8, 256], BF16, nsub=8, name="ktm")
                    vtm = P.sb([128, 8, 256], BF16, nsub=8, name="vtm")
                    vf = P.sb([128, 8, 256], BF16, nsub=8, name="vf")
                    vb = P.sb([128, 8, 256], BF16, nsub=8, name="vb")
                    lg = P.sb([128, 12], F32, name="lg")
                    patf = P.sb([128, 2, 128], F32, name="patf")
                    patb = P.sb([128, 2, 128], F32, name="patb")
                    cdc = P.sb([128, 4], F32, name="cdc")
                    DM = P.sb([128, 512], F32, name="DM")
                    kdp = P.sb([128, 2, 256], F32, name="kdp")
                    tfb = P.sb([128, 2, 128], F32, name="tfb")
                    S = P.sb([128, NSEQ * 2, 2, 128], F32, nsub=NSEQ * 2, name="S")
                    Sbf = P.sb([128, 8, 4, 128], BF16, nsub=16, name="Sbf")
                    SD = P.sb([128, 8, 512], BF16, nsub=8, name="SD")
                    P.push()
                    qTz = P.sb([128, 4, NTOK], BF16, name="qTz")
                    memset("pool", qTz[:], 0.0, [qTz])
                    kT = P.sb([128, 2, NTOK], BF16, name="kT")
                    act(lg[:], colp[:, C_LG:C_LG + 12], AF.Exp, [colp], [lg], scale=-1.0)
                    ts("dve", lg[:], lg[:], 1.0, None, ALU.add, None, [lg], [lg])
                    act(lg[:], lg[:], AF.Ln, [lg], [lg])
                    ts("dve", lg[:], lg[:], -1.0, None, ALU.mult, None, [lg], [lg])
                    for c in range(2):
                        act(patf.t[:, c, :], cst[:, K_IOTA1:K_IOTA1 + 128], AF.Exp, [cst, lg], [patf], scale=lg[:, c:c + 1])
                        act(patb.t[:, c, :], cst[:, K_REV:K_REV + 128], AF.Exp, [cst, lg], [patb], scale=lg[:, 2 + c:3 + c])
                    act(cdc[:], lg[:, 0:4], AF.Exp, [lg], [cdc], scale=128.0)
                    for hh in range(4):
                        act(tfb.t[:, 0, :], cst[:, K_LAGP:K_LAGP + 128], AF.Exp, [cst, lg], [tfb], scale=lg[:, 4 + hh:5 + hh])
                        act(tfb.t[:, 1, :], cst[:, K_LAGN:K_LAGN + 128], AF.Exp, [cst, lg], [tfb], scale=lg[:, 8 + hh:9 + hh])
                        stt(tfb.t[:, 0, :], tfb.t[:, 0, :], 0.125, cst[:, K_U:K_U + 128], ALU.mult, ALU.mult, [tfb, cst], [tfb])
                        stt(tfb.t[:, 1, :], tfb.t[:, 1, :], 0.125, cst[:, K_LO:K_LO + 128], ALU.mult, ALU.mult, [tfb, cst], [tfb])
                        tt("dve", DM[:, hh * 128:(hh + 1) * 128], tfb.t[:, 0, :], tfb.t[:, 1, :], ALU.add, [tfb], [DM])
                        act(kdp.t[:, 0, hh * 64:(hh + 1) * 64], cst[:, K_REVC:K_REVC + 64], AF.Exp, [cst, lg], [kdp], scale=lg[:, 4 + hh:5 + hh])
                        act(kdp.t[:, 1, hh * 64:(hh + 1) * 64], cst[:, K_POSC:K_POSC + 64], AF.Exp, [cst, lg], [kdp], scale=lg[:, 8 + hh:9 + hh])
                    ts("dve", kdp[:], kdp[:], 0.125, None, ALU.mult, None, [kdp], [kdp])
                    RS = 99
                    if RS <= 1:
                        P.pop()
                        return
                    w = wload(I["wretA"][l], 6144)
                    wv = w.t[:, 0:6144].rearrange("p (k n) -> p k n", k=8)
                    for c in range(2):
                        def ev_q(ps, tti, c=c):
                            cs = slice(tti * 512, (tti + 1) * 512)
                            for hf in range(2):
                                cp("act", qTz.t[hf * 64:(hf + 1) * 64, 2 * c + hf, cs], ps[hf * 64:(hf + 1) * 64, :], [ps], [qTz])
                            p3 = ps[:, :].rearrange("p (a b) -> p a b", a=4)
                            tt("dve", qf.t[:, c, cs].rearrange("p (a b) -> p a b", a=4), p3, patf.t[:, c:c + 1, :].to_broadcast([128, 4, 128]), ALU.mult, [ps, patf], [qf])
                            tt("dve", qb.t[:, c, cs].rearrange("p (a b) -> p a b", a=4), p3, patb.t[:, c:c + 1, :].to_broadcast([128, 4, 128]), ALU.mult, [ps, patb], [qb])
                        SUB = _os.environ.get("MK_RET_SUB", "qkg")
                        if "q" in SUB:
                            proj_fm(w, wv, c * 128, 128, ev_q)

                        def ev_k(ps, tti, c=c):
                            evac(kT.t[:, c, tti * 512:(tti + 1) * 512], ps[:, :], [ps], [kT])
                        if "k" in SUB:
                            proj_fm(w, wv, 256 + c * 128, 128, ev_k)

                        def ev_g(ps, tti, c=c):
                            act(sg.t[:, c, tti * 512:(tti + 1) * 512], ps[:, :], AF.Silu, [ps], [sg])
                        if "g" in SUB:
                            proj_fm(w, wv, 512 + c * 128, 128, ev_g)
                    if RS <= 2:
                        P.pop()
                        return
                    w2 = wload(I["wretB"][l], 4096)
                    wv2 = w2.t[:, 0:4096].rearrange("p (k n) -> p k n", k=8)

                    def ev_tm(ps, blk):
                        cp("act", ktm.t[:, blk, :], ps[:, 0:256], [ps], [(ktm, blk)])
                        cp("act", vtm.t[:, blk, :], ps[:, 256:512], [ps], [(vtm, blk)])
                        tt("dve", vf.t[:, blk, :], ps[:, 256:512], kdp.t[:, 0, :], ALU.mult, [ps, kdp], [(vf, blk)])
                        tt("dve", vb.t[:, blk, :], ps[:, 256:512], kdp.t[:, 1, :], ALU.mult, [ps, kdp], [(vb, blk)])
                    proj_tm(w2, wv2, 0, 512, ev_tm)
                    if RS <= 3:
                        P.pop()
                        return
                    for blk in range(8):
                        bc = slice(blk * 128, (blk + 1) * 128)
                        ps = pst()
                        for hh in range(4):
                            c, po = hh // 2, (hh % 2) * 64
                            mm(ps[:, hh * 128:(hh + 1) * 128], kT.t[:, c, bc], qTz.t[:, hh, bc], True, True, [kT, qTz], [ps])
                        tt("dve", SD.t[:, blk, :], ps[:, :], DM[:], ALU.mult, [ps, DM], [(SD, blk)])
                    if RS <= 4:
                        P.pop()
                        return
                    P.pop()
                    osb = P.sb([128, 2, NTOK], F32, nsub=4, name="osb")
                    memset("pool", S[:], 0.0, [S])
                    if SAMPLE:
                        for dr in range(2):
                            for c in range(2):
                                for hf in range(2):
                                    hh = 2 * c + hf
                                    P.dma("sp", S.t[hf * 64:(hf + 1) * 64, dr, c, hf * 64:(hf + 1) * 64], I["sret"][l, dr, hh], writes=[(S, dr)])
                    for step in range(nch):
                        for s in range(NSEQ):
                            for dr in range(2):
                                n = step if dr == 0 else nch - 1 - step
                                ch = s * 2 + dr
                                blk = s * nch + n
                                tt("pool", Sbf.t[:, blk, dr * 2:dr * 2 + 2, :], S.t[:, ch, :, :], BD.t[:, None, :].to_broadcast([128, 2, 128]), ALU.mult, [(S, ch), BD], [(Sbf, blk * 2 + dr)])
                                ps = pst()
                                vv = vf if dr == 0 else vb
                                for c in range(2):
                                    mm(ps[:, c * 128:(c + 1) * 128], ktm.t[:, blk, c * 128:(c + 1) * 128], vv.t[:, blk, c * 128:(c + 1) * 128], True, True, [(ktm, blk), (vv, blk)], [ps])
                                for c in range(2):
                                    stt(S.t[:, ch, c, :], S.t[:, ch, c, :], cdc[:, dr * 2 + c:dr * 2 + c + 1], ps[:, c * 128:(c + 1) * 128], ALU.mult, ALU.add, [(S, ch), cdc, ps], [(S, ch)])
                    if not SAMPLE:
                        for s in range(NSEQ):
                            for dr in range(2):
                                for c in range(2):
                                    for hf in range(2):
                                        hh = 2 * c + hf
                                        P.dma("sp", O["nsr"][s, l, dr, hh], S.t[hf * 64:(hf + 1) * 64, s * 2 + dr, c, hf * 64:(hf + 1) * 64], reads=[(S, s * 2 + dr)], is_output=True)
                    if RS <= 5:
                        P.pop()
                        return
                    for blk in range(8):
                        bc = slice(blk * 128, (blk + 1) * 128)
                        ps = pst()
                        for c in range(2):
                            for hf in range(2):
                                hh = 2 * c + hf
                                po = hf * 64
                                o_ = ps[:, (c * 2 + hf) * 128:(c * 2 + hf + 1) * 128]
                                mm(o_, vtm.t[:, blk, c * 128:(c + 1) * 128], SD.t[:, blk, hh * 128:(hh + 1) * 128], True, False, [(vtm, blk), (SD, blk)], [ps])
                                mm(o_, Sbf.t[:, blk, c, :], qf.t[:, c, bc], False, False, [(Sbf, blk * 2), qf], [ps])
                                mm(o_, Sbf.t[:, blk, 2 + c, :], qb.t[:, c, bc], False, True, [(Sbf, blk * 2 + 1), qb], [ps])
                        for c in range(2):
                            for hf in range(2):
                                po = hf * 64
                                evac(osb.t[po:po + 64, c, bc], ps[po:po + 64, (c * 2 + hf) * 128:(c * 2 + hf + 1) * 128], [ps], [(osb, c * 2 + blk // 4)])
                    if RS <= 6:
                        P.pop()
                        return
                    cen = tmpf[0]
                    sq = tmpf[1]
                    rs = rstd2[1]
                    for c in range(2):
                        for tti in range(2):
                            cs = slice(tti * 512, (tti + 1) * 512)
                            ps = pst()
                            mm(ps[:, :], BO[:, :], osb.t[:, c, cs], True, True, [BO, (osb, c * 2 + tti)], [ps])
                            tt("dve", cen[:], osb.t[:, c, cs], ps[:, :], ALU.subtract, [(osb, c * 2 + tti), ps], [cen])
                            act(sq[:], cen[:], AF.Square, [cen], [sq])
                            ps2 = pst()
                            mm(ps2[:, :], BO[:, :], sq[:], True, True, [BO, sq], [ps2])
                            act(rs[:], ps2[:, :], AF.Sqrt, [ps2, cc], [rs], scale=1.0, bias=epsc)
                            recip(rs[:], rs[:], [rs], [rs])
                            tt("dve", cen[:], cen[:], rs[:], ALU.mult, [cen, rs], [cen])
                            stt(ymix.t[:, c, cs], cen[:], colp[:, C_GN + c:C_GN + c + 1], sg.t[:, c, cs], ALU.mult, ALU.mult, [cen, colp, sg], [s2(ymix, c, tti)])
                    P.pop()

                def gqa():
                    P.push()
                    NCTX = 512 if SAMPLE else 0
                    NBK = 8 + (4 if SAMPLE else 0)
                    qT = P.sb([128, 2, NTOK], BF16, name="gqT")
                    kT = P.sb([128, 2, NCTX + NTOK], BF16, name="gkT")
                    memset("pool", kT[:], 0.0, [kT])
                    vaug = P.sb([128, NBK, 2, 192], BF16, name="gva")
                    tmo = P.sb([128, 8, 288], F32, name="tmo")
                    PT = [P.sb([128, 512], BF16, name="PT%d" % i) for i in range(3)]
                    pti = {"i": 0}
                    den = tmpf[0]

                    def nPT():
                        t = PT[pti["i"] % 3]
                        pti["i"] += 1
                        return t
                    memset("pool", vaug[:], 1.0, [vaug])
                    w = wload(I["wgqa"][l], 8 * 672)
                    wv = w.t[:, 0:8 * 672].rearrange("p (k n) -> p k n", k=8)
                    if SAMPLE:
                        ropt = P.sb([128, 2, NTOK], F32, name="ropt")
                        P.dma("sp", ropt[:], I["rope"][:, 0:2, :], writes=[ropt])
                        xfs = [P.sb([128, 512], F32, name="xf%d" % i) for i in range(2)]
                        xbs = [P.sb([128, 512], BF16, name="xb%d" % i) for i in range(2)]
                        t1s = [P.sb([128, 512], F32, name="t1%d" % i) for i in range(2)]
                        t2s = [P.sb([128, 512], F32, name="t2%d" % i) for i in range(2)]
                        rst = {"i": 0}

                        def ev_rope(dst_fn):
                            def f(ps, tti):
                                k_ = rst["i"] % 2
                                rst["i"] += 1
                                xf, xb, t1, t2 = xfs[k_], xbs[k_], t1s[k_], t2s[k_]
                                cs = slice(tti * 512, (tti + 1) * 512)
                                cp("act", xf[:], ps[:, :], [ps], [xf])
                                cp("dve", xb[:], ps[:, :], [ps], [xb])
                                ps2 = pst()
                                mm(ps2[:, :], rmat.t[:, 0, :], xb[:], True, True, [rmat, xb], [ps2])
                                tt("pool", t1[:], xf[:], ropt.t[:, 0, cs], ALU.mult, [xf, ropt], [t1])
                                tt("dve", t2[:], ps2[:, :], ropt.t[:, 1, cs], ALU.mult, [ps2, ropt], [t2])
                                for (dst, ap, r0, r1) in dst_fn(cs):
                                    tt("pool", ap, t1[r0:r1, :], t2[r0:r1, :], ALU.add, [t1, t2], [dst])
                            return f
                        for c in range(2):
                            proj_fm(w, wv, c * 128, 128, ev_rope(lambda cs, c=c: [(qT, qT.t[:, c, cs], 0, 128)]))
                        proj_fm(w, wv, 256, 128, ev_rope(lambda cs: [(kT, kT.t[kv_ * 64:(kv_ + 1) * 64, kv_, NCTX + cs.start:NCTX + cs.stop], kv_ * 64, (kv_ + 1) * 64) for kv_ in range(2)]))
                    else:
                        for c in range(2):
                            proj_fm(w, wv, c * 128, 128, lambda ps, tti, c=c: evac(qT.t[:, c, tti * 512:(tti + 1) * 512], ps[:, :], [ps], [qT]))
                        def ev_gk(ps, tti):
                            for kv_ in range(2):
                                evac(kT.t[kv_ * 64:(kv_ + 1) * 64, kv_, tti * 512:(tti + 1) * 512], ps[kv_ * 64:(kv_ + 1) * 64, :], [ps], [kT])
                        proj_fm(w, wv, 256, 128, ev_gk)
                    nb0 = 4 if SAMPLE else 0

                    def ev_tm(ps, blk):
                        if not SAMPLE:
                            cp("act", tmo.t[:, blk, :], ps[:, 0:288], [ps], [tmo])
                        cp("dve", vaug.t[:, nb0 + blk, :, 64:128], ps[:, 128:256].rearrange("p (k d) -> p k d", k=2), [ps], [vaug])
                    proj_tm(w, wv, 384, 288, ev_tm)
                    if not SAMPLE:
                        for s in range(NSEQ):
                            for k_ in range(2):
                                P.dma("sp", O["ngk"][s, l, k_].rearrange("(b p) d -> p b d", p=128),
                                      tmo.t[:, 2 * s:2 * s + 2, k_ * 64:(k_ + 1) * 64], reads=[tmo], is_output=True)
                                P.dma("sp", O["ngv"][s, l, k_].rearrange("(b p) d -> p b d", p=128),
                                      tmo.t[:, 2 * s:2 * s + 2, 128 + k_ * 64:128 + (k_ + 1) * 64], reads=[tmo], is_output=True)
                            P.dma("sp", O["nkr"][s, l].rearrange("(b p) d -> p b d", p=128),
                                  tmo.t[:, 2 * s:2 * s + 2, 256:288], reads=[tmo], is_output=True)
                    if SAMPLE:
                        ctm = P.sb([128, 4, 2, 64], F32, name="ctm")
                        cvm = P.sb([128, 4, 2, 64], F32, name="cvm")
                        for k_ in range(2):
                            P.dma("sp", ctm.t[:, :, k_, :], I["cgk"][l, k_].rearrange("(c p) d -> p c d", p=128), writes=[ctm])
                            P.dma("sp", cvm.t[:, :, k_, :], I["cgv"][l, k_].rearrange("(c p) d -> p c d", p=128), writes=[cvm])
                        ps = pst()
                        for cb in range(4):
                            tr(ps[:, cb * 128:(cb + 1) * 128], ctm.t[:, cb, :, :].rearrange("p k d -> p (k d)"), 128, [ctm], [ps])
                        for kv_ in range(2):
                            evac(kT.t[kv_ * 64:(kv_ + 1) * 64, kv_, 0:512], ps[kv_ * 64:(kv_ + 1) * 64, :], [ps], [kT])
                        cp("pool", vaug.t[:, 0:4, :, 64:128], cvm[:], [cvm], [vaug])

                    dst_ = {"i": 0}

                    def normalize(pso, po, ncols, hh, cq, c0):
                        den = tmpf[dst_["i"] % 2]
                        dst_["i"] += 1
                        nr = slice(po, po + 64)
                        dr_ = slice(64 - po, 128 - po)
                        ts("dve", den[nr, 0:ncols], pso[dr_, 0:ncols], colp[nr, C_SINK + hh:C_SINK + hh + 1], None, ALU.add, None, [pso, colp], [den])
                        recip(den[nr, 0:ncols], den[nr, 0:ncols], [den], [den])
                        tt("dve", ymix.t[nr, 4 + cq, c0:c0 + ncols], pso[nr, 0:ncols], den[nr, 0:ncols], ALU.mult, [pso, den], [sc(ymix, 4 + cq)])

                    act(colp[:, C_SINK:C_SINK + 4], colp[:, C_SINK:C_SINK + 4], AF.Exp, [colp], [colp])
                    if not SAMPLE:
                        for s in range(NSEQ):
                            for hh in range(4):
                                cq, kv = hh % 2, hh // 2
                                po = kv * 64
                                vs = slice(64, 192) if po == 0 else slice(0, 128)
                                ps = pst()
                                for kb in range(2):
                                    kc_ = slice(s * 256 + kb * 128, s * 256 + (kb + 1) * 128)
                                    mm(ps[:, kb * 256:(kb + 1) * 256], kT.t[:, kv, kc_], qT.t[:, cq, s * 256:(s + 1) * 256], True, True, [kT, qT], [ps])
                                pt = nPT()
                                act(pt[:], ps[:, :], AF.Exp, [ps], [pt], scale=0.125)
                                pso = pst()
                                for kb in range(2):
                                    mm(pso[:, 0:256], vaug.t[:, s * 2 + kb, kv, vs], pt[:, kb * 256:(kb + 1) * 256], kb == 0, kb == 1, [vaug, pt], [pso])
                                normalize(pso, po, 256, hh, cq, s * 256)
                    else:
                        for hh in range(4):
                            cq, kv = hh % 2, hh // 2
                            po = kv * 64
                            vs = slice(64, 192) if po == 0 else slice(0, 128)
                            acc = [pss[4 + 2 * (hh % 2)], pss[5 + 2 * (hh % 2)]]
                            for cb in range(4):
                                for tti in range(2):
                                    ps = pst()
                                    mm(ps[:, :], kT.t[:, kv, cb * 128:(cb + 1) * 128], qT.t[:, cq, tti * 512:(tti + 1) * 512], True, True, [kT, qT], [ps])
                                    pt = nPT()
                                    act(pt[:], ps[:, :], AF.Exp, [ps], [pt], scale=0.125)
                                    mm(acc[tti][:, :], vaug.t[:, cb, kv, vs], pt[:], cb == 0, False, [vaug, pt], [acc[tti]])
                            for m in range(8):
                                qlo, qhi = max(m - 1, 0), min(m + 1, 7)
                                n = (qhi - qlo + 1) * 128
                                ps = pst()
                                mm(ps[:, 0:n], kT.t[:, kv, 512 + m * 128:512 + (m + 1) * 128], qT.t[:, cq, qlo * 128:qlo * 128 + n], True, True, [kT, qT], [ps])
                                pt = nPT()
                                act(pt[:, 0:n], ps[:, 0:n], AF.Exp, [ps], [pt], scale=0.125)
                                if m - 1 >= 0:
                                    o0 = (m - 1 - qlo) * 128
                                    P.op("pool", lambda e, pt=pt, o0=o0: e.affine_select(out=pt[:, o0:o0 + 128], in_=pt[:, o0:o0 + 128], pattern=[[1, 128]], compare_op=ALU.is_ge, fill=0.0, base=0, channel_multiplier=-1), reads=[pt], writes=[pt])
                                if m + 1 <= 7:
                                    o0 = (m + 1 - qlo) * 128
                                    P.op("pool", lambda e, pt=pt, o0=o0: e.affine_select(out=pt[:, o0:o0 + 128], in_=pt[:, o0:o0 + 128], pattern=[[-1, 128]], compare_op=ALU.is_ge, fill=0.0, base=0, channel_multiplier=1), reads=[pt], writes=[pt])
                                for nq in range(qlo, qhi + 1):
                                    a = acc[nq // 4]
                                    mm(a[:, (nq % 4) * 128:(nq % 4 + 1) * 128], vaug.t[:, 4 + m, kv, vs], pt[:, (nq - qlo) * 128:(nq - qlo + 1) * 128], False, True, [vaug, pt], [a])
                            for tti in range(2):
                                normalize(acc[tti], po, 512, hh, cq, tti * 512)
                    P.pop()

                def mla():
                    P.push()
                    NCTX = 512 if SAMPLE else 0
                    NK = NCTX + NTOK
                    NBK = NK // 128
                    qn = P.sb([128, 2, NTOK], BF16, nsub=4, name="qn")
                    ckvT = P.sb([128, NTOK], F32, nsub=2, name="ckvT")
                    ckb = P.sb([128, NK], BF16, name="ckb")
                    krT = P.sb([128, NK], BF16, name="krT")
                    mw = P.sb([128, 1280], BF16, name="mw")
                    pti = {"i": 0}

                    def nPT():
                        t = PT[pti["i"] % 3]
                        pti["i"] += 1
                        return t
                    P.dma("pool", mw[:], I["mlaw"][l], writes=[mw])
                    uq = mw.t[:, 0:768].rearrange("p (k n) -> p k n", k=2)
                    w = wload(I["wmla"][l], 8 * 480)
                    wv = w.t[:, 0:8 * 480].rearrange("p (k n) -> p k n", k=8)
                    if SAMPLE:
                        ropm = P.sb([128, 2, NTOK], F32, name="ropm")
                        P.dma("sp", ropm[:], I["rope"][:, 2:4, :], writes=[ropm])
                        mxbs = [P.sb([128, 512], BF16, name="mxb%d" % i) for i in range(2)]
                        mt1s = [P.sb([128, 512], F32, name="mt10"), tmpf[1]]
                        mt2s = [P.sb([128, 512], F32, name="mt2%d" % i) for i in range(2)]
                        mrst = {"i": 0}

                        def rope96(ps, cs, lo, dst, dap):
                            k_ = mrst["i"] % 2
                            mrst["i"] += 1
                            xb, t1, t2 = mxbs[k_], mt1s[k_], mt2s[k_]
                            cp("act", xb[0:96, :], ps[0:96, :], [ps], [xb])
                            ps2 = pst()
                            mm(ps2[0:96, :], rmat.t[0:96, 1, 0:96], xb[0:96, :], True, True, [rmat, xb], [ps2])
                            tt("dve", t1[lo:96, :], ps[lo:96, :], ropm.t[lo:96, 0, cs], ALU.mult, [ps, ropm], [t1])
                            tt("dve", t2[lo:96, :], ps2[lo:96, :], ropm.t[lo:96, 1, cs], ALU.mult, [ps2, ropm], [t2])
                            tt("pool", dap, t1[lo:96, :], t2[lo:96, :], ALU.add, [t1, t2], [dst])
                    P.push()
                    ql = P.sb([128, 2, NTOK], F32, nsub=4, name="ql")
                    kvl = P.sb([128, NTOK], F32, nsub=2, name="kvl")
                    sqb = [P.sb([128, 512], BF16, name="sqb%d" % i) for i in range(2)]
                    for c in range(2):
                        proj_fm(w, wv, c * 128, 128, lambda ps, tti, c=c: evac(ql.t[:, c, tti * 512:(tti + 1) * 512], ps[:, :], [ps], [(ql, c * 2 + tti)]))
                    proj_fm(w, wv, 256, 128, lambda ps, tti: evac(kvl[:, tti * 512:(tti + 1) * 512], ps[:, :], [ps], [(kvl, tti)]))

                    def ev_kr(ps, tti):
                        cs = slice(tti * 512, (tti + 1) * 512)
                        if SAMPLE:
                            rope96(ps, cs, 64, krT, krT[64:96, NCTX + cs.start:NCTX + cs.stop])
                        else:
                            evac(krT[64:96, cs], ps[64:96, :], [ps], [krT])
                    proj_fm(w, wv, 384, 96, ev_kr)
                    for tti in range(2):
                        cs = slice(tti * 512, (tti + 1) * 512)
                        rms_rstd([(ql, ql.t[:, c, cs], c * 2 + tti) for c in range(2)], tti, 256, rstd, sqb)
                        for c in range(2):
                            stt(qn.t[:, c, cs], ql.t[:, c, cs], colp[:, C_QN + c:C_QN + c + 1], rstd[:], ALU.mult, ALU.mult, [(ql, c * 2 + tti), colp, rstd], [(qn, c * 2 + tti)])
                        rms_rstd([(kvl, kvl[:, cs], tti)], tti, 128, rstd, sqb)
                        stt(ckvT[:, cs], kvl[:, cs], colp[:, C_KVN:C_KVN + 1], rstd[:], ALU.mult, ALU.mult, [(kvl, tti), colp, rstd], [(ckvT, tti)])
                        cp("pool", ckb[:, NCTX + cs.start:NCTX + cs.stop], ckvT[:, cs], [(ckvT, tti)], [ckb])
                    if not SAMPLE:
                        otm = P.sb([128, 8, 128], F32, name="otm")
                        for half in range(2):
                            ps = pst()
                            for b4 in range(4):
                                blk = half * 4 + b4
                                tr(ps[:, b4 * 128:(b4 + 1) * 128], ckvT[:, blk * 128:(blk + 1) * 128], 128, [(ckvT, half)], [ps])
                            evac(otm.t[:, half * 4:half * 4 + 4, :], ps[:, :].rearrange("p (a b) -> p a b", a=4), [ps], [otm])
                        for s in range(NSEQ):
                            P.dma("sp", O["nckv"][s, l].rearrange("(b p) d -> p b d", p=128), otm.t[:, 2 * s:2 * s + 2, :], reads=[otm], is_output=True)
                    else:
                        ctm = P.sb([128, 4, 128], F32, name="mctm")
                        krm = P.sb([128, 4, 96], F32, name="krm")
                        P.dma("sp", ctm[:], I["cckv"][l].rearrange("(c p) d -> p c d", p=128), writes=[ctm])
                        memset("pool", krm[:], 0.0, [krm])
                        P.dma("sp", krm.t[:, :, 64:96], I["ckr"][l].rearrange("(c p) d -> p c d", p=128), writes=[krm])
                        ps = pst()
                        for cb in range(4):
                            tr(ps[:, cb * 128:(cb + 1) * 128], ctm.t[:, cb, :], 128, [ctm], [ps])
                        evac(ckb[:, 0:512], ps[:, :], [ps], [ckb])
                        ps = pst()
                        for cb in range(4):
                            tr(ps[0:96, cb * 128:(cb + 1) * 128], krm.t[:, cb, :], 128, [krm], [ps])
                        evac(krT[64:96, 0:512], ps[64:96, :], [ps], [krT])
                    P.pop()
                    qh = P.sb([128, 4, NTOK], BF16, name="qh")
                    kTh = P.sb([128, 4, NK], BF16, name="kTh")
                    vaug = P.sb([128, NBK, 4, 192], BF16, name="mva")
                    PT = [P.sb([128, 512], BF16, name="mPT%d" % i) for i in range(3)]
                    den = tmpf[0]
                    memset("pool", vaug[:], 1.0, [vaug])
                    for hh in range(4):
                        for tti in range(2):
                            cs = slice(tti * 512, (tti + 1) * 512)
                            ps = pst()
                            for kc in range(2):
                                mm(ps[0:96, :], uq[:, kc, hh * 96:(hh + 1) * 96], qn.t[:, kc, cs], kc == 0, kc == 1, [mw, (qn, kc * 2 + tti)], [ps])
                            if SAMPLE:
                                rope96(ps, cs, 0, qh, qh.t[0:96, hh, cs])
                            else:
                                evac(qh.t[0:96, hh, cs], ps[0:96, :], [ps], [qh])
                    for hh in range(4):
                        for kt in range(NK // 512):
                            ks = slice(kt * 512, (kt + 1) * 512)
                            ps = pst()
                            mm(ps[0:64, :], mw[:, 768 + hh * 64:768 + (hh + 1) * 64], ckb[:, ks], True, True, [mw, ckb], [ps])
                            evac(kTh.t[0:64, hh, ks], ps[0:64, :], [ps], [kTh])
                        cp("pool", kTh.t[64:96, hh, :], krT[64:96, :], [krT], [kTh])
                    for b in range(NBK):
                        ps = pst()
                        mm(ps[:, 0:256], ckb[:, b * 128:(b + 1) * 128], mw[:, 1024:1280], True, True, [ckb, mw], [ps])
                        evac(vaug.t[:, b, :, 64:128], ps[:, 0:256].rearrange("p (k d) -> p k d", k=4), [ps], [vaug])
                    SCL = float(96 ** -0.5)

                    dst_ = {"i": 0}

                    def normalize(pso, po, ncols, cm, c0):
                        den = tmpf[0] if SAMPLE else tmpf[dst_["i"] % 2]
                        dst_["i"] += 1
                        nr = slice(po, po + 64)
                        dr_ = slice(64 - po, 128 - po)
                        recip(den[nr, 0:ncols], pso[dr_, 0:ncols], [pso], [den])
                        tt("dve", ymix.t[nr, 6 + cm, c0:c0 + ncols], pso[nr, 0:ncols], den[nr, 0:ncols], ALU.mult, [pso, den], [sc(ymix, 6 + cm)])
                    if not SAMPLE:
                        for s in range(NSEQ):
                            for hh in range(4):
                                cm, po = hh // 2, (hh % 2) * 64
                                vs = slice(64, 192) if po == 0 else slice(0, 128)
                                ps = pst()
                                for kb in range(2):
                                    kc_ = slice(s * 256 + kb * 128, s * 256 + (kb + 1) * 128)
                                    mm(ps[:, kb * 256:(kb + 1) * 256], kTh.t[0:96, hh, kc_], qh.t[0:96, hh, s * 256:(s + 1) * 256], True, True, [kTh, qh], [ps])
                                pt = nPT()
                                act(pt[:], ps[:, :], AF.Exp, [ps], [pt], scale=SCL)
                                pso = pst()
                                for kb in range(2):
                                    mm(pso[:, 0:256], vaug.t[:, s * 2 + kb, hh, vs], pt[:, kb * 256:(kb + 1) * 256], kb == 0, kb == 1, [vaug, pt], [pso])
                                normalize(pso, po, 256, cm, s * 256)
                    else:
                        for hh in range(4):
                            cm, po = hh // 2, (hh % 2) * 64
                            vs = slice(64, 192) if po == 0 else slice(0, 128)
                            acc = [pss[4 + 2 * (hh % 2)], pss[5 + 2 * (hh % 2)]]
                            for kb in range(NBK):
                                for tti in range(2):
                                    ps = pst()
                                    mm(ps[:, :], kTh.t[0:96, hh, kb * 128:(kb + 1) * 128], qh.t[0:96, hh, tti * 512:(tti + 1) * 512], True, True, [kTh, qh], [ps])
                                    pt = nPT()
                                    act(pt[:], ps[:, :], AF.Exp, [ps], [pt], scale=SCL)
                                    mm(acc[tti][:, :], vaug.t[:, kb, hh, vs], pt[:], kb == 0, kb == NBK - 1, [vaug, pt], [acc[tti]])
                            for tti in range(2):
                                normalize(acc[tti], po, 512, cm, tti * 512)
                    P.pop()

                def hyena():
                    P.push()
                    h2 = P.sb([64, 2, L], F32, name="h2")
                    hw = P.sb([64, 1152], F32, name="hw")
                    absd = P.sb([128, 512], F32, name="absd")
                    tn = P.sb([128, 2, nch], F32, name="tn")
                    P.dma("sp", hw[:], I["hyw"][l], writes=[hw])
                    P.dma("sp", absd[:], I["rowp"][l], writes=[absd])
                    P.dma("sp", tn[:], G.tn, writes=[tn])
                    act(absd[:], absd[:], AF.Abs, [absd], [absd])
                    PH = P.phase
                    P.phase = PH + ".mlp"
                    P.push()
                    zg = P.sb([17, 2, L], F32, name="zg")
                    h1 = P.sb([64, 512], F32, name="h1")
                    ri = P.sb([64, 512], I32, name="ri")
                    rf = P.sb([64, 512], F32, name="rf")
                    ra = P.sb([64, 512], F32, name="ra")
                    P.dma("sp", zg[:], G.zg, writes=[zg])
                    nt = min(L, 512)

                    def sin_layer(ps, bcol, out_ap, out_t):
                        ts("dve", ra[:, 0:nt], ps[0:64, 0:nt], colp[0:64, bcol:bcol + 1], float(PI + 16 * TWO_PI), ALU.add, ALU.add, [ps, colp], [ra])
                        ts("dve", ri[:, 0:nt], ra[:, 0:nt], float(1.0 / TWO_PI), None, ALU.mult, None, [ra], [ri])
                        cp("dve", rf[:, 0:nt], ri[:, 0:nt], [ri], [rf])
                        stt(ra[:, 0:nt], rf[:, 0:nt], float(-TWO_PI), ra[:, 0:nt], ALU.mult, ALU.add, [rf, ra], [ra])
                        ts("dve", rf[:, 0:nt], ra[:, 0:nt], 0.0, float(TWO_PI), ALU.is_lt, ALU.mult, [ra], [rf])
                        tt("dve", ra[:, 0:nt], ra[:, 0:nt], rf[:, 0:nt], ALU.add, [ra, rf], [ra])
                        ts("dve", ra[:, 0:nt], ra[:, 0:nt], float(PI), 3.1415925, ALU.subtract, ALU.min, [ra], [ra])
                        ts("dve", ra[:, 0:nt], ra[:, 0:nt], -3.1415925, None, ALU.max, None, [ra], [ra])
                        act(out_ap, ra[:, 0:nt], AF.Sin, [ra], [out_t])
                    for g in range(2):
                        for ti in range(L // nt):
                            cs = slice(ti * nt, (ti + 1) * nt)
                            ps = pst()
                            mm(ps[0:64, 0:nt], hw[0:17, 0:64], zg.t[0:17, g, cs], True, True, [hw, zg], [ps])
                            sin_layer(ps, C_HB1, h1[:, 0:nt], h1)
                            ps2 = pst()
                            mm(ps2[0:64, 0:nt], hw[0:64, 64:128], h1[:, 0:nt], True, True, [hw, h1], [ps2])
                            sin_layer(ps2, C_HB2, h2.t[:, g, cs], h2)
                    P.pop()
                    hpads = [P.sb([128, NSEQ, L + 2], F32, name="hpad%d" % i) for i in range(2)]
                    for hp_ in hpads:
                        memset("pool", hp_[:], 0.0, [hp_])
                    u = P.sb([128, 3, NTOK], F32, nsub=6, name="u")
                    z1 = P.sb([128, NTOK], F32, nsub=2, name="z1")
                    utm = P.sb([128, nch, NSEQ, 128], BF16, name="utm")
                    AB = P.sb([128, nch, 2, 2, 128], BF16, name="AB")
                    Gt = P.sb([128, nch, 2, 128], F32, name="Gt")
                    Ur = P.sb([128, nch, NSEQ, 128], F32, name="Ur")
                    Y = P.sb([128, nch, 2, NSEQ, 128], BF16, name="Y")
                    ffs = [P.sb([128, 2, 128], F32, name="ff%d" % i) for i in range(2)]
                    fbs = [P.sb([128, 2, 128], F32, name="fb%d" % i) for i in range(2)]
                    wins = [P.sb([128, 2, 128], F32, name="win%d" % i) for i in range(2)]
                    m1s = [P.sb([128, 2, 128], F32, name="m1%d" % i) for i in range(2)]
                    m2s = [P.sb([128, 2, 128], F32, name="m2%d" % i) for i in range(2)]
                    nsg = 2 if NSEQ > 1 else 1
                    sgs = [(a, a + nsg) for a in range(0, NSEQ, nsg)]
                    dft = G.dft
                    nel = nch * L

                    NH = 2 if L > 512 else 1
                    HW_ = L // NH
                    FPH = HW_ // 128

                    def dload(i):
                        res = []
                        for hv in range(NH):
                            k = wstate["h"] % (2 * NW)
                            wstate["h"] += 1
                            t, hf = wring[k // 2], k % 2
                            P.dma("sp", t[:, hf * 4096:hf * 4096 + nch * HW_], dft[i, hv], writes=[(t, hf)])
                            res.append((t, hf, t.t[:, hf * 4096:hf * 4096 + nch * HW_].rearrange("p (k n) -> p k n", k=nch)))
                        return res

                    def mcol(res, fch):
                        t, hf, v = res[fch // FPH]
                        c0 = (fch % FPH) * 128
                        return (t, hf), v, c0
                    for cc_ in range(2):
                        P.phase = PH + ".proj"
                        w = wload(I["why"][l], 6144)
                        wv = w.t[:, 0:6144].rearrange("p (k n) -> p k n", k=8)
                        for part in range(3):
                            ci = part * 2 + cc_
                            hpad = hpads[part % 2]

                            def ev_h(ps, tti):
                                if NSEQ > 1:
                                    evac(hpad.t[:, 2 * tti:2 * tti + 2, 1:L + 1], ps[:, :].rearrange("p (s t) -> p s t", s=2), [ps], [hpad])
                                else:
                                    evac(hpad.t[:, 0, 1 + tti * 512:1 + (tti + 1) * 512], ps[:, :], [ps], [hpad])
                            proj_fm(w, wv, ci * 128, 128, ev_h)
                            uv = u.t[:, part, :].rearrange("p (s t) -> p s t", s=NSEQ)
                            act(uv, hpad.t[:, :, 1:L + 1], AF.Identity, [hpad, colp], [sc(u, part)], scale=colp[:, C_HSW + 6 + ci:C_HSW + 7 + ci], bias=colp[:, C_HSB + ci:C_HSB + ci + 1])
                            stt(uv, hpad.t[:, :, 0:L], colp[:, C_HSW + ci:C_HSW + ci + 1], uv, ALU.mult, ALU.add, [hpad, colp, sc(u, part)], [sc(u, part)])
                            stt(uv, hpad.t[:, :, 2:L + 2], colp[:, C_HSW + 12 + ci:C_HSW + 13 + ci], uv, ALU.mult, ALU.add, [hpad, colp, sc(u, part)], [sc(u, part)])
                        P.phase = PH + ".filt"
                        for tch in range(nch):
                            tcs = slice(tch * 128, (tch + 1) * 128)
                            ps = pst()
                            for o_ in range(2):
                                cf = 128 + o_ * 512 + cc_ * 128
                                mm(ps[:, o_ * 128:(o_ + 1) * 128], h2.t[:, 0, tcs], hw[0:64, cf:cf + 128], True, True, [h2, hw], [ps])
                                mm(ps[:, 256 + o_ * 128:256 + (o_ + 1) * 128], h2.t[:, 1, tcs], hw[0:64, cf + 256:cf + 384], True, True, [h2, hw], [ps])
                            win, ff, fb = wins[tch % 2], ffs[tch % 2], fbs[tch % 2]
                            act(win.t[:, 0, :], absd[:, cc_ * 128:(cc_ + 1) * 128], AF.Exp, [absd, tn], [win], scale=tn.t[:, 0, tch:tch + 1])
                            act(win.t[:, 1, :], absd[:, 256 + cc_ * 128:256 + (cc_ + 1) * 128], AF.Exp, [absd, tn], [win], scale=tn.t[:, 1, tch:tch + 1])
                            tt("dve", ff[:], ps[:, 0:256].rearrange("p (o c) -> p o c", o=2), win.t[:, 0:1, :].to_broadcast([128, 2, 128]), ALU.mult, [ps, win], [ff])
                            tt("dve", fb[:], ps[:, 256:512].rearrange("p (o c) -> p o c", o=2), win.t[:, 1:2, :].to_broadcast([128, 2, 128]), ALU.mult, [ps, win], [fb])
                            if tch == 0:
                                memset("dve", fb[0:1, :, :], 0.0, [fb])
                            tt("pool", AB.t[:, tch, :, 0, :], ff[:], fb[:], ALU.add, [ff, fb], [AB])
                            tt("pool", AB.t[:, tch, :, 1, :], ff[:], fb[:], ALU.subtract, [ff, fb], [AB])
                        for o in range(2):
                            P.phase = PH + ".tr"
                            for blk in range(8):
                                s_, jc = blk // nch, blk % nch
                                ps = pst()
                                if o == 0:
                                    tr(ps[:, 0:128], u.t[:, 2, blk * 128:(blk + 1) * 128], 128, [(u, 4 + blk // 4)], [ps])
                                else:
                                    tr(ps[:, 0:128], z1[:, blk * 128:(blk + 1) * 128], 128, [(z1, blk // 4)], [ps])
                                evac(utm.t[:, jc, s_, :], ps[:, 0:128], [ps], [utm])
                            P.phase = PH + ".A"
                            mres = dload(0)
                            for fch in range(nch):
                                ps = pst()
                                mt, mv, c0 = mcol(mres, fch)
                                for tch in range(nch):
                                    mm(ps[:, 0:128], mv[:, tch, c0:c0 + 128], AB.t[:, tch, o, 0, :], tch == 0, tch == nch - 1, [mt, AB], [ps])
                                evac(Gt.t[:, fch, 0, :], ps[:, 0:128], [ps], [Gt])
                            for (sa, sb_) in sgs:
                                ns = sb_ - sa
                                for fch in range(nch):
                                    ps = pst()
                                    mt, mv, c0 = mcol(mres, fch)
                                    for jc in range(nch):
                                        mm(ps[:, 0:ns * 128], mv[:, jc, c0:c0 + 128], utm.t[:, jc, sa:sb_, :], jc == 0, jc == nch - 1, [mt, utm], [ps])
                                    evac(Ur.t[:, fch, sa:sb_, :], ps[:, 0:ns * 128].rearrange("p (s c) -> p s c", s=ns), [ps], [Ur])
                            P.phase = PH + ".B"
                            mres = dload(1)
                            for fch in range(nch):
                                ps = pst()
                                mt, mv, c0 = mcol(mres, fch)
                                for tch in range(nch):
                                    mm(ps[:, 0:128], mv[:, tch, c0:c0 + 128], AB.t[:, tch, o, 1, :], tch == 0, tch == nch - 1, [mt, AB], [ps])
                                evac(Gt.t[:, fch, 1, :], ps[:, 0:128], [ps], [Gt])
                            for (sa, sb_) in sgs:
                                ns = sb_ - sa
                                for fch in range(nch):
                                    ps = pst()
                                    mt, mv, c0 = mcol(mres, fch)
                                    for jc in range(nch):
                                        mm(ps[:, 0:ns * 128], mv[:, jc, c0:c0 + 128], utm.t[:, jc, sa:sb_, :], jc == 0, jc == nch - 1, [mt, utm], [ps])
                                    p3 = ps[:, 0:ns * 128].rearrange("p (s c) -> p s c", s=ns)
                                    gr = Gt.t[:, fch, 0:1, :].to_broadcast([128, ns, 128])
                                    gi = Gt.t[:, fch, 1:2, :].to_broadcast([128, ns, 128])
                                    ur = Ur.t[:, fch, sa:sb_, :]
                                    m1, m2 = m1s[fch % 2], m2s[fch % 2]
                                    tt("pool", m1.t[:, 0:ns, :], ur, gr, ALU.mult, [Ur, Gt], [m1])
                                    tt("dve", m2.t[:, 0:ns, :], p3, gi, ALU.mult, [ps, Gt], [m2])
                                    tt("pool", Y.t[:, fch, 0, sa:sb_, :], m1.t[:, 0:ns, :], m2.t[:, 0:ns, :], ALU.subtract, [m1, m2], [Y])
                                    tt("pool", m1.t[:, 0:ns, :], ur, gi, ALU.mult, [Ur, Gt], [m1])
                                    tt("dve", m2.t[:, 0:ns, :], p3, gr, ALU.mult, [ps, Gt], [m2])
                                    tt("pool", Y.t[:, fch, 1, sa:sb_, :], m1.t[:, 0:ns, :], m2.t[:, 0:ns, :], ALU.add, [m1, m2], [Y])
                            P.phase = PH + ".inv"
                            mrc = dload(2)
                            mrs = dload(3)
                            nt = min(L, 512)
                            tiles = [(s_, it, pst()) for s_ in range(NSEQ) for it in range(L // nt)]
                            for (s_, it, ps) in tiles:
                                t_, hf_, v_ = mrc[it]
                                for fch in range(nch):
                                    mm(ps[:, 0:nt], Y.t[:, fch, 0, s_, :], v_[:, fch, 0:nt], fch == 0, False, [Y, (t_, hf_)], [ps])
                            for (s_, it, ps) in tiles:
                                t_, hf_, v_ = mrs[it]
                                for fch in range(nch):
                                    mm(ps[:, 0:nt], Y.t[:, fch, 1, s_, :], v_[:, fch, 0:nt], False, fch == nch - 1, [Y, (t_, hf_)], [ps])
                            for (s_, it, ps) in tiles:
                                if True:
                                    t0 = s_ * L + it * nt
                                    tcs = slice(t0, t0 + nt)
                                    tix = t0 // 512
                                    t_ = tmp()
                                    bcol = colp[:, C_HBIAS + o * 2 + cc_:C_HBIAS + o * 2 + cc_ + 1]
                                    if o == 0:
                                        stt(t_[:, 0:nt], u.t[:, 2, tcs], bcol, ps[:, 0:nt], ALU.mult, ALU.add, [(u, 4 + tix), colp, ps], [t_])
                                        tt("pool", z1[:, tcs], t_[:, 0:nt], u.t[:, 0, tcs], ALU.mult, [t_, (u, 0 + tix)], [(z1, tix)])
                                    else:
                                        stt(t_[:, 0:nt], z1[:, tcs], bcol, ps[:, 0:nt], ALU.mult, ALU.add, [(z1, tix), colp, ps], [t_])
                                        tt("pool", ymix.t[:, 2 + cc_, tcs], t_[:, 0:nt], u.t[:, 1, tcs], ALU.mult, [t_, (u, 2 + tix)], [s2(ymix, 2 + cc_, tix)])
                    P.pop()

                skip = _os.environ.get("MK_SKIP", "")
                if "ret" not in skip:
                    P.phase = "ret"
                    retention()
                pstate["n"] = 4 if SAMPLE else 8
                if "gqa" not in skip:
                    P.phase = "gqa" + G.name
                    gqa()
                if "mla" not in skip:
                    P.phase = "mla" + G.name
                    mla()
                pstate["n"] = 8
                if "hy" not in skip:
                    P.phase = "hy" + G.name
                    hyena()
                P.phase = "wout"
                P.pop()
                if dbg and l == 0:
                    dump("ymix_" + G.name, ymix, ymix[:], [128, 8, NTOK], BF16)

                P.push()
                y = P.sb([128, 8, NTOK], F32, nsub=16, name="y")

                def post_norm_add(ci):
                    for tti in range(2):
                        big_stats(y, tti)
                    for tti in range(2):
                        cs = slice(tti * 512, (tti + 1) * 512)
                        for hv in range(2):
                            tt("pool", xr[:], y.t[:, hv * 4:hv * 4 + 4, cs], rstd2[tti].t[:, None, :].to_broadcast([128, 4, 512]), ALU.mult,
                               [(y, [c * 2 + tti for c in range(hv * 4, hv * 4 + 4)]), rstd2[tti]], [xr])
                            for c4 in range(4):
                                c = hv * 4 + c4
                                stt(x.t[:, c, cs], xr.t[:, c4, :], mods.t[:, ci, c:c + 1], x.t[:, c, cs], ALU.mult, ALU.add, [(xr, c4), mods, s2(x, c, tti)], [s2(x, c, tti)])
                w = wload(I["wout"][l], 8192)
                wv = w.t[:, :].rearrange("p (k n) -> p k n", k=8)
                for m in range(8):
                    for tti in range(2):
                        ps = pst()
                        for kc in range(8):
                            mm(ps[:, :], wv[:, kc, m * 128:(m + 1) * 128], ymix.t[:, kc, tti * 512:(tti + 1) * 512], kc == 0, kc == 7, [w, s2(ymix, kc, tti)], [ps])
                        evac(y.t[:, m, tti * 512:(tti + 1) * 512], ps[:, :], [ps], [s2(y, m, tti)])
                P.phase = "wout.post"
                post_norm_add(2)
                if dbg and l == 0:
                    dump("x1_" + G.name, x, x[:], [128, 8, NTOK])
                if "ffn" in skip:
                    P.pop()
                    continue
                P.phase = "ffn"
                h2_ = ymix
                norm_mod(x, 3, 4, h2_)
                P.phase = "ffn.up"
                actb = P.sb([128, NHC, NTOK], BF16, name="actb")
                gps = [P.sb([128, NSEQ, L + 2], F32, name="gp%d" % i) for i in range(2)]
                gts = [P.sb([128, NTOK], F32, name="gt%d" % i) for i in range(2)]
                for gp in gps:
                    memset("pool", gp[:], 0.0, [gp])
                psus = {}
                wcur = {}

                def ffn_tail(hc):
                    gt = gts[hc % 2]
                    act(gt[:], gt[:], AF.Silu, [gt], [gt])
                    for tti in range(2):
                        cs = slice(tti * 512, (tti + 1) * 512)
                        tt("dve", actb.t[:, hc, cs], gt[:, cs], psus[hc][tti][:, :], ALU.mult, [gt, psus[hc][tti]], [actb])
                    del psus[hc]

                for hc in range(NHC + 1):
                    if hc < NHC:
                        b_, j = hc // 2, hc % 2
                        if j == 0:
                            wcur["w"] = wload_h(I["wup"][l, b_], 4096)
                        w, whf, wo = wcur["w"]
                        wv = w.t[:, wo:wo + 4096].rearrange("p (k n) -> p k n", k=8)
                        gp, gt = gps[hc % 2], gts[hc % 2]
                        psg = []
                        for tti in range(2):
                            ps = pst()
                            for kc in range(8):
                                mm(ps[:, :], wv[:, kc, j * 128:(j + 1) * 128], h2_.t[:, kc, tti * 512:(tti + 1) * 512], kc == 0, kc == 7, [(w, whf), s2(h2_, kc, tti)], [ps])
                            psg.append(ps)
                        pu = []
                        for tti in range(2):
                            ps = pst()
                            for kc in range(8):
                                mm(ps[:, :], wv[:, kc, 256 + j * 128:256 + (j + 1) * 128], h2_.t[:, kc, tti * 512:(tti + 1) * 512], kc == 0, kc == 7, [(w, whf), s2(h2_, kc, tti)], [ps])
                            pu.append(ps)
                        psus[hc] = pu
                    if hc >= 1:
                        ffn_tail(hc - 1)
                    if hc < NHC:
                        for tti in range(2):
                            ps = psg[tti]
                            if NSEQ > 1:
                                cp("act", gp.t[:, 2 * tti:2 * tti + 2, 1:L + 1], ps[:, :].rearrange("p (s t) -> p s t", s=2), [ps], [gp])
                            else:
                                cp("act", gp.t[:, 0, 1 + tti * 512:1 + (tti + 1) * 512], ps[:, :], [ps], [gp])
                        gv = gt[:, :].rearrange("p (s t) -> p s t", s=NSEQ)
                        act(gv, gp.t[:, :, 1:L + 1], AF.Identity, [gp, colp], [gt], scale=colp[:, C_FCW + 22 + hc:C_FCW + 23 + hc], bias=colp[:, C_FCB + hc:C_FCB + hc + 1])
                        stt(gv, gp.t[:, :, 0:L], colp[:, C_FCW + hc:C_FCW + hc + 1], gv, ALU.mult, ALU.add, [gp, colp, gt], [gt])
                        stt(gv, gp.t[:, :, 2:L + 2], colp[:, C_FCW + 44 + hc:C_FCW + 45 + hc], gv, ALU.mult, ALU.add, [gp, colp, gt], [gt])
                P.phase = "ffn.dn"
                for m in range(8):
                    w, whf, wo = wload_h(I["wdn"][l, m], 22 * 128)
                    wv = w.t[:, wo:wo + 22 * 128].rearrange("p (k n) -> p k n", k=22)
                    for tti in range(2):
                        ps = pst()
                        for kc in range(NHC):
                            mm(ps[:, :], wv[:, kc, :], actb.t[:, kc, tti * 512:(tti + 1) * 512], kc == 0, kc == NHC - 1, [(w, whf), actb], [ps])
                        evac(y.t[:, m, tti * 512:(tti + 1) * 512], ps[:, :], [ps], [s2(y, m, tti)])
                P.phase = "ffn.post"
                post_norm_add(5)
                P.pop()
                if dbg and l == 0:
                    dump("x2_" + G.name, x, x[:], [128, 8, NTOK])

            P.phase = "io"
            P.push()
            ot = [P.sb([128, D], F32, name="ot%d" % i) for i in range(2)]
            for blk in range(8):
                o_ = ot[blk % 2]
                for half in range(2):
                    ps = pst()
                    for c4 in range(4):
                        c = half * 4 + c4
                        tr(ps[:, c4 * 128:(c4 + 1) * 128], x.t[:, c, blk * 128:(blk + 1) * 128], 128, [s2(x, c, blk // 4)], [ps])
                    evac(o_[:, half * 512:(half + 1) * 512], ps[:, :], [ps], [o_])
                P.dma("sp", G.y_out[blk * 128:(blk + 1) * 128, :], o_[:], reads=[o_], is_output=True)
            P.pop()
            P.pop()
            mstate["ready"] = True

        GP = Grp()
        GP.name, GP.L, GP.NSEQ, GP.sample, GP.gi = "P", 256, 4, False, 0
        GP.x_in, GP.y_out, GP.dft, GP.zg, GP.tn = I["xp"], O["yp"], I["dftP"], I["zgP"], I["tnP"]
        GS = Grp()
        GS.name, GS.L, GS.NSEQ, GS.sample, GS.gi = "S", 1024, 1, True, 1
        GS.x_in, GS.y_out, GS.dft, GS.zg, GS.tn = I["xs"], O["ys"], I["dftS"], I["zgS"], I["tnS"]
        import os as _os_mod
        which = _os_mod.environ.get("MK_GROUPS", "PS")
        if "P" in which:
            run_group(GP)
        if "S" in which:
            run_group(GS)
        if _os_mod.environ.get("MK_PHASES"):
            import json as _json
            _json.dump(P.pe_phase, open(_os_mod.environ["MK_PHASES"], "w"))
        P.emit()
    return nc, DBG


_CONST = {}


def prep_inputs(inp):
    if "c" not in _CONST:
        _CONST["c"] = host_consts()
    cst = _CONST["c"]
    w = host_weights(inp)
    shared = dict(w)
    shared.update({"cst": cst["cst"], "dftP": cst["dftP"], "dftS": cst["dftS"], "zgP": cst["zgP"], "zgS": cst["zgS"],
                   "tnP": cst["tnP"], "tnS": cst["tnS"], "rope": cst["rope"], "rmat": cst["rmat"]})
    in_maps = []
    for core in range(8):
        b = core % 2
        m = dict(shared)
        m["xp"] = np.ascontiguousarray(inp["x_prompt"][core * 4:(core + 1) * 4].reshape(NTOK, D))
        m["xs"] = np.ascontiguousarray(inp["x_sample"][b])
        m["sret"] = np.ascontiguousarray(inp["state_ret"][b])
        m["cgk"] = np.ascontiguousarray(inp["cache_gqa_k"][b])
        m["cgv"] = np.ascontiguousarray(inp["cache_gqa_v"][b])
        m["cckv"] = np.ascontiguousarray(inp["cache_mla_ckv"][b])
        m["ckr"] = np.ascontiguousarray(inp["cache_mla_krope"][b])
        cond = np.stack([col_tile(inp["c_ctx"]), col_tile(inp["c"][b])]).astype(np.float32)
        m["cond"] = np.ascontiguousarray(cond)
        in_maps.append(m)
    return in_maps


def kernel(**inputs):
    inp = {k: np.asarray(v) for k, v in inputs.items()}
    in_maps = prep_inputs(inp)
    nc, _ = build(False)
    res = run_bass_kernel_spmd(nc, in_maps, core_ids=list(range(8)))
    rs = res.results
    yp = np.concatenate([rs[c]["yp"].reshape(4, 256, D) for c in range(8)], axis=0)
    ys = np.stack([rs[0]["ys"], rs[1]["ys"]], axis=0)
    nsr = np.concatenate([rs[c]["nsr"] for c in range(8)], axis=0)
    ngk = np.concatenate([rs[c]["ngk"] for c in range(8)], axis=0)
    ngv = np.concatenate([rs[c]["ngv"] for c in range(8)], axis=0)
    nckv = np.concatenate([rs[c]["nckv"] for c in range(8)], axis=0)
    nkr = np.concatenate([rs[c]["nkr"] for c in range(8)], axis=0)
    f = lambda a: np.ascontiguousarray(a, dtype=np.float32)
    return (f(yp), f(ys), f(nsr), f(ngk), f(ngv), f(nckv), f(nkr))
```

```python
import numpy as np
import concourse.bass as bass
import concourse.mybir as mybir
from concourse.bass_utils import run_bass_kernel_spmd

F32 = mybir.dt.float32
BF16 = mybir.dt.bfloat16
I32 = mybir.dt.int32
ALU = mybir.AluOpType
AF = mybir.ActivationFunctionType
AX = mybir.AxisListType

COMPUTE = ("pe", "act", "dve", "pool")
NDMASEM = 6


class T:
    def __init__(self, t, nsub=1, name=""):
        self.t = t
        self.nsub = nsub
        self.name = name
        self.lw = [None] * nsub
        self.rd = [[] for _ in range(nsub)]
        self.psum = False

    def __getitem__(self, idx):
        return self.t[idx]


class Prog:
    def __init__(self, nc, ctx):
        self.nc = nc
        self.ctx = ctx
        self.ops = {e: [] for e in COMPUTE + ("sp",)}
        self.cnt = {e: 0 for e in COMPUTE}
        self.semobj = {}
        self.sem = {}
        for e in COMPUTE:
            self.semobj["s_" + e] = ctx.enter_context(nc.semaphore("s_" + e))
            self.sem[e] = "s_" + e
        self.dsem = {}
        self.dcnt = {}
        for q in ("sp", "act", "pool"):
            self.dsem[q] = []
            for i in range(NDMASEM):
                nm = "d_%s%d" % (q, i)
                self.semobj[nm] = ctx.enter_context(nc.semaphore(nm))
                self.dsem[q].append(nm)
            self.dcnt[q] = 0
        self.known = {e: {} for e in COMPUTE + ("sp",)}
        self.out_deps = []
        self.nalloc = 0
        self.free_deps = {}
        self.phase = ""
        self.pe_phase = []
        self.scopes = []
        self.nps = 0

    def sb(self, shape, dt=F32, nsub=1, name=None):
        self.nalloc += 1
        name = (name or "sb") + ("_%d" % self.nalloc)
        if self.scopes:
            st, lst = self.scopes[-1]
        else:
            st, lst = self.ctx, None
        t = st.enter_context(self.nc.sbuf_tensor(name, list(shape), dt))
        tt = T(t, nsub, name)
        if self.free_deps:
            fd = [(k, v[0], v[1]) for k, v in self.free_deps.items()]
            for i in range(nsub):
                tt.rd[i] = list(fd)
        if lst is not None:
            lst.append(tt)
        return tt

    def push(self):
        import contextlib as _cl
        st = _cl.ExitStack()
        self.scopes.append((st, []))

    def pop(self):
        st, lst = self.scopes.pop()
        for tt in lst:
            for i in range(tt.nsub):
                for d in ([tt.lw[i]] if tt.lw[i] is not None else []) + tt.rd[i]:
                    k, v, src = d
                    if self.free_deps.get(k, (0, None))[0] < v:
                        self.free_deps[k] = (v, src)
        st.close()

    def ps(self, shape, dt=F32, nsub=1, name=None):
        self.nalloc += 1
        name = name or ("ps%d" % self.nalloc)
        t = self.ctx.enter_context(self.nc.psum_tensor(name, list(shape), dt))
        tt = T(t, nsub, name)
        tt.psum = True
        return tt

    @staticmethod
    def _norm(lst):
        out = []
        for x in lst or []:
            if isinstance(x, T):
                out.extend((x, i) for i in range(x.nsub))
            else:
                t, s = x
                if s is None:
                    out.extend((t, i) for i in range(t.nsub))
                elif isinstance(s, (list, tuple, range)):
                    out.extend((t, i) for i in s)
                else:
                    out.append((t, s))
        return out

    def _collect(self, eng, reads, writes):
        deps = {}

        def add(d):
            if d is None:
                return
            key, val, src = d
            if src == eng and eng == "pe":
                return
            if deps.get(key, 0) < val:
                deps[key] = val

        for (t, s) in reads:
            add(t.lw[s])
            if t.psum:
                for d in t.rd[s]:
                    if d[2] != eng:
                        add(d)
        for (t, s) in writes:
            add(t.lw[s])
            for d in t.rd[s]:
                add(d)
        waits = []
        kn = self.known[eng]
        for key, val in deps.items():
            if kn.get(key, 0) >= val:
                continue
            kn[key] = val
            waits.append((key, val))
        return waits

    def _commit(self, dep, reads, writes):
        for (t, s) in reads:
            t.rd[s].append(dep)
        for (t, s) in writes:
            t.lw[s] = dep
            t.rd[s] = []

    def op(self, eng, fn, reads=None, writes=None):
        reads = self._norm(reads)
        writes = self._norm(writes)
        waits = self._collect(eng, reads, writes)
        self.cnt[eng] += 1
        n = self.cnt[eng]
        sem = self.sem[eng]
        dep = (sem, n, eng)
        self.ops[eng].append((waits, fn, (sem, 1)))
        if eng == "pe":
            self.pe_phase.append(self.phase)
        self._commit(dep, reads, writes)
        return dep

    def dma(self, q, out, in_, reads=None, writes=None, is_output=False):
        reads = self._norm(reads)
        writes = self._norm(writes)
        waits = self._collect(q, reads, writes)
        i = self.dcnt[q]
        self.dcnt[q] += 1
        sem = self.dsem[q][i % NDMASEM]
        prev = 16 * (i // NDMASEM)
        val = prev + 16
        kn = self.known[q]
        if prev > 0 and kn.get(sem, 0) < prev:
            kn[sem] = prev
            waits.append((sem, prev))
        dep = (sem, val, "dma")

        def fn(e, out=out, in_=in_):
            return e.dma_start(out=out, in_=in_)

        self.ops[q].append((waits, fn, (sem, 16)))
        self._commit(dep, reads, writes)
        if is_output:
            self.out_deps.append(dep)
        return dep

    def emit(self):
        nc = self.nc
        fin = []
        for (sem, val, _) in self.out_deps:
            fin.append((sem, val))
        ops = self.ops
        so = self.semobj
        with nc.Block() as block:
            def run(e, lst, extra=None):
                for (waits, fn, inc) in lst:
                    for (s, v) in waits:
                        e.wait_ge(so[s], v)
                    ins = fn(e)
                    if inc is not None:
                        ins.then_inc(so[inc[0]], inc[1])
                if extra:
                    best = {}
                    for (s, v) in extra:
                        if best.get(s, 0) < v:
                            best[s] = v
                    for s, v in best.items():
                        e.wait_ge(so[s], v)

            @block.sync
            def _(e):
                run(e, ops["sp"], fin)

            @block.tensor
            def _(e):
                run(e, ops["pe"])

            @block.scalar
            def _(e):
                run(e, ops["act"])

            @block.vector
            def _(e):
                run(e, ops["dve"])

            @block.gpsimd
            def _(e):
                run(e, ops["pool"])


import math
import contextlib
import ml_dtypes

D = 1024
NTOK = 1024
DEPTH = 4
EPS = 1e-6
DFF = 2816
NHC = 22
PI = math.pi
TWO_PI = 2.0 * math.pi
BF = ml_dtypes.bfloat16

C_ADAB = 0
C_NORM = 48
C_HSW = 80
C_HSB = 98
C_HBIAS = 104
C_GN = 108
C_QN = 110
C_KVN = 112
C_FCW = 113
C_FCB = 179
C_HB1 = 201
C_HB2 = 202
C_LG = 203
C_SINK = 215
NCOLP = 224

K_IOTA1 = 0
K_REV = 128
K_LAGP = 256
K_LAGN = 384
K_U = 512
K_LO = 640
K_REVC = 768
K_POSC = 832
NCST = 896


def kc_tile(W):
    K, N = W.shape
    return np.ascontiguousarray(W.reshape(K // 128, 128, N).transpose(1, 0, 2)).reshape(128, -1)


def col_tile(v):
    v = np.asarray(v)
    lead = v.shape[:-1]
    n = v.shape[-1] // 128
    a = v.reshape(lead + (n, 128))
    a = np.moveaxis(a, -1, 0)
    return np.ascontiguousarray(a).reshape(128, -1)


def host_consts():
    c = {}
    cst = np.zeros((128, NCST), np.float32)
    i = np.arange(128, dtype=np.float32)
    j = np.arange(128, dtype=np.float32)[:, None]
    cst[:, K_IOTA1:K_IOTA1 + 128] = (i + 1)[None, :]
    cst[:, K_REV:K_REV + 128] = (128 - i)[None, :]
    cst[:, K_LAGP:K_LAGP + 128] = np.maximum(i[None, :] - j, 0)
    cst[:, K_LAGN:K_LAGN + 128] = np.maximum(j - i[None, :], 0)
    cst[:, K_U:K_U + 128] = (i[None, :] >= j)
    cst[:, K_LO:K_LO + 128] = (j >= i[None, :])
    cst[:, K_REVC:K_REVC + 64] = (127 - j)
    cst[:, K_POSC:K_POSC + 64] = j
    c["cst"] = cst
    for nm, L in (("P", 256), ("S", 1024)):
        N = 2 * L
        jj = np.arange(L, dtype=np.float64)[:, None]
        ff = np.arange(L, dtype=np.float64)[None, :]
        ang = PI * (2 * ff + 1) * jj / N
        Cc = np.cos(ang)
        Sc = np.sin(ang)
        mats = [Cc, Sc, Cc.T * (2.0 / N), Sc.T * (2.0 / N)]
        nh = 2 if L > 512 else 1
        hwid = L // nh
        c["dft" + nm] = np.stack([np.stack([kc_tile(np.ascontiguousarray(m[:, hv * hwid:(hv + 1) * hwid]).astype(np.float32)) for hv in range(nh)]) for m in mats]).astype(BF)
        zg = np.zeros((17, 2, L), np.float32)
        tn = np.zeros((128, 2, L // 128), np.float32)
        for g in range(2):
            t = ((np.arange(L) - g) / L).astype(np.float32)
            bands = np.arange(1, 9, dtype=np.float32)
            a = (2.0 * PI) * t[:, None] * bands
            z = np.concatenate([t[:, None], np.cos(a), np.sin(a)], axis=-1).astype(np.float32)
            zg[:, g, :] = z.T
            tn[:, g, :] = -(t.reshape(L // 128, 128).T)
        c["zg" + nm] = zg
        c["tn" + nm] = tn
    Ls = 1024
    rows = (np.arange(Ls) // 64).astype(np.float32)
    cols = (np.arange(Ls) % 64).astype(np.float32)

    def tables(dim):
        q = dim // 4
        inv = (10000.0 ** (-np.arange(q, dtype=np.float32) / q)).astype(np.float32)
        ang = np.concatenate([rows[:, None] * inv, cols[:, None] * inv], axis=-1)
        return np.cos(ang).astype(np.float32), np.sin(ang).astype(np.float32)

    cg, sg = tables(64)
    rope = np.zeros((128, 4, Ls), np.float32)
    for p in range(128):
        d = p % 64
        r = d % 32
        rope[p, 0] = cg[:, r]
        rope[p, 1] = sg[:, r] * (-1.0 if d < 32 else 1.0)
    cm, sm = tables(32)
    rope[:, 2] = 1.0
    for p in range(64, 96):
        d = p - 64
        r = d % 16
        rope[p, 2] = cm[:, r]
        rope[p, 3] = sm[:, r] * (-1.0 if d < 16 else 1.0)
    c["rope"] = rope
    R = np.zeros((128, 2, 128), np.float32)
    for m in range(128):
        d = m % 64
        base = m - d
        if d < 32:
            R[base + d + 32, 0, m] = 1.0
        else:
            R[base + d - 32, 0, m] = 1.0
    for m in range(64, 96):
        d = m - 64
        if d < 16:
            R[64 + d + 16, 1, m] = 1.0
        else:
            R[64 + d - 16, 1, m] = 1.0
    c["rmat"] = R.astype(BF)
    return c


def host_weights(inp):
    w = {}
    L = DEPTH
    ada_w = inp["ada_w"]
    w["adaw"] = np.stack([np.stack([kc_tile(ada_w[l][:, b * 512:(b + 1) * 512]) for b in range(12)]) for l in range(L)])
    w_in = inp["w_in"]
    retA = np.r_[0:256, 256:512, 768:1024]
    retB = np.r_[256:768]
    hy = np.r_[1024:1792]
    gb = 1792
    mb = 2304
    qperm = np.concatenate([gb + h * 64 + np.arange(64) for h in (0, 2, 1, 3)])
    gq = np.concatenate([qperm, gb + np.r_[256:384], gb + np.r_[256:384], gb + np.r_[384:512], mb + np.r_[384:416]])
    ml = np.concatenate([mb + np.r_[0:384], mb + np.r_[320:416]])
    w["wretA"] = np.stack([kc_tile(w_in[l][:, retA]) for l in range(L)])
    w["wretB"] = np.stack([kc_tile(w_in[l][:, retB]) for l in range(L)])
    w["why"] = np.stack([kc_tile(w_in[l][:, hy]) for l in range(L)])
    w["wgqa"] = np.stack([kc_tile(w_in[l][:, gq]) for l in range(L)])
    w["wmla"] = np.stack([kc_tile(w_in[l][:, ml]) for l in range(L)])
    w["mlaw"] = np.stack([np.concatenate([kc_tile(inp["mla_w_uq"][l]), inp["mla_w_uk"][l], inp["mla_w_uv"][l]], axis=1) for l in range(L)])
    perm = np.concatenate([np.r_[0:512], 512 + np.concatenate([h * 64 + np.arange(64) for h in (0, 2, 1, 3)]), np.r_[768:1024]])
    w["wout"] = np.stack([kc_tile(inp["w_out"][l][perm, :]) for l in range(L)])
    up = inp["ffn_w_up"]
    w["wup"] = np.stack([np.stack([kc_tile(np.concatenate([up[l][:, 256 * b:256 * b + 256], up[l][:, DFF + 256 * b:DFF + 256 * b + 256]], axis=1)) for b in range(11)]) for l in range(L)])
    dn = inp["ffn_w_down"]
    w["wdn"] = np.stack([np.stack([kc_tile(dn[l][:, 128 * b:128 * b + 128]) for b in range(8)]) for l in range(L)])
    colp = np.zeros((L, 128, NCOLP), np.float32)
    rowp = np.zeros((L, 128, 512), np.float32)
    hyw = np.zeros((L, 64, 1152), np.float32)
    for l in range(L):
        colp[l, :, C_ADAB:C_ADAB + 48] = col_tile(inp["ada_b"][l])
        colp[l, :, C_NORM:C_NORM + 32] = col_tile(inp["norm_g"][l])
        colp[l, :, C_HSW:C_HSW + 18] = col_tile(inp["hy_short_w"][l])
        colp[l, :, C_HSB:C_HSB + 6] = col_tile(inp["hy_short_b"][l])
        colp[l, :, C_HBIAS:C_HBIAS + 4] = col_tile(inp["hy_bias"][l])
        colp[l, :, C_GN:C_GN + 2] = col_tile(inp["ret_gn_g"][l])
        colp[l, :, C_QN:C_QN + 2] = col_tile(inp["mla_q_norm"][l])
        colp[l, :, C_KVN:C_KVN + 1] = col_tile(inp["mla_kv_norm"][l])
        colp[l, :, C_FCW:C_FCW + 66] = col_tile(inp["ffn_conv_w"][l])
        colp[l, :, C_FCB:C_FCB + 22] = col_tile(inp["ffn_conv_b"][l])
        colp[l, 0:64, C_HB1] = inp["hy_b1"][l]
        colp[l, 0:64, C_HB2] = inp["hy_b2"][l]
        lg = inp["ret_decay_logit"][l]
        for d in range(2):
            for c in range(2):
                colp[l, 0:64, C_LG + d * 2 + c] = lg[d, 2 * c]
                colp[l, 64:128, C_LG + d * 2 + c] = lg[d, 2 * c + 1]
            for h in range(4):
                colp[l, :, C_LG + 4 + d * 4 + h] = lg[d, h]
        for h in range(4):
            colp[l, :, C_SINK + h] = inp["gqa_sink"][l, h]
        rowp[l, :, :] = inp["hy_decay"][l].reshape(1, 512)
        hyw[l, 0:17, 0:64] = inp["hy_w1"][l]
        hyw[l, :, 64:128] = inp["hy_w2"][l]
        hyw[l, :, 128:1152] = inp["hy_w3"][l]
    w["colp"] = colp
    w["rowp"] = rowp
    w["hyw"] = hyw
    return w


class Grp:
    pass


def build(dbg=False):
    nc = bass.Bass("TRN2", target_bir_lowering=False)

    def din(name, shape, dt=F32):
        return nc.dram_tensor(name, list(shape), dt, kind="ExternalInput").ap()

    def dout(name, shape, dt=F32):
        return nc.dram_tensor(name, list(shape), dt, kind="ExternalOutput").ap()

    I = {}
    I["xp"] = din("xp", [NTOK, D])
    I["xs"] = din("xs", [NTOK, D])
    I["sret"] = din("sret", [DEPTH, 2, 4, 64, 64])
    I["cgk"] = din("cgk", [DEPTH, 2, 512, 64])
    I["cgv"] = din("cgv", [DEPTH, 2, 512, 64])
    I["cckv"] = din("cckv", [DEPTH, 512, 128])
    I["ckr"] = din("ckr", [DEPTH, 512, 32])
    I["cond"] = din("cond", [2, 128, 8])
    I["adaw"] = din("adaw", [DEPTH, 12, 128, 4096])
    I["wretA"] = din("wretA", [DEPTH, 128, 8 * 768])
    I["wretB"] = din("wretB", [DEPTH, 128, 8 * 512])
    I["why"] = din("why", [DEPTH, 128, 8 * 768])
    I["wgqa"] = din("wgqa", [DEPTH, 128, 8 * 672])
    I["wmla"] = din("wmla", [DEPTH, 128, 8 * 480])
    I["mlaw"] = din("mlaw", [DEPTH, 128, 1280])
    I["wout"] = din("wout", [DEPTH, 128, 8192])
    I["wup"] = din("wup", [DEPTH, 11, 128, 4096])
    I["wdn"] = din("wdn", [DEPTH, 8, 128, 22 * 128])
    I["colp"] = din("colp", [DEPTH, 128, NCOLP])
    I["rowp"] = din("rowp", [DEPTH, 128, 512])
    I["hyw"] = din("hyw", [DEPTH, 64, 1152])
    I["cst"] = din("cst", [128, NCST])
    I["dftP"] = din("dftP", [4, 1, 128, 2 * 256], BF16)
    I["dftS"] = din("dftS", [4, 2, 128, 8 * 512], BF16)
    I["zgP"] = din("zgP", [17, 2, 256])
    I["zgS"] = din("zgS", [17, 2, 1024])
    I["tnP"] = din("tnP", [128, 2, 2])
    I["tnS"] = din("tnS", [128, 2, 8])
    I["rope"] = din("rope", [128, 4, 1024])
    I["rmat"] = din("rmat", [128, 2, 128], BF16)
    O = {}
    O["yp"] = dout("yp", [NTOK, D])
    O["ys"] = dout("ys", [NTOK, D])
    O["nsr"] = dout("nsr", [4, DEPTH, 2, 4, 64, 64])
    O["ngk"] = dout("ngk", [4, DEPTH, 2, 256, 64])
    O["ngv"] = dout("ngv", [4, DEPTH, 2, 256, 64])
    O["nckv"] = dout("nckv", [4, DEPTH, 256, 128])
    O["nkr"] = dout("nkr", [4, DEPTH, 256, 32])
    DBG = {}

    with contextlib.ExitStack() as ctx:
        P = Prog(nc, ctx)
        ident = P.sb([128, 128], F32, name="ident")
        ones_bf = P.sb([128, 128], BF16, name="ones")
        BO = P.sb([128, 128], F32, name="BO")
        cc = P.sb([128, 4], F32, name="cc")
        cst = P.sb([128, NCST], F32, name="cst")
        rmat = P.sb([128, 2, 128], BF16, name="rmat")
        NW = 2
        wring = [P.sb([128, 8192], BF16, nsub=2, name="wr%d" % i) for i in range(NW)]
        wstate = {"i": 0, "h": 0}
        colp = P.sb([128, NCOLP], F32, name="colp")
        modt = P.sb([128, 48], F32, name="modt")
        mods = P.sb([128, 6, 8], F32, name="mods")
        scond2 = P.sb([128, 8, 33], BF16, name="scond2")
        modrow = [P.sb([33, 512], F32, name="modrow0")] * 2
        condt2 = P.sb([128, 2, 8], F32, name="condt2")
        modS = P.sb([128, DEPTH, 48], F32, name="modS")
        mstate = {"ready": False}
        pss = [P.ps([128, 512], F32, name="psr%d" % i) for i in range(8)]
        acc = [pss[6], pss[7]]
        pstate = {"i": 0, "n": 8}
        evs = {"i": 0}

        def pst():
            t = pss[pstate["i"] % pstate["n"]]
            pstate["i"] += 1
            return t

        def wslot():
            t = wring[wstate["i"] % NW]
            wstate["i"] += 1
            return t

        def wload(src2d, n, q="pool"):
            t = wslot()
            P.dma(q, t[:, 0:n], src2d, writes=[t])
            return t

        def wload_h(src2d, n):
            k = wstate["h"] % (2 * NW)
            wstate["h"] += 1
            t, hf = wring[k // 2], k % 2
            P.dma("pool", t[:, hf * 4096:hf * 4096 + n], src2d, writes=[(t, hf)])
            return t, hf, hf * 4096

        def mm(out, lhsT, rhs, start, stop, rd, wr):
            P.op("pe", lambda e: e.matmul(out, lhsT=lhsT, rhs=rhs, start=start, stop=stop, skip_group_check=True), reads=rd, writes=wr)

        def tr(out, in_, k, rd, wr):
            P.op("pe", lambda e: e.transpose(out=out, in_=in_, identity=ident[0:k, 0:k]), reads=rd + [ident], writes=wr)

        def cp(eng, out, in_, rd, wr):
            if eng == "act":
                P.op("act", lambda e: e.activation(out=out, in_=in_, func=AF.Copy), reads=rd, writes=wr)
            else:
                P.op(eng, lambda e: e.tensor_copy(out=out, in_=in_), reads=rd, writes=wr)

        def evac(out, in_, rd, wr):
            evs["i"] += 1
            cp("act" if evs["i"] % 2 else "dve", out, in_, rd, wr)

        def act(out, in_, func, rd, wr, scale=None, bias=None):
            kw = {}
            if scale is not None:
                kw["scale"] = scale
            if bias is not None:
                kw["bias"] = bias
            P.op("act", lambda e: e.activation(out=out, in_=in_, func=func, **kw), reads=rd, writes=wr)

        def tt(eng, out, in0, in1, op, rd, wr):
            P.op(eng, lambda e: e.tensor_tensor(out=out, in0=in0, in1=in1, op=op), reads=rd, writes=wr)

        def ts(eng, out, in0, s1, s2, op0, op1, rd, wr):
            if op1 is None:
                P.op(eng, lambda e: e.tensor_scalar(out=out, in0=in0, scalar1=s1, scalar2=None, op0=op0), reads=rd, writes=wr)
            else:
                P.op(eng, lambda e: e.tensor_scalar(out=out, in0=in0, scalar1=s1, scalar2=s2, op0=op0, op1=op1), reads=rd, writes=wr)

        def stt(out, in0, scalar, in1, op0, op1, rd, wr):
            P.op("dve", lambda e: e.scalar_tensor_tensor(out=out, in0=in0, scalar=scalar, in1=in1, op0=op0, op1=op1), reads=rd, writes=wr)

        def recip(out, in_, rd, wr):
            P.op("dve", lambda e: e.reciprocal(out=out, in_=in_), reads=rd, writes=wr)

        def memset(eng, ap, val, wr):
            P.op(eng, lambda e: e.memset(ap, val), writes=wr)

        def dump(name, tile, ap, shape, dt=F32):
            if not dbg:
                return
            d = dout("dbg_" + name, shape, F32)
            DBG[name] = d
            P.dma("pool" if dt != F32 else "sp", d, ap, reads=[tile], is_output=True)

        P.dma("sp", cst[:], I["cst"], writes=[cst])
        P.dma("sp", rmat[:], I["rmat"], writes=[rmat])
        memset("pool", ident[:], 1.0, [ident])
        P.op("pool", lambda e: e.affine_select(out=ident[:], in_=ident[:], pattern=[[-1, 128]], compare_op=ALU.is_equal, fill=0.0, base=0, channel_multiplier=1), reads=[ident], writes=[ident])
        memset("dve", ones_bf[:], 1.0, [ones_bf])
        memset("dve", BO[:], 0.0, [BO])
        memset("dve", BO[0:64, 0:64], 1.0 / 64, [BO])
        memset("dve", BO[64:128, 64:128], 1.0 / 64, [BO])
        BOb = P.sb([128, 128], BF16, name="BOb")
        memset("dve", BOb[:], 0.0, [BOb])
        memset("dve", BOb[0:64, 0:64], 1.0 / 64, [BOb])
        memset("dve", BOb[64:128, 64:128], 1.0 / 64, [BOb])
        BD = P.sb([128, 128], F32, name="BD")
        memset("dve", BD[:], 0.0, [BD])
        memset("dve", BD[0:64, 0:64], 1.0, [BD])
        memset("dve", BD[64:128, 64:128], 1.0, [BD])
        memset("dve", cc[:, 0:1], EPS, [cc])
        memset("dve", cc[:, 1:2], 1.0, [cc])
        memset("dve", cc[:, 2:3], 0.0, [cc])
        epsc = cc[:, 0:1]

        def s2(t, c, tti):
            return (t, c * 2 + tti)

        def sc(t, c):
            return (t, [c * 2, c * 2 + 1])

        def rms_rstd(srcs, tti, dim, rstd, sqb):
            ps = pst()
            n = len(srcs)
            for i, (t, ap, sub) in enumerate(srcs):
                sq = sqb[i % 2]
                if i % 2 == 0:
                    act(sq[:], ap, AF.Square, [(t, sub)], [sq])
                else:
                    tt("pool", sq[:], ap, ap, ALU.mult, [(t, sub)], [sq])
                mm(ps[:, :], ones_bf[:, :], sq[:], i == 0, i == n - 1, [ones_bf, sq], [ps])
            act(rstd[:], ps[:, :], AF.Sqrt, [ps, cc], [rstd], scale=1.0 / dim, bias=epsc)
            recip(rstd[:], rstd[:], [rstd], [rstd])

        def run_group(G):
            L, NSEQ, nch = G.L, G.NSEQ, G.L // 128
            SAMPLE = G.sample
            P.push()
            x = P.sb([128, 8, NTOK], F32, nsub=16, name="x")
            ymix = P.sb([128, 8, NTOK], BF16, nsub=16, name="ymix")
            rstd2 = [P.sb([128, 512], F32, name="rstdb%d" % i) for i in range(2)]
            rstd = rstd2[0]
            sqbig = P.sb([128, 8, 512], BF16, nsub=2, name="sqbig")
            xr = P.sb([128, 4, 512], F32, nsub=4, name="xr")
            tmpf = [P.sb([128, 512], F32, name="tmpf%d" % i) for i in range(2)]
            tst = {"i": 0}

            def tmp():
                t = tmpf[tst["i"] % 2]
                tst["i"] += 1
                return t

            P.phase = "io"
            P.push()
            xt = [P.sb([128, D], F32, name="xt%d" % i) for i in range(2)]
            for blk in range(8):
                xb_ = xt[blk % 2]
                P.dma("sp", xb_[:], G.x_in[blk * 128:(blk + 1) * 128, :], writes=[xb_])
                for half in range(2):
                    ps = pst()
                    for c4 in range(4):
                        c = half * 4 + c4
                        tr(ps[:, c4 * 128:(c4 + 1) * 128], xb_[:, c * 128:(c + 1) * 128], 128, [xb_], [ps])
                    evac(x.t[:, half * 4:half * 4 + 4, blk * 128:(blk + 1) * 128],
                         ps[:, :].rearrange("p (a b) -> p a b", a=4), [ps],
                         [(x, [(half * 4 + c4) * 2 + blk // 4 for c4 in range(4)])])
            P.pop()
            first_group = not mstate["ready"]
            if first_group:
                P.dma("sp", condt2[:], I["cond"].rearrange("g p c -> p g c"), writes=[condt2])
                memset("dve", scond2[:], 0.0, [scond2])
                act(scond2.t[:, :, 0], condt2.t[:, G.gi, :], AF.Silu, [condt2], [scond2])
                act(scond2.t[:, :, 32], condt2.t[:, 1 - G.gi, :], AF.Silu, [condt2], [scond2])

            import os as _os
            for l in range(int(_os.environ.get('MK_DEPTH', DEPTH))):
                P.phase = "mod"
                P.dma("sp", colp[:], I["colp"][l], writes=[colp])
                if first_group:
                    psm, psm2 = pss[7], pss[6]
                    pstate["n"] = 6
                    for b in range(12):
                        w, whf, wo = wload_h(I["adaw"][l, b], 4096)
                        wv = w.t[:, wo:wo + 4096].rearrange("p (k n) -> p k n", k=8)
                        ps = pst()
                        for kc in range(8):
                            mm(ps[0:33, :], scond2.t[:, kc, :], wv[:, kc, :], kc == 0, kc == 7, [(w, whf), scond2], [ps])
                        mr = modrow[b % 2]
                        evac(mr[0:33, :], ps[0:33, :], [ps], [mr])
                        for j4 in range(4):
                            j = b * 4 + j4
                            mm(psm[:, j:j + 1], mr[0:1, j4 * 128:(j4 + 1) * 128], cc[0:1, 1:2], True, True, [mr, cc], [psm])
                            mm(psm2[:, j:j + 1], mr[32:33, j4 * 128:(j4 + 1) * 128], cc[32:33, 1:2], True, True, [mr, cc], [psm2])
                    pstate["n"] = 8
                    tt("dve", modt[:], psm[:, 0:48], colp[:, C_ADAB:C_ADAB + 48], ALU.add, [psm, colp], [modt])
                    tt("dve", modS.t[:, l, :], psm2[:, 0:48], colp[:, C_ADAB:C_ADAB + 48], ALU.add, [psm2, colp], [modS])
                else:
                    cp("dve", modt[:], modS.t[:, l, :], [modS], [modt])
                stt(mods.t[:, 0, :], modt[:, 8:16], 1.0, colp[:, C_NORM:C_NORM + 8], ALU.add, ALU.mult, [modt, colp], [mods])
                cp("dve", mods.t[:, 1, :], modt[:, 0:8], [modt], [mods])
                tt("dve", mods.t[:, 2, :], modt[:, 16:24], colp[:, C_NORM + 8:C_NORM + 16], ALU.mult, [modt, colp], [mods])
                stt(mods.t[:, 3, :], modt[:, 32:40], 1.0, colp[:, C_NORM + 16:C_NORM + 24], ALU.add, ALU.mult, [modt, colp], [mods])
                cp("dve", mods.t[:, 4, :], modt[:, 24:32], [modt], [mods])
                tt("dve", mods.t[:, 5, :], modt[:, 40:48], colp[:, C_NORM + 24:C_NORM + 32], ALU.mult, [modt, colp], [mods])

                def big_stats(src, tti):
                    cs = slice(tti * 512, (tti + 1) * 512)
                    act(sqbig.t[:, 0:4, :], src.t[:, 0:4, cs], AF.Square, [(src, [c * 2 + tti for c in range(4)])], [(sqbig, 0)])
                    tt("dve", sqbig.t[:, 4:8, :], src.t[:, 4:8, cs], src.t[:, 4:8, cs], ALU.mult, [(src, [c * 2 + tti for c in range(4, 8)])], [(sqbig, 1)])
                    ps = pst()
                    for c in range(8):
                        mm(ps[:, :], ones_bf[:, :], sqbig.t[:, c, :], c == 0, c == 7, [ones_bf, (sqbig, c // 4)], [ps])
                    r_ = rstd2[tti]
                    act(r_[:], ps[:, :], AF.Sqrt, [ps, cc], [r_], scale=1.0 / D, bias=epsc)
                    recip(r_[:], r_[:], [r_], [r_])

                def norm_mod(src, ai, bi, dst):
                    for tti in range(2):
                        big_stats(src, tti)
                    for tti in range(2):
                        cs = slice(tti * 512, (tti + 1) * 512)
                        for q in range(4):
                            x0 = (q % 2) * 2
                            tt("dve", xr.t[:, x0:x0 + 2, :], src.t[:, 2 * q:2 * q + 2, cs], rstd2[tti].t[:, None, :].to_broadcast([128, 2, 512]), ALU.mult,
                               [(src, [c * 2 + tti for c in (2 * q, 2 * q + 1)]), rstd2[tti]], [(xr, [x0, x0 + 1])])
                            for c2 in range(2):
                                c = 2 * q + c2
                                if c2 == 0:
                                    act(dst.t[:, c, cs], xr.t[:, x0 + c2, :], AF.Identity, [(xr, x0 + c2), mods], [s2(dst, c, tti)], scale=mods.t[:, ai, c:c + 1], bias=mods.t[:, bi, c:c + 1])
                                else:
                                    ts("pool", dst.t[:, c, cs], xr.t[:, x0 + c2, :], mods.t[:, ai, c:c + 1], mods.t[:, bi, c:c + 1], ALU.mult, ALU.add, [(xr, x0 + c2), mods], [s2(dst, c, tti)])

                P.push()
                h = P.sb([128, 8, NTOK], BF16, nsub=16, name="h")
                P.phase = "norm1"
                norm_mod(x, 0, 1, h)

                def proj_fm(w, wv, col0, M, evf):
                    for tti in range(2):
                        ps = pst()
                        for kc in range(8):
                            mm(ps[0:M, :], wv[:, kc, col0:col0 + M], h.t[:, kc, tti * 512:(tti + 1) * 512], kc == 0, kc == 7, [w, s2(h, kc, tti)], [ps])
                        evf(ps, tti)

                def proj_tm(w, wv, col0, n, evf):
                    for blk in range(8):
                        ps = pst()
                        for kc in range(8):
                            mm(ps[:, 0:n], h.t[:, kc, blk * 128:(blk + 1) * 128], wv[:, kc, col0:col0 + n], kc == 0, kc == 7, [w, s2(h, kc, blk // 4)], [ps])
                        evf(ps, blk)

                def retention():
                    P.push()
                    qf = P.sb([128, 2, NTOK], BF16, name="qf")
                    qb = P.sb([128, 2, NTOK], BF16, name="qb")
                    sg = P.sb([128, 2, NTOK], BF16, name="sg")
                    ktm = P.sb([128, 8, 256], BF16, nsub=8, name="ktm")
                    vtm = P.sb([128, 8, 256], BF16, nsub=8, name="vtm")
                    vf = P.sb([128, 8, 256], BF16, nsub=8, name="vf")
                    vb = P.sb([128, 8, 256], BF16, nsub=8, name="vb")
                    lg = P.sb([128, 12], F32, name="lg")
                    patf = P.sb([128, 2, 128], F32, name="patf")
                    patb = P.sb([128, 2, 128], F32, name="patb")
                    cdc = P.sb([128, 4], F32, name="cdc")
                    DM = P.sb([128, 512], F32, name="DM")
                    kdp = P.sb([128, 2, 256], F32, name="kdp")
                    tfb = P.sb([128, 2, 128], F32, name="tfb")
                    S = P.sb([128, NSEQ * 2, 2, 128], F32, nsub=NSEQ * 2, name="S")
                    Sbf = P.sb([128, 8, 4, 128], BF16, nsub=16, name="Sbf")
                    SD = P.sb([128, 8, 512], BF16, nsub=8, name="SD")
                    P.push()
                    qTz = P.sb([128, 4, NTOK], BF16, name="qTz")
                    memset("pool", qTz[:], 0.0, [qTz])
                    kT = P.sb([128, 2, NTOK], BF16, name="kT")
                    act(lg[:], colp[:, C_LG:C_LG + 12], AF.Exp, [colp], [lg], scale=-1.0)
                    ts("dve", lg[:], lg[:], 1.0, None, ALU.add, None, [lg], [lg])
                    act(lg[:], lg[:], AF.Ln, [lg], [lg])
                    ts("dve", lg[:], lg[:], -1.0, None, ALU.mult, None, [lg], [lg])
                    for c in range(2):
                        act(patf.t[:, c, :], cst[:, K_IOTA1:K_IOTA1 + 128], AF.Exp, [cst, lg], [patf], scale=lg[:, c:c + 1])
                        act(patb.t[:, c, :], cst[:, K_REV:K_REV + 128], AF.Exp, [cst, lg], [patb], scale=lg[:, 2 + c:3 + c])
                    act(cdc[:], lg[:, 0:4], AF.Exp, [lg], [cdc], scale=128.0)
                    for hh in range(4):
                        act(tfb.t[:, 0, :], cst[:, K_LAGP:K_LAGP + 128], AF.Exp, [cst, lg], [tfb], scale=lg[:, 4 + hh:5 + hh])
                        act(tfb.t[:, 1, :], cst[:, K_LAGN:K_LAGN + 128], AF.Exp, [cst, lg], [tfb], scale=lg[:, 8 + hh:9 + hh])
                        stt(tfb.t[:, 0, :], tfb.t[:, 0, :], 0.125, cst[:, K_U:K_U + 128], ALU.mult, ALU.mult, [tfb, cst], [tfb])
                        stt(tfb.t[:, 1, :], tfb.t[:, 1, :], 0.125, cst[:, K_LO:K_LO + 128], ALU.mult, ALU.mult, [tfb, cst], [tfb])
                        tt("dve", DM[:, hh * 128:(hh + 1) * 128], tfb.t[:, 0, :], tfb.t[:, 1, :], ALU.add, [tfb], [DM])
                        act(kdp.t[:, 0, hh * 64:(hh + 1) * 64], cst[:, K_REVC:K_REVC + 64], AF.Exp, [cst, lg], [kdp], scale=lg[:, 4 + hh:5 + hh])
                        act(kdp.t[:, 1, hh * 64:(hh + 1) * 64], cst[:, K_POSC:K_POSC + 64], AF.Exp, [cst, lg], [kdp], scale=lg[:, 8 + hh:9 + hh])
                    ts("dve", kdp[:], kdp[:], 0.125, None, ALU.mult, None, [kdp], [kdp])
                    RS = 99
                    if RS <= 1:
                        P.pop()
                        return
                    w = wload(I["wretA"][l], 6144)
                    wv = w.t[:, 0:6144].rearrange("p (k n) -> p k n", k=8)
                    for c in range(2):
                        def ev_q(ps, tti, c=c):
                            cs = slice(tti * 512, (tti + 1) * 512)
                            for hf in range(2):
                                cp("act", qTz.t[hf * 64:(hf + 1) * 64, 2 * c + hf, cs], ps[hf * 64:(hf + 1) * 64, :], [ps], [qTz])
                            p3 = ps[:, :].rearrange("p (a b) -> p a b", a=4)
                            tt("dve", qf.t[:, c, cs].rearrange("p (a b) -> p a b", a=4), p3, patf.t[:, c:c + 1, :].to_broadcast([128, 4, 128]), ALU.mult, [ps, patf], [qf])
                            tt("dve", qb.t[:, c, cs].rearrange("p (a b) -> p a b", a=4), p3, patb.t[:, c:c + 1, :].to_broadcast([128, 4, 128]), ALU.mult, [ps, patb], [qb])
                        SUB = _os.environ.get("MK_RET_SUB", "qkg")
                        if "q" in SUB:
                            proj_fm(w, wv, c * 128, 128, ev_q)

                        def ev_k(ps, tti, c=c):
                            evac(kT.t[:, c, tti * 512:(tti + 1) * 512], ps[:, :], [ps], [kT])
                        if "k" in SUB:
                            proj_fm(w, wv, 256 + c * 128, 128, ev_k)

                        def ev_g(ps, tti, c=c):
                            act(sg.t[:, c, tti * 512:(tti + 1) * 512], ps[:, :], AF.Silu, [ps], [sg])
                        if "g" in SUB:
                            proj_fm(w, wv, 512 + c * 128, 128, ev_g)
                    if RS <= 2:
                        P.pop()
                        return
                    w2 = wload(I["wretB"][l], 4096)
                    wv2 = w2.t[:, 0:4096].rearrange("p (k n) -> p k n", k=8)

                    def ev_tm(ps, blk):
                        cp("act", ktm.t[:, blk, :], ps[:, 0:256], [ps], [(ktm, blk)])
                        cp("act", vtm.t[:, blk, :], ps[:, 256:512], [ps], [(vtm, blk)])
                        tt("dve", vf.t[:, blk, :], ps[:, 256:512], kdp.t[:, 0, :], ALU.mult, [ps, kdp], [(vf, blk)])
                        tt("dve", vb.t[:, blk, :], ps[:, 256:512], kdp.t[:, 1, :], ALU.mult, [ps, kdp], [(vb, blk)])
                    proj_tm(w2, wv2, 0, 512, ev_tm)
                    if RS <= 3:
                        P.pop()
                        return
                    for blk in range(8):
                        bc = slice(blk * 128, (blk + 1) * 128)
                        ps = pst()
                        for hh in range(4):
                            c, po = hh // 2, (hh % 2) * 64
                            mm(ps[:, hh * 128:(hh + 1) * 128], kT.t[:, c, bc], qTz.t[:, hh, bc], True, True, [kT, qTz], [ps])
                        tt("dve", SD.t[:, blk, :], ps[:, :], DM[:], ALU.mult, [ps, DM], [(SD, blk)])
                    if RS <= 4:
                        P.pop()
                        return
                    P.pop()
                    osb = P.sb([128, 2, NTOK], F32, nsub=4, name="osb")
                    memset("pool", S[:], 0.0, [S])
                    if SAMPLE:
                        for dr in range(2):
                            for c in range(2):
                                for hf in range(2):
                                    hh = 2 * c + hf
                                    P.dma("sp", S.t[hf * 64:(hf + 1) * 64, dr, c, hf * 64:(hf + 1) * 64], I["sret"][l, dr, hh], writes=[(S, dr)])
                    for step in range(nch):
                        for s in range(NSEQ):
                            for dr in range(2):
                                n = step if dr == 0 else nch - 1 - step
                                ch = s * 2 + dr
                                blk = s * nch + n
                                tt("pool", Sbf.t[:, blk, dr * 2:dr * 2 + 2, :], S.t[:, ch, :, :], BD.t[:, None, :].to_broadcast([128, 2, 128]), ALU.mult, [(S, ch), BD], [(Sbf, blk * 2 + dr)])
                                ps = pst()
                                vv = vf if dr == 0 else vb
                                for c in range(2):
                                    mm(ps[:, c * 128:(c + 1) * 128], ktm.t[:, blk, c * 128:(c + 1) * 128], vv.t[:, blk, c * 128:(c + 1) * 128], True, True, [(ktm, blk), (vv, blk)], [ps])
                                for c in range(2):
                                    stt(S.t[:, ch, c, :], S.t[:, ch, c, :], cdc[:, dr * 2 + c:dr * 2 + c + 1], ps[:, c * 128:(c + 1) * 128], ALU.mult, ALU.add, [(S, ch), cdc, ps], [(S, ch)])
                    if not SAMPLE:
                        for s in range(NSEQ):
                            for dr in range(2):
                                for c in range(2):
                                    for hf in range(2):
                                        hh = 2 * c + hf
                                        P.dma("sp", O["nsr"][s, l, dr, hh], S.t[hf * 64:(hf + 1) * 64, s * 2 + dr, c, hf * 64:(hf + 1) * 64], reads=[(S, s * 2 + dr)], is_output=True)
                    if RS <= 5:
                        P.pop()
                        return
                    for blk in range(8):
                        bc = slice(blk * 128, (blk + 1) * 128)
                        ps = pst()
                        for c in range(2):
                            for hf in range(2):
                                hh = 2 * c + hf
                                po = hf * 64
                                o_ = ps[:, (c * 2 + hf) * 128:(c * 2 + hf + 1) * 128]
                                mm(o_, vtm.t[:, blk, c * 128:(c + 1) * 128], SD.t[:, blk, hh * 128:(hh + 1) * 128], True, False, [(vtm, blk), (SD, blk)], [ps])
                                mm(o_, Sbf.t[:, blk, c, :], qf.t[:, c, bc], False, False, [(Sbf, blk * 2), qf], [ps])
                                mm(o_, Sbf.t[:, blk, 2 + c, :], qb.t[:, c, bc], False, True, [(Sbf, blk * 2 + 1), qb], [ps])
                        for c in range(2):
                            for hf in range(2):
                                po = hf * 64
                                evac(osb.t[po:po + 64, c, bc], ps[po:po + 64, (c * 2 + hf) * 128:(c * 2 + hf + 1) * 128], [ps], [(osb, c * 2 + blk // 4)])
                    if RS <= 6:
                        P.pop()
                        return
                    ch4 = [(c, tti) for c in range(2) for tti in range(2)]
                    cens = [(tmpf[0], tmpf[0][:, :], tmpf[0]), (tmpf[1], tmpf[1][:, :], tmpf[1]),
                            (xr, xr.t[:, 0, :], (xr, 0)), (xr, xr.t[:, 1, :], (xr, 1))]
                    rss = [(rstd2[0], rstd2[0][:, :], rstd2[0]), (rstd2[1], rstd2[1][:, :], rstd2[1]),
                           (xr, xr.t[:, 2, :], (xr, 2)), (xr, xr.t[:, 3, :], (xr, 3))]
                    pm, pv = {}, {}
                    for k, (c, tti) in enumerate(ch4):
                        cs = slice(tti * 512, (tti + 1) * 512)
                        cp("act", sqbig.t[:, k, :], osb.t[:, c, cs], [(osb, c * 2 + tti)], [(sqbig, 0)])
                    for k, (c, tti) in enumerate(ch4):
                        pm[k] = pst()
                        mm(pm[k][:, :], BOb[:, :], sqbig.t[:, k, :], True, True, [BOb, (sqbig, 0)], [pm[k]])
                    for k, (c, tti) in enumerate(ch4):
                        cs = slice(tti * 512, (tti + 1) * 512)
                        tt("dve", cens[k][1], osb.t[:, c, cs], pm[k][:, :], ALU.subtract, [(osb, c * 2 + tti), pm[k]], [cens[k][2]])
                    for k in range(4):
                        act(sqbig.t[:, 4 + k, :], cens[k][1], AF.Square, [cens[k][2]], [(sqbig, 1)])
                    for k in range(4):
                        pv[k] = pst()
                        mm(pv[k][:, :], BOb[:, :], sqbig.t[:, 4 + k, :], True, True, [BOb, (sqbig, 1)], [pv[k]])
                    for k in range(4):
                        act(rss[k][1], pv[k][:, :], AF.Sqrt, [pv[k], cc], [rss[k][2]], scale=1.0, bias=epsc)
                    for k in range(4):
                        recip(rss[k][1], rss[k][1], [rss[k][2]], [rss[k][2]])
                    for k, (c, tti) in enumerate(ch4):
                        cs = slice(tti * 512, (tti + 1) * 512)
                        tt("dve", cens[k][1], cens[k][1], rss[k][1], ALU.mult, [cens[k][2], rss[k][2]], [cens[k][2]])
                        stt(ymix.t[:, c, cs], cens[k][1], colp[:, C_GN + c:C_GN + c + 1], sg.t[:, c, cs], ALU.mult, ALU.mult, [cens[k][2], colp, sg], [s2(ymix, c, tti)])
                    P.pop()

                def gqa():
                    P.push()
                    NCTX = 512 if SAMPLE else 0
                    NBK = 8 + (4 if SAMPLE else 0)
                    qT = P.sb([128, 2, NTOK], BF16, name="gqT")
                    kT = P.sb([128, 2, NCTX + NTOK], BF16, name="gkT")
                    memset("pool", kT[:], 0.0, [kT])
                    vaug = P.sb([128, NBK, 2, 192], BF16, name="gva")
                    tmo = P.sb([128, 8, 288], F32, name="tmo")
                    PT = [P.sb([128, 512], BF16, name="PT%d" % i) for i in range(3)]
                    pti = {"i": 0}
                    den = tmpf[0]

                    def nPT():
                        t = PT[pti["i"] % 3]
                        pti["i"] += 1
                        return t
                    memset("pool", vaug[:], 1.0, [vaug])
                    w = wload(I["wgqa"][l], 8 * 672)
                    wv = w.t[:, 0:8 * 672].rearrange("p (k n) -> p k n", k=8)
                    if SAMPLE:
                        ropt = P.sb([128, 2, NTOK], F32, name="ropt")
                        P.dma("sp", ropt[:], I["rope"][:, 0:2, :], writes=[ropt])
                        xfs = [P.sb([128, 512], F32, name="xf%d" % i) for i in range(2)]
                        xbs = [P.sb([128, 512], BF16, name="xb%d" % i) for i in range(2)]
                        t1s = [P.sb([128, 512], F32, name="t1%d" % i) for i in range(2)]
                        t2s = [P.sb([128, 512], F32, name="t2%d" % i) for i in range(2)]
                        rst = {"i": 0}

                        def ev_rope(dst_fn):
                            def f(ps, tti):
                                k_ = rst["i"] % 2
                                rst["i"] += 1
                                xf, xb, t1, t2 = xfs[k_], xbs[k_], t1s[k_], t2s[k_]
                                cs = slice(tti * 512, (tti + 1) * 512)
                                cp("act", xf[:], ps[:, :], [ps], [xf])
                                cp("dve", xb[:], ps[:, :], [ps], [xb])
                                ps2 = pst()
                                mm(ps2[:, :], rmat.t[:, 0, :], xb[:], True, True, [rmat, xb], [ps2])
                                tt("pool", t1[:], xf[:], ropt.t[:, 0, cs], ALU.mult, [xf, ropt], [t1])
                                tt("dve", t2[:], ps2[:, :], ropt.t[:, 1, cs], ALU.mult, [ps2, ropt], [t2])
                                for (dst, ap, r0, r1) in dst_fn(cs):
                                    tt("pool", ap, t1[r0:r1, :], t2[r0:r1, :], ALU.add, [t1, t2], [dst])
                            return f
                        for c in range(2):
                            proj_fm(w, wv, c * 128, 128, ev_rope(lambda cs, c=c: [(qT, qT.t[:, c, cs], 0, 128)]))
                        proj_fm(w, wv, 256, 128, ev_rope(lambda cs: [(kT, kT.t[kv_ * 64:(kv_ + 1) * 64, kv_, NCTX + cs.start:NCTX + cs.stop], kv_ * 64, (kv_ + 1) * 64) for kv_ in range(2)]))
                    else:
                        for c in range(2):
                            proj_fm(w, wv, c * 128, 128, lambda ps, tti, c=c: evac(qT.t[:, c, tti * 512:(tti + 1) * 512], ps[:, :], [ps], [qT]))
                        def ev_gk(ps, tti):
                            for kv_ in range(2):
                                evac(kT.t[kv_ * 64:(kv_ + 1) * 64, kv_, tti * 512:(tti + 1) * 512], ps[kv_ * 64:(kv_ + 1) * 64, :], [ps], [kT])
                        proj_fm(w, wv, 256, 128, ev_gk)
                    nb0 = 4 if SAMPLE else 0

                    def ev_tm(ps, blk):
                        if not SAMPLE:
                            cp("act", tmo.t[:, blk, :], ps[:, 0:288], [ps], [tmo])
                        cp("dve", vaug.t[:, nb0 + blk, :, 64:128], ps[:, 128:256].rearrange("p (k d) -> p k d", k=2), [ps], [vaug])
                    proj_tm(w, wv, 384, 288, ev_tm)
                    if not SAMPLE:
                        for s in range(NSEQ):
                            for k_ in range(2):
                                P.dma("sp", O["ngk"][s, l, k_].rearrange("(b p) d -> p b d", p=128),
                                      tmo.t[:, 2 * s:2 * s + 2, k_ * 64:(k_ + 1) * 64], reads=[tmo], is_output=True)
                                P.dma("sp", O["ngv"][s, l, k_].rearrange("(b p) d -> p b d", p=128),
                                      tmo.t[:, 2 * s:2 * s + 2, 128 + k_ * 64:128 + (k_ + 1) * 64], reads=[tmo], is_output=True)
                            P.dma("sp", O["nkr"][s, l].rearrange("(b p) d -> p b d", p=128),
                                  tmo.t[:, 2 * s:2 * s + 2, 256:288], reads=[tmo], is_output=True)
                    if SAMPLE:
                        ctm = P.sb([128, 4, 2, 64], F32, name="ctm")
                        cvm = P.sb([128, 4, 2, 64], F32, name="cvm")
                        for k_ in range(2):
                            P.dma("sp", ctm.t[:, :, k_, :], I["cgk"][l, k_].rearrange("(c p) d -> p c d", p=128), writes=[ctm])
                            P.dma("sp", cvm.t[:, :, k_, :], I["cgv"][l, k_].rearrange("(c p) d -> p c d", p=128), writes=[cvm])
                        ps = pst()
                        for cb in range(4):
                            tr(ps[:, cb * 128:(cb + 1) * 128], ctm.t[:, cb, :, :].rearrange("p k d -> p (k d)"), 128, [ctm], [ps])
                        for kv_ in range(2):
                            evac(kT.t[kv_ * 64:(kv_ + 1) * 64, kv_, 0:512], ps[kv_ * 64:(kv_ + 1) * 64, :], [ps], [kT])
                        cp("pool", vaug.t[:, 0:4, :, 64:128], cvm[:], [cvm], [vaug])

                    dst_ = {"i": 0}

                    def normalize(pso, po, ncols, hh, cq, c0):
                        den = tmpf[dst_["i"] % 2]
                        dst_["i"] += 1
                        nr = slice(po, po + 64)
                        dr_ = slice(64 - po, 128 - po)
                        ts("dve", den[nr, 0:ncols], pso[dr_, 0:ncols], colp[nr, C_SINK + hh:C_SINK + hh + 1], None, ALU.add, None, [pso, colp], [den])
                        recip(den[nr, 0:ncols], den[nr, 0:ncols], [den], [den])
                        tt("dve", ymix.t[nr, 4 + cq, c0:c0 + ncols], pso[nr, 0:ncols], den[nr, 0:ncols], ALU.mult, [pso, den], [sc(ymix, 4 + cq)])

                    act(colp[:, C_SINK:C_SINK + 4], colp[:, C_SINK:C_SINK + 4], AF.Exp, [colp], [colp])
                    if not SAMPLE:
                        for s in range(NSEQ):
                            for hh in range(4):
                                cq, kv = hh % 2, hh // 2
                                po = kv * 64
                                vs = slice(64, 192) if po == 0 else slice(0, 128)
                                ps = pst()
                                for kb in range(2):
                                    kc_ = slice(s * 256 + kb * 128, s * 256 + (kb + 1) * 128)
                                    mm(ps[:, kb * 256:(kb + 1) * 256], kT.t[:, kv, kc_], qT.t[:, cq, s * 256:(s + 1) * 256], True, True, [kT, qT], [ps])
                                pt = nPT()
                                act(pt[:], ps[:, :], AF.Exp, [ps], [pt], scale=0.125)
                                pso = pst()
                                for kb in range(2):
                                    mm(pso[:, 0:256], vaug.t[:, s * 2 + kb, kv, vs], pt[:, kb * 256:(kb + 1) * 256], kb == 0, kb == 1, [vaug, pt], [pso])
                                normalize(pso, po, 256, hh, cq, s * 256)
                    else:
                        for hh in range(4):
                            cq, kv = hh % 2, hh // 2
                            po = kv * 64
                            vs = slice(64, 192) if po == 0 else slice(0, 128)
                            acc = [pss[4 + 2 * (hh % 2)], pss[5 + 2 * (hh % 2)]]
                            for cb in range(4):
                                for tti in range(2):
                                    ps = pst()
                                    mm(ps[:, :], kT.t[:, kv, cb * 128:(cb + 1) * 128], qT.t[:, cq, tti * 512:(tti + 1) * 512], True, True, [kT, qT], [ps])
                                    pt = nPT()
                                    act(pt[:], ps[:, :], AF.Exp, [ps], [pt], scale=0.125)
                                    mm(acc[tti][:, :], vaug.t[:, cb, kv, vs], pt[:], cb == 0, False, [vaug, pt], [acc[tti]])
                            for m in range(8):
                                qlo, qhi = max(m - 1, 0), min(m + 1, 7)
                                n = (qhi - qlo + 1) * 128
                                ps = pst()
                                mm(ps[:, 0:n], kT.t[:, kv, 512 + m * 128:512 + (m + 1) * 128], qT.t[:, cq, qlo * 128:qlo * 128 + n], True, True, [kT, qT], [ps])
                                pt = nPT()
                                act(pt[:, 0:n], ps[:, 0:n], AF.Exp, [ps], [pt], scale=0.125)
                                if m - 1 >= 0:
                                    o0 = (m - 1 - qlo) * 128
                                    P.op("pool", lambda e, pt=pt, o0=o0: e.affine_select(out=pt[:, o0:o0 + 128], in_=pt[:, o0:o0 + 128], pattern=[[1, 128]], compare_op=ALU.is_ge, fill=0.0, base=0, channel_multiplier=-1), reads=[pt], writes=[pt])
                                if m + 1 <= 7:
                                    o0 = (m + 1 - qlo) * 128
                                    P.op("pool", lambda e, pt=pt, o0=o0: e.affine_select(out=pt[:, o0:o0 + 128], in_=pt[:, o0:o0 + 128], pattern=[[-1, 128]], compare_op=ALU.is_ge, fill=0.0, base=0, channel_multiplier=1), reads=[pt], writes=[pt])
                                for nq in range(qlo, qhi + 1):
                                    a = acc[nq // 4]
                                    mm(a[:, (nq % 4) * 128:(nq % 4 + 1) * 128], vaug.t[:, 4 + m, kv, vs], pt[:, (nq - qlo) * 128:(nq - qlo + 1) * 128], False, True, [vaug, pt], [a])
                            for tti in range(2):
                                normalize(acc[tti], po, 512, hh, cq, tti * 512)
                    P.pop()

                def mla():
                    P.push()
                    NCTX = 512 if SAMPLE else 0
                    NK = NCTX + NTOK
                    NBK = NK // 128
                    qn = P.sb([128, 2, NTOK], BF16, nsub=4, name="qn")
                    ckvT = P.sb([128, NTOK], F32, nsub=2, name="ckvT")
                    ckb = P.sb([128, NK], BF16, name="ckb")
                    krT = P.sb([128, NK], BF16, name="krT")
                    mw = P.sb([128, 1280], BF16, name="mw")
                    pti = {"i": 0}

                    def nPT():
                        t = PT[pti["i"] % 3]
                        pti["i"] += 1
                        return t
                    P.dma("pool", mw[:], I["mlaw"][l], writes=[mw])
                    uq = mw.t[:, 0:768].rearrange("p (k n) -> p k n", k=2)
                    w = wload(I["wmla"][l], 8 * 480)
                    wv = w.t[:, 0:8 * 480].rearrange("p (k n) -> p k n", k=8)
                    if SAMPLE:
                        ropm = P.sb([128, 2, NTOK], F32, name="ropm")
                        P.dma("sp", ropm[:], I["rope"][:, 2:4, :], writes=[ropm])
                        mxbs = [P.sb([128, 512], BF16, name="mxb%d" % i) for i in range(2)]
                        mt1s = [P.sb([128, 512], F32, name="mt10"), tmpf[1]]
                        mt2s = [P.sb([128, 512], F32, name="mt2%d" % i) for i in range(2)]
                        mrst = {"i": 0}

                        def rope96(ps, cs, lo, dst, dap):
                            k_ = mrst["i"] % 2
                            mrst["i"] += 1
                            xb, t1, t2 = mxbs[k_], mt1s[k_], mt2s[k_]
                            cp("act", xb[0:96, :], ps[0:96, :], [ps], [xb])
                            ps2 = pst()
                            mm(ps2[0:96, :], rmat.t[0:96, 1, 0:96], xb[0:96, :], True, True, [rmat, xb], [ps2])
                            tt("dve", t1[lo:96, :], ps[lo:96, :], ropm.t[lo:96, 0, cs], ALU.mult, [ps, ropm], [t1])
                            tt("dve", t2[lo:96, :], ps2[lo:96, :], ropm.t[lo:96, 1, cs], ALU.mult, [ps2, ropm], [t2])
                            tt("pool", dap, t1[lo:96, :], t2[lo:96, :], ALU.add, [t1, t2], [dst])
                    P.push()
                    ql = P.sb([128, 2, NTOK], F32, nsub=4, name="ql")
                    kvl = P.sb([128, NTOK], F32, nsub=2, name="kvl")
                    sqb = [P.sb([128, 512], BF16, name="sqb%d" % i) for i in range(2)]
                    for c in range(2):
                        proj_fm(w, wv, c * 128, 128, lambda ps, tti, c=c: evac(ql.t[:, c, tti * 512:(tti + 1) * 512], ps[:, :], [ps], [(ql, c * 2 + tti)]))
                    proj_fm(w, wv, 256, 128, lambda ps, tti: evac(kvl[:, tti * 512:(tti + 1) * 512], ps[:, :], [ps], [(kvl, tti)]))

                    def ev_kr(ps, tti):
                        cs = slice(tti * 512, (tti + 1) * 512)
                        if SAMPLE:
                            rope96(ps, cs, 64, krT, krT[64:96, NCTX + cs.start:NCTX + cs.stop])
                        else:
                            evac(krT[64:96, cs], ps[64:96, :], [ps], [krT])
                    proj_fm(w, wv, 384, 96, ev_kr)
                    for tti in range(2):
                        cs = slice(tti * 512, (tti + 1) * 512)
                        rms_rstd([(ql, ql.t[:, c, cs], c * 2 + tti) for c in range(2)], tti, 256, rstd, sqb)
                        for c in range(2):
                            stt(qn.t[:, c, cs], ql.t[:, c, cs], colp[:, C_QN + c:C_QN + c + 1], rstd[:], ALU.mult, ALU.mult, [(ql, c * 2 + tti), colp, rstd], [(qn, c * 2 + tti)])
                        rms_rstd([(kvl, kvl[:, cs], tti)], tti, 128, rstd, sqb)
                        stt(ckvT[:, cs], kvl[:, cs], colp[:, C_KVN:C_KVN + 1], rstd[:], ALU.mult, ALU.mult, [(kvl, tti), colp, rstd], [(ckvT, tti)])
                        cp("pool", ckb[:, NCTX + cs.start:NCTX + cs.stop], ckvT[:, cs], [(ckvT, tti)], [ckb])
                    if not SAMPLE:
                        otm = P.sb([128, 8, 128], F32, name="otm")
                        for half in range(2):
                            ps = pst()
                            for b4 in range(4):
                                blk = half * 4 + b4
                                tr(ps[:, b4 * 128:(b4 + 1) * 128], ckvT[:, blk * 128:(blk + 1) * 128], 128, [(ckvT, half)], [ps])
                            evac(otm.t[:, half * 4:half * 4 + 4, :], ps[:, :].rearrange("p (a b) -> p a b", a=4), [ps], [otm])
                        for s in range(NSEQ):
                            P.dma("sp", O["nckv"][s, l].rearrange("(b p) d -> p b d", p=128), otm.t[:, 2 * s:2 * s + 2, :], reads=[otm], is_output=True)
                    else:
                        ctm = P.sb([128, 4, 128], F32, name="mctm")
                        krm = P.sb([128, 4, 96], F32, name="krm")
                        P.dma("sp", ctm[:], I["cckv"][l].rearrange("(c p) d -> p c d", p=128), writes=[ctm])
                        memset("pool", krm[:], 0.0, [krm])
                        P.dma("sp", krm.t[:, :, 64:96], I["ckr"][l].rearrange("(c p) d -> p c d", p=128), writes=[krm])
                        ps = pst()
                        for cb in range(4):
                            tr(ps[:, cb * 128:(cb + 1) * 128], ctm.t[:, cb, :], 128, [ctm], [ps])
                        evac(ckb[:, 0:512], ps[:, :], [ps], [ckb])
                        ps = pst()
                        for cb in range(4):
                            tr(ps[0:96, cb * 128:(cb + 1) * 128], krm.t[:, cb, :], 128, [krm], [ps])
                        evac(krT[64:96, 0:512], ps[64:96, :], [ps], [krT])
                    P.pop()
                    qh = P.sb([128, 4, NTOK], BF16, name="qh")
                    kTh = P.sb([128, 4, NK], BF16, name="kTh")
                    vaug = P.sb([128, NBK, 4, 192], BF16, name="mva")
                    PT = [P.sb([128, 512], BF16, name="mPT%d" % i) for i in range(3)]
                    den = tmpf[0]
                    memset("pool", vaug[:], 1.0, [vaug])
                    for hh in range(4):
                        for tti in range(2):
                            cs = slice(tti * 512, (tti + 1) * 512)
                            ps = pst()
                            for kc in range(2):
                                mm(ps[0:96, :], uq[:, kc, hh * 96:(hh + 1) * 96], qn.t[:, kc, cs], kc == 0, kc == 1, [mw, (qn, kc * 2 + tti)], [ps])
                            if SAMPLE:
                                rope96(ps, cs, 0, qh, qh.t[0:96, hh, cs])
                            else:
                                evac(qh.t[0:96, hh, cs], ps[0:96, :], [ps], [qh])
                    for hh in range(4):
                        for kt in range(NK // 512):
                            ks = slice(kt * 512, (kt + 1) * 512)
                            ps = pst()
                            mm(ps[0:64, :], mw[:, 768 + hh * 64:768 + (hh + 1) * 64], ckb[:, ks], True, True, [mw, ckb], [ps])
                            evac(kTh.t[0:64, hh, ks], ps[0:64, :], [ps], [kTh])
                        cp("pool", kTh.t[64:96, hh, :], krT[64:96, :], [krT], [kTh])
                    for b in range(NBK):
                        ps = pst()
                        mm(ps[:, 0:256], ckb[:, b * 128:(b + 1) * 128], mw[:, 1024:1280], True, True, [ckb, mw], [ps])
                        evac(vaug.t[:, b, :, 64:128], ps[:, 0:256].rearrange("p (k d) -> p k d", k=4), [ps], [vaug])
                    SCL = float(96 ** -0.5)

                    dst_ = {"i": 0}

                    def normalize(pso, po, ncols, cm, c0):
                        den = tmpf[0] if SAMPLE else tmpf[dst_["i"] % 2]
                        dst_["i"] += 1
                        nr = slice(po, po + 64)
                        dr_ = slice(64 - po, 128 - po)
                        recip(den[nr, 0:ncols], pso[dr_, 0:ncols], [pso], [den])
                        tt("dve", ymix.t[nr, 6 + cm, c0:c0 + ncols], pso[nr, 0:ncols], den[nr, 0:ncols], ALU.mult, [pso, den], [sc(ymix, 6 + cm)])
                    if not SAMPLE:
                        for s in range(NSEQ):
                            for hh in range(4):
                                cm, po = hh // 2, (hh % 2) * 64
                                vs = slice(64, 192) if po == 0 else slice(0, 128)
                                ps = pst()
                                for kb in range(2):
                                    kc_ = slice(s * 256 + kb * 128, s * 256 + (kb + 1) * 128)
                                    mm(ps[:, kb * 256:(kb + 1) * 256], kTh.t[0:96, hh, kc_], qh.t[0:96, hh, s * 256:(s + 1) * 256], True, True, [kTh, qh], [ps])
                                pt = nPT()
                                act(pt[:], ps[:, :], AF.Exp, [ps], [pt], scale=SCL)
                                pso = pst()
                                for kb in range(2):
                                    mm(pso[:, 0:256], vaug.t[:, s * 2 + kb, hh, vs], pt[:, kb * 256:(kb + 1) * 256], kb == 0, kb == 1, [vaug, pt], [pso])
                                normalize(pso, po, 256, cm, s * 256)
                    else:
                        for hh in range(4):
                            cm, po = hh // 2, (hh % 2) * 64
                            vs = slice(64, 192) if po == 0 else slice(0, 128)
                            acc = [pss[4 + 2 * (hh % 2)], pss[5 + 2 * (hh % 2)]]
                            for kb in range(NBK):
                                for tti in range(2):
                                    ps = pst()
                                    mm(ps[:, :], kTh.t[0:96, hh, kb * 128:(kb + 1) * 128], qh.t[0:96, hh, tti * 512:(tti + 1) * 512], True, True, [kTh, qh], [ps])
                                    pt = nPT()
                                    act(pt[:], ps[:, :], AF.Exp, [ps], [pt], scale=SCL)
                                    mm(acc[tti][:, :], vaug.t[:, kb, hh, vs], pt[:], kb == 0, kb == NBK - 1, [vaug, pt], [acc[tti]])
                            for tti in range(2):
                                normalize(acc[tti], po, 512, cm, tti * 512)
                    P.pop()

                def hyena():
                    P.push()
                    h2 = P.sb([64, 2, L], F32, name="h2")
                    hw = P.sb([64, 1152], F32, name="hw")
                    absd = P.sb([128, 512], F32, name="absd")
                    tn = P.sb([128, 2, nch], F32, name="tn")
                    P.dma("sp", hw[:], I["hyw"][l], writes=[hw])
                    P.dma("sp", absd[:], I["rowp"][l], writes=[absd])
                    P.dma("sp", tn[:], G.tn, writes=[tn])
                    act(absd[:], absd[:], AF.Abs, [absd], [absd])
                    PH = P.phase
                    P.phase = PH + ".mlp"
                    P.push()
                    zg = P.sb([17, 2, L], F32, name="zg")
                    h1 = P.sb([64, 512], F32, name="h1")
                    ri = P.sb([64, 512], I32, name="ri")
                    rf = P.sb([64, 512], F32, name="rf")
                    ra = P.sb([64, 512], F32, name="ra")
                    P.dma("sp", zg[:], G.zg, writes=[zg])
                    nt = min(L, 512)

                    def sin_layer(ps, bcol, out_ap, out_t):
                        ts("dve", ra[:, 0:nt], ps[0:64, 0:nt], colp[0:64, bcol:bcol + 1], float(PI + 16 * TWO_PI), ALU.add, ALU.add, [ps, colp], [ra])
                        ts("dve", ri[:, 0:nt], ra[:, 0:nt], float(1.0 / TWO_PI), None, ALU.mult, None, [ra], [ri])
                        cp("dve", rf[:, 0:nt], ri[:, 0:nt], [ri], [rf])
                        stt(ra[:, 0:nt], rf[:, 0:nt], float(-TWO_PI), ra[:, 0:nt], ALU.mult, ALU.add, [rf, ra], [ra])
                        ts("dve", rf[:, 0:nt], ra[:, 0:nt], 0.0, float(TWO_PI), ALU.is_lt, ALU.mult, [ra], [rf])
                        tt("dve", ra[:, 0:nt], ra[:, 0:nt], rf[:, 0:nt], ALU.add, [ra, rf], [ra])
                        ts("dve", ra[:, 0:nt], ra[:, 0:nt], float(PI), 3.1415925, ALU.subtract, ALU.min, [ra], [ra])
                        ts("dve", ra[:, 0:nt], ra[:, 0:nt], -3.1415925, None, ALU.max, None, [ra], [ra])
                        act(out_ap, ra[:, 0:nt], AF.Sin, [ra], [out_t])
                    for g in range(2):
                        for ti in range(L // nt):
                            cs = slice(ti * nt, (ti + 1) * nt)
                            ps = pst()
                            mm(ps[0:64, 0:nt], hw[0:17, 0:64], zg.t[0:17, g, cs], True, True, [hw, zg], [ps])
                            sin_layer(ps, C_HB1, h1[:, 0:nt], h1)
                            ps2 = pst()
                            mm(ps2[0:64, 0:nt], hw[0:64, 64:128], h1[:, 0:nt], True, True, [hw, h1], [ps2])
                            sin_layer(ps2, C_HB2, h2.t[:, g, cs], h2)
                    P.pop()
                    hpads = [P.sb([128, NSEQ, L + 2], F32, name="hpad%d" % i) for i in range(2)]
                    for hp_ in hpads:
                        memset("pool", hp_[:], 0.0, [hp_])
                    u = P.sb([128, 3, NTOK], F32, nsub=6, name="u")
                    z1 = P.sb([128, NTOK], F32, nsub=2, name="z1")
                    utm = P.sb([128, nch, NSEQ, 128], BF16, name="utm")
                    AB = P.sb([128, nch, 2, 2, 128], BF16, name="AB")
                    Gt = P.sb([128, nch, 2, 128], F32, name="Gt")
                    Ur = P.sb([128, nch, NSEQ, 128], F32, name="Ur")
                    Y = P.sb([128, nch, 2, NSEQ, 128], BF16, name="Y")
                    ffs = [P.sb([128, 2, 128], F32, name="ff%d" % i) for i in range(2)]
                    fbs = [P.sb([128, 2, 128], F32, name="fb%d" % i) for i in range(2)]
                    wins = [P.sb([128, 2, 128], F32, name="win%d" % i) for i in range(2)]
                    m1s = [P.sb([128, 2, 128], F32, name="m1%d" % i) for i in range(2)]
                    m2s = [P.sb([128, 2, 128], F32, name="m2%d" % i) for i in range(2)]
                    nsg = 2 if NSEQ > 1 else 1
                    sgs = [(a, a + nsg) for a in range(0, NSEQ, nsg)]
                    dft = G.dft
                    nel = nch * L

                    NH = 2 if L > 512 else 1
                    HW_ = L // NH
                    FPH = HW_ // 128

                    def dload(i):
                        res = []
                        for hv in range(NH):
                            k = wstate["h"] % (2 * NW)
                            wstate["h"] += 1
                            t, hf = wring[k // 2], k % 2
                            P.dma("sp", t[:, hf * 4096:hf * 4096 + nch * HW_], dft[i, hv], writes=[(t, hf)])
                            res.append((t, hf, t.t[:, hf * 4096:hf * 4096 + nch * HW_].rearrange("p (k n) -> p k n", k=nch)))
                        return res

                    def mcol(res, fch):
                        t, hf, v = res[fch // FPH]
                        c0 = (fch % FPH) * 128
                        return (t, hf), v, c0
                    for cc_ in range(2):
                        P.phase = PH + ".proj"
                        w = wload(I["why"][l], 6144)
                        wv = w.t[:, 0:6144].rearrange("p (k n) -> p k n", k=8)
                        for part in range(3):
                            ci = part * 2 + cc_
                            hpad = hpads[part % 2]

                            def ev_h(ps, tti):
                                if NSEQ > 1:
                                    evac(hpad.t[:, 2 * tti:2 * tti + 2, 1:L + 1], ps[:, :].rearrange("p (s t) -> p s t", s=2), [ps], [hpad])
                                else:
                                    evac(hpad.t[:, 0, 1 + tti * 512:1 + (tti + 1) * 512], ps[:, :], [ps], [hpad])
                            proj_fm(w, wv, ci * 128, 128, ev_h)
                            uv = u.t[:, part, :].rearrange("p (s t) -> p s t", s=NSEQ)
                            act(uv, hpad.t[:, :, 1:L + 1], AF.Identity, [hpad, colp], [sc(u, part)], scale=colp[:, C_HSW + 6 + ci:C_HSW + 7 + ci], bias=colp[:, C_HSB + ci:C_HSB + ci + 1])
                            stt(uv, hpad.t[:, :, 0:L], colp[:, C_HSW + ci:C_HSW + ci + 1], uv, ALU.mult, ALU.add, [hpad, colp, sc(u, part)], [sc(u, part)])
                            stt(uv, hpad.t[:, :, 2:L + 2], colp[:, C_HSW + 12 + ci:C_HSW + 13 + ci], uv, ALU.mult, ALU.add, [hpad, colp, sc(u, part)], [sc(u, part)])
                        P.phase = PH + ".filt"
                        for tch in range(nch):
                            tcs = slice(tch * 128, (tch + 1) * 128)
                            ps = pst()
                            for o_ in range(2):
                                cf = 128 + o_ * 512 + cc_ * 128
                                mm(ps[:, o_ * 128:(o_ + 1) * 128], h2.t[:, 0, tcs], hw[0:64, cf:cf + 128], True, True, [h2, hw], [ps])
                                mm(ps[:, 256 + o_ * 128:256 + (o_ + 1) * 128], h2.t[:, 1, tcs], hw[0:64, cf + 256:cf + 384], True, True, [h2, hw], [ps])
                            win, ff, fb = wins[tch % 2], ffs[tch % 2], fbs[tch % 2]
                            act(win.t[:, 0, :], absd[:, cc_ * 128:(cc_ + 1) * 128], AF.Exp, [absd, tn], [win], scale=tn.t[:, 0, tch:tch + 1])
                            act(win.t[:, 1, :], absd[:, 256 + cc_ * 128:256 + (cc_ + 1) * 128], AF.Exp, [absd, tn], [win], scale=tn.t[:, 1, tch:tch + 1])
                            tt("dve", ff[:], ps[:, 0:256].rearrange("p (o c) -> p o c", o=2), win.t[:, 0:1, :].to_broadcast([128, 2, 128]), ALU.mult, [ps, win], [ff])
                            tt("dve", fb[:], ps[:, 256:512].rearrange("p (o c) -> p o c", o=2), win.t[:, 1:2, :].to_broadcast([128, 2, 128]), ALU.mult, [ps, win], [fb])
                            if tch == 0:
                                memset("dve", fb[0:1, :, :], 0.0, [fb])
                            tt("pool", AB.t[:, tch, :, 0, :], ff[:], fb[:], ALU.add, [ff, fb], [AB])
                            tt("pool", AB.t[:, tch, :, 1, :], ff[:], fb[:], ALU.subtract, [ff, fb], [AB])
                        for o in range(2):
                            P.phase = PH + ".tr"
                            for blk in range(8):
                                s_, jc = blk // nch, blk % nch
                                ps = pst()
                                if o == 0:
                                    tr(ps[:, 0:128], u.t[:, 2, blk * 128:(blk + 1) * 128], 128, [(u, 4 + blk // 4)], [ps])
                                else:
                                    tr(ps[:, 0:128], z1[:, blk * 128:(blk + 1) * 128], 128, [(z1, blk // 4)], [ps])
                                evac(utm.t[:, jc, s_, :], ps[:, 0:128], [ps], [utm])
                            P.phase = PH + ".A"
                            mres = dload(0)
                            for fch in range(nch):
                                ps = pst()
                                mt, mv, c0 = mcol(mres, fch)
                                for tch in range(nch):
                                    mm(ps[:, 0:128], mv[:, tch, c0:c0 + 128], AB.t[:, tch, o, 0, :], tch == 0, tch == nch - 1, [mt, AB], [ps])
                                evac(Gt.t[:, fch, 0, :], ps[:, 0:128], [ps], [Gt])
                            for (sa, sb_) in sgs:
                                ns = sb_ - sa
                                for fch in range(nch):
                                    ps = pst()
                                    mt, mv, c0 = mcol(mres, fch)
                                    for jc in range(nch):
                                        mm(ps[:, 0:ns * 128], mv[:, jc, c0:c0 + 128], utm.t[:, jc, sa:sb_, :], jc == 0, jc == nch - 1, [mt, utm], [ps])
                                    evac(Ur.t[:, fch, sa:sb_, :], ps[:, 0:ns * 128].rearrange("p (s c) -> p s c", s=ns), [ps], [Ur])
                            P.phase = PH + ".B"
                            mres = dload(1)
                            for fch in range(nch):
                                ps = pst()
                                mt, mv, c0 = mcol(mres, fch)
                                for tch in range(nch):
                                    mm(ps[:, 0:128], mv[:, tch, c0:c0 + 128], AB.t[:, tch, o, 1, :], tch == 0, tch == nch - 1, [mt, AB], [ps])
                                evac(Gt.t[:, fch, 1, :], ps[:, 0:128], [ps], [Gt])
                            for (sa, sb_) in sgs:
                                ns = sb_ - sa
                                for fch in range(nch):
                                    ps = pst()
                                    mt, mv, c0 = mcol(mres, fch)
                                    for jc in range(nch):
                                        mm(ps[:, 0:ns * 128], mv[:, jc, c0:c0 + 128], utm.t[:, jc, sa:sb_, :], jc == 0, jc == nch - 1, [mt, utm], [ps])
                                    p3 = ps[:, 0:ns * 128].rearrange("p (s c) -> p s c", s=ns)
                                    gr = Gt.t[:, fch, 0:1, :].to_broadcast([128, ns, 128])
                                    gi = Gt.t[:, fch, 1:2, :].to_broadcast([128, ns, 128])
                                    ur = Ur.t[:, fch, sa:sb_, :]
                                    m1, m2 = m1s[fch % 2], m2s[fch % 2]
                                    tt("pool", m1.t[:, 0:ns, :], ur, gr, ALU.mult, [Ur, Gt], [m1])
                                    tt("dve", m2.t[:, 0:ns, :], p3, gi, ALU.mult, [ps, Gt], [m2])
                                    tt("pool", Y.t[:, fch, 0, sa:sb_, :], m1.t[:, 0:ns, :], m2.t[:, 0:ns, :], ALU.subtract, [m1, m2], [Y])
                                    tt("pool", m1.t[:, 0:ns, :], ur, gi, ALU.mult, [Ur, Gt], [m1])
                                    tt("dve", m2.t[:, 0:ns, :], p3, gr, ALU.mult, [ps, Gt], [m2])
                                    tt("pool", Y.t[:, fch, 1, sa:sb_, :], m1.t[:, 0:ns, :], m2.t[:, 0:ns, :], ALU.add, [m1, m2], [Y])
                            P.phase = PH + ".inv"
                            mrc = dload(2)
                            mrs = dload(3)
                            nt = min(L, 512)
                            tiles = [(s_, it, pst()) for s_ in range(NSEQ) for it in range(L // nt)]
                            for (s_, it, ps) in tiles:
                                t_, hf_, v_ = mrc[it]
                                for fch in range(nch):
                                    mm(ps[:, 0:nt], Y.t[:, fch, 0, s_, :], v_[:, fch, 0:nt], fch == 0, False, [Y, (t_, hf_)], [ps])
                            for (s_, it, ps) in tiles:
                                t_, hf_, v_ = mrs[it]
                                for fch in range(nch):
                                    mm(ps[:, 0:nt], Y.t[:, fch, 1, s_, :], v_[:, fch, 0:nt], False, fch == nch - 1, [Y, (t_, hf_)], [ps])
                            for (s_, it, ps) in tiles:
                                if True:
                                    t0 = s_ * L + it * nt
                                    tcs = slice(t0, t0 + nt)
                                    tix = t0 // 512
                                    t_ = tmp()
                                    bcol = colp[:, C_HBIAS + o * 2 + cc_:C_HBIAS + o * 2 + cc_ + 1]
                                    if o == 0:
                                        stt(t_[:, 0:nt], u.t[:, 2, tcs], bcol, ps[:, 0:nt], ALU.mult, ALU.add, [(u, 4 + tix), colp, ps], [t_])
                                        tt("pool", z1[:, tcs], t_[:, 0:nt], u.t[:, 0, tcs], ALU.mult, [t_, (u, 0 + tix)], [(z1, tix)])
                                    else:
                                        stt(t_[:, 0:nt], z1[:, tcs], bcol, ps[:, 0:nt], ALU.mult, ALU.add, [(z1, tix), colp, ps], [t_])
                                        tt("pool", ymix.t[:, 2 + cc_, tcs], t_[:, 0:nt], u.t[:, 1, tcs], ALU.mult, [t_, (u, 2 + tix)], [s2(ymix, 2 + cc_, tix)])
                    P.pop()

                skip = _os.environ.get("MK_SKIP", "")
                if "ret" not in skip:
                    P.phase = "ret"
                    retention()
                pstate["n"] = 4 if SAMPLE else 8
                if "gqa" not in skip:
                    P.phase = "gqa" + G.name
                    gqa()
                if "mla" not in skip:
                    P.phase = "mla" + G.name
                    mla()
                pstate["n"] = 8
                if "hy" not in skip:
                    P.phase = "hy" + G.name
                    hyena()
                P.phase = "wout"
                P.pop()
                if dbg and l == 0:
                    dump("ymix_" + G.name, ymix, ymix[:], [128, 8, NTOK], BF16)

                P.push()
                y = P.sb([128, 8, NTOK], F32, nsub=16, name="y")

                def post_norm_add(ci):
                    for tti in range(2):
                        big_stats(y, tti)
                    for tti in range(2):
                        cs = slice(tti * 512, (tti + 1) * 512)
                        for q in range(4):
                            x0 = (q % 2) * 2
                            tt("dve" if q % 2 == 0 else "pool", xr.t[:, x0:x0 + 2, :], y.t[:, 2 * q:2 * q + 2, cs], rstd2[tti].t[:, None, :].to_broadcast([128, 2, 512]), ALU.mult,
                               [(y, [c * 2 + tti for c in (2 * q, 2 * q + 1)]), rstd2[tti]], [(xr, [x0, x0 + 1])])
                            for c2 in range(2):
                                c = 2 * q + c2
                                stt(x.t[:, c, cs], xr.t[:, x0 + c2, :], mods.t[:, ci, c:c + 1], x.t[:, c, cs], ALU.mult, ALU.add, [(xr, x0 + c2), mods, s2(x, c, tti)], [s2(x, c, tti)])
                w = wload(I["wout"][l], 8192)
                wv = w.t[:, :].rearrange("p (k n) -> p k n", k=8)
                for m in range(8):
                    for tti in range(2):
                        ps = pst()
                        for kc in range(8):
                            mm(ps[:, :], wv[:, kc, m * 128:(m + 1) * 128], ymix.t[:, kc, tti * 512:(tti + 1) * 512], kc == 0, kc == 7, [w, s2(ymix, kc, tti)], [ps])
                        evac(y.t[:, m, tti * 512:(tti + 1) * 512], ps[:, :], [ps], [s2(y, m, tti)])
                P.phase = "wout.post"
                post_norm_add(2)
                if dbg and l == 0:
                    dump("x1_" + G.name, x, x[:], [128, 8, NTOK])
                if "ffn" in skip:
                    P.pop()
                    continue
                P.phase = "ffn"
                h2_ = ymix
                norm_mod(x, 3, 4, h2_)
                P.phase = "ffn.up"
                actb = P.sb([128, NHC, NTOK], BF16, name="actb")
                gps = [P.sb([128, NSEQ, L + 2], F32, name="gp%d" % i) for i in range(2)]
                gts = [P.sb([128, NTOK], F32, name="gt%d" % i) for i in range(2)]
                for gp in gps:
                    memset("pool", gp[:], 0.0, [gp])
                psus = {}
                wcur = {}

                def ffn_tail(hc):
                    gt = gts[hc % 2]
                    act(gt[:], gt[:], AF.Silu, [gt], [gt])
                    for tti in range(2):
                        cs = slice(tti * 512, (tti + 1) * 512)
                        tt("dve", actb.t[:, hc, cs], gt[:, cs], psus[hc][tti][:, :], ALU.mult, [gt, psus[hc][tti]], [actb])
                    del psus[hc]

                for hc in range(NHC + 1):
                    if hc < NHC:
                        b_, j = hc // 2, hc % 2
                        if j == 0:
                            wcur["w"] = wload_h(I["wup"][l, b_], 4096)
                        w, whf, wo = wcur["w"]
                        wv = w.t[:, wo:wo + 4096].rearrange("p (k n) -> p k n", k=8)
                        gp, gt = gps[hc % 2], gts[hc % 2]
                        psg = []
                        for tti in range(2):
                            ps = pst()
                            for kc in range(8):
                                mm(ps[:, :], wv[:, kc, j * 128:(j + 1) * 128], h2_.t[:, kc, tti * 512:(tti + 1) * 512], kc == 0, kc == 7, [(w, whf), s2(h2_, kc, tti)], [ps])
                            psg.append(ps)
                        pu = []
                        for tti in range(2):
                            ps = pst()
                            for kc in range(8):
                                mm(ps[:, :], wv[:, kc, 256 + j * 128:256 + (j + 1) * 128], h2_.t[:, kc, tti * 512:(tti + 1) * 512], kc == 0, kc == 7, [(w, whf), s2(h2_, kc, tti)], [ps])
                            pu.append(ps)
                        psus[hc] = pu
                    if hc >= 1:
                        ffn_tail(hc - 1)
                    if hc < NHC:
                        for tti in range(2):
                            ps = psg[tti]
                            if NSEQ > 1:
                                cp("act", gp.t[:, 2 * tti:2 * tti + 2, 1:L + 1], ps[:, :].rearrange("p (s t) -> p s t", s=2), [ps], [gp])
                            else:
                                cp("act", gp.t[:, 0, 1 + tti * 512:1 + (tti + 1) * 512], ps[:, :], [ps], [gp])
                        gv = gt[:, :].rearrange("p (s t) -> p s t", s=NSEQ)
                        act(gv, gp.t[:, :, 1:L + 1], AF.Identity, [gp, colp], [gt], scale=colp[:, C_FCW + 22 + hc:C_FCW + 23 + hc], bias=colp[:, C_FCB + hc:C_FCB + hc + 1])
                        stt(gv, gp.t[:, :, 0:L], colp[:, C_FCW + hc:C_FCW + hc + 1], gv, ALU.mult, ALU.add, [gp, colp, gt], [gt])
                        stt(gv, gp.t[:, :, 2:L + 2], colp[:, C_FCW + 44 + hc:C_FCW + 45 + hc], gv, ALU.mult, ALU.add, [gp, colp, gt], [gt])
                P.phase = "ffn.dn"
                for m in range(8):
                    w, whf, wo = wload_h(I["wdn"][l, m], 22 * 128)
                    wv = w.t[:, wo:wo + 22 * 128].rearrange("p (k n) -> p k n", k=22)
                    for tti in range(2):
                        ps = pst()
                        for kc in range(NHC):
                            mm(ps[:, :], wv[:, kc, :], actb.t[:, kc, tti * 512:(tti + 1) * 512], kc == 0, kc == NHC - 1, [(w, whf), actb], [ps])
                        evac(y.t[:, m, tti * 512:(tti + 1) * 512], ps[:, :], [ps], [s2(y, m, tti)])
                P.phase = "ffn.post"
                post_norm_add(5)
                P.pop()
                if dbg and l == 0:
                    dump("x2_" + G.name, x, x[:], [128, 8, NTOK])

            P.phase = "io"
            P.push()
            ot = [P.sb([128, D], F32, name="ot%d" % i) for i in range(2)]
            for blk in range(8):
                o_ = ot[blk % 2]
                for half in range(2):
                    ps = pst()
                    for c4 in range(4):
                        c = half * 4 + c4
                        tr(ps[:, c4 * 128:(c4 + 1) * 128], x.t[:, c, blk * 128:(blk + 1) * 128], 128, [s2(x, c, blk // 4)], [ps])
                    evac(o_[:, half * 512:(half + 1) * 512], ps[:, :], [ps], [o_])
                P.dma("sp", G.y_out[blk * 128:(blk + 1) * 128, :], o_[:], reads=[o_], is_output=True)
            P.pop()
            P.pop()
            mstate["ready"] = True

        GP = Grp()
        GP.name, GP.L, GP.NSEQ, GP.sample, GP.gi = "P", 256, 4, False, 0
        GP.x_in, GP.y_out, GP.dft, GP.zg, GP.tn = I["xp"], O["yp"], I["dftP"], I["zgP"], I["tnP"]
        GS = Grp()
        GS.name, GS.L, GS.NSEQ, GS.sample, GS.gi = "S", 1024, 1, True, 1
        GS.x_in, GS.y_out, GS.dft, GS.zg, GS.tn = I["xs"], O["ys"], I["dftS"], I["zgS"], I["tnS"]
        import os as _os_mod
        which = _os_mod.environ.get("MK_GROUPS", "PS")
        if "P" in which:
            run_group(GP)
        if "S" in which:
            run_group(GS)
        if _os_mod.environ.get("MK_PHASES"):
            import json as _json
            _json.dump(P.pe_phase, open(_os_mod.environ["MK_PHASES"], "w"))
        P.emit()
    return nc, DBG


_CONST = {}


def prep_inputs(inp):
    if "c" not in _CONST:
        _CONST["c"] = host_consts()
    cst = _CONST["c"]
    w = host_weights(inp)
    shared = dict(w)
    shared.update({"cst": cst["cst"], "dftP": cst["dftP"], "dftS": cst["dftS"], "zgP": cst["zgP"], "zgS": cst["zgS"],
                   "tnP": cst["tnP"], "tnS": cst["tnS"], "rope": cst["rope"], "rmat": cst["rmat"]})
    in_maps = []
    for core in range(8):
        b = core % 2
        m = dict(shared)
        m["xp"] = np.ascontiguousarray(inp["x_prompt"][core * 4:(core + 1) * 4].reshape(NTOK, D))
        m["xs"] = np.ascontiguousarray(inp["x_sample"][b])
        m["sret"] = np.ascontiguousarray(inp["state_ret"][b])
        m["cgk"] = np.ascontiguousarray(inp["cache_gqa_k"][b])
        m["cgv"] = np.ascontiguousarray(inp["cache_gqa_v"][b])
        m["cckv"] = np.ascontiguousarray(inp["cache_mla_ckv"][b])
        m["ckr"] = np.ascontiguousarray(inp["cache_mla_krope"][b])
        cond = np.stack([col_tile(inp["c_ctx"]), col_tile(inp["c"][b])]).astype(np.float32)
        m["cond"] = np.ascontiguousarray(cond)
        in_maps.append(m)
    return in_maps


def kernel(**inputs):
    inp = {k: np.asarray(v) for k, v in inputs.items()}
    in_maps = prep_inputs(inp)
    nc, _ = build(False)
    res = run_bass_kernel_spmd(nc, in_maps, core_ids=list(range(8)))
    rs = res.results
    yp = np.concatenate([rs[c]["yp"].reshape(4, 256, D) for c in range(8)], axis=0)
    ys = np.stack([rs[0]["ys"], rs[1]["ys"]], axis=0)
    nsr = np.concatenate([rs[c]["nsr"] for c in range(8)], axis=0)
    ngk = np.concatenate([rs[c]["ngk"] for c in range(8)], axis=0)
    ngv = np.concatenate([rs[c]["ngv"] for c in range(8)], axis=0)
    nckv = np.concatenate([rs[c]["nckv"] for c in range(8)], axis=0)
    nkr = np.concatenate([rs[c]["nkr"] for c in range(8)], axis=0)
    f = lambda a: np.ascontiguousarray(a, dtype=np.float32)
    return (f(yp), f(ys), f(nsr), f(ngk), f(ngv), f(nckv), f(nkr))
```

```python
import numpy as np
import concourse.bass as bass
import concourse.mybir as mybir
from concourse.bass_utils import run_bass_kernel_spmd

F32 = mybir.dt.float32
BF16 = mybir.dt.bfloat16
I32 = mybir.dt.int32
ALU = mybir.AluOpType
AF = mybir.ActivationFunctionType
AX = mybir.AxisListType

COMPUTE = ("pe", "act", "dve", "pool")
NDMASEM = 6


class T:
    def __init__(self, t, nsub=1, name=""):
        self.t = t
        self.nsub = nsub
        self.name = name
        self.lw = [None] * nsub
        self.rd = [[] for _ in range(nsub)]
        self.psum = False

    def __getitem__(self, idx):
        return self.t[idx]


class Prog:
    def __init__(self, nc, ctx):
        self.nc = nc
        self.ctx = ctx
        self.ops = {e: [] for e in COMPUTE + ("sp",)}
        self.cnt = {e: 0 for e in COMPUTE}
        self.semobj = {}
        self.sem = {}
        for e in COMPUTE:
            self.semobj["s_" + e] = ctx.enter_context(nc.semaphore("s_" + e))
            self.sem[e] = "s_" + e
        self.dsem = {}
        self.dcnt = {}
        for q in ("sp", "act", "pool"):
            self.dsem[q] = []
            for i in range(NDMASEM):
                nm = "d_%s%d" % (q, i)
                self.semobj[nm] = ctx.enter_context(nc.semaphore(nm))
                self.dsem[q].append(nm)
            self.dcnt[q] = 0
        self.known = {e: {} for e in COMPUTE + ("sp",)}
        self.out_deps = []
        self.nalloc = 0
        self.free_deps = {}
        self.phase = ""
        self.pe_phase = []
        self.scopes = []
        self.nps = 0

    def sb(self, shape, dt=F32, nsub=1, name=None):
        self.nalloc += 1
        name = (name or "sb") + ("_%d" % self.nalloc)
        if self.scopes:
            st, lst = self.scopes[-1]
        else:
            st, lst = self.ctx, None
        t = st.enter_context(self.nc.sbuf_tensor(name, list(shape), dt))
        tt = T(t, nsub, name)
        if self.free_deps:
            fd = [(k, v[0], v[1]) for k, v in self.free_deps.items()]
            for i in range(nsub):
                tt.rd[i] = list(fd)
        if lst is not None:
            lst.append(tt)
        return tt

    def push(self):
        import contextlib as _cl
        st = _cl.ExitStack()
        self.scopes.append((st, []))

    def pop(self):
        st, lst = self.scopes.pop()
        for tt in lst:
            for i in range(tt.nsub):
                for d in ([tt.lw[i]] if tt.lw[i] is not None else []) + tt.rd[i]:
                    k, v, src = d
                    if self.free_deps.get(k, (0, None))[0] < v:
                        self.free_deps[k] = (v, src)
        st.close()

    def ps(self, shape, dt=F32, nsub=1, name=None):
        self.nalloc += 1
        name = name or ("ps%d" % self.nalloc)
        t = self.ctx.enter_context(self.nc.psum_tensor(name, list(shape), dt))
        tt = T(t, nsub, name)
        tt.psum = True
        return tt

    @staticmethod
    def _norm(lst):
        out = []
        for x in lst or []:
            if isinstance(x, T):
                out.extend((x, i) for i in range(x.nsub))
            else:
                t, s = x
                if s is None:
                    out.extend((t, i) for i in range(t.nsub))
                elif isinstance(s, (list, tuple, range)):
                    out.extend((t, i) for i in s)
                else:
                    out.append((t, s))
        return out

    def _collect(self, eng, reads, writes):
        deps = {}

        def add(d):
            if d is None:
                return
            key, val, src = d
            if src == eng and eng == "pe":
                return
            if deps.get(key, 0) < val:
                deps[key] = val

        for (t, s) in reads:
            add(t.lw[s])
            if t.psum:
                for d in t.rd[s]:
                    if d[2] != eng:
                        add(d)
        for (t, s) in writes:
            add(t.lw[s])
            for d in t.rd[s]:
                add(d)
        waits = []
        kn = self.known[eng]
        for key, val in deps.items():
            if kn.get(key, 0) >= val:
                continue
            kn[key] = val
            waits.append((key, val))
        return waits

    def _commit(self, dep, reads, writes):
        for (t, s) in reads:
            t.rd[s].append(dep)
        for (t, s) in writes:
            t.lw[s] = dep
            t.rd[s] = []

    def op(self, eng, fn, reads=None, writes=None):
        reads = self._norm(reads)
        writes = self._norm(writes)
        waits = self._collect(eng, reads, writes)
        self.cnt[eng] += 1
        n = self.cnt[eng]
        sem = self.sem[eng]
        dep = (sem, n, eng)
        self.ops[eng].append((waits, fn, (sem, 1)))
        if eng == "pe":
            self.pe_phase.append(self.phase)
        self._commit(dep, reads, writes)
        return dep

    def dma(self, q, out, in_, reads=None, writes=None, is_output=False):
        reads = self._norm(reads)
        writes = self._norm(writes)
        waits = self._collect(q, reads, writes)
        i = self.dcnt[q]
        self.dcnt[q] += 1
        sem = self.dsem[q][i % NDMASEM]
        prev = 16 * (i // NDMASEM)
        val = prev + 16
        kn = self.known[q]
        if prev > 0 and kn.get(sem, 0) < prev:
            kn[sem] = prev
            waits.append((sem, prev))
        dep = (sem, val, "dma")

        def fn(e, out=out, in_=in_):
            return e.dma_start(out=out, in_=in_)

        self.ops[q].append((waits, fn, (sem, 16)))
        self._commit(dep, reads, writes)
        if is_output:
            self.out_deps.append(dep)
        return dep

    def emit(self):
        nc = self.nc
        fin = []
        for (sem, val, _) in self.out_deps:
            fin.append((sem, val))
        ops = self.ops
        so = self.semobj
        with nc.Block() as block:
            def run(e, lst, extra=None):
                for (waits, fn, inc) in lst:
                    for (s, v) in waits:
                        e.wait_ge(so[s], v)
                    ins = fn(e)
                    if inc is not None:
                        ins.then_inc(so[inc[0]], inc[1])
                if extra:
                    best = {}
                    for (s, v) in extra:
                        if best.get(s, 0) < v:
                            best[s] = v
                    for s, v in best.items():
                        e.wait_ge(so[s], v)

            @block.sync
            def _(e):
                run(e, ops["sp"], fin)

            @block.tensor
            def _(e):
                run(e, ops["pe"])

            @block.scalar
            def _(e):
                run(e, ops["act"])

            @block.vector
            def _(e):
                run(e, ops["dve"])

            @block.gpsimd
            def _(e):
                run(e, ops["pool"])


import math
import contextlib
import ml_dtypes

D = 1024
NTOK = 1024
DEPTH = 4
EPS = 1e-6
DFF = 2816
NHC = 22
PI = math.pi
TWO_PI = 2.0 * math.pi
BF = ml_dtypes.bfloat16

C_ADAB = 0
C_NORM = 48
C_HSW = 80
C_HSB = 98
C_HBIAS = 104
C_GN = 108
C_QN = 110
C_KVN = 112
C_FCW = 113
C_FCB = 179
C_HB1 = 201
C_HB2 = 202
C_LG = 203
C_SINK = 215
NCOLP = 224

K_IOTA1 = 0
K_REV = 128
K_LAGP = 256
K_LAGN = 384
K_U = 512
K_LO = 640
K_REVC = 768
K_POSC = 832
NCST = 896


def kc_tile(W):
    K, N = W.shape
    return np.ascontiguousarray(W.reshape(K // 128, 128, N).transpose(1, 0, 2)).reshape(128, -1)


def col_tile(v):
    v = np.asarray(v)
    lead = v.shape[:-1]
    n = v.shape[-1] // 128
    a = v.reshape(lead + (n, 128))
    a = np.moveaxis(a, -1, 0)
    return np.ascontiguousarray(a).reshape(128, -1)


def host_consts():
    c = {}
    cst = np.zeros((128, NCST), np.float32)
    i = np.arange(128, dtype=np.float32)
    j = np.arange(128, dtype=np.float32)[:, None]
    cst[:, K_IOTA1:K_IOTA1 + 128] = (i + 1)[None, :]
    cst[:, K_REV:K_REV + 128] = (128 - i)[None, :]
    cst[:, K_LAGP:K_LAGP + 128] = np.maximum(i[None, :] - j, 0)
    cst[:, K_LAGN:K_LAGN + 128] = np.maximum(j - i[None, :], 0)
    cst[:, K_U:K_U + 128] = (i[None, :] >= j)
    cst[:, K_LO:K_LO + 128] = (j >= i[None, :])
    cst[:, K_REVC:K_REVC + 64] = (127 - j)
    cst[:, K_POSC:K_POSC + 64] = j
    c["cst"] = cst
    for nm, L in (("P", 256), ("S", 1024)):
        N = 2 * L
        jj = np.arange(L, dtype=np.float64)[:, None]
        ff = np.arange(L, dtype=np.float64)[None, :]
        ang = PI * (2 * ff + 1) * jj / N
        Cc = np.cos(ang)
        Sc = np.sin(ang)
        mats = [Cc, Sc, Cc.T * (2.0 / N), Sc.T * (2.0 / N)]
        nh = 2 if L > 512 else 1
        hwid = L // nh
        c["dft" + nm] = np.stack([np.stack([kc_tile(np.ascontiguousarray(m[:, hv * hwid:(hv + 1) * hwid]).astype(np.float32)) for hv in range(nh)]) for m in mats]).astype(BF)
        zg = np.zeros((17, 2, L), np.float32)
        tn = np.zeros((128, 2, L // 128), np.float32)
        for g in range(2):
            t = ((np.arange(L) - g) / L).astype(np.float32)
            bands = np.arange(1, 9, dtype=np.float32)
            a = (2.0 * PI) * t[:, None] * bands
            z = np.concatenate([t[:, None], np.cos(a), np.sin(a)], axis=-1).astype(np.float32)
            zg[:, g, :] = z.T
            tn[:, g, :] = -(t.reshape(L // 128, 128).T)
        c["zg" + nm] = zg
        c["tn" + nm] = tn
    Ls = 1024
    rows = (np.arange(Ls) // 64).astype(np.float32)
    cols = (np.arange(Ls) % 64).astype(np.float32)

    def tables(dim):
        q = dim // 4
        inv = (10000.0 ** (-np.arange(q, dtype=np.float32) / q)).astype(np.float32)
        ang = np.concatenate([rows[:, None] * inv, cols[:, None] * inv], axis=-1)
        return np.cos(ang).astype(np.float32), np.sin(ang).astype(np.float32)

    cg, sg = tables(64)
    rope = np.zeros((128, 4, Ls), np.float32)
    for p in range(128):
        d = p % 64
        r = d % 32
        rope[p, 0] = cg[:, r]
        rope[p, 1] = sg[:, r] * (-1.0 if d < 32 else 1.0)
    cm, sm = tables(32)
    rope[:, 2] = 1.0
    for p in range(64, 96):
        d = p - 64
        r = d % 16
        rope[p, 2] = cm[:, r]
        rope[p, 3] = sm[:, r] * (-1.0 if d < 16 else 1.0)
    c["rope"] = rope
    R = np.zeros((128, 2, 128), np.float32)
    for m in range(128):
        d = m % 64
        base = m - d
        if d < 32:
            R[base + d + 32, 0, m] = 1.0
        else:
            R[base + d - 32, 0, m] = 1.0
    for m in range(64, 96):
        d = m - 64
        if d < 16:
            R[64 + d + 16, 1, m] = 1.0
        else:
            R[64 + d - 16, 1, m] = 1.0
    c["rmat"] = R.astype(BF)
    return c


def host_weights(inp):
    w = {}
    L = DEPTH
    ada_w = inp["ada_w"]
    w["adaw"] = np.stack([np.stack([kc_tile(ada_w[l][:, b * 512:(b + 1) * 512]) for b in range(12)]) for l in range(L)])
    w_in = inp["w_in"]
    retA = np.r_[0:256, 256:512, 768:1024]
    retB = np.r_[256:768]
    hy = np.r_[1024:1792]
    gb = 1792
    mb = 2304
    qperm = np.concatenate([gb + h * 64 + np.arange(64) for h in (0, 2, 1, 3)])
    gq = np.concatenate([qperm, gb + np.r_[256:384], gb + np.r_[256:384], gb + np.r_[384:512], mb + np.r_[384:416]])
    ml = np.concatenate([mb + np.r_[0:384], mb + np.r_[320:416]])
    w["wretA"] = np.stack([kc_tile(w_in[l][:, retA]) for l in range(L)])
    w["wretB"] = np.stack([kc_tile(w_in[l][:, retB]) for l in range(L)])
    w["why"] = np.stack([kc_tile(w_in[l][:, hy]) for l in range(L)])
    w["wgqa"] = np.stack([kc_tile(w_in[l][:, gq]) for l in range(L)])
    w["wmla"] = np.stack([kc_tile(w_in[l][:, ml]) for l in range(L)])
    w["mlaw"] = np.stack([np.concatenate([kc_tile(inp["mla_w_uq"][l]), inp["mla_w_uk"][l], inp["mla_w_uv"][l]], axis=1) for l in range(L)])
    perm = np.concatenate([np.r_[0:512], 512 + np.concatenate([h * 64 + np.arange(64) for h in (0, 2, 1, 3)]), np.r_[768:1024]])
    w["wout"] = np.stack([kc_tile(inp["w_out"][l][perm, :]) for l in range(L)])
    up = inp["ffn_w_up"]
    w["wup"] = np.stack([np.stack([kc_tile(np.concatenate([up[l][:, 256 * b:256 * b + 256], up[l][:, DFF + 256 * b:DFF + 256 * b + 256]], axis=1)) for b in range(11)]) for l in range(L)])
    dn = inp["ffn_w_down"]
    w["wdn"] = np.stack([np.stack([kc_tile(dn[l][:, 128 * b:128 * b + 128]) for b in range(8)]) for l in range(L)])
    colp = np.zeros((L, 128, NCOLP), np.float32)
    rowp = np.zeros((L, 128, 512), np.float32)
    hyw = np.zeros((L, 64, 1152), np.float32)
    for l in range(L):
        colp[l, :, C_ADAB:C_ADAB + 48] = col_tile(inp["ada_b"][l])
        colp[l, :, C_NORM:C_NORM + 32] = col_tile(inp["norm_g"][l])
        colp[l, :, C_HSW:C_HSW + 18] = col_tile(inp["hy_short_w"][l])
        colp[l, :, C_HSB:C_HSB + 6] = col_tile(inp["hy_short_b"][l])
        colp[l, :, C_HBIAS:C_HBIAS + 4] = col_tile(inp["hy_bias"][l])
        colp[l, :, C_GN:C_GN + 2] = col_tile(inp["ret_gn_g"][l])
        colp[l, :, C_QN:C_QN + 2] = col_tile(inp["mla_q_norm"][l])
        colp[l, :, C_KVN:C_KVN + 1] = col_tile(inp["mla_kv_norm"][l])
        colp[l, :, C_FCW:C_FCW + 66] = col_tile(inp["ffn_conv_w"][l])
        colp[l, :, C_FCB:C_FCB + 22] = col_tile(inp["ffn_conv_b"][l])
        colp[l, 0:64, C_HB1] = inp["hy_b1"][l]
        colp[l, 0:64, C_HB2] = inp["hy_b2"][l]
        lg = inp["ret_decay_logit"][l]
        for d in range(2):
            for c in range(2):
                colp[l, 0:64, C_LG + d * 2 + c] = lg[d, 2 * c]
                colp[l, 64:128, C_LG + d * 2 + c] = lg[d, 2 * c + 1]
            for h in range(4):
                colp[l, :, C_LG + 4 + d * 4 + h] = lg[d, h]
        for h in range(4):
            colp[l, :, C_SINK + h] = inp["gqa_sink"][l, h]
        rowp[l, :, :] = inp["hy_decay"][l].reshape(1, 512)
        hyw[l, 0:17, 0:64] = inp["hy_w1"][l]
        hyw[l, :, 64:128] = inp["hy_w2"][l]
        hyw[l, :, 128:1152] = inp["hy_w3"][l]
    w["colp"] = colp
    w["rowp"] = rowp
    w["hyw"] = hyw
    return w


class Grp:
    pass


def build(dbg=False):
    nc = bass.Bass("TRN2", target_bir_lowering=False)

    def din(name, shape, dt=F32):
        return nc.dram_tensor(name, list(shape), dt, kind="ExternalInput").ap()

    def dout(name, shape, dt=F32):
        return nc.dram_tensor(name, list(shape), dt, kind="ExternalOutput").ap()

    I = {}
    I["xp"] = din("xp", [NTOK, D])
    I["xs"] = din("xs", [NTOK, D])
    I["sret"] = din("sret", [DEPTH, 2, 4, 64, 64])
    I["cgk"] = din("cgk", [DEPTH, 2, 512, 64])
    I["cgv"] = din("cgv", [DEPTH, 2, 512, 64])
    I["cckv"] = din("cckv", [DEPTH, 512, 128])
    I["ckr"] = din("ckr", [DEPTH, 512, 32])
    I["cond"] = din("cond", [2, 128, 8])
    I["adaw"] = din("adaw", [DEPTH, 12, 128, 4096])
    I["wretA"] = din("wretA", [DEPTH, 128, 8 * 768])
    I["wretB"] = din("wretB", [DEPTH, 128, 8 * 512])
    I["why"] = din("why", [DEPTH, 128, 8 * 768])
    I["wgqa"] = din("wgqa", [DEPTH, 128, 8 * 672])
    I["wmla"] = din("wmla", [DEPTH, 128, 8 * 480])
    I["mlaw"] = din("mlaw", [DEPTH, 128, 1280])
    I["wout"] = din("wout", [DEPTH, 128, 8192])
    I["wup"] = din("wup", [DEPTH, 11, 128, 4096])
    I["wdn"] = din("wdn", [DEPTH, 8, 128, 22 * 128])
    I["colp"] = din("colp", [DEPTH, 128, NCOLP])
    I["rowp"] = din("rowp", [DEPTH, 128, 512])
    I["hyw"] = din("hyw", [DEPTH, 64, 1152])
    I["cst"] = din("cst", [128, NCST])
    I["dftP"] = din("dftP", [4, 1, 128, 2 * 256], BF16)
    I["dftS"] = din("dftS", [4, 2, 128, 8 * 512], BF16)
    I["zgP"] = din("zgP", [17, 2, 256])
    I["zgS"] = din("zgS", [17, 2, 1024])
    I["tnP"] = din("tnP", [128, 2, 2])
    I["tnS"] = din("tnS", [128, 2, 8])
    I["rope"] = din("rope", [128, 4, 1024])
    I["rmat"] = din("rmat", [128, 2, 128], BF16)
    O = {}
    O["yp"] = dout("yp", [NTOK, D])
    O["ys"] = dout("ys", [NTOK, D])
    O["nsr"] = dout("nsr", [4, DEPTH, 2, 4, 64, 64])
    O["ngk"] = dout("ngk", [4, DEPTH, 2, 256, 64])
    O["ngv"] = dout("ngv", [4, DEPTH, 2, 256, 64])
    O["nckv"] = dout("nckv", [4, DEPTH, 256, 128])
    O["nkr"] = dout("nkr", [4, DEPTH, 256, 32])
    DBG = {}

    with contextlib.ExitStack() as ctx:
        P = Prog(nc, ctx)
        ident = P.sb([128, 128], F32, name="ident")
        ones_bf = P.sb([128, 128], BF16, name="ones")
        BO = P.sb([128, 128], F32, name="BO")
        cc = P.sb([128, 4], F32, name="cc")
        cst = P.sb([128, NCST], F32, name="cst")
        rmat = P.sb([128, 2, 128], BF16, name="rmat")
        NW = 2
        wring = [P.sb([128, 8192], BF16, nsub=2, name="wr%d" % i) for i in range(NW)]
        wstate = {"i": 0, "h": 0}
        colp = P.sb([128, NCOLP], F32, name="colp")
        modt = P.sb([128, 48], F32, name="modt")
        mods = P.sb([128, 6, 8], F32, name="mods")
        scond2 = P.sb([128, 8, 33], BF16, name="scond2")
        modrow = [P.sb([33, 512], F32, name="modrow0")] * 2
        condt2 = P.sb([128, 2, 8], F32, name="condt2")
        modS = P.sb([128, DEPTH, 48], F32, name="modS")
        mstate = {"ready": False}
        pss = [P.ps([128, 512], F32, name="psr%d" % i) for i in range(8)]
        acc = [pss[6], pss[7]]
        pstate = {"i": 0, "n": 8}
        evs = {"i": 0}

        def pst():
            t = pss[pstate["i"] % pstate["n"]]
            pstate["i"] += 1
            return t

        def wslot():
            t = wring[wstate["i"] % NW]
            wstate["i"] += 1
            return t

        def wload(src2d, n, q="pool"):
            t = wslot()
            P.dma(q, t[:, 0:n], src2d, writes=[t])
            return t

        def wload_h(src2d, n):
            k = wstate["h"] % (2 * NW)
            wstate["h"] += 1
            t, hf = wring[k // 2], k % 2
            P.dma("pool", t[:, hf * 4096:hf * 4096 + n], src2d, writes=[(t, hf)])
            return t, hf, hf * 4096

        def mm(out, lhsT, rhs, start, stop, rd, wr):
            P.op("pe", lambda e: e.matmul(out, lhsT=lhsT, rhs=rhs, start=start, stop=stop, skip_group_check=True), reads=rd, writes=wr)

        def tr(out, in_, k, rd, wr):
            P.op("pe", lambda e: e.transpose(out=out, in_=in_, identity=ident[0:k, 0:k]), reads=rd + [ident], writes=wr)

        def cp(eng, out, in_, rd, wr):
            if eng == "act":
                P.op("act", lambda e: e.activation(out=out, in_=in_, func=AF.Copy), reads=rd, writes=wr)
            else:
                P.op(eng, lambda e: e.tensor_copy(out=out, in_=in_), reads=rd, writes=wr)

        def evac(out, in_, rd, wr):
            evs["i"] += 1
            cp("act" if evs["i"] % 2 else "dve", out, in_, rd, wr)

        def act(out, in_, func, rd, wr, scale=None, bias=None):
            kw = {}
            if scale is not None:
                kw["scale"] = scale
            if bias is not None:
                kw["bias"] = bias
            P.op("act", lambda e: e.activation(out=out, in_=in_, func=func, **kw), reads=rd, writes=wr)

        def tt(eng, out, in0, in1, op, rd, wr):
            P.op(eng, lambda e: e.tensor_tensor(out=out, in0=in0, in1=in1, op=op), reads=rd, writes=wr)

        def ts(eng, out, in0, s1, s2, op0, op1, rd, wr):
            if op1 is None:
                P.op(eng, lambda e: e.tensor_scalar(out=out, in0=in0, scalar1=s1, scalar2=None, op0=op0), reads=rd, writes=wr)
            else:
                P.op(eng, lambda e: e.tensor_scalar(out=out, in0=in0, scalar1=s1, scalar2=s2, op0=op0, op1=op1), reads=rd, writes=wr)

        def stt(out, in0, scalar, in1, op0, op1, rd, wr):
            P.op("dve", lambda e: e.scalar_tensor_tensor(out=out, in0=in0, scalar=scalar, in1=in1, op0=op0, op1=op1), reads=rd, writes=wr)

        def recip(out, in_, rd, wr):
            P.op("dve", lambda e: e.reciprocal(out=out, in_=in_), reads=rd, writes=wr)

        def memset(eng, ap, val, wr):
            P.op(eng, lambda e: e.memset(ap, val), writes=wr)

        def dump(name, tile, ap, shape, dt=F32):
            if not dbg:
                return
            d = dout("dbg_" + name, shape, F32)
            DBG[name] = d
            P.dma("pool" if dt != F32 else "sp", d, ap, reads=[tile], is_output=True)

        P.dma("sp", cst[:], I["cst"], writes=[cst])
        P.dma("sp", rmat[:], I["rmat"], writes=[rmat])
        memset("pool", ident[:], 1.0, [ident])
        P.op("pool", lambda e: e.affine_select(out=ident[:], in_=ident[:], pattern=[[-1, 128]], compare_op=ALU.is_equal, fill=0.0, base=0, channel_multiplier=1), reads=[ident], writes=[ident])
        memset("dve", ones_bf[:], 1.0, [ones_bf])
        memset("dve", BO[:], 0.0, [BO])
        memset("dve", BO[0:64, 0:64], 1.0 / 64, [BO])
        memset("dve", BO[64:128, 64:128], 1.0 / 64, [BO])
        BOb = P.sb([128, 128], BF16, name="BOb")
        memset("dve", BOb[:], 0.0, [BOb])
        memset("dve", BOb[0:64, 0:64], 1.0 / 64, [BOb])
        memset("dve", BOb[64:128, 64:128], 1.0 / 64, [BOb])
        BD = P.sb([128, 128], F32, name="BD")
        memset("dve", BD[:], 0.0, [BD])
        memset("dve", BD[0:64, 0:64], 1.0, [BD])
        memset("dve", BD[64:128, 64:128], 1.0, [BD])
        memset("dve", cc[:, 0:1], EPS, [cc])
        memset("dve", cc[:, 1:2], 1.0, [cc])
        memset("dve", cc[:, 2:3], 0.0, [cc])
        epsc = cc[:, 0:1]

        def s2(t, c, tti):
            return (t, c * 2 + tti)

        def sc(t, c):
            return (t, [c * 2, c * 2 + 1])

        def rms_rstd(srcs, tti, dim, rstd, sqb):
            ps = pst()
            n = len(srcs)
            for i, (t, ap, sub) in enumerate(srcs):
                sq = sqb[i % 2]
                if i % 2 == 0:
                    act(sq[:], ap, AF.Square, [(t, sub)], [sq])
                else:
                    tt("pool", sq[:], ap, ap, ALU.mult, [(t, sub)], [sq])
                mm(ps[:, :], ones_bf[:, :], sq[:], i == 0, i == n - 1, [ones_bf, sq], [ps])
            act(rstd[:], ps[:, :], AF.Sqrt, [ps, cc], [rstd], scale=1.0 / dim, bias=epsc)
            recip(rstd[:], rstd[:], [rstd], [rstd])

        def run_group(G):
            L, NSEQ, nch = G.L, G.NSEQ, G.L // 128
            SAMPLE = G.sample
            P.push()
            x = P.sb([128, 8, NTOK], F32, nsub=16, name="x")
            ymix = P.sb([128, 8, NTOK], BF16, nsub=16, name="ymix")
            rstd2 = [P.sb([128, 512], F32, name="rstdb%d" % i) for i in range(2)]
            rstd = rstd2[0]
            sqbig = P.sb([128, 8, 512], BF16, nsub=2, name="sqbig")
            xr = P.sb([128, 4, 512], F32, nsub=4, name="xr")
            tmpf = [P.sb([128, 512], F32, name="tmpf%d" % i) for i in range(2)]
            tst = {"i": 0}

            def tmp():
                t = tmpf[tst["i"] % 2]
                tst["i"] += 1
                return t

            P.phase = "io"
            P.push()
            xt = [P.sb([128, D], F32, name="xt%d" % i) for i in range(2)]
            for blk in range(8):
                xb_ = xt[blk % 2]
                P.dma("sp", xb_[:], G.x_in[blk * 128:(blk + 1) * 128, :], writes=[xb_])
                for half in range(2):
                    ps = pst()
                    for c4 in range(4):
                        c = half * 4 + c4
                        tr(ps[:, c4 * 128:(c4 + 1) * 128], xb_[:, c * 128:(c + 1) * 128], 128, [xb_], [ps])
                    evac(x.t[:, half * 4:half * 4 + 4, blk * 128:(blk + 1) * 128],
                         ps[:, :].rearrange("p (a b) -> p a b", a=4), [ps],
                         [(x, [(half * 4 + c4) * 2 + blk // 4 for c4 in range(4)])])
            P.pop()
            first_group = not mstate["ready"]
            if first_group:
                P.dma("sp", condt2[:], I["cond"].rearrange("g p c -> p g c"), writes=[condt2])
                memset("dve", scond2[:], 0.0, [scond2])
                act(scond2.t[:, :, 0], condt2.t[:, G.gi, :], AF.Silu, [condt2], [scond2])
                act(scond2.t[:, :, 32], condt2.t[:, 1 - G.gi, :], AF.Silu, [condt2], [scond2])

            import os as _os
            for l in range(int(_os.environ.get('MK_DEPTH', DEPTH))):
                P.phase = "mod"
                P.dma("sp", colp[:], I["colp"][l], writes=[colp])
                if first_group:
                    psm, psm2 = pss[7], pss[6]
                    pstate["n"] = 6
                    for b in range(12):
                        w, whf, wo = wload_h(I["adaw"][l, b], 4096)
                        wv = w.t[:, wo:wo + 4096].rearrange("p (k n) -> p k n", k=8)
                        ps = pst()
                        for kc in range(8):
                            mm(ps[0:33, :], scond2.t[:, kc, :], wv[:, kc, :], kc == 0, kc == 7, [(w, whf), scond2], [ps])
                        mr = modrow[b % 2]
                        evac(mr[0:33, :], ps[0:33, :], [ps], [mr])
                        for j4 in range(4):
                            j = b * 4 + j4
                            mm(psm[:, j:j + 1], mr[0:1, j4 * 128:(j4 + 1) * 128], cc[0:1, 1:2], True, True, [mr, cc], [psm])
                            mm(psm2[:, j:j + 1], mr[32:33, j4 * 128:(j4 + 1) * 128], cc[32:33, 1:2], True, True, [mr, cc], [psm2])
                    pstate["n"] = 8
                    tt("dve", modt[:], psm[:, 0:48], colp[:, C_ADAB:C_ADAB + 48], ALU.add, [psm, colp], [modt])
                    tt("dve", modS.t[:, l, :], psm2[:, 0:48], colp[:, C_ADAB:C_ADAB + 48], ALU.add, [psm2, colp], [modS])
                else:
                    cp("dve", modt[:], modS.t[:, l, :], [modS], [modt])
                stt(mods.t[:, 0, :], modt[:, 8:16], 1.0, colp[:, C_NORM:C_NORM + 8], ALU.add, ALU.mult, [modt, colp], [mods])
                cp("dve", mods.t[:, 1, :], modt[:, 0:8], [modt], [mods])
                tt("dve", mods.t[:, 2, :], modt[:, 16:24], colp[:, C_NORM + 8:C_NORM + 16], ALU.mult, [modt, colp], [mods])
                stt(mods.t[:, 3, :], modt[:, 32:40], 1.0, colp[:, C_NORM + 16:C_NORM + 24], ALU.add, ALU.mult, [modt, colp], [mods])
                cp("dve", mods.t[:, 4, :], modt[:, 24:32], [modt], [mods])
                tt("dve", mods.t[:, 5, :], modt[:, 40:48], colp[:, C_NORM + 24:C_NORM + 32], ALU.mult, [modt, colp], [mods])

                def big_stats(src, tti):
                    cs = slice(tti * 512, (tti + 1) * 512)
                    act(sqbig.t[:, 0:4, :], src.t[:, 0:4, cs], AF.Square, [(src, [c * 2 + tti for c in range(4)])], [(sqbig, 0)])
                    tt("pool", sqbig.t[:, 4:8, :], src.t[:, 4:8, cs], src.t[:, 4:8, cs], ALU.mult, [(src, [c * 2 + tti for c in range(4, 8)])], [(sqbig, 1)])
                    ps = pst()
                    for c in range(8):
                        mm(ps[:, :], ones_bf[:, :], sqbig.t[:, c, :], c == 0, c == 7, [ones_bf, (sqbig, c // 4)], [ps])
                    r_ = rstd2[tti]
                    act(r_[:], ps[:, :], AF.Sqrt, [ps, cc], [r_], scale=1.0 / D, bias=epsc)
                    recip(r_[:], r_[:], [r_], [r_])

                def norm_mod(src, ai, bi, dst):
                    for tti in range(2):
                        big_stats(src, tti)
                    for tti in range(2):
                        cs = slice(tti * 512, (tti + 1) * 512)
                        for hv in range(2):
                            tt("dve", xr[:], src.t[:, hv * 4:hv * 4 + 4, cs], rstd2[tti].t[:, None, :].to_broadcast([128, 4, 512]), ALU.mult,
                               [(src, [c * 2 + tti for c in range(hv * 4, hv * 4 + 4)]), rstd2[tti]], [xr])
                            for c4 in range(4):
                                c = hv * 4 + c4
                                if c % 2 == 0:
                                    act(dst.t[:, c, cs], xr.t[:, c4, :], AF.Identity, [(xr, c4), mods], [s2(dst, c, tti)], scale=mods.t[:, ai, c:c + 1], bias=mods.t[:, bi, c:c + 1])
                                else:
                                    ts("pool", dst.t[:, c, cs], xr.t[:, c4, :], mods.t[:, ai, c:c + 1], mods.t[:, bi, c:c + 1], ALU.mult, ALU.add, [(xr, c4), mods], [s2(dst, c, tti)])

                P.push()
                h = P.sb([128, 8, NTOK], BF16, nsub=16, name="h")
                P.phase = "norm1"
                norm_mod(x, 0, 1, h)

                def proj_fm(w, wv, col0, M, evf):
                    for tti in range(2):
                        ps = pst()
                        for kc in range(8):
                            mm(ps[0:M, :], wv[:, kc, col0:col0 + M], h.t[:, kc, tti * 512:(tti + 1) * 512], kc == 0, kc == 7, [w, s2(h, kc, tti)], [ps])
                        evf(ps, tti)

                def proj_tm(w, wv, col0, n, evf):
                    for blk in range(8):
                        ps = pst()
                        for kc in range(8):
                            mm(ps[:, 0:n], h.t[:, kc, blk * 128:(blk + 1) * 128], wv[:, kc, col0:col0 + n], kc == 0, kc == 7, [w, s2(h, kc, blk // 4)], [ps])
                        evf(ps, blk)

                def retention():
                    P.push()
                    qf = P.sb([128, 2, NTOK], BF16, name="qf")
                    qb = P.sb([128, 2, NTOK], BF16, name="qb")
                    sg = P.sb([128, 2, NTOK], BF16, name="sg")
                    ktm = P.sb([128, 8, 256], BF16, nsub=8, name="ktm")
                    vtm = P.sb([128, 8, 256], BF16, nsub=8, name="vtm")
                    vf = P.sb([128, 8, 256], BF16, nsub=8, name="vf")
                    vb = P.sb([128, 8, 256], BF16, nsub=8, name="vb")
                    lg = P.sb([128, 12], F32, name="lg")
                    patf = P.sb([128, 2, 128], F32, name="patf")
                    patb = P.sb([128, 2, 128], F32, name="patb")
                    cdc = P.sb([128, 4], F32, name="cdc")
                    DM = P.sb([128, 512], F32, name="DM")
                    kdp = P.sb([128, 2, 256], F32, name="kdp")
                    tfb = P.sb([128, 2, 128], F32, name="tfb")
                    S = P.sb([128, NSEQ * 2, 2, 128], F32, nsub=NSEQ * 2, name="S")
                    Sbf = P.sb([128, 8, 4, 128], BF16, nsub=16, name="Sbf")
                    SD = P.sb([128, 8, 512], BF16, nsub=8, name="SD")
                    P.push()
                    qTz = P.sb([128, 4, NTOK], BF16, name="qTz")
                    memset("pool", qTz[:], 0.0, [qTz])
                    kT = P.sb([128, 2, NTOK], BF16, name="kT")
                    act(lg[:], colp[:, C_LG:C_LG + 12], AF.Exp, [colp], [lg], scale=-1.0)
                    ts("dve", lg[:], lg[:], 1.0, None, ALU.add, None, [lg], [lg])
                    act(lg[:], lg[:], AF.Ln, [lg], [lg])
                    ts("dve", lg[:], lg[:], -1.0, None, ALU.mult, None, [lg], [lg])
                    for c in range(2):
                        act(patf.t[:, c, :], cst[:, K_IOTA1:K_IOTA1 + 128], AF.Exp, [cst, lg], [patf], scale=lg[:, c:c + 1])
                        act(patb.t[:, c, :], cst[:, K_REV:K_REV + 128], AF.Exp, [cst, lg], [patb], scale=lg[:, 2 + c:3 + c])
                    act(cdc[:], lg[:, 0:4], AF.Exp, [lg], [cdc], scale=128.0)
                    for hh in range(4):
                        act(tfb.t[:, 0, :], cst[:, K_LAGP:K_LAGP + 128], AF.Exp, [cst, lg], [tfb], scale=lg[:, 4 + hh:5 + hh])
                        act(tfb.t[:, 1, :], cst[:, K_LAGN:K_LAGN + 128], AF.Exp, [cst, lg], [tfb], scale=lg[:, 8 + hh:9 + hh])
                        stt(tfb.t[:, 0, :], tfb.t[:, 0, :], 0.125, cst[:, K_U:K_U + 128], ALU.mult, ALU.mult, [tfb, cst], [tfb])
                        stt(tfb.t[:, 1, :], tfb.t[:, 1, :], 0.125, cst[:, K_LO:K_LO + 128], ALU.mult, ALU.mult, [tfb, cst], [tfb])
                        tt("dve", DM[:, hh * 128:(hh + 1) * 128], tfb.t[:, 0, :], tfb.t[:, 1, :], ALU.add, [tfb], [DM])
                        act(kdp.t[:, 0, hh * 64:(hh + 1) * 64], cst[:, K_REVC:K_REVC + 64], AF.Exp, [cst, lg], [kdp], scale=lg[:, 4 + hh:5 + hh])
                        act(kdp.t[:, 1, hh * 64:(hh + 1) * 64], cst[:, K_POSC:K_POSC + 64], AF.Exp, [cst, lg], [kdp], scale=lg[:, 8 + hh:9 + hh])
                    ts("dve", kdp[:], kdp[:], 0.125, None, ALU.mult, None, [kdp], [kdp])
                    RS = 99
                    if RS <= 1:
                        P.pop()
                        return
                    w = wload(I["wretA"][l], 6144)
                    wv = w.t[:, 0:6144].rearrange("p (k n) -> p k n", k=8)
                    for c in range(2):
                        def ev_q(ps, tti, c=c):
                            cs = slice(tti * 512, (tti + 1) * 512)
                            for hf in range(2):
                                cp("act", qTz.t[hf * 64:(hf + 1) * 64, 2 * c + hf, cs], ps[hf * 64:(hf + 1) * 64, :], [ps], [qTz])
                            p3 = ps[:, :].rearrange("p (a b) -> p a b", a=4)
                            tt("dve", qf.t[:, c, cs].rearrange("p (a b) -> p a b", a=4), p3, patf.t[:, c:c + 1, :].to_broadcast([128, 4, 128]), ALU.mult, [ps, patf], [qf])
                            tt("dve", qb.t[:, c, cs].rearrange("p (a b) -> p a b", a=4), p3, patb.t[:, c:c + 1, :].to_broadcast([128, 4, 128]), ALU.mult, [ps, patb], [qb])
                        SUB = _os.environ.get("MK_RET_SUB", "qkg")
                        if "q" in SUB:
                            proj_fm(w, wv, c * 128, 128, ev_q)

                        def ev_k(ps, tti, c=c):
                            evac(kT.t[:, c, tti * 512:(tti + 1) * 512], ps[:, :], [ps], [kT])
                        if "k" in SUB:
                            proj_fm(w, wv, 256 + c * 128, 128, ev_k)

                        def ev_g(ps, tti, c=c):
                            act(sg.t[:, c, tti * 512:(tti + 1) * 512], ps[:, :], AF.Silu, [ps], [sg])
                        if "g" in SUB:
                            proj_fm(w, wv, 512 + c * 128, 128, ev_g)
                    if RS <= 2:
                        P.pop()
                        return
                    w2 = wload(I["wretB"][l], 4096)
                    wv2 = w2.t[:, 0:4096].rearrange("p (k n) -> p k n", k=8)

                    def ev_tm(ps, blk):
                        cp("act", ktm.t[:, blk, :], ps[:, 0:256], [ps], [(ktm, blk)])
                        cp("act", vtm.t[:, blk, :], ps[:, 256:512], [ps], [(vtm, blk)])
                        tt("dve", vf.t[:, blk, :], ps[:, 256:512], kdp.t[:, 0, :], ALU.mult, [ps, kdp], [(vf, blk)])
                        tt("dve", vb.t[:, blk, :], ps[:, 256:512], kdp.t[:, 1, :], ALU.mult, [ps, kdp], [(vb, blk)])
                    proj_tm(w2, wv2, 0, 512, ev_tm)
                    if RS <= 3:
                        P.pop()
                        return
                    for blk in range(8):
                        bc = slice(blk * 128, (blk + 1) * 128)
                        ps = pst()
                        for hh in range(4):
                            c, po = hh // 2, (hh % 2) * 64
                            mm(ps[:, hh * 128:(hh + 1) * 128], kT.t[:, c, bc], qTz.t[:, hh, bc], True, True, [kT, qTz], [ps])
                        tt("dve", SD.t[:, blk, :], ps[:, :], DM[:], ALU.mult, [ps, DM], [(SD, blk)])
                    if RS <= 4:
                        P.pop()
                        return
                    P.pop()
                    osb = P.sb([128, 2, NTOK], F32, nsub=4, name="osb")
                    memset("pool", S[:], 0.0, [S])
                    if SAMPLE:
                        for dr in range(2):
                            for c in range(2):
                                for hf in range(2):
                                    hh = 2 * c + hf
                                    P.dma("sp", S.t[hf * 64:(hf + 1) * 64, dr, c, hf * 64:(hf + 1) * 64], I["sret"][l, dr, hh], writes=[(S, dr)])
                    for step in range(nch):
                        for s in range(NSEQ):
                            for dr in range(2):
                                n = step if dr == 0 else nch - 1 - step
                                ch = s * 2 + dr
                                blk = s * nch + n
                                tt("pool", Sbf.t[:, blk, dr * 2:dr * 2 + 2, :], S.t[:, ch, :, :], BD.t[:, None, :].to_broadcast([128, 2, 128]), ALU.mult, [(S, ch), BD], [(Sbf, blk * 2 + dr)])
                                ps = pst()
                                vv = vf if dr == 0 else vb
                                for c in range(2):
                                    mm(ps[:, c * 128:(c + 1) * 128], ktm.t[:, blk, c * 128:(c + 1) * 128], vv.t[:, blk, c * 128:(c + 1) * 128], True, True, [(ktm, blk), (vv, blk)], [ps])
                                for c in range(2):
                                    stt(S.t[:, ch, c, :], S.t[:, ch, c, :], cdc[:, dr * 2 + c:dr * 2 + c + 1], ps[:, c * 128:(c + 1) * 128], ALU.mult, ALU.add, [(S, ch), cdc, ps], [(S, ch)])
                    if not SAMPLE:
                        for s in range(NSEQ):
                            for dr in range(2):
                                for c in range(2):
                                    for hf in range(2):
                                        hh = 2 * c + hf
                                        P.dma("sp", O["nsr"][s, l, dr, hh], S.t[hf * 64:(hf + 1) * 64, s * 2 + dr, c, hf * 64:(hf + 1) * 64], reads=[(S, s * 2 + dr)], is_output=True)
                    if RS <= 5:
                        P.pop()
                        return
                    for blk in range(8):
                        bc = slice(blk * 128, (blk + 1) * 128)
                        ps = pst()
                        for c in range(2):
                            for hf in range(2):
                                hh = 2 * c + hf
                                po = hf * 64
                                o_ = ps[:, (c * 2 + hf) * 128:(c * 2 + hf + 1) * 128]
                                mm(o_, vtm.t[:, blk, c * 128:(c + 1) * 128], SD.t[:, blk, hh * 128:(hh + 1) * 128], True, False, [(vtm, blk), (SD, blk)], [ps])
                                mm(o_, Sbf.t[:, blk, c, :], qf.t[:, c, bc], False, False, [(Sbf, blk * 2), qf], [ps])
                                mm(o_, Sbf.t[:, blk, 2 + c, :], qb.t[:, c, bc], False, True, [(Sbf, blk * 2 + 1), qb], [ps])
                        for c in range(2):
                            for hf in range(2):
                                po = hf * 64
                                evac(osb.t[po:po + 64, c, bc], ps[po:po + 64, (c * 2 + hf) * 128:(c * 2 + hf + 1) * 128], [ps], [(osb, c * 2 + blk // 4)])
                    if RS <= 6:
                        P.pop()
                        return
                    ch4 = [(c, tti) for c in range(2) for tti in range(2)]
                    cens = [(tmpf[0], tmpf[0][:, :], tmpf[0]), (tmpf[1], tmpf[1][:, :], tmpf[1]),
                            (xr, xr.t[:, 0, :], (xr, 0)), (xr, xr.t[:, 1, :], (xr, 1))]
                    rss = [(rstd2[0], rstd2[0][:, :], rstd2[0]), (rstd2[1], rstd2[1][:, :], rstd2[1]),
                           (xr, xr.t[:, 2, :], (xr, 2)), (xr, xr.t[:, 3, :], (xr, 3))]
                    pm, pv = {}, {}
                    for k, (c, tti) in enumerate(ch4):
                        cs = slice(tti * 512, (tti + 1) * 512)
                        cp("act", sqbig.t[:, k, :], osb.t[:, c, cs], [(osb, c * 2 + tti)], [(sqbig, 0)])
                    for k, (c, tti) in enumerate(ch4):
                        pm[k] = pst()
                        mm(pm[k][:, :], BOb[:, :], sqbig.t[:, k, :], True, True, [BOb, (sqbig, 0)], [pm[k]])
                    for k, (c, tti) in enumerate(ch4):
                        cs = slice(tti * 512, (tti + 1) * 512)
                        tt("dve", cens[k][1], osb.t[:, c, cs], pm[k][:, :], ALU.subtract, [(osb, c * 2 + tti), pm[k]], [cens[k][2]])
                    for k in range(4):
                        act(sqbig.t[:, 4 + k, :], cens[k][1], AF.Square, [cens[k][2]], [(sqbig, 1)])
                    for k in range(4):
                        pv[k] = pst()
                        mm(pv[k][:, :], BOb[:, :], sqbig.t[:, 4 + k, :], True, True, [BOb, (sqbig, 1)], [pv[k]])
                    for k in range(4):
                        act(rss[k][1], pv[k][:, :], AF.Sqrt, [pv[k], cc], [rss[k][2]], scale=1.0, bias=epsc)
                    for k in range(4):
                        recip(rss[k][1], rss[k][1], [rss[k][2]], [rss[k][2]])
                    for k, (c, tti) in enumerate(ch4):
                        cs = slice(tti * 512, (tti + 1) * 512)
                        tt("dve", cens[k][1], cens[k][1], rss[k][1], ALU.mult, [cens[k][2], rss[k][2]], [cens[k][2]])
                        stt(ymix.t[:, c, cs], cens[k][1], colp[:, C_GN + c:C_GN + c + 1], sg.t[:, c, cs], ALU.mult, ALU.mult, [cens[k][2], colp, sg], [s2(ymix, c, tti)])
                    P.pop()

                def gqa():
                    P.push()
                    NCTX = 512 if SAMPLE else 0
                    NBK = 8 + (4 if SAMPLE else 0)
                    qT = P.sb([128, 2, NTOK], BF16, name="gqT")
                    kT = P.sb([128, 2, NCTX + NTOK], BF16, name="gkT")
                    memset("pool", kT[:], 0.0, [kT])
                    vaug = P.sb([128, NBK, 2, 192], BF16, name="gva")
                    tmo = P.sb([128, 8, 288], F32, name="tmo")
                    PT = [P.sb([128, 512], BF16, name="PT%d" % i) for i in range(3)]
                    pti = {"i": 0}
                    den = tmpf[0]

                    def nPT():
                        t = PT[pti["i"] % 3]
                        pti["i"] += 1
                        return t
                    memset("pool", vaug[:], 1.0, [vaug])
                    w = wload(I["wgqa"][l], 8 * 672)
                    wv = w.t[:, 0:8 * 672].rearrange("p (k n) -> p k n", k=8)
                    if SAMPLE:
                        ropt = P.sb([128, 2, NTOK], F32, name="ropt")
                        P.dma("sp", ropt[:], I["rope"][:, 0:2, :], writes=[ropt])
                        xfs = [P.sb([128, 512], F32, name="xf%d" % i) for i in range(2)]
                        xbs = [P.sb([128, 512], BF16, name="xb%d" % i) for i in range(2)]
                        t1s = [P.sb([128, 512], F32, name="t1%d" % i) for i in range(2)]
                        t2s = [P.sb([128, 512], F32, name="t2%d" % i) for i in range(2)]
                        rst = {"i": 0}

                        def ev_rope(dst_fn):
                            def f(ps, tti):
                                k_ = rst["i"] % 2
                                rst["i"] += 1
                                xf, xb, t1, t2 = xfs[k_], xbs[k_], t1s[k_], t2s[k_]
                                cs = slice(tti * 512, (tti + 1) * 512)
                                cp("act", xf[:], ps[:, :], [ps], [xf])
                                cp("dve", xb[:], ps[:, :], [ps], [xb])
                                ps2 = pst()
                                mm(ps2[:, :], rmat.t[:, 0, :], xb[:], True, True, [rmat, xb], [ps2])
                                tt("pool", t1[:], xf[:], ropt.t[:, 0, cs], ALU.mult, [xf, ropt], [t1])
                                tt("dve", t2[:], ps2[:, :], ropt.t[:, 1, cs], ALU.mult, [ps2, ropt], [t2])
                                for (dst, ap, r0, r1) in dst_fn(cs):
                                    tt("pool", ap, t1[r0:r1, :], t2[r0:r1, :], ALU.add, [t1, t2], [dst])
                            return f
                        for c in range(2):
                            proj_fm(w, wv, c * 128, 128, ev_rope(lambda cs, c=c: [(qT, qT.t[:, c, cs], 0, 128)]))
                        proj_fm(w, wv, 256, 128, ev_rope(lambda cs: [(kT, kT.t[kv_ * 64:(kv_ + 1) * 64, kv_, NCTX + cs.start:NCTX + cs.stop], kv_ * 64, (kv_ + 1) * 64) for kv_ in range(2)]))
                    else:
                        for c in range(2):
                            proj_fm(w, wv, c * 128, 128, lambda ps, tti, c=c: evac(qT.t[:, c, tti * 512:(tti + 1) * 512], ps[:, :], [ps], [qT]))
                        def ev_gk(ps, tti):
                            for kv_ in range(2):
                                evac(kT.t[kv_ * 64:(kv_ + 1) * 64, kv_, tti * 512:(tti + 1) * 512], ps[kv_ * 64:(kv_ + 1) * 64, :], [ps], [kT])
                        proj_fm(w, wv, 256, 128, ev_gk)
                    nb0 = 4 if SAMPLE else 0

                    def ev_tm(ps, blk):
                        if not SAMPLE:
                            cp("act", tmo.t[:, blk, :], ps[:, 0:288], [ps], [tmo])
                        cp("dve", vaug.t[:, nb0 + blk, :, 64:128], ps[:, 128:256].rearrange("p (k d) -> p k d", k=2), [ps], [vaug])
                    proj_tm(w, wv, 384, 288, ev_tm)
                    if not SAMPLE:
                        for s in range(NSEQ):
                            for k_ in range(2):
                                P.dma("sp", O["ngk"][s, l, k_].rearrange("(b p) d -> p b d", p=128),
                                      tmo.t[:, 2 * s:2 * s + 2, k_ * 64:(k_ + 1) * 64], reads=[tmo], is_output=True)
                                P.dma("sp", O["ngv"][s, l, k_].rearrange("(b p) d -> p b d", p=128),
                                      tmo.t[:, 2 * s:2 * s + 2, 128 + k_ * 64:128 + (k_ + 1) * 64], reads=[tmo], is_output=True)
                            P.dma("sp", O["nkr"][s, l].rearrange("(b p) d -> p b d", p=128),
                                  tmo.t[:, 2 * s:2 * s + 2, 256:288], reads=[tmo], is_output=True)
                    if SAMPLE:
                        ctm = P.sb([128, 4, 2, 64], F32, name="ctm")
                        cvm = P.sb([128, 4, 2, 64], F32, name="cvm")
                        for k_ in range(2):
                            P.dma("sp", ctm.t[:, :, k_, :], I["cgk"][l, k_].rearrange("(c p) d -> p c d", p=128), writes=[ctm])
                            P.dma("sp", cvm.t[:, :, k_, :], I["cgv"][l, k_].rearrange("(c p) d -> p c d", p=128), writes=[cvm])
                        ps = pst()
                        for cb in range(4):
                            tr(ps[:, cb * 128:(cb + 1) * 128], ctm.t[:, cb, :, :].rearrange("p k d -> p (k d)"), 128, [ctm], [ps])
                        for kv_ in range(2):
                            evac(kT.t[kv_ * 64:(kv_ + 1) * 64, kv_, 0:512], ps[kv_ * 64:(kv_ + 1) * 64, :], [ps], [kT])
                        cp("pool", vaug.t[:, 0:4, :, 64:128], cvm[:], [cvm], [vaug])

                    dst_ = {"i": 0}

                    def normalize(pso, po, ncols, hh, cq, c0):
                        den = tmpf[dst_["i"] % 2]
                        dst_["i"] += 1
                        nr = slice(po, po + 64)
                        dr_ = slice(64 - po, 128 - po)
                        ts("dve", den[nr, 0:ncols], pso[dr_, 0:ncols], colp[nr, C_SINK + hh:C_SINK + hh + 1], None, ALU.add, None, [pso, colp], [den])
                        recip(den[nr, 0:ncols], den[nr, 0:ncols], [den], [den])
                        tt("dve", ymix.t[nr, 4 + cq, c0:c0 + ncols], pso[nr, 0:ncols], den[nr, 0:ncols], ALU.mult, [pso, den], [sc(ymix, 4 + cq)])

                    act(colp[:, C_SINK:C_SINK + 4], colp[:, C_SINK:C_SINK + 4], AF.Exp, [colp], [colp])
                    if not SAMPLE:
                        for s in range(NSEQ):
                            for hh in range(4):
                                cq, kv = hh % 2, hh // 2
                                po = kv * 64
                                vs = slice(64, 192) if po == 0 else slice(0, 128)
                                ps = pst()
                                for kb in range(2):
                                    kc_ = slice(s * 256 + kb * 128, s * 256 + (kb + 1) * 128)
                                    mm(ps[:, kb * 256:(kb + 1) * 256], kT.t[:, kv, kc_], qT.t[:, cq, s * 256:(s + 1) * 256], True, True, [kT, qT], [ps])
                                pt = nPT()
                                act(pt[:], ps[:, :], AF.Exp, [ps], [pt], scale=0.125)
                                pso = pst()
                                for kb in range(2):
                                    mm(pso[:, 0:256], vaug.t[:, s * 2 + kb, kv, vs], pt[:, kb * 256:(kb + 1) * 256], kb == 0, kb == 1, [vaug, pt], [pso])
                                normalize(pso, po, 256, hh, cq, s * 256)
                    else:
                        for hh in range(4):
                            cq, kv = hh % 2, hh // 2
                            po = kv * 64
                            vs = slice(64, 192) if po == 0 else slice(0, 128)
                            acc = [pss[4 + 2 * (hh % 2)], pss[5 + 2 * (hh % 2)]]
                            steps = [("c", cb, tti) for cb in range(4) for tti in range(2)] + [("b", m, 0) for m in range(8)]

                            def g_score(st):
                                ps = pst()
                                if st[0] == "c":
                                    cb, tti = st[1], st[2]
                                    mm(ps[:, :], kT.t[:, kv, cb * 128:(cb + 1) * 128], qT.t[:, cq, tti * 512:(tti + 1) * 512], True, True, [kT, qT], [ps])
                                else:
                                    m = st[1]
                                    qlo, qhi = max(m - 1, 0), min(m + 1, 7)
                                    n = (qhi - qlo + 1) * 128
                                    mm(ps[:, 0:n], kT.t[:, kv, 512 + m * 128:512 + (m + 1) * 128], qT.t[:, cq, qlo * 128:qlo * 128 + n], True, True, [kT, qT], [ps])
                                return ps

                            def g_finish(st, ps):
                                pt = nPT()
                                if st[0] == "c":
                                    cb, tti = st[1], st[2]
                                    act(pt[:], ps[:, :], AF.Exp, [ps], [pt], scale=0.125)
                                    mm(acc[tti][:, :], vaug.t[:, cb, kv, vs], pt[:], cb == 0, False, [vaug, pt], [acc[tti]])
                                    return
                                m = st[1]
                                qlo, qhi = max(m - 1, 0), min(m + 1, 7)
                                n = (qhi - qlo + 1) * 128
                                act(pt[:, 0:n], ps[:, 0:n], AF.Exp, [ps], [pt], scale=0.125)
                                if m - 1 >= 0:
                                    o0 = (m - 1 - qlo) * 128
                                    P.op("pool", lambda e, pt=pt, o0=o0: e.affine_select(out=pt[:, o0:o0 + 128], in_=pt[:, o0:o0 + 128], pattern=[[1, 128]], compare_op=ALU.is_ge, fill=0.0, base=0, channel_multiplier=-1), reads=[pt], writes=[pt])
                                if m + 1 <= 7:
                                    o0 = (m + 1 - qlo) * 128
                                    P.op("pool", lambda e, pt=pt, o0=o0: e.affine_select(out=pt[:, o0:o0 + 128], in_=pt[:, o0:o0 + 128], pattern=[[-1, 128]], compare_op=ALU.is_ge, fill=0.0, base=0, channel_multiplier=1), reads=[pt], writes=[pt])
                                for nq in range(qlo, qhi + 1):
                                    a = acc[nq // 4]
                                    mm(a[:, (nq % 4) * 128:(nq % 4 + 1) * 128], vaug.t[:, 4 + m, kv, vs], pt[:, (nq - qlo) * 128:(nq - qlo + 1) * 128], False, True, [vaug, pt], [a])

                            prev = None
                            for st in steps:
                                cur = (st, g_score(st))
                                if prev is not None:
                                    g_finish(*prev)
                                prev = cur
                            g_finish(*prev)
                            for tti in range(2):
                                normalize(acc[tti], po, 512, hh, cq, tti * 512)
                    P.pop()

                def mla():
                    P.push()
                    NCTX = 512 if SAMPLE else 0
                    NK = NCTX + NTOK
                    NBK = NK // 128
                    qn = P.sb([128, 2, NTOK], BF16, nsub=4, name="qn")
                    ckvT = P.sb([128, NTOK], F32, nsub=2, name="ckvT")
                    ckb = P.sb([128, NK], BF16, name="ckb")
                    krT = P.sb([128, NK], BF16, name="krT")
                    mw = P.sb([128, 1280], BF16, name="mw")
                    pti = {"i": 0}

                    def nPT():
                        t = PT[pti["i"] % 3]
                        pti["i"] += 1
                        return t
                    P.dma("pool", mw[:], I["mlaw"][l], writes=[mw])
                    uq = mw.t[:, 0:768].rearrange("p (k n) -> p k n", k=2)
                    w = wload(I["wmla"][l], 8 * 480)
                    wv = w.t[:, 0:8 * 480].rearrange("p (k n) -> p k n", k=8)
                    if SAMPLE:
                        ropm = P.sb([128, 2, NTOK], F32, name="ropm")
                        P.dma("sp", ropm[:], I["rope"][:, 2:4, :], writes=[ropm])
                        mxbs = [P.sb([128, 512], BF16, name="mxb%d" % i) for i in range(2)]
                        mt1s = [P.sb([128, 512], F32, name="mt10"), tmpf[1]]
                        mt2s = [P.sb([128, 512], F32, name="mt2%d" % i) for i in range(2)]
                        mrst = {"i": 0}

                        def rope96(ps, cs, lo, dst, dap):
                            k_ = mrst["i"] % 2
                            mrst["i"] += 1
                            xb, t1, t2 = mxbs[k_], mt1s[k_], mt2s[k_]
                            cp("act", xb[0:96, :], ps[0:96, :], [ps], [xb])
                            ps2 = pst()
                            mm(ps2[0:96, :], rmat.t[0:96, 1, 0:96], xb[0:96, :], True, True, [rmat, xb], [ps2])
                            tt("dve", t1[lo:96, :], ps[lo:96, :], ropm.t[lo:96, 0, cs], ALU.mult, [ps, ropm], [t1])
                            tt("dve", t2[lo:96, :], ps2[lo:96, :], ropm.t[lo:96, 1, cs], ALU.mult, [ps2, ropm], [t2])
                            tt("pool", dap, t1[lo:96, :], t2[lo:96, :], ALU.add, [t1, t2], [dst])
                    P.push()
                    ql = P.sb([128, 2, NTOK], F32, nsub=4, name="ql")
                    kvl = P.sb([128, NTOK], F32, nsub=2, name="kvl")
                    sqb = [P.sb([128, 512], BF16, name="sqb%d" % i) for i in range(2)]
                    for c in range(2):
                        proj_fm(w, wv, c * 128, 128, lambda ps, tti, c=c: evac(ql.t[:, c, tti * 512:(tti + 1) * 512], ps[:, :], [ps], [(ql, c * 2 + tti)]))
                    proj_fm(w, wv, 256, 128, lambda ps, tti: evac(kvl[:, tti * 512:(tti + 1) * 512], ps[:, :], [ps], [(kvl, tti)]))

                    def ev_kr(ps, tti):
                        cs = slice(tti * 512, (tti + 1) * 512)
                        if SAMPLE:
                            rope96(ps, cs, 64, krT, krT[64:96, NCTX + cs.start:NCTX + cs.stop])
                        else:
                            evac(krT[64:96, cs], ps[64:96, :], [ps], [krT])
                    proj_fm(w, wv, 384, 96, ev_kr)
                    for tti in range(2):
                        cs = slice(tti * 512, (tti + 1) * 512)
                        rms_rstd([(ql, ql.t[:, c, cs], c * 2 + tti) for c in range(2)], tti, 256, rstd, sqb)
                        for c in range(2):
                            stt(qn.t[:, c, cs], ql.t[:, c, cs], colp[:, C_QN + c:C_QN + c + 1], rstd[:], ALU.mult, ALU.mult, [(ql, c * 2 + tti), colp, rstd], [(qn, c * 2 + tti)])
                        rms_rstd([(kvl, kvl[:, cs], tti)], tti, 128, rstd, sqb)
                        stt(ckvT[:, cs], kvl[:, cs], colp[:, C_KVN:C_KVN + 1], rstd[:], ALU.mult, ALU.mult, [(kvl, tti), colp, rstd], [(ckvT, tti)])
                        cp("pool", ckb[:, NCTX + cs.start:NCTX + cs.stop], ckvT[:, cs], [(ckvT, tti)], [ckb])
                    if not SAMPLE:
                        otm = P.sb([128, 8, 128], F32, name="otm")
                        for half in range(2):
                            ps = pst()
                            for b4 in range(4):
                                blk = half * 4 + b4
                                tr(ps[:, b4 * 128:(b4 + 1) * 128], ckvT[:, blk * 128:(blk + 1) * 128], 128, [(ckvT, half)], [ps])
                            evac(otm.t[:, half * 4:half * 4 + 4, :], ps[:, :].rearrange("p (a b) -> p a b", a=4), [ps], [otm])
                        for s in range(NSEQ):
                            P.dma("sp", O["nckv"][s, l].rearrange("(b p) d -> p b d", p=128), otm.t[:, 2 * s:2 * s + 2, :], reads=[otm], is_output=True)
                    else:
                        ctm = P.sb([128, 4, 128], F32, name="mctm")
                        krm = P.sb([128, 4, 96], F32, name="krm")
                        P.dma("sp", ctm[:], I["cckv"][l].rearrange("(c p) d -> p c d", p=128), writes=[ctm])
                        memset("pool", krm[:], 0.0, [krm])
                        P.dma("sp", krm.t[:, :, 64:96], I["ckr"][l].rearrange("(c p) d -> p c d", p=128), writes=[krm])
                        ps = pst()
                        for cb in range(4):
                            tr(ps[:, cb * 128:(cb + 1) * 128], ctm.t[:, cb, :], 128, [ctm], [ps])
                        evac(ckb[:, 0:512], ps[:, :], [ps], [ckb])
                        ps = pst()
                        for cb in range(4):
                            tr(ps[0:96, cb * 128:(cb + 1) * 128], krm.t[:, cb, :], 128, [krm], [ps])
                        evac(krT[64:96, 0:512], ps[64:96, :], [ps], [krT])
                    P.pop()
                    qh = P.sb([128, 4, NTOK], BF16, name="qh")
                    kTh = P.sb([128, 4, NK], BF16, name="kTh")
                    vaug = P.sb([128, NBK, 4, 192], BF16, name="mva")
                    PT = [P.sb([128, 512], BF16, name="mPT%d" % i) for i in range(3)]
                    den = tmpf[0]
                    memset("pool", vaug[:], 1.0, [vaug])
                    for hh in range(4):
                        for tti in range(2):
                            cs = slice(tti * 512, (tti + 1) * 512)
                            ps = pst()
                            for kc in range(2):
                                mm(ps[0:96, :], uq[:, kc, hh * 96:(hh + 1) * 96], qn.t[:, kc, cs], kc == 0, kc == 1, [mw, (qn, kc * 2 + tti)], [ps])
                            if SAMPLE:
                                rope96(ps, cs, 0, qh, qh.t[0:96, hh, cs])
                            else:
                                evac(qh.t[0:96, hh, cs], ps[0:96, :], [ps], [qh])
                    for hh in range(4):
                        for kt in range(NK // 512):
                            ks = slice(kt * 512, (kt + 1) * 512)
                            ps = pst()
                            mm(ps[0:64, :], mw[:, 768 + hh * 64:768 + (hh + 1) * 64], ckb[:, ks], True, True, [mw, ckb], [ps])
                            evac(kTh.t[0:64, hh, ks], ps[0:64, :], [ps], [kTh])
                        cp("pool", kTh.t[64:96, hh, :], krT[64:96, :], [krT], [kTh])
                    for b in range(NBK):
                        ps = pst()
                        mm(ps[:, 0:256], ckb[:, b * 128:(b + 1) * 128], mw[:, 1024:1280], True, True, [ckb, mw], [ps])
                        evac(vaug.t[:, b, :, 64:128], ps[:, 0:256].rearrange("p (k d) -> p k d", k=4), [ps], [vaug])
                    SCL = float(96 ** -0.5)

                    dst_ = {"i": 0}

                    def normalize(pso, po, ncols, cm, c0):
                        den = tmpf[0] if SAMPLE else tmpf[dst_["i"] % 2]
                        dst_["i"] += 1
                        nr = slice(po, po + 64)
                        dr_ = slice(64 - po, 128 - po)
                        recip(den[nr, 0:ncols], pso[dr_, 0:ncols], [pso], [den])
                        tt("dve", ymix.t[nr, 6 + cm, c0:c0 + ncols], pso[nr, 0:ncols], den[nr, 0:ncols], ALU.mult, [pso, den], [sc(ymix, 6 + cm)])
                    if not SAMPLE:
                        for s in range(NSEQ):
                            for hh in range(4):
                                cm, po = hh // 2, (hh % 2) * 64
                                vs = slice(64, 192) if po == 0 else slice(0, 128)
                                ps = pst()
                                for kb in range(2):
                                    kc_ = slice(s * 256 + kb * 128, s * 256 + (kb + 1) * 128)
                                    mm(ps[:, kb * 256:(kb + 1) * 256], kTh.t[0:96, hh, kc_], qh.t[0:96, hh, s * 256:(s + 1) * 256], True, True, [kTh, qh], [ps])
                                pt = nPT()
                                act(pt[:], ps[:, :], AF.Exp, [ps], [pt], scale=SCL)
                                pso = pst()
                                for kb in range(2):
                                    mm(pso[:, 0:256], vaug.t[:, s * 2 + kb, hh, vs], pt[:, kb * 256:(kb + 1) * 256], kb == 0, kb == 1, [vaug, pt], [pso])
                                normalize(pso, po, 256, cm, s * 256)
                    else:
                        for hh in range(4):
                            cm, po = hh // 2, (hh % 2) * 64
                            vs = slice(64, 192) if po == 0 else slice(0, 128)
                            acc = [pss[4 + 2 * (hh % 2)], pss[5 + 2 * (hh % 2)]]
                            prev = None
                            for kb in range(NBK):
                                for tti in range(2):
                                    ps = pst()
                                    mm(ps[:, :], kTh.t[0:96, hh, kb * 128:(kb + 1) * 128], qh.t[0:96, hh, tti * 512:(tti + 1) * 512], True, True, [kTh, qh], [ps])
                                    cur = (kb, tti, ps)
                                    if prev is not None:
                                        kb_, tti_, ps_ = prev
                                        pt = nPT()
                                        act(pt[:], ps_[:, :], AF.Exp, [ps_], [pt], scale=SCL)
                                        mm(acc[tti_][:, :], vaug.t[:, kb_, hh, vs], pt[:], kb_ == 0, kb_ == NBK - 1, [vaug, pt], [acc[tti_]])
                                    prev = cur
                            kb_, tti_, ps_ = prev
                            pt = nPT()
                            act(pt[:], ps_[:, :], AF.Exp, [ps_], [pt], scale=SCL)
                            mm(acc[tti_][:, :], vaug.t[:, kb_, hh, vs], pt[:], kb_ == 0, kb_ == NBK - 1, [vaug, pt], [acc[tti_]])
                            for tti in range(2):
                                normalize(acc[tti], po, 512, cm, tti * 512)
                    P.pop()

                def hyena():
                    P.push()
                    h2 = P.sb([64, 2, L], F32, name="h2")
                    hw = P.sb([64, 1152], F32, name="hw")
                    absd = P.sb([128, 512], F32, name="absd")
                    tn = P.sb([128, 2, nch], F32, name="tn")
                    P.dma("sp", hw[:], I["hyw"][l], writes=[hw])
                    P.dma("sp", absd[:], I["rowp"][l], writes=[absd])
                    P.dma("sp", tn[:], G.tn, writes=[tn])
                    act(absd[:], absd[:], AF.Abs, [absd], [absd])
                    PH = P.phase
                    P.phase = PH + ".mlp"
                    P.push()
                    zg = P.sb([17, 2, L], F32, name="zg")
                    h1 = P.sb([64, 512], F32, name="h1")
                    ri = P.sb([64, 512], I32, name="ri")
                    rf = P.sb([64, 512], F32, name="rf")
                    ra = P.sb([64, 512], F32, name="ra")
                    P.dma("sp", zg[:], G.zg, writes=[zg])
                    nt = min(L, 512)

                    def sin_layer(ps, bcol, out_ap, out_t):
                        ts("dve", ra[:, 0:nt], ps[0:64, 0:nt], colp[0:64, bcol:bcol + 1], float(PI + 16 * TWO_PI), ALU.add, ALU.add, [ps, colp], [ra])
                        ts("dve", ri[:, 0:nt], ra[:, 0:nt], float(1.0 / TWO_PI), None, ALU.mult, None, [ra], [ri])
                        cp("dve", rf[:, 0:nt], ri[:, 0:nt], [ri], [rf])
                        stt(ra[:, 0:nt], rf[:, 0:nt], float(-TWO_PI), ra[:, 0:nt], ALU.mult, ALU.add, [rf, ra], [ra])
                        ts("dve", rf[:, 0:nt], ra[:, 0:nt], 0.0, float(TWO_PI), ALU.is_lt, ALU.mult, [ra], [rf])
                        tt("dve", ra[:, 0:nt], ra[:, 0:nt], rf[:, 0:nt], ALU.add, [ra, rf], [ra])
                        ts("dve", ra[:, 0:nt], ra[:, 0:nt], float(PI), 3.1415925, ALU.subtract, ALU.min, [ra], [ra])
                        ts("dve", ra[:, 0:nt], ra[:, 0:nt], -3.1415925, None, ALU.max, None, [ra], [ra])
                        act(out_ap, ra[:, 0:nt], AF.Sin, [ra], [out_t])
                    for g in range(2):
                        for ti in range(L // nt):
                            cs = slice(ti * nt, (ti + 1) * nt)
                            ps = pst()
                            mm(ps[0:64, 0:nt], hw[0:17, 0:64], zg.t[0:17, g, cs], True, True, [hw, zg], [ps])
                            sin_layer(ps, C_HB1, h1[:, 0:nt], h1)
                            ps2 = pst()
                            mm(ps2[0:64, 0:nt], hw[0:64, 64:128], h1[:, 0:nt], True, True, [hw, h1], [ps2])
                            sin_layer(ps2, C_HB2, h2.t[:, g, cs], h2)
                    P.pop()
                    hpads = [P.sb([128, NSEQ, L + 2], F32, name="hpad%d" % i) for i in range(2)]
                    for hp_ in hpads:
                        memset("pool", hp_[:], 0.0, [hp_])
                    u = P.sb([128, 3, NTOK], F32, nsub=6, name="u")
                    z1 = P.sb([128, NTOK], F32, nsub=2, name="z1")
                    utm = P.sb([128, nch, NSEQ, 128], BF16, name="utm")
                    AB = P.sb([128, nch, 2, 2, 128], BF16, name="AB")
                    Gt = P.sb([128, nch, 2, 128], F32, name="Gt")
                    Ur = P.sb([128, nch, NSEQ, 128], F32, name="Ur")
                    Y = P.sb([128, nch, 2, NSEQ, 128], BF16, name="Y")
                    ffs = [P.sb([128, 2, 128], F32, name="ff%d" % i) for i in range(2)]
                    fbs = [P.sb([128, 2, 128], F32, name="fb%d" % i) for i in range(2)]
                    wins = [P.sb([128, 2, 128], F32, name="win%d" % i) for i in range(2)]
                    m1s = [P.sb([128, 2, 128], F32, name="m1%d" % i) for i in range(2)]
                    m2s = [P.sb([128, 2, 128], F32, name="m2%d" % i) for i in range(2)]
                    nsg = 2 if NSEQ > 1 else 1
                    sgs = [(a, a + nsg) for a in range(0, NSEQ, nsg)]
                    dft = G.dft
                    nel = nch * L

                    NH = 2 if L > 512 else 1
                    HW_ = L // NH
                    FPH = HW_ // 128

                    def dload(i):
                        res = []
                        for hv in range(NH):
                            k = wstate["h"] % (2 * NW)
                            wstate["h"] += 1
                            t, hf = wring[k // 2], k % 2
                            P.dma("sp", t[:, hf * 4096:hf * 4096 + nch * HW_], dft[i, hv], writes=[(t, hf)])
                            res.append((t, hf, t.t[:, hf * 4096:hf * 4096 + nch * HW_].rearrange("p (k n) -> p k n", k=nch)))
                        return res

                    def mcol(res, fch):
                        t, hf, v = res[fch // FPH]
                        c0 = (fch % FPH) * 128
                        return (t, hf), v, c0
                    for cc_ in range(2):
                        P.phase = PH + ".proj"
                        w = wload(I["why"][l], 6144)
                        wv = w.t[:, 0:6144].rearrange("p (k n) -> p k n", k=8)
                        for part in range(3):
                            ci = part * 2 + cc_
                            hpad = hpads[part % 2]

                            def ev_h(ps, tti):
                                if NSEQ > 1:
                                    evac(hpad.t[:, 2 * tti:2 * tti + 2, 1:L + 1], ps[:, :].rearrange("p (s t) -> p s t", s=2), [ps], [hpad])
                                else:
                                    evac(hpad.t[:, 0, 1 + tti * 512:1 + (tti + 1) * 512], ps[:, :], [ps], [hpad])
                            proj_fm(w, wv, ci * 128, 128, ev_h)
                            uv = u.t[:, part, :].rearrange("p (s t) -> p s t", s=NSEQ)
                            act(uv, hpad.t[:, :, 1:L + 1], AF.Identity, [hpad, colp], [sc(u, part)], scale=colp[:, C_HSW + 6 + ci:C_HSW + 7 + ci], bias=colp[:, C_HSB + ci:C_HSB + ci + 1])
                            stt(uv, hpad.t[:, :, 0:L], colp[:, C_HSW + ci:C_HSW + ci + 1], uv, ALU.mult, ALU.add, [hpad, colp, sc(u, part)], [sc(u, part)])
                            stt(uv, hpad.t[:, :, 2:L + 2], colp[:, C_HSW + 12 + ci:C_HSW + 13 + ci], uv, ALU.mult, ALU.add, [hpad, colp, sc(u, part)], [sc(u, part)])
                        P.phase = PH + ".filt"
                        for tch in range(nch):
                            tcs = slice(tch * 128, (tch + 1) * 128)
                            ps = pst()
                            for o_ in range(2):
                                cf = 128 + o_ * 512 + cc_ * 128
                                mm(ps[:, o_ * 128:(o_ + 1) * 128], h2.t[:, 0, tcs], hw[0:64, cf:cf + 128], True, True, [h2, hw], [ps])
                                mm(ps[:, 256 + o_ * 128:256 + (o_ + 1) * 128], h2.t[:, 1, tcs], hw[0:64, cf + 256:cf + 384], True, True, [h2, hw], [ps])
                            win, ff, fb = wins[tch % 2], ffs[tch % 2], fbs[tch % 2]
                            act(win.t[:, 0, :], absd[:, cc_ * 128:(cc_ + 1) * 128], AF.Exp, [absd, tn], [win], scale=tn.t[:, 0, tch:tch + 1])
                            act(win.t[:, 1, :], absd[:, 256 + cc_ * 128:256 + (cc_ + 1) * 128], AF.Exp, [absd, tn], [win], scale=tn.t[:, 1, tch:tch + 1])
                            tt("dve", ff[:], ps[:, 0:256].rearrange("p (o c) -> p o c", o=2), win.t[:, 0:1, :].to_broadcast([128, 2, 128]), ALU.mult, [ps, win], [ff])
                            tt("dve", fb[:], ps[:, 256:512].rearrange("p (o c) -> p o c", o=2), win.t[:, 1:2, :].to_broadcast([128, 2, 128]), ALU.mult, [ps, win], [fb])
                            if tch == 0:
                                memset("dve", fb[0:1, :, :], 0.0, [fb])
                            tt("pool", AB.t[:, tch, :, 0, :], ff[:], fb[:], ALU.add, [ff, fb], [AB])
                            tt("pool", AB.t[:, tch, :, 1, :], ff[:], fb[:], ALU.subtract, [ff, fb], [AB])
                        for o in range(2):
                            P.phase = PH + ".tr"
                            for blk in range(8):
                                s_, jc = blk // nch, blk % nch
                                ps = pst()
                                if o == 0:
                                    tr(ps[:, 0:128], u.t[:, 2, blk * 128:(blk + 1) * 128], 128, [(u, 4 + blk // 4)], [ps])
                                else:
                                    tr(ps[:, 0:128], z1[:, blk * 128:(blk + 1) * 128], 128, [(z1, blk // 4)], [ps])
                                evac(utm.t[:, jc, s_, :], ps[:, 0:128], [ps], [utm])
                            P.phase = PH + ".A"
                            mres = dload(0)
                            for fch in range(nch):
                                ps = pst()
                                mt, mv, c0 = mcol(mres, fch)
                                for tch in range(nch):
                                    mm(ps[:, 0:128], mv[:, tch, c0:c0 + 128], AB.t[:, tch, o, 0, :], tch == 0, tch == nch - 1, [mt, AB], [ps])
                                evac(Gt.t[:, fch, 0, :], ps[:, 0:128], [ps], [Gt])
                            for (sa, sb_) in sgs:
                                ns = sb_ - sa
                                for fch in range(nch):
                                    ps = pst()
                                    mt, mv, c0 = mcol(mres, fch)
                                    for jc in range(nch):
                                        mm(ps[:, 0:ns * 128], mv[:, jc, c0:c0 + 128], utm.t[:, jc, sa:sb_, :], jc == 0, jc == nch - 1, [mt, utm], [ps])
                                    evac(Ur.t[:, fch, sa:sb_, :], ps[:, 0:ns * 128].rearrange("p (s c) -> p s c", s=ns), [ps], [Ur])
                            P.phase = PH + ".B"
                            mres = dload(1)
                            for fch in range(nch):
                                ps = pst()
                                mt, mv, c0 = mcol(mres, fch)
                                for tch in range(nch):
                                    mm(ps[:, 0:128], mv[:, tch, c0:c0 + 128], AB.t[:, tch, o, 1, :], tch == 0, tch == nch - 1, [mt, AB], [ps])
                                evac(Gt.t[:, fch, 1, :], ps[:, 0:128], [ps], [Gt])
                            for (sa, sb_) in sgs:
                                ns = sb_ - sa
                                for fch in range(nch):
                                    ps = pst()
                                    mt, mv, c0 = mcol(mres, fch)
                                    for jc in range(nch):
                                        mm(ps[:, 0:ns * 128], mv[:, jc, c0:c0 + 128], utm.t[:, jc, sa:sb_, :], jc == 0, jc == nch - 1, [mt, utm], [ps])
                                    p3 = ps[:, 0:ns * 128].rearrange("p (s c) -> p s c", s=ns)
                                    gr = Gt.t[:, fch, 0:1, :].to_broadcast([128, ns, 128])
                                    gi = Gt.t[:, fch, 1:2, :].to_broadcast([128, ns, 128])
                                    ur = Ur.t[:, fch, sa:sb_, :]
                                    m1, m2 = m1s[fch % 2], m2s[fch % 2]
                                    tt("pool", m1.t[:, 0:ns, :], ur, gr, ALU.mult, [Ur, Gt], [m1])
                                    tt("dve", m2.t[:, 0:ns, :], p3, gi, ALU.mult, [ps, Gt], [m2])
                                    tt("pool", Y.t[:, fch, 0, sa:sb_, :], m1.t[:, 0:ns, :], m2.t[:, 0:ns, :], ALU.subtract, [m1, m2], [Y])
                                    tt("pool", m1.t[:, 0:ns, :], ur, gi, ALU.mult, [Ur, Gt], [m1])
                                    tt("dve", m2.t[:, 0:ns, :], p3, gr, ALU.mult, [ps, Gt], [m2])
                                    tt("pool", Y.t[:, fch, 1, sa:sb_, :], m1.t[:, 0:ns, :], m2.t[:, 0:ns, :], ALU.add, [m1, m2], [Y])
                            P.phase = PH + ".inv"
                            mrc = dload(2)
                            mrs = dload(3)
                            nt = min(L, 512)
                            tiles = [(s_, it, pst()) for s_ in range(NSEQ) for it in range(L // nt)]
                            for (s_, it, ps) in tiles:
                                t_, hf_, v_ = mrc[it]
                                for fch in range(nch):
                                    mm(ps[:, 0:nt], Y.t[:, fch, 0, s_, :], v_[:, fch, 0:nt], fch == 0, False, [Y, (t_, hf_)], [ps])
                            for (s_, it, ps) in tiles:
                                t_, hf_, v_ = mrs[it]
                                for fch in range(nch):
                                    mm(ps[:, 0:nt], Y.t[:, fch, 1, s_, :], v_[:, fch, 0:nt], False, fch == nch - 1, [Y, (t_, hf_)], [ps])
                            for (s_, it, ps) in tiles:
                                if True:
                                    t0 = s_ * L + it * nt
                                    tcs = slice(t0, t0 + nt)
                                    tix = t0 // 512
                                    t_ = tmp()
                                    bcol = colp[:, C_HBIAS + o * 2 + cc_:C_HBIAS + o * 2 + cc_ + 1]
                                    if o == 0:
                                        stt(t_[:, 0:nt], u.t[:, 2, tcs], bcol, ps[:, 0:nt], ALU.mult, ALU.add, [(u, 4 + tix), colp, ps], [t_])
                                        tt("pool", z1[:, tcs], t_[:, 0:nt], u.t[:, 0, tcs], ALU.mult, [t_, (u, 0 + tix)], [(z1, tix)])
                                    else:
                                        stt(t_[:, 0:nt], z1[:, tcs], bcol, ps[:, 0:nt], ALU.mult, ALU.add, [(z1, tix), colp, ps], [t_])
                                        tt("pool", ymix.t[:, 2 + cc_, tcs], t_[:, 0:nt], u.t[:, 1, tcs], ALU.mult, [t_, (u, 2 + tix)], [s2(ymix, 2 + cc_, tix)])
                    P.pop()

                skip = _os.environ.get("MK_SKIP", "")
                if "ret" not in skip:
                    P.phase = "ret"
                    retention()
                pstate["n"] = 4 if SAMPLE else 8
                if "gqa" not in skip:
                    P.phase = "gqa" + G.name
                    gqa()
                if "mla" not in skip:
                    P.phase = "mla" + G.name
                    mla()
                pstate["n"] = 8
                if "hy" not in skip:
                    P.phase = "hy" + G.name
                    hyena()
                P.phase = "wout"
                P.pop()
                if dbg and l == 0:
                    dump("ymix_" + G.name, ymix, ymix[:], [128, 8, NTOK], BF16)

                P.push()
                y = P.sb([128, 8, NTOK], F32, nsub=16, name="y")

                def post_norm_add(ci):
                    for tti in range(2):
                        big_stats(y, tti)
                    for tti in range(2):
                        cs = slice(tti * 512, (tti + 1) * 512)
                        for hv in range(2):
                            tt("pool", xr[:], y.t[:, hv * 4:hv * 4 + 4, cs], rstd2[tti].t[:, None, :].to_broadcast([128, 4, 512]), ALU.mult,
                               [(y, [c * 2 + tti for c in range(hv * 4, hv * 4 + 4)]), rstd2[tti]], [xr])
                            for c4 in range(4):
                                c = hv * 4 + c4
                                stt(x.t[:, c, cs], xr.t[:, c4, :], mods.t[:, ci, c:c + 1], x.t[:, c, cs], ALU.mult, ALU.add, [(xr, c4), mods, s2(x, c, tti)], [s2(x, c, tti)])
                w = wload(I["wout"][l], 8192)
                wv = w.t[:, :].rearrange("p (k n) -> p k n", k=8)
                for m in range(8):
                    for tti in range(2):
                        ps = pst()
                        for kc in range(8):
                            mm(ps[:, :], wv[:, kc, m * 128:(m + 1) * 128], ymix.t[:, kc, tti * 512:(tti + 1) * 512], kc == 0, kc == 7, [w, s2(ymix, kc, tti)], [ps])
                        evac(y.t[:, m, tti * 512:(tti + 1) * 512], ps[:, :], [ps], [s2(y, m, tti)])
                P.phase = "wout.post"
                post_norm_add(2)
                if dbg and l == 0:
                    dump("x1_" + G.name, x, x[:], [128, 8, NTOK])
                if "ffn" in skip:
                    P.pop()
                    continue
                P.phase = "ffn"
                h2_ = ymix
                norm_mod(x, 3, 4, h2_)
                P.phase = "ffn.up"
                actb = P.sb([128, NHC, NTOK], BF16, name="actb")
                gps = [P.sb([128, NSEQ, L + 2], F32, name="gp%d" % i) for i in range(2)]
                gts = [P.sb([128, NTOK], F32, name="gt%d" % i) for i in range(2)]
                for gp in gps:
                    memset("pool", gp[:], 0.0, [gp])
                psus = {}
                wcur = {}

                def ffn_tail(hc):
                    gt = gts[hc % 2]
                    act(gt[:], gt[:], AF.Silu, [gt], [gt])
                    for tti in range(2):
                        cs = slice(tti * 512, (tti + 1) * 512)
                        tt("dve", actb.t[:, hc, cs], gt[:, cs], psus[hc][tti][:, :], ALU.mult, [gt, psus[hc][tti]], [actb])
                    del psus[hc]

                for hc in range(NHC + 1):
                    if hc < NHC:
                        b_, j = hc // 2, hc % 2
                        if j == 0:
                            wcur["w"] = wload_h(I["wup"][l, b_], 4096)
                        w, whf, wo = wcur["w"]
                        wv = w.t[:, wo:wo + 4096].rearrange("p (k n) -> p k n", k=8)
                        gp, gt = gps[hc % 2], gts[hc % 2]
                        psg = []
                        for tti in range(2):
                            ps = pst()
                            for kc in range(8):
                                mm(ps[:, :], wv[:, kc, j * 128:(j + 1) * 128], h2_.t[:, kc, tti * 512:(tti + 1) * 512], kc == 0, kc == 7, [(w, whf), s2(h2_, kc, tti)], [ps])
                            psg.append(ps)
                        pu = []
                        for tti in range(2):
                            ps = pst()
                            for kc in range(8):
                                mm(ps[:, :], wv[:, kc, 256 + j * 128:256 + (j + 1) * 128], h2_.t[:, kc, tti * 512:(tti + 1) * 512], kc == 0, kc == 7, [(w, whf), s2(h2_, kc, tti)], [ps])
                            pu.append(ps)
                        psus[hc] = pu
                    if hc >= 1:
                        ffn_tail(hc - 1)
                    if hc < NHC:
                        for tti in range(2):
                            ps = psg[tti]
                            if NSEQ > 1:
                                cp("act", gp.t[:, 2 * tti:2 * tti + 2, 1:L + 1], ps[:, :].rearrange("p (s t) -> p s t", s=2), [ps], [gp])
                            else:
                                cp("act", gp.t[:, 0, 1 + tti * 512:1 + (tti + 1) * 512], ps[:, :], [ps], [gp])
                        gv = gt[:, :].rearrange("p (s t) -> p s t", s=NSEQ)
                        act(gv, gp.t[:, :, 1:L + 1], AF.Identity, [gp, colp], [gt], scale=colp[:, C_FCW + 22 + hc:C_FCW + 23 + hc], bias=colp[:, C_FCB + hc:C_FCB + hc + 1])
                        stt(gv, gp.t[:, :, 0:L], colp[:, C_FCW + hc:C_FCW + hc + 1], gv, ALU.mult, ALU.add, [gp, colp, gt], [gt])
                        stt(gv, gp.t[:, :, 2:L + 2], colp[:, C_FCW + 44 + hc:C_FCW + 45 + hc], gv, ALU.mult, ALU.add, [gp, colp, gt], [gt])
                P.phase = "ffn.dn"
                for m in range(8):
                    w, whf, wo = wload_h(I["wdn"][l, m], 22 * 128)
                    wv = w.t[:, wo:wo + 22 * 128].rearrange("p (k n) -> p k n", k=22)
                    for tti in range(2):
                        ps = pst()
                        for kc in range(NHC):
                            mm(ps[:, :], wv[:, kc, :], actb.t[:, kc, tti * 512:(tti + 1) * 512], kc == 0, kc == NHC - 1, [(w, whf), actb], [ps])
                        evac(y.t[:, m, tti * 512:(tti + 1) * 512], ps[:, :], [ps], [s2(y, m, tti)])
                P.phase = "ffn.post"
                post_norm_add(5)
                P.pop()
                if dbg and l == 0:
                    dump("x2_" + G.name, x, x[:], [128, 8, NTOK])

            P.phase = "io"
            P.push()
            ot = [P.sb([128, D], F32, name="ot%d" % i) for i in range(2)]
            for blk in range(8):
                o_ = ot[blk % 2]
                for half in range(2):
                    ps = pst()
                    for c4 in range(4):
                        c = half * 4 + c4
                        tr(ps[:, c4 * 128:(c4 + 1) * 128], x.t[:, c, blk * 128:(blk + 1) * 128], 128, [s2(x, c, blk // 4)], [ps])
                    evac(o_[:, half * 512:(half + 1) * 512], ps[:, :], [ps], [o_])
                P.dma("sp", G.y_out[blk * 128:(blk + 1) * 128, :], o_[:], reads=[o_], is_output=True)
            P.pop()
            P.pop()
            mstate["ready"] = True

        GP = Grp()
        GP.name, GP.L, GP.NSEQ, GP.sample, GP.gi = "P", 256, 4, False, 0
        GP.x_in, GP.y_out, GP.dft, GP.zg, GP.tn = I["xp"], O["yp"], I["dftP"], I["zgP"], I["tnP"]
        GS = Grp()
        GS.name, GS.L, GS.NSEQ, GS.sample, GS.gi = "S", 1024, 1, True, 1
        GS.x_in, GS.y_out, GS.dft, GS.zg, GS.tn = I["xs"], O["ys"], I["dftS"], I["zgS"], I["tnS"]
        import os as _os_mod
        which = _os_mod.environ.get("MK_GROUPS", "PS")
        if "P" in which:
            run_group(GP)
        if "S" in which:
            run_group(GS)
        if _os_mod.environ.get("MK_PHASES"):
            import json as _json
            _json.dump(P.pe_phase, open(_os_mod.environ["MK_PHASES"], "w"))
        P.emit()
    return nc, DBG


_CONST = {}


def prep_inputs(inp):
    if "c" not in _CONST:
        _CONST["c"] = host_consts()
    cst = _CONST["c"]
    w = host_weights(inp)
    shared = dict(w)
    shared.update({"cst": cst["cst"], "dftP": cst["dftP"], "dftS": cst["dftS"], "zgP": cst["zgP"], "zgS": cst["zgS"],
                   "tnP": cst["tnP"], "tnS": cst["tnS"], "rope": cst["rope"], "rmat": cst["rmat"]})
    in_maps = []
    for core in range(8):
        b = core % 2
        m = dict(shared)
        m["xp"] = np.ascontiguousarray(inp["x_prompt"][core * 4:(core + 1) * 4].reshape(NTOK, D))
        m["xs"] = np.ascontiguousarray(inp["x_sample"][b])
        m["sret"] = np.ascontiguousarray(inp["state_ret"][b])
        m["cgk"] = np.ascontiguousarray(inp["cache_gqa_k"][b])
        m["cgv"] = np.ascontiguousarray(inp["cache_gqa_v"][b])
        m["cckv"] = np.ascontiguousarray(inp["cache_mla_ckv"][b])
        m["ckr"] = np.ascontiguousarray(inp["cache_mla_krope"][b])
        cond = np.stack([col_tile(inp["c_ctx"]), col_tile(inp["c"][b])]).astype(np.float32)
        m["cond"] = np.ascontiguousarray(cond)
        in_maps.append(m)
    return in_maps


def kernel(**inputs):
    inp = {k: np.asarray(v) for k, v in inputs.items()}
    in_maps = prep_inputs(inp)
    nc, _ = build(False)
    res = run_bass_kernel_spmd(nc, in_maps, core_ids=list(range(8)))
    rs = res.results
    yp = np.concatenate([rs[c]["yp"].reshape(4, 256, D) for c in range(8)], axis=0)
    ys = np.stack([rs[0]["ys"], rs[1]["ys"]], axis=0)
    nsr = np.concatenate([rs[c]["nsr"] for c in range(8)], axis=0)
    ngk = np.concatenate([rs[c]["ngk"] for c in range(8)], axis=0)
    ngv = np.concatenate([rs[c]["ngv"] for c in range(8)], axis=0)
    nckv = np.concatenate([rs[c]["nckv"] for c in range(8)], axis=0)
    nkr = np.concatenate([rs[c]["nkr"] for c in range(8)], axis=0)
    f = lambda a: np.ascontiguousarray(a, dtype=np.float32)
    return (f(yp), f(ys), f(nsr), f(ngk), f(ngv), f(nckv), f(nkr))
```

```python
import numpy as np
import concourse.bass as bass
import concourse.mybir as mybir
from concourse.bass_utils import run_bass_kernel_spmd

F32 = mybir.dt.float32
BF16 = mybir.dt.bfloat16
I32 = mybir.dt.int32
ALU = mybir.AluOpType
AF = mybir.ActivationFunctionType
AX = mybir.AxisListType

COMPUTE = ("pe", "act", "dve", "pool")
NDMASEM = 6


class T:
    def __init__(self, t, nsub=1, name=""):
        self.t = t
        self.nsub = nsub
        self.name = name
        self.lw = [None] * nsub
        self.rd = [[] for _ in range(nsub)]
        self.psum = False

    def __getitem__(self, idx):
        return self.t[idx]


class Prog:
    def __init__(self, nc, ctx):
        self.nc = nc
        self.ctx = ctx
        self.ops = {e: [] for e in COMPUTE + ("sp",)}
        self.cnt = {e: 0 for e in COMPUTE}
        self.semobj = {}
        self.sem = {}
        for e in COMPUTE:
            self.semobj["s_" + e] = ctx.enter_context(nc.semaphore("s_" + e))
            self.sem[e] = "s_" + e
        self.dsem = {}
        self.dcnt = {}
        for q in ("sp", "act", "pool"):
            self.dsem[q] = []
            for i in range(NDMASEM):
                nm = "d_%s%d" % (q, i)
                self.semobj[nm] = ctx.enter_context(nc.semaphore(nm))
                self.dsem[q].append(nm)
            self.dcnt[q] = 0
        self.known = {e: {} for e in COMPUTE + ("sp",)}
        self.out_deps = []
        self.nalloc = 0
        self.free_deps = {}
        self.phase = ""
        self.pe_phase = []
        self.scopes = []
        self.nps = 0

    def sb(self, shape, dt=F32, nsub=1, name=None):
        self.nalloc += 1
        name = (name or "sb") + ("_%d" % self.nalloc)
        if self.scopes:
            st, lst = self.scopes[-1]
        else:
            st, lst = self.ctx, None
        t = st.enter_context(self.nc.sbuf_tensor(name, list(shape), dt))
        tt = T(t, nsub, name)
        if self.free_deps:
            fd = [(k, v[0], v[1]) for k, v in self.free_deps.items()]
            for i in range(nsub):
                tt.rd[i] = list(fd)
        if lst is not None:
            lst.append(tt)
        return tt

    def push(self):
        import contextlib as _cl
        st = _cl.ExitStack()
        self.scopes.append((st, []))

    def pop(self):
        st, lst = self.scopes.pop()
        for tt in lst:
            for i in range(tt.nsub):
                for d in ([tt.lw[i]] if tt.lw[i] is not None else []) + tt.rd[i]:
                    k, v, src = d
                    if self.free_deps.get(k, (0, None))[0] < v:
                        self.free_deps[k] = (v, src)
        st.close()

    def ps(self, shape, dt=F32, nsub=1, name=None):
        self.nalloc += 1
        name = name or ("ps%d" % self.nalloc)
        t = self.ctx.enter_context(self.nc.psum_tensor(name, list(shape), dt))
        tt = T(t, nsub, name)
        tt.psum = True
        return tt

    @staticmethod
    def _norm(lst):
        out = []
        for x in lst or []:
            if isinstance(x, T):
                out.extend((x, i) for i in range(x.nsub))
            else:
                t, s = x
                if s is None:
                    out.extend((t, i) for i in range(t.nsub))
                elif isinstance(s, (list, tuple, range)):
                    out.extend((t, i) for i in s)
                else:
                    out.append((t, s))
        return out

    def _collect(self, eng, reads, writes):
        deps = {}

        def add(d):
            if d is None:
                return
            key, val, src = d
            if src == eng and eng == "pe":
                return
            if deps.get(key, 0) < val:
                deps[key] = val

        for (t, s) in reads:
            add(t.lw[s])
            if t.psum:
                for d in t.rd[s]:
                    if d[2] != eng:
                        add(d)
        for (t, s) in writes:
            add(t.lw[s])
            for d in t.rd[s]:
                add(d)
        waits = []
        kn = self.known[eng]
        for key, val in deps.items():
            if kn.get(key, 0) >= val:
                continue
            kn[key] = val
            waits.append((key, val))
        return waits

    def _commit(self, dep, reads, writes):
        for (t, s) in reads:
            t.rd[s].append(dep)
        for (t, s) in writes:
            t.lw[s] = dep
            t.rd[s] = []

    def op(self, eng, fn, reads=None, writes=None):
        reads = self._norm(reads)
        writes = self._norm(writes)
        waits = self._collect(eng, reads, writes)
        self.cnt[eng] += 1
        n = self.cnt[eng]
        sem = self.sem[eng]
        dep = (sem, n, eng)
        self.ops[eng].append((waits, fn, (sem, 1)))
        if eng == "pe":
            self.pe_phase.append(self.phase)
        self._commit(dep, reads, writes)
        return dep

    def dma(self, q, out, in_, reads=None, writes=None, is_output=False):
        reads = self._norm(reads)
        writes = self._norm(writes)
        waits = self._collect(q, reads, writes)
        i = self.dcnt[q]
        self.dcnt[q] += 1
        sem = self.dsem[q][i % NDMASEM]
        prev = 16 * (i // NDMASEM)
        val = prev + 16
        kn = self.known[q]
        if prev > 0 and kn.get(sem, 0) < prev:
            kn[sem] = prev
            waits.append((sem, prev))
        dep = (sem, val, "dma")

        def fn(e, out=out, in_=in_):
            return e.dma_start(out=out, in_=in_)

        self.ops[q].append((waits, fn, (sem, 16)))
        self._commit(dep, reads, writes)
        if is_output:
            self.out_deps.append(dep)
        return dep

    def emit(self):
        nc = self.nc
        fin = []
        for (sem, val, _) in self.out_deps:
            fin.append((sem, val))
        ops = self.ops
        so = self.semobj
        with nc.Block() as block:
            def run(e, lst, extra=None):
                for (waits, fn, inc) in lst:
                    for (s, v) in waits:
                        e.wait_ge(so[s], v)
                    ins = fn(e)
                    if inc is not None:
                        ins.then_inc(so[inc[0]], inc[1])
                if extra:
                    best = {}
                    for (s, v) in extra:
                        if best.get(s, 0) < v:
                            best[s] = v
                    for s, v in best.items():
                        e.wait_ge(so[s], v)

            @block.sync
            def _(e):
                run(e, ops["sp"], fin)

            @block.tensor
            def _(e):
                run(e, ops["pe"])

            @block.scalar
            def _(e):
                run(e, ops["act"])

            @block.vector
            def _(e):
                run(e, ops["dve"])

            @block.gpsimd
            def _(e):
                run(e, ops["pool"])


import math
import contextlib
import ml_dtypes

D = 1024
NTOK = 1024
DEPTH = 4
EPS = 1e-6
DFF = 2816
NHC = 22
PI = math.pi
TWO_PI = 2.0 * math.pi
BF = ml_dtypes.bfloat16

C_ADAB = 0
C_NORM = 48
C_HSW = 80
C_HSB = 98
C_HBIAS = 104
C_GN = 108
C_QN = 110
C_KVN = 112
C_FCW = 113
C_FCB = 179
C_HB1 = 201
C_HB2 = 202
C_LG = 203
C_SINK = 215
NCOLP = 224

K_IOTA1 = 0
K_REV = 128
K_LAGP = 256
K_LAGN = 384
K_U = 512
K_LO = 640
K_REVC = 768
K_POSC = 832
NCST = 896


def kc_tile(W):
    K, N = W.shape
    return np.ascontiguousarray(W.reshape(K // 128, 128, N).transpose(1, 0, 2)).reshape(128, -1)


def col_tile(v):
    v = np.asarray(v)
    lead = v.shape[:-1]
    n = v.shape[-1] // 128
    a = v.reshape(lead + (n, 128))
    a = np.moveaxis(a, -1, 0)
    return np.ascontiguousarray(a).reshape(128, -1)


def host_consts():
    c = {}
    cst = np.zeros((128, NCST), np.float32)
    i = np.arange(128, dtype=np.float32)
    j = np.arange(128, dtype=np.float32)[:, None]
    cst[:, K_IOTA1:K_IOTA1 + 128] = (i + 1)[None, :]
    cst[:, K_REV:K_REV + 128] = (128 - i)[None, :]
    cst[:, K_LAGP:K_LAGP + 128] = np.maximum(i[None, :] - j, 0)
    cst[:, K_LAGN:K_LAGN + 128] = np.maximum(j - i[None, :], 0)
    cst[:, K_U:K_U + 128] = (i[None, :] >= j)
    cst[:, K_LO:K_LO + 128] = (j >= i[None, :])
    cst[:, K_REVC:K_REVC + 64] = (127 - j)
    cst[:, K_POSC:K_POSC + 64] = j
    c["cst"] = cst
    for nm, L in (("P", 256), ("S", 1024)):
        N = 2 * L
        jj = np.arange(L, dtype=np.float64)[:, None]
        ff = np.arange(L, dtype=np.float64)[None, :]
        ang = PI * (2 * ff + 1) * jj / N
        Cc = np.cos(ang)
        Sc = np.sin(ang)
        mats = [Cc, Sc, Cc.T * (2.0 / N), Sc.T * (2.0 / N)]
        nh = 2 if L > 512 else 1
        hwid = L // nh
        c["dft" + nm] = np.stack([np.stack([kc_tile(np.ascontiguousarray(m[:, hv * hwid:(hv + 1) * hwid]).astype(np.float32)) for hv in range(nh)]) for m in mats]).astype(BF)
        zg = np.zeros((17, 2, L), np.float32)
        tn = np.zeros((128, 2, L // 128), np.float32)
        for g in range(2):
            t = ((np.arange(L) - g) / L).astype(np.float32)
            bands = np.arange(1, 9, dtype=np.float32)
            a = (2.0 * PI) * t[:, None] * bands
            z = np.concatenate([t[:, None], np.cos(a), np.sin(a)], axis=-1).astype(np.float32)
            zg[:, g, :] = z.T
            tn[:, g, :] = -(t.reshape(L // 128, 128).T)
        c["zg" + nm] = zg
        c["tn" + nm] = tn
    Ls = 1024
    rows = (np.arange(Ls) // 64).astype(np.float32)
    cols = (np.arange(Ls) % 64).astype(np.float32)

    def tables(dim):
        q = dim // 4
        inv = (10000.0 ** (-np.arange(q, dtype=np.float32) / q)).astype(np.float32)
        ang = np.concatenate([rows[:, None] * inv, cols[:, None] * inv], axis=-1)
        return np.cos(ang).astype(np.float32), np.sin(ang).astype(np.float32)

    cg, sg = tables(64)
    rope = np.zeros((128, 4, Ls), np.float32)
    for p in range(128):
        d = p % 64
        r = d % 32
        rope[p, 0] = cg[:, r]
        rope[p, 1] = sg[:, r] * (-1.0 if d < 32 else 1.0)
    cm, sm = tables(32)
    rope[:, 2] = 1.0
    for p in range(64, 96):
        d = p - 64
        r = d % 16
        rope[p, 2] = cm[:, r]
        rope[p, 3] = sm[:, r] * (-1.0 if d < 16 else 1.0)
    c["rope"] = rope
    R = np.zeros((128, 2, 128), np.float32)
    for m in range(128):
        d = m % 64
        base = m - d
        if d < 32:
            R[base + d + 32, 0, m] = 1.0
        else:
            R[base + d - 32, 0, m] = 1.0
    for m in range(64, 96):
        d = m - 64
        if d < 16:
            R[64 + d + 16, 1, m] = 1.0
        else:
            R[64 + d - 16, 1, m] = 1.0
    c["rmat"] = R.astype(BF)
    return c


def host_weights(inp):
    w = {}
    L = DEPTH
    ada_w = inp["ada_w"]
    w["adaw"] = np.stack([np.stack([kc_tile(ada_w[l][:, b * 512:(b + 1) * 512]) for b in range(12)]) for l in range(L)])
    w_in = inp["w_in"]
    retA = np.r_[0:256, 256:512, 768:1024]
    retB = np.r_[256:768]
    hy = np.r_[1024:1792]
    gb = 1792
    mb = 2304
    qperm = np.concatenate([gb + h * 64 + np.arange(64) for h in (0, 2, 1, 3)])
    gq = np.concatenate([qperm, gb + np.r_[256:384], gb + np.r_[256:384], gb + np.r_[384:512], mb + np.r_[384:416]])
    ml = np.concatenate([mb + np.r_[0:384], mb + np.r_[320:416]])
    w["wretA"] = np.stack([kc_tile(w_in[l][:, retA]) for l in range(L)])
    w["wretB"] = np.stack([kc_tile(w_in[l][:, retB]) for l in range(L)])
    w["why"] = np.stack([kc_tile(w_in[l][:, hy]) for l in range(L)])
    w["wgqa"] = np.stack([kc_tile(w_in[l][:, gq]) for l in range(L)])
    w["wmla"] = np.stack([kc_tile(w_in[l][:, ml]) for l in range(L)])
    w["mlaw"] = np.stack([np.concatenate([kc_tile(inp["mla_w_uq"][l]), inp["mla_w_uk"][l], inp["mla_w_uv"][l]], axis=1) for l in range(L)])
    perm = np.concatenate([np.r_[0:512], 512 + np.concatenate([h * 64 + np.arange(64) for h in (0, 2, 1, 3)]), np.r_[768:1024]])
    w["wout"] = np.stack([kc_tile(inp["w_out"][l][perm, :]) for l in range(L)])
    up = inp["ffn_w_up"]
    w["wup"] = np.stack([np.stack([kc_tile(np.concatenate([up[l][:, 256 * b:256 * b + 256], up[l][:, DFF + 256 * b:DFF + 256 * b + 256]], axis=1)) for b in range(11)]) for l in range(L)])
    dn = inp["ffn_w_down"]
    w["wdn"] = np.stack([np.stack([kc_tile(dn[l][:, 128 * b:128 * b + 128]) for b in range(8)]) for l in range(L)])
    colp = np.zeros((L, 128, NCOLP), np.float32)
    rowp = np.zeros((L, 128, 512), np.float32)
    hyw = np.zeros((L, 64, 1152), np.float32)
    for l in range(L):
        colp[l, :, C_ADAB:C_ADAB + 48] = col_tile(inp["ada_b"][l])
        colp[l, :, C_NORM:C_NORM + 32] = col_tile(inp["norm_g"][l])
        colp[l, :, C_HSW:C_HSW + 18] = col_tile(inp["hy_short_w"][l])
        colp[l, :, C_HSB:C_HSB + 6] = col_tile(inp["hy_short_b"][l])
        colp[l, :, C_HBIAS:C_HBIAS + 4] = col_tile(inp["hy_bias"][l])
        colp[l, :, C_GN:C_GN + 2] = col_tile(inp["ret_gn_g"][l])
        colp[l, :, C_QN:C_QN + 2] = col_tile(inp["mla_q_norm"][l])
        colp[l, :, C_KVN:C_KVN + 1] = col_tile(inp["mla_kv_norm"][l])
        colp[l, :, C_FCW:C_FCW + 66] = col_tile(inp["ffn_conv_w"][l])
        colp[l, :, C_FCB:C_FCB + 22] = col_tile(inp["ffn_conv_b"][l])
        colp[l, 0:64, C_HB1] = inp["hy_b1"][l]
        colp[l, 0:64, C_HB2] = inp["hy_b2"][l]
        lg = inp["ret_decay_logit"][l]
        for d in range(2):
            for c in range(2):
                colp[l, 0:64, C_LG + d * 2 + c] = lg[d, 2 * c]
                colp[l, 64:128, C_LG + d * 2 + c] = lg[d, 2 * c + 1]
            for h in range(4):
                colp[l, :, C_LG + 4 + d * 4 + h] = lg[d, h]
        for h in range(4):
            colp[l, :, C_SINK + h] = inp["gqa_sink"][l, h]
        rowp[l, :, :] = inp["hy_decay"][l].reshape(1, 512)
        hyw[l, 0:17, 0:64] = inp["hy_w1"][l]
        hyw[l, :, 64:128] = inp["hy_w2"][l]
        hyw[l, :, 128:1152] = inp["hy_w3"][l]
    w["colp"] = colp
    w["rowp"] = rowp
    w["hyw"] = hyw
    return w


class Grp:
    pass


def build(dbg=False):
    nc = bass.Bass("TRN2", target_bir_lowering=False)

    def din(name, shape, dt=F32):
        return nc.dram_tensor(name, list(shape), dt, kind="ExternalInput").ap()

    def dout(name, shape, dt=F32):
        return nc.dram_tensor(name, list(shape), dt, kind="ExternalOutput").ap()

    I = {}
    I["xp"] = din("xp", [NTOK, D])
    I["xs"] = din("xs", [NTOK, D])
    I["sret"] = din("sret", [DEPTH, 2, 4, 64, 64])
    I["cgk"] = din("cgk", [DEPTH, 2, 512, 64])
    I["cgv"] = din("cgv", [DEPTH, 2, 512, 64])
    I["cckv"] = din("cckv", [DEPTH, 512, 128])
    I["ckr"] = din("ckr", [DEPTH, 512, 32])
    I["cond"] = din("cond", [2, 128, 8])
    I["adaw"] = din("adaw", [DEPTH, 12, 128, 4096])
    I["wretA"] = din("wretA", [DEPTH, 128, 8 * 768])
    I["wretB"] = din("wretB", [DEPTH, 128, 8 * 512])
    I["why"] = din("why", [DEPTH, 128, 8 * 768])
    I["wgqa"] = din("wgqa", [DEPTH, 128, 8 * 672])
    I["wmla"] = din("wmla", [DEPTH, 128, 8 * 480])
    I["mlaw"] = din("mlaw", [DEPTH, 128, 1280])
    I["wout"] = din("wout", [DEPTH, 128, 8192])
    I["wup"] = din("wup", [DEPTH, 11, 128, 4096])
    I["wdn"] = din("wdn", [DEPTH, 8, 128, 22 * 128])
    I["colp"] = din("colp", [DEPTH, 128, NCOLP])
    I["rowp"] = din("rowp", [DEPTH, 128, 512])
    I["hyw"] = din("hyw", [DEPTH, 64, 1152])
    I["cst"] = din("cst", [128, NCST])
    I["dftP"] = din("dftP", [4, 1, 128, 2 * 256], BF16)
    I["dftS"] = din("dftS", [4, 2, 128, 8 * 512], BF16)
    I["zgP"] = din("zgP", [17, 2, 256])
    I["zgS"] = din("zgS", [17, 2, 1024])
    I["tnP"] = din("tnP", [128, 2, 2])
    I["tnS"] = din("tnS", [128, 2, 8])
    I["rope"] = din("rope", [128, 4, 1024])
    I["rmat"] = din("rmat", [128, 2, 128], BF16)
    O = {}
    O["yp"] = dout("yp", [NTOK, D])
    O["ys"] = dout("ys", [NTOK, D])
    O["nsr"] = dout("nsr", [4, DEPTH, 2, 4, 64, 64])
    O["ngk"] = dout("ngk", [4, DEPTH, 2, 256, 64])
    O["ngv"] = dout("ngv", [4, DEPTH, 2, 256, 64])
    O["nckv"] = dout("nckv", [4, DEPTH, 256, 128])
    O["nkr"] = dout("nkr", [4, DEPTH, 256, 32])
    DBG = {}

    with contextlib.ExitStack() as ctx:
        P = Prog(nc, ctx)
        ident = P.sb([128, 128], F32, name="ident")
        ones_bf = P.sb([128, 128], BF16, name="ones")
        BO = P.sb([128, 128], F32, name="BO")
        cc = P.sb([128, 4], F32, name="cc")
        cst = P.sb([128, NCST], F32, name="cst")
        rmat = P.sb([128, 2, 128], BF16, name="rmat")
        NW = 2
        wring = [P.sb([128, 8192], BF16, nsub=2, name="wr%d" % i) for i in range(NW)]
        wstate = {"i": 0, "h": 0}
        colp = P.sb([128, NCOLP], F32, name="colp")
        modt = P.sb([128, 48], F32, name="modt")
        mods = P.sb([128, 6, 8], F32, name="mods")
        scond2 = P.sb([128, 8, 33], BF16, name="scond2")
        modrow = [P.sb([33, 512], F32, name="modrow0")] * 2
        condt2 = P.sb([128, 2, 8], F32, name="condt2")
        modS = P.sb([128, DEPTH, 48], F32, name="modS")
        mstate = {"ready": False}
        pss = [P.ps([128, 512], F32, name="psr%d" % i) for i in range(8)]
        acc = [pss[6], pss[7]]
        pstate = {"i": 0, "n": 8}
        evs = {"i": 0}

        def pst():
            t = pss[pstate["i"] % pstate["n"]]
            pstate["i"] += 1
            return t

        def wslot():
            t = wring[wstate["i"] % NW]
            wstate["i"] += 1
            return t

        def wload(src2d, n, q="pool"):
            t = wslot()
            P.dma(q, t[:, 0:n], src2d, writes=[t])
            return t

        def wload_h(src2d, n):
            k = wstate["h"] % (2 * NW)
            wstate["h"] += 1
            t, hf = wring[k // 2], k % 2
            P.dma("pool", t[:, hf * 4096:hf * 4096 + n], src2d, writes=[(t, hf)])
            return t, hf, hf * 4096

        def mm(out, lhsT, rhs, start, stop, rd, wr):
            P.op("pe", lambda e: e.matmul(out, lhsT=lhsT, rhs=rhs, start=start, stop=stop, skip_group_check=True), reads=rd, writes=wr)

        def tr(out, in_, k, rd, wr):
            P.op("pe", lambda e: e.transpose(out=out, in_=in_, identity=ident[0:k, 0:k]), reads=rd + [ident], writes=wr)

        def cp(eng, out, in_, rd, wr):
            if eng == "act":
                P.op("act", lambda e: e.activation(out=out, in_=in_, func=AF.Copy), reads=rd, writes=wr)
            else:
                P.op(eng, lambda e: e.tensor_copy(out=out, in_=in_), reads=rd, writes=wr)

        def evac(out, in_, rd, wr):
            evs["i"] += 1
            cp("act" if evs["i"] % 2 else "dve", out, in_, rd, wr)

        def act(out, in_, func, rd, wr, scale=None, bias=None):
            kw = {}
            if scale is not None:
                kw["scale"] = scale
            if bias is not None:
                kw["bias"] = bias
            P.op("act", lambda e: e.activation(out=out, in_=in_, func=func, **kw), reads=rd, writes=wr)

        def tt(eng, out, in0, in1, op, rd, wr):
            P.op(eng, lambda e: e.tensor_tensor(out=out, in0=in0, in1=in1, op=op), reads=rd, writes=wr)

        def ts(eng, out, in0, s1, s2, op0, op1, rd, wr):
            if op1 is None:
                P.op(eng, lambda e: e.tensor_scalar(out=out, in0=in0, scalar1=s1, scalar2=None, op0=op0), reads=rd, writes=wr)
            else:
                P.op(eng, lambda e: e.tensor_scalar(out=out, in0=in0, scalar1=s1, scalar2=s2, op0=op0, op1=op1), reads=rd, writes=wr)

        def stt(out, in0, scalar, in1, op0, op1, rd, wr):
            P.op("dve", lambda e: e.scalar_tensor_tensor(out=out, in0=in0, scalar=scalar, in1=in1, op0=op0, op1=op1), reads=rd, writes=wr)

        def recip(out, in_, rd, wr):
            P.op("dve", lambda e: e.reciprocal(out=out, in_=in_), reads=rd, writes=wr)

        def memset(eng, ap, val, wr):
            P.op(eng, lambda e: e.memset(ap, val), writes=wr)

        def dump(name, tile, ap, shape, dt=F32):
            if not dbg:
                return
            d = dout("dbg_" + name, shape, F32)
            DBG[name] = d
            P.dma("pool" if dt != F32 else "sp", d, ap, reads=[tile], is_output=True)

        P.dma("sp", cst[:], I["cst"], writes=[cst])
        P.dma("sp", rmat[:], I["rmat"], writes=[rmat])
        memset("pool", ident[:], 1.0, [ident])
        P.op("pool", lambda e: e.affine_select(out=ident[:], in_=ident[:], pattern=[[-1, 128]], compare_op=ALU.is_equal, fill=0.0, base=0, channel_multiplier=1), reads=[ident], writes=[ident])
        memset("dve", ones_bf[:], 1.0, [ones_bf])
        memset("dve", BO[:], 0.0, [BO])
        memset("dve", BO[0:64, 0:64], 1.0 / 64, [BO])
        memset("dve", BO[64:128, 64:128], 1.0 / 64, [BO])
        BOb = P.sb([128, 128], BF16, name="BOb")
        memset("dve", BOb[:], 0.0, [BOb])
        memset("dve", BOb[0:64, 0:64], 1.0 / 64, [BOb])
        memset("dve", BOb[64:128, 64:128], 1.0 / 64, [BOb])
        BD = P.sb([128, 128], F32, name="BD")
        memset("dve", BD[:], 0.0, [BD])
        memset("dve", BD[0:64, 0:64], 1.0, [BD])
        memset("dve", BD[64:128, 64:128], 1.0, [BD])
        memset("dve", cc[:, 0:1], EPS, [cc])
        memset("dve", cc[:, 1:2], 1.0, [cc])
        memset("dve", cc[:, 2:3], 0.0, [cc])
        epsc = cc[:, 0:1]

        def s2(t, c, tti):
            return (t, c * 2 + tti)

        def sc(t, c):
            return (t, [c * 2, c * 2 + 1])

        def rms_rstd(srcs, tti, dim, rstd, sqb):
            ps = pst()
            n = len(srcs)
            for i, (t, ap, sub) in enumerate(srcs):
                sq = sqb[i % 2]
                if i % 2 == 0:
                    act(sq[:], ap, AF.Square, [(t, sub)], [sq])
                else:
                    tt("pool", sq[:], ap, ap, ALU.mult, [(t, sub)], [sq])
                mm(ps[:, :], ones_bf[:, :], sq[:], i == 0, i == n - 1, [ones_bf, sq], [ps])
            act(rstd[:], ps[:, :], AF.Sqrt, [ps, cc], [rstd], scale=1.0 / dim, bias=epsc)
            recip(rstd[:], rstd[:], [rstd], [rstd])

        def run_group(G):
            L, NSEQ, nch = G.L, G.NSEQ, G.L // 128
            SAMPLE = G.sample
            P.push()
            x = P.sb([128, 8, NTOK], F32, nsub=16, name="x")
            ymix = P.sb([128, 8, NTOK], BF16, nsub=16, name="ymix")
            rstd2 = [P.sb([128, 512], F32, name="rstdb%d" % i) for i in range(2)]
            rstd = rstd2[0]
            sqbig = P.sb([128, 8, 512], BF16, nsub=2, name="sqbig")
            xr = P.sb([128, 4, 512], F32, nsub=4, name="xr")
            tmpf = [P.sb([128, 512], F32, name="tmpf%d" % i) for i in range(2)]
            tst = {"i": 0}

            def tmp():
                t = tmpf[tst["i"] % 2]
                tst["i"] += 1
                return t

            P.phase = "io"
            P.push()
            xt = [P.sb([128, D], F32, name="xt%d" % i) for i in range(2)]
            for blk in range(8):
                xb_ = xt[blk % 2]
                P.dma("sp", xb_[:], G.x_in[blk * 128:(blk + 1) * 128, :], writes=[xb_])
                for half in range(2):
                    ps = pst()
                    for c4 in range(4):
                        c = half * 4 + c4
                        tr(ps[:, c4 * 128:(c4 + 1) * 128], xb_[:, c * 128:(c + 1) * 128], 128, [xb_], [ps])
                    evac(x.t[:, half * 4:half * 4 + 4, blk * 128:(blk + 1) * 128],
                         ps[:, :].rearrange("p (a b) -> p a b", a=4), [ps],
                         [(x, [(half * 4 + c4) * 2 + blk // 4 for c4 in range(4)])])
            P.pop()
            first_group = not mstate["ready"]
            if first_group:
                P.dma("sp", condt2[:], I["cond"].rearrange("g p c -> p g c"), writes=[condt2])
                memset("dve", scond2[:], 0.0, [scond2])
                act(scond2.t[:, :, 0], condt2.t[:, G.gi, :], AF.Silu, [condt2], [scond2])
                act(scond2.t[:, :, 32], condt2.t[:, 1 - G.gi, :], AF.Silu, [condt2], [scond2])

            import os as _os
            for l in range(int(_os.environ.get('MK_DEPTH', DEPTH))):
                P.phase = "mod"
                P.dma("sp", colp[:], I["colp"][l], writes=[colp])
                if first_group:
                    psm, psm2 = pss[7], pss[6]
                    pstate["n"] = 6
                    for b in range(12):
                        w, whf, wo = wload_h(I["adaw"][l, b], 4096)
                        wv = w.t[:, wo:wo + 4096].rearrange("p (k n) -> p k n", k=8)
                        ps = pst()
                        for kc in range(8):
                            mm(ps[0:33, :], scond2.t[:, kc, :], wv[:, kc, :], kc == 0, kc == 7, [(w, whf), scond2], [ps])
                        mr = modrow[b % 2]
                        evac(mr[0:33, :], ps[0:33, :], [ps], [mr])
                        for j4 in range(4):
                            j = b * 4 + j4
                            mm(psm[:, j:j + 1], mr[0:1, j4 * 128:(j4 + 1) * 128], cc[0:1, 1:2], True, True, [mr, cc], [psm])
                            mm(psm2[:, j:j + 1], mr[32:33, j4 * 128:(j4 + 1) * 128], cc[32:33, 1:2], True, True, [mr, cc], [psm2])
                    pstate["n"] = 8
                    tt("dve", modt[:], psm[:, 0:48], colp[:, C_ADAB:C_ADAB + 48], ALU.add, [psm, colp], [modt])
                    tt("dve", modS.t[:, l, :], psm2[:, 0:48], colp[:, C_ADAB:C_ADAB + 48], ALU.add, [psm2, colp], [modS])
                else:
                    cp("dve", modt[:], modS.t[:, l, :], [modS], [modt])
                stt(mods.t[:, 0, :], modt[:, 8:16], 1.0, colp[:, C_NORM:C_NORM + 8], ALU.add, ALU.mult, [modt, colp], [mods])
                cp("dve", mods.t[:, 1, :], modt[:, 0:8], [modt], [mods])
                tt("dve", mods.t[:, 2, :], modt[:, 16:24], colp[:, C_NORM + 8:C_NORM + 16], ALU.mult, [modt, colp], [mods])
                stt(mods.t[:, 3, :], modt[:, 32:40], 1.0, colp[:, C_NORM + 16:C_NORM + 24], ALU.add, ALU.mult, [modt, colp], [mods])
                cp("dve", mods.t[:, 4, :], modt[:, 24:32], [modt], [mods])
                tt("dve", mods.t[:, 5, :], modt[:, 40:48], colp[:, C_NORM + 24:C_NORM + 32], ALU.mult, [modt, colp], [mods])

                def stats_sq(src, tti):
                    cs = slice(tti * 512, (tti + 1) * 512)
                    act(sqbig.t[:, 0:4, :], src.t[:, 0:4, cs], AF.Square, [(src, [c * 2 + tti for c in range(4)])], [(sqbig, 0)])
                    tt("dve", sqbig.t[:, 4:8, :], src.t[:, 4:8, cs], src.t[:, 4:8, cs], ALU.mult, [(src, [c * 2 + tti for c in range(4, 8)])], [(sqbig, 1)])

                def stats_mm(tti):
                    ps = pst()
                    for c in range(8):
                        mm(ps[:, :], ones_bf[:, :], sqbig.t[:, c, :], c == 0, c == 7, [ones_bf, (sqbig, c // 4)], [ps])
                    r_ = rstd2[tti]
                    act(r_[:], ps[:, :], AF.Sqrt, [ps, cc], [r_], scale=1.0 / D, bias=epsc)

                def norm_pairs(src, tti, fin):
                    cs = slice(tti * 512, (tti + 1) * 512)
                    for q in range(4):
                        x0 = (q % 2) * 2
                        tt("dve", xr.t[:, x0:x0 + 2, :], src.t[:, 2 * q:2 * q + 2, cs], rstd2[tti].t[:, None, :].to_broadcast([128, 2, 512]), ALU.mult,
                           [(src, [c * 2 + tti for c in (2 * q, 2 * q + 1)]), rstd2[tti]], [(xr, [x0, x0 + 1])])
                        for c2 in range(2):
                            fin(2 * q + c2, x0 + c2, cs, tti)

                def norm_seq(src, fin):
                    stats_sq(src, 0)
                    stats_mm(0)
                    stats_sq(src, 1)
                    recip(rstd2[0][:], rstd2[0][:], [rstd2[0]], [rstd2[0]])
                    stats_mm(1)
                    norm_pairs(src, 0, fin)
                    recip(rstd2[1][:], rstd2[1][:], [rstd2[1]], [rstd2[1]])
                    norm_pairs(src, 1, fin)

                def big_stats(src, tti):
                    stats_sq(src, tti)
                    stats_mm(tti)
                    recip(rstd2[tti][:], rstd2[tti][:], [rstd2[tti]], [rstd2[tti]])

                def norm_mod(src, ai, bi, dst):
                    def fin(c, xi, cs, tti):
                        if c % 2 == 0:
                            act(dst.t[:, c, cs], xr.t[:, xi, :], AF.Identity, [(xr, xi), mods], [s2(dst, c, tti)], scale=mods.t[:, ai, c:c + 1], bias=mods.t[:, bi, c:c + 1])
                        else:
                            ts("pool", dst.t[:, c, cs], xr.t[:, xi, :], mods.t[:, ai, c:c + 1], mods.t[:, bi, c:c + 1], ALU.mult, ALU.add, [(xr, xi), mods], [s2(dst, c, tti)])
                    norm_seq(src, fin)

                P.push()
                h = P.sb([128, 8, NTOK], BF16, nsub=16, name="h")
                P.phase = "norm1"
                norm_mod(x, 0, 1, h)

                def proj_fm(w, wv, col0, M, evf):
                    pend = []
                    for tti in range(2):
                        ps = pst()
                        for kc in range(8):
                            mm(ps[0:M, :], wv[:, kc, col0:col0 + M], h.t[:, kc, tti * 512:(tti + 1) * 512], kc == 0, kc == 7, [w, s2(h, kc, tti)], [ps])
                        pend.append((ps, tti))
                    for (ps, tti) in pend:
                        evf(ps, tti)

                def proj_tm(w, wv, col0, n, evf):
                    for blk in range(8):
                        ps = pst()
                        for kc in range(8):
                            mm(ps[:, 0:n], h.t[:, kc, blk * 128:(blk + 1) * 128], wv[:, kc, col0:col0 + n], kc == 0, kc == 7, [w, s2(h, kc, blk // 4)], [ps])
                        evf(ps, blk)

                def retention():
                    P.push()
                    qf = P.sb([128, 2, NTOK], BF16, name="qf")
                    qb = P.sb([128, 2, NTOK], BF16, name="qb")
                    sg = P.sb([128, 2, NTOK], BF16, name="sg")
                    ktm = P.sb([128, 8, 256], BF16, nsub=8, name="ktm")
                    vtm = P.sb([128, 8, 256], BF16, nsub=8, name="vtm")
                    vf = P.sb([128, 8, 256], BF16, nsub=8, name="vf")
                    vb = P.sb([128, 8, 256], BF16, nsub=8, name="vb")
                    lg = P.sb([128, 12], F32, name="lg")
                    patf = P.sb([128, 2, 128], F32, name="patf")
                    patb = P.sb([128, 2, 128], F32, name="patb")
                    cdc = P.sb([128, 4], F32, name="cdc")
                    DM = P.sb([128, 512], F32, name="DM")
                    kdp = P.sb([128, 2, 256], F32, name="kdp")
                    tfb = P.sb([128, 2, 128], F32, name="tfb")
                    S = P.sb([128, NSEQ * 2, 2, 128], F32, nsub=NSEQ * 2, name="S")
                    Sbf = P.sb([128, 8, 4, 128], BF16, nsub=16, name="Sbf")
                    SD = P.sb([128, 8, 512], BF16, nsub=8, name="SD")
                    P.push()
                    qTz = P.sb([128, 4, NTOK], BF16, name="qTz")
                    memset("pool", qTz[:], 0.0, [qTz])
                    kT = P.sb([128, 2, NTOK], BF16, name="kT")
                    act(lg[:], colp[:, C_LG:C_LG + 12], AF.Exp, [colp], [lg], scale=-1.0)
                    ts("dve", lg[:], lg[:], 1.0, None, ALU.add, None, [lg], [lg])
                    act(lg[:], lg[:], AF.Ln, [lg], [lg])
                    ts("dve", lg[:], lg[:], -1.0, None, ALU.mult, None, [lg], [lg])
                    for c in range(2):
                        act(patf.t[:, c, :], cst[:, K_IOTA1:K_IOTA1 + 128], AF.Exp, [cst, lg], [patf], scale=lg[:, c:c + 1])
                        act(patb.t[:, c, :], cst[:, K_REV:K_REV + 128], AF.Exp, [cst, lg], [patb], scale=lg[:, 2 + c:3 + c])
                    act(cdc[:], lg[:, 0:4], AF.Exp, [lg], [cdc], scale=128.0)
                    for hh in range(4):
                        act(tfb.t[:, 0, :], cst[:, K_LAGP:K_LAGP + 128], AF.Exp, [cst, lg], [tfb], scale=lg[:, 4 + hh:5 + hh])
                        act(tfb.t[:, 1, :], cst[:, K_LAGN:K_LAGN + 128], AF.Exp, [cst, lg], [tfb], scale=lg[:, 8 + hh:9 + hh])
                        stt(tfb.t[:, 0, :], tfb.t[:, 0, :], 0.125, cst[:, K_U:K_U + 128], ALU.mult, ALU.mult, [tfb, cst], [tfb])
                        stt(tfb.t[:, 1, :], tfb.t[:, 1, :], 0.125, cst[:, K_LO:K_LO + 128], ALU.mult, ALU.mult, [tfb, cst], [tfb])
                        tt("dve", DM[:, hh * 128:(hh + 1) * 128], tfb.t[:, 0, :], tfb.t[:, 1, :], ALU.add, [tfb], [DM])
                        act(kdp.t[:, 0, hh * 64:(hh + 1) * 64], cst[:, K_REVC:K_REVC + 64], AF.Exp, [cst, lg], [kdp], scale=lg[:, 4 + hh:5 + hh])
                        act(kdp.t[:, 1, hh * 64:(hh + 1) * 64], cst[:, K_POSC:K_POSC + 64], AF.Exp, [cst, lg], [kdp], scale=lg[:, 8 + hh:9 + hh])
                    ts("dve", kdp[:], kdp[:], 0.125, None, ALU.mult, None, [kdp], [kdp])
                    RS = 99
                    if RS <= 1:
                        P.pop()
                        return
                    w = wload(I["wretA"][l], 6144)
                    wv = w.t[:, 0:6144].rearrange("p (k n) -> p k n", k=8)
                    for c in range(2):
                        def ev_q(ps, tti, c=c):
                            cs = slice(tti * 512, (tti + 1) * 512)
                            for hf in range(2):
                                cp("act", qTz.t[hf * 64:(hf + 1) * 64, 2 * c + hf, cs], ps[hf * 64:(hf + 1) * 64, :], [ps], [qTz])
                            p3 = ps[:, :].rearrange("p (a b) -> p a b", a=4)
                            tt("dve", qf.t[:, c, cs].rearrange("p (a b) -> p a b", a=4), p3, patf.t[:, c:c + 1, :].to_broadcast([128, 4, 128]), ALU.mult, [ps, patf], [qf])
                            tt("dve", qb.t[:, c, cs].rearrange("p (a b) -> p a b", a=4), p3, patb.t[:, c:c + 1, :].to_broadcast([128, 4, 128]), ALU.mult, [ps, patb], [qb])
                        SUB = _os.environ.get("MK_RET_SUB", "qkg")
                        if "q" in SUB:
                            proj_fm(w, wv, c * 128, 128, ev_q)

                        def ev_k(ps, tti, c=c):
                            evac(kT.t[:, c, tti * 512:(tti + 1) * 512], ps[:, :], [ps], [kT])
                        if "k" in SUB:
                            proj_fm(w, wv, 256 + c * 128, 128, ev_k)

                        def ev_g(ps, tti, c=c):
                            act(sg.t[:, c, tti * 512:(tti + 1) * 512], ps[:, :], AF.Silu, [ps], [sg])
                        if "g" in SUB:
                            proj_fm(w, wv, 512 + c * 128, 128, ev_g)
                    if RS <= 2:
                        P.pop()
                        return
                    w2 = wload(I["wretB"][l], 4096)
                    wv2 = w2.t[:, 0:4096].rearrange("p (k n) -> p k n", k=8)

                    def ev_tm(ps, blk):
                        cp("act", ktm.t[:, blk, :], ps[:, 0:256], [ps], [(ktm, blk)])
                        cp("act", vtm.t[:, blk, :], ps[:, 256:512], [ps], [(vtm, blk)])
                        tt("dve", vf.t[:, blk, :], ps[:, 256:512], kdp.t[:, 0, :], ALU.mult, [ps, kdp], [(vf, blk)])
                        tt("dve", vb.t[:, blk, :], ps[:, 256:512], kdp.t[:, 1, :], ALU.mult, [ps, kdp], [(vb, blk)])
                    proj_tm(w2, wv2, 0, 512, ev_tm)
                    if RS <= 3:
                        P.pop()
                        return
                    for blk in range(8):
                        bc = slice(blk * 128, (blk + 1) * 128)
                        ps = pst()
                        for hh in range(4):
                            c, po = hh // 2, (hh % 2) * 64
                            mm(ps[:, hh * 128:(hh + 1) * 128], kT.t[:, c, bc], qTz.t[:, hh, bc], True, True, [kT, qTz], [ps])
                        tt("dve", SD.t[:, blk, :], ps[:, :], DM[:], ALU.mult, [ps, DM], [(SD, blk)])
                    if RS <= 4:
                        P.pop()
                        return
                    P.pop()
                    osb = P.sb([128, 2, NTOK], F32, nsub=4, name="osb")
                    memset("pool", S[:], 0.0, [S])
                    if SAMPLE:
                        for dr in range(2):
                            for c in range(2):
                                for hf in range(2):
                                    hh = 2 * c + hf
                                    P.dma("sp", S.t[hf * 64:(hf + 1) * 64, dr, c, hf * 64:(hf + 1) * 64], I["sret"][l, dr, hh], writes=[(S, dr)])
                    for step in range(nch):
                        for s in range(NSEQ):
                            for dr in range(2):
                                n = step if dr == 0 else nch - 1 - step
                                ch = s * 2 + dr
                                blk = s * nch + n
                                tt("pool", Sbf.t[:, blk, dr * 2:dr * 2 + 2, :], S.t[:, ch, :, :], BD.t[:, None, :].to_broadcast([128, 2, 128]), ALU.mult, [(S, ch), BD], [(Sbf, blk * 2 + dr)])
                                ps = pst()
                                vv = vf if dr == 0 else vb
                                for c in range(2):
                                    mm(ps[:, c * 128:(c + 1) * 128], ktm.t[:, blk, c * 128:(c + 1) * 128], vv.t[:, blk, c * 128:(c + 1) * 128], True, True, [(ktm, blk), (vv, blk)], [ps])
                                for c in range(2):
                                    stt(S.t[:, ch, c, :], S.t[:, ch, c, :], cdc[:, dr * 2 + c:dr * 2 + c + 1], ps[:, c * 128:(c + 1) * 128], ALU.mult, ALU.add, [(S, ch), cdc, ps], [(S, ch)])
                    if not SAMPLE:
                        for s in range(NSEQ):
                            for dr in range(2):
                                for c in range(2):
                                    for hf in range(2):
                                        hh = 2 * c + hf
                                        P.dma("sp", O["nsr"][s, l, dr, hh], S.t[hf * 64:(hf + 1) * 64, s * 2 + dr, c, hf * 64:(hf + 1) * 64], reads=[(S, s * 2 + dr)], is_output=True)
                    if RS <= 5:
                        P.pop()
                        return
                    for blk in range(8):
                        bc = slice(blk * 128, (blk + 1) * 128)
                        ps = pst()
                        for c in range(2):
                            for hf in range(2):
                                hh = 2 * c + hf
                                po = hf * 64
                                o_ = ps[:, (c * 2 + hf) * 128:(c * 2 + hf + 1) * 128]
                                mm(o_, vtm.t[:, blk, c * 128:(c + 1) * 128], SD.t[:, blk, hh * 128:(hh + 1) * 128], True, False, [(vtm, blk), (SD, blk)], [ps])
                                mm(o_, Sbf.t[:, blk, c, :], qf.t[:, c, bc], False, False, [(Sbf, blk * 2), qf], [ps])
                                mm(o_, Sbf.t[:, blk, 2 + c, :], qb.t[:, c, bc], False, True, [(Sbf, blk * 2 + 1), qb], [ps])
                        for c in range(2):
                            for hf in range(2):
                                po = hf * 64
                                evac(osb.t[po:po + 64, c, bc], ps[po:po + 64, (c * 2 + hf) * 128:(c * 2 + hf + 1) * 128], [ps], [(osb, c * 2 + blk // 4)])
                    if RS <= 6:
                        P.pop()
                        return
                    ch4 = [(c, tti) for c in range(2) for tti in range(2)]
                    cens = [(tmpf[0], tmpf[0][:, :], tmpf[0]), (tmpf[1], tmpf[1][:, :], tmpf[1]),
                            (xr, xr.t[:, 0, :], (xr, 0)), (xr, xr.t[:, 1, :], (xr, 1))]
                    rss = [(rstd2[0], rstd2[0][:, :], rstd2[0]), (rstd2[1], rstd2[1][:, :], rstd2[1]),
                           (xr, xr.t[:, 2, :], (xr, 2)), (xr, xr.t[:, 3, :], (xr, 3))]
                    pm, pv = {}, {}
                    for k, (c, tti) in enumerate(ch4):
                        cs = slice(tti * 512, (tti + 1) * 512)
                        cp("act", sqbig.t[:, k, :], osb.t[:, c, cs], [(osb, c * 2 + tti)], [(sqbig, 0)])
                    for k, (c, tti) in enumerate(ch4):
                        pm[k] = pst()
                        mm(pm[k][:, :], BOb[:, :], sqbig.t[:, k, :], True, True, [BOb, (sqbig, 0)], [pm[k]])
                    for k, (c, tti) in enumerate(ch4):
                        cs = slice(tti * 512, (tti + 1) * 512)
                        tt("dve", cens[k][1], osb.t[:, c, cs], pm[k][:, :], ALU.subtract, [(osb, c * 2 + tti), pm[k]], [cens[k][2]])
                    for k in range(4):
                        act(sqbig.t[:, 4 + k, :], cens[k][1], AF.Square, [cens[k][2]], [(sqbig, 1)])
                    for k in range(4):
                        pv[k] = pst()
                        mm(pv[k][:, :], BOb[:, :], sqbig.t[:, 4 + k, :], True, True, [BOb, (sqbig, 1)], [pv[k]])
                    for k in range(4):
                        act(rss[k][1], pv[k][:, :], AF.Sqrt, [pv[k], cc], [rss[k][2]], scale=1.0, bias=epsc)
                    for k in range(4):
                        recip(rss[k][1], rss[k][1], [rss[k][2]], [rss[k][2]])
                    for k, (c, tti) in enumerate(ch4):
                        cs = slice(tti * 512, (tti + 1) * 512)
                        tt("dve", cens[k][1], cens[k][1], rss[k][1], ALU.mult, [cens[k][2], rss[k][2]], [cens[k][2]])
                        stt(ymix.t[:, c, cs], cens[k][1], colp[:, C_GN + c:C_GN + c + 1], sg.t[:, c, cs], ALU.mult, ALU.mult, [cens[k][2], colp, sg], [s2(ymix, c, tti)])
                    P.pop()

                def gqa():
                    P.push()
                    NCTX = 512 if SAMPLE else 0
                    NBK = 8 + (4 if SAMPLE else 0)
                    qT = P.sb([128, 2, NTOK], BF16, name="gqT")
                    kT = P.sb([128, 2, NCTX + NTOK], BF16, name="gkT")
                    memset("pool", kT[:], 0.0, [kT])
                    vaug = P.sb([128, NBK, 2, 192], BF16, name="gva")
                    tmo = P.sb([128, 8, 288], F32, name="tmo")
                    PT = [P.sb([128, 512], BF16, name="PT%d" % i) for i in range(3)]
                    pti = {"i": 0}
                    den = tmpf[0]

                    def nPT():
                        t = PT[pti["i"] % 3]
                        pti["i"] += 1
                        return t
                    memset("pool", vaug[:], 1.0, [vaug])
                    w = wload(I["wgqa"][l], 8 * 672)
                    wv = w.t[:, 0:8 * 672].rearrange("p (k n) -> p k n", k=8)
                    if SAMPLE:
                        ropt = P.sb([128, 2, NTOK], F32, name="ropt")
                        P.dma("sp", ropt[:], I["rope"][:, 0:2, :], writes=[ropt])
                        xfs = [P.sb([128, 512], F32, name="xf%d" % i) for i in range(2)]
                        xbs = [P.sb([128, 512], BF16, name="xb%d" % i) for i in range(2)]
                        t1s = [P.sb([128, 512], F32, name="t1%d" % i) for i in range(2)]
                        t2s = [P.sb([128, 512], F32, name="t2%d" % i) for i in range(2)]
                        rst = {"i": 0}

                        def ev_rope(dst_fn):
                            def f(ps, tti):
                                k_ = rst["i"] % 2
                                rst["i"] += 1
                                xf, xb, t1, t2 = xfs[k_], xbs[k_], t1s[k_], t2s[k_]
                                cs = slice(tti * 512, (tti + 1) * 512)
                                cp("act", xf[:], ps[:, :], [ps], [xf])
                                cp("dve", xb[:], ps[:, :], [ps], [xb])
                                ps2 = pst()
                                mm(ps2[:, :], rmat.t[:, 0, :], xb[:], True, True, [rmat, xb], [ps2])
                                tt("pool", t1[:], xf[:], ropt.t[:, 0, cs], ALU.mult, [xf, ropt], [t1])
                                tt("dve", t2[:], ps2[:, :], ropt.t[:, 1, cs], ALU.mult, [ps2, ropt], [t2])
                                for (dst, ap, r0, r1) in dst_fn(cs):
                                    tt("pool", ap, t1[r0:r1, :], t2[r0:r1, :], ALU.add, [t1, t2], [dst])
                            return f
                        for c in range(2):
                            proj_fm(w, wv, c * 128, 128, ev_rope(lambda cs, c=c: [(qT, qT.t[:, c, cs], 0, 128)]))
                        proj_fm(w, wv, 256, 128, ev_rope(lambda cs: [(kT, kT.t[kv_ * 64:(kv_ + 1) * 64, kv_, NCTX + cs.start:NCTX + cs.stop], kv_ * 64, (kv_ + 1) * 64) for kv_ in range(2)]))
                    else:
                        for c in range(2):
                            proj_fm(w, wv, c * 128, 128, lambda ps, tti, c=c: evac(qT.t[:, c, tti * 512:(tti + 1) * 512], ps[:, :], [ps], [qT]))
                        def ev_gk(ps, tti):
                            for kv_ in range(2):
                                evac(kT.t[kv_ * 64:(kv_ + 1) * 64, kv_, tti * 512:(tti + 1) * 512], ps[kv_ * 64:(kv_ + 1) * 64, :], [ps], [kT])
                        proj_fm(w, wv, 256, 128, ev_gk)
                    nb0 = 4 if SAMPLE else 0

                    def ev_tm(ps, blk):
                        if not SAMPLE:
                            cp("act", tmo.t[:, blk, :], ps[:, 0:288], [ps], [tmo])
                        cp("dve", vaug.t[:, nb0 + blk, :, 64:128], ps[:, 128:256].rearrange("p (k d) -> p k d", k=2), [ps], [vaug])
                    proj_tm(w, wv, 384, 288, ev_tm)
                    if not SAMPLE:
                        for s in range(NSEQ):
                            for k_ in range(2):
                                P.dma("sp", O["ngk"][s, l, k_].rearrange("(b p) d -> p b d", p=128),
                                      tmo.t[:, 2 * s:2 * s + 2, k_ * 64:(k_ + 1) * 64], reads=[tmo], is_output=True)
                                P.dma("sp", O["ngv"][s, l, k_].rearrange("(b p) d -> p b d", p=128),
                                      tmo.t[:, 2 * s:2 * s + 2, 128 + k_ * 64:128 + (k_ + 1) * 64], reads=[tmo], is_output=True)
                            P.dma("sp", O["nkr"][s, l].rearrange("(b p) d -> p b d", p=128),
                                  tmo.t[:, 2 * s:2 * s + 2, 256:288], reads=[tmo], is_output=True)
                    if SAMPLE:
                        ctm = P.sb([128, 4, 2, 64], F32, name="ctm")
                        cvm = P.sb([128, 4, 2, 64], F32, name="cvm")
                        for k_ in range(2):
                            P.dma("sp", ctm.t[:, :, k_, :], I["cgk"][l, k_].rearrange("(c p) d -> p c d", p=128), writes=[ctm])
                            P.dma("sp", cvm.t[:, :, k_, :], I["cgv"][l, k_].rearrange("(c p) d -> p c d", p=128), writes=[cvm])
                        ps = pst()
                        for cb in range(4):
                            tr(ps[:, cb * 128:(cb + 1) * 128], ctm.t[:, cb, :, :].rearrange("p k d -> p (k d)"), 128, [ctm], [ps])
                        for kv_ in range(2):
                            evac(kT.t[kv_ * 64:(kv_ + 1) * 64, kv_, 0:512], ps[kv_ * 64:(kv_ + 1) * 64, :], [ps], [kT])
                        cp("pool", vaug.t[:, 0:4, :, 64:128], cvm[:], [cvm], [vaug])

                    dst_ = {"i": 0}

                    def normalize(pso, po, ncols, hh, cq, c0):
                        den = tmpf[dst_["i"] % 2]
                        dst_["i"] += 1
                        nr = slice(po, po + 64)
                        dr_ = slice(64 - po, 128 - po)
                        ts("dve", den[nr, 0:ncols], pso[dr_, 0:ncols], colp[nr, C_SINK + hh:C_SINK + hh + 1], None, ALU.add, None, [pso, colp], [den])
                        recip(den[nr, 0:ncols], den[nr, 0:ncols], [den], [den])
                        tt("dve", ymix.t[nr, 4 + cq, c0:c0 + ncols], pso[nr, 0:ncols], den[nr, 0:ncols], ALU.mult, [pso, den], [sc(ymix, 4 + cq)])

                    act(colp[:, C_SINK:C_SINK + 4], colp[:, C_SINK:C_SINK + 4], AF.Exp, [colp], [colp])
                    if not SAMPLE:
                        def gp_fin(ps, s, hh):
                            cq, kv = hh % 2, hh // 2
                            po = kv * 64
                            vs = slice(64, 192) if po == 0 else slice(0, 128)
                            pt = nPT()
                            act(pt[:], ps[:, :], AF.Exp, [ps], [pt], scale=0.125)
                            pso = pst()
                            for kb in range(2):
                                mm(pso[:, 0:256], vaug.t[:, s * 2 + kb, kv, vs], pt[:, kb * 256:(kb + 1) * 256], kb == 0, kb == 1, [vaug, pt], [pso])
                            normalize(pso, po, 256, hh, cq, s * 256)
                        prevc = None
                        for s in range(NSEQ):
                            for hh in range(4):
                                cq, kv = hh % 2, hh // 2
                                ps = pst()
                                for kb in range(2):
                                    kc_ = slice(s * 256 + kb * 128, s * 256 + (kb + 1) * 128)
                                    mm(ps[:, kb * 256:(kb + 1) * 256], kT.t[:, kv, kc_], qT.t[:, cq, s * 256:(s + 1) * 256], True, True, [kT, qT], [ps])
                                if prevc is not None:
                                    gp_fin(*prevc)
                                prevc = (ps, s, hh)
                        gp_fin(*prevc)
                    else:
                        for hh in range(4):
                            cq, kv = hh % 2, hh // 2
                            po = kv * 64
                            vs = slice(64, 192) if po == 0 else slice(0, 128)
                            acc = [pss[4 + 2 * (hh % 2)], pss[5 + 2 * (hh % 2)]]
                            steps = [("c", cb, tti) for cb in range(4) for tti in range(2)] + [("b", m, 0) for m in range(8)]

                            def g_score(st):
                                ps = pst()
                                if st[0] == "c":
                                    cb, tti = st[1], st[2]
                                    mm(ps[:, :], kT.t[:, kv, cb * 128:(cb + 1) * 128], qT.t[:, cq, tti * 512:(tti + 1) * 512], True, True, [kT, qT], [ps])
                                else:
                                    m = st[1]
                                    qlo, qhi = max(m - 1, 0), min(m + 1, 7)
                                    n = (qhi - qlo + 1) * 128
                                    mm(ps[:, 0:n], kT.t[:, kv, 512 + m * 128:512 + (m + 1) * 128], qT.t[:, cq, qlo * 128:qlo * 128 + n], True, True, [kT, qT], [ps])
                                return ps

                            def g_finish(st, ps):
                                pt = nPT()
                                if st[0] == "c":
                                    cb, tti = st[1], st[2]
                                    act(pt[:], ps[:, :], AF.Exp, [ps], [pt], scale=0.125)
                                    mm(acc[tti][:, :], vaug.t[:, cb, kv, vs], pt[:], cb == 0, False, [vaug, pt], [acc[tti]])
                                    return
                                m = st[1]
                                qlo, qhi = max(m - 1, 0), min(m + 1, 7)
                                n = (qhi - qlo + 1) * 128
                                act(pt[:, 0:n], ps[:, 0:n], AF.Exp, [ps], [pt], scale=0.125)
                                if m - 1 >= 0:
                                    o0 = (m - 1 - qlo) * 128
                                    P.op("pool", lambda e, pt=pt, o0=o0: e.affine_select(out=pt[:, o0:o0 + 128], in_=pt[:, o0:o0 + 128], pattern=[[1, 128]], compare_op=ALU.is_ge, fill=0.0, base=0, channel_multiplier=-1), reads=[pt], writes=[pt])
                                if m + 1 <= 7:
                                    o0 = (m + 1 - qlo) * 128
                                    P.op("pool", lambda e, pt=pt, o0=o0: e.affine_select(out=pt[:, o0:o0 + 128], in_=pt[:, o0:o0 + 128], pattern=[[-1, 128]], compare_op=ALU.is_ge, fill=0.0, base=0, channel_multiplier=1), reads=[pt], writes=[pt])
                                for nq in range(qlo, qhi + 1):
                                    a = acc[nq // 4]
                                    mm(a[:, (nq % 4) * 128:(nq % 4 + 1) * 128], vaug.t[:, 4 + m, kv, vs], pt[:, (nq - qlo) * 128:(nq - qlo + 1) * 128], False, True, [vaug, pt], [a])

                            prev = None
                            for st in steps:
                                cur = (st, g_score(st))
                                if prev is not None:
                                    g_finish(*prev)
                                prev = cur
                            g_finish(*prev)
                            for tti in range(2):
                                normalize(acc[tti], po, 512, hh, cq, tti * 512)
                    P.pop()

                def mla():
                    P.push()
                    NCTX = 512 if SAMPLE else 0
                    NK = NCTX + NTOK
                    NBK = NK // 128
                    qn = P.sb([128, 2, NTOK], BF16, nsub=4, name="qn")
                    ckvT = P.sb([128, NTOK], F32, nsub=2, name="ckvT")
                    ckb = P.sb([128, NK], BF16, name="ckb")
                    krT = P.sb([128, NK], BF16, name="krT")
                    mw = P.sb([128, 1280], BF16, name="mw")
                    pti = {"i": 0}

                    def nPT():
                        t = PT[pti["i"] % 3]
                        pti["i"] += 1
                        return t
                    P.dma("pool", mw[:], I["mlaw"][l], writes=[mw])
                    uq = mw.t[:, 0:768].rearrange("p (k n) -> p k n", k=2)
                    w = wload(I["wmla"][l], 8 * 480)
                    wv = w.t[:, 0:8 * 480].rearrange("p (k n) -> p k n", k=8)
                    if SAMPLE:
                        ropm = P.sb([128, 2, NTOK], F32, name="ropm")
                        P.dma("sp", ropm[:], I["rope"][:, 2:4, :], writes=[ropm])
                        mxbs = [P.sb([128, 512], BF16, name="mxb%d" % i) for i in range(2)]
                        mt1s = [P.sb([128, 512], F32, name="mt10"), tmpf[1]]
                        mt2s = [P.sb([128, 512], F32, name="mt2%d" % i) for i in range(2)]
                        mrst = {"i": 0}

                        def rope96(ps, cs, lo, dst, dap):
                            k_ = mrst["i"] % 2
                            mrst["i"] += 1
                            xb, t1, t2 = mxbs[k_], mt1s[k_], mt2s[k_]
                            cp("act", xb[0:96, :], ps[0:96, :], [ps], [xb])
                            ps2 = pst()
                            mm(ps2[0:96, :], rmat.t[0:96, 1, 0:96], xb[0:96, :], True, True, [rmat, xb], [ps2])
                            tt("dve", t1[lo:96, :], ps[lo:96, :], ropm.t[lo:96, 0, cs], ALU.mult, [ps, ropm], [t1])
                            tt("dve", t2[lo:96, :], ps2[lo:96, :], ropm.t[lo:96, 1, cs], ALU.mult, [ps2, ropm], [t2])
                            tt("pool", dap, t1[lo:96, :], t2[lo:96, :], ALU.add, [t1, t2], [dst])
                    P.push()
                    ql = P.sb([128, 2, NTOK], F32, nsub=4, name="ql")
                    kvl = P.sb([128, NTOK], F32, nsub=2, name="kvl")
                    sqb = [P.sb([128, 512], BF16, name="sqb%d" % i) for i in range(2)]
                    for c in range(2):
                        proj_fm(w, wv, c * 128, 128, lambda ps, tti, c=c: evac(ql.t[:, c, tti * 512:(tti + 1) * 512], ps[:, :], [ps], [(ql, c * 2 + tti)]))
                    proj_fm(w, wv, 256, 128, lambda ps, tti: evac(kvl[:, tti * 512:(tti + 1) * 512], ps[:, :], [ps], [(kvl, tti)]))

                    def ev_kr(ps, tti):
                        cs = slice(tti * 512, (tti + 1) * 512)
                        if SAMPLE:
                            rope96(ps, cs, 64, krT, krT[64:96, NCTX + cs.start:NCTX + cs.stop])
                        else:
                            evac(krT[64:96, cs], ps[64:96, :], [ps], [krT])
                    proj_fm(w, wv, 384, 96, ev_kr)
                    for tti in range(2):
                        cs = slice(tti * 512, (tti + 1) * 512)
                        rms_rstd([(ql, ql.t[:, c, cs], c * 2 + tti) for c in range(2)], tti, 256, rstd, sqb)
                        for c in range(2):
                            stt(qn.t[:, c, cs], ql.t[:, c, cs], colp[:, C_QN + c:C_QN + c + 1], rstd[:], ALU.mult, ALU.mult, [(ql, c * 2 + tti), colp, rstd], [(qn, c * 2 + tti)])
                        rms_rstd([(kvl, kvl[:, cs], tti)], tti, 128, rstd, sqb)
                        stt(ckvT[:, cs], kvl[:, cs], colp[:, C_KVN:C_KVN + 1], rstd[:], ALU.mult, ALU.mult, [(kvl, tti), colp, rstd], [(ckvT, tti)])
                        cp("pool", ckb[:, NCTX + cs.start:NCTX + cs.stop], ckvT[:, cs], [(ckvT, tti)], [ckb])
                    if not SAMPLE:
                        otm = P.sb([128, 8, 128], F32, name="otm")
                        for half in range(2):
                            ps = pst()
                            for b4 in range(4):
                                blk = half * 4 + b4
                                tr(ps[:, b4 * 128:(b4 + 1) * 128], ckvT[:, blk * 128:(blk + 1) * 128], 128, [(ckvT, half)], [ps])
                            evac(otm.t[:, half * 4:half * 4 + 4, :], ps[:, :].rearrange("p (a b) -> p a b", a=4), [ps], [otm])
                        for s in range(NSEQ):
                            P.dma("sp", O["nckv"][s, l].rearrange("(b p) d -> p b d", p=128), otm.t[:, 2 * s:2 * s + 2, :], reads=[otm], is_output=True)
                    else:
                        ctm = P.sb([128, 4, 128], F32, name="mctm")
                        krm = P.sb([128, 4, 96], F32, name="krm")
                        P.dma("sp", ctm[:], I["cckv"][l].rearrange("(c p) d -> p c d", p=128), writes=[ctm])
                        memset("pool", krm[:], 0.0, [krm])
                        P.dma("sp", krm.t[:, :, 64:96], I["ckr"][l].rearrange("(c p) d -> p c d", p=128), writes=[krm])
                        ps = pst()
                        for cb in range(4):
                            tr(ps[:, cb * 128:(cb + 1) * 128], ctm.t[:, cb, :], 128, [ctm], [ps])
                        evac(ckb[:, 0:512], ps[:, :], [ps], [ckb])
                        ps = pst()
                        for cb in range(4):
                            tr(ps[0:96, cb * 128:(cb + 1) * 128], krm.t[:, cb, :], 128, [krm], [ps])
                        evac(krT[64:96, 0:512], ps[64:96, :], [ps], [krT])
                    P.pop()
                    qh = P.sb([128, 4, NTOK], BF16, name="qh")
                    kTh = P.sb([128, 4, NK], BF16, name="kTh")
                    vaug = P.sb([128, NBK, 4, 192], BF16, name="mva")
                    PT = [P.sb([128, 512], BF16, name="mPT%d" % i) for i in range(3)]
                    den = tmpf[0]
                    memset("pool", vaug[:], 1.0, [vaug])
                    def q_fin(ps, hh, cs):
                        if SAMPLE:
                            rope96(ps, cs, 0, qh, qh.t[0:96, hh, cs])
                        else:
                            evac(qh.t[0:96, hh, cs], ps[0:96, :], [ps], [qh])
                    prevq = None
                    for hh in range(4):
                        for tti in range(2):
                            cs = slice(tti * 512, (tti + 1) * 512)
                            ps = pst()
                            for kc in range(2):
                                mm(ps[0:96, :], uq[:, kc, hh * 96:(hh + 1) * 96], qn.t[:, kc, cs], kc == 0, kc == 1, [mw, (qn, kc * 2 + tti)], [ps])
                            if prevq is not None:
                                q_fin(*prevq)
                            prevq = (ps, hh, cs)
                    q_fin(*prevq)
                    for hh in range(4):
                        for kt in range(NK // 512):
                            ks = slice(kt * 512, (kt + 1) * 512)
                            ps = pst()
                            mm(ps[0:64, :], mw[:, 768 + hh * 64:768 + (hh + 1) * 64], ckb[:, ks], True, True, [mw, ckb], [ps])
                            evac(kTh.t[0:64, hh, ks], ps[0:64, :], [ps], [kTh])
                        cp("pool", kTh.t[64:96, hh, :], krT[64:96, :], [krT], [kTh])
                    for b in range(NBK):
                        ps = pst()
                        mm(ps[:, 0:256], ckb[:, b * 128:(b + 1) * 128], mw[:, 1024:1280], True, True, [ckb, mw], [ps])
                        evac(vaug.t[:, b, :, 64:128], ps[:, 0:256].rearrange("p (k d) -> p k d", k=4), [ps], [vaug])
                    SCL = float(96 ** -0.5)

                    dst_ = {"i": 0}

                    def normalize(pso, po, ncols, cm, c0):
                        den = tmpf[0] if SAMPLE else tmpf[dst_["i"] % 2]
                        dst_["i"] += 1
                        nr = slice(po, po + 64)
                        dr_ = slice(64 - po, 128 - po)
                        recip(den[nr, 0:ncols], pso[dr_, 0:ncols], [pso], [den])
                        tt("dve", ymix.t[nr, 6 + cm, c0:c0 + ncols], pso[nr, 0:ncols], den[nr, 0:ncols], ALU.mult, [pso, den], [sc(ymix, 6 + cm)])
                    if not SAMPLE:
                        def mp_fin(ps, s, hh):
                            cm, po = hh // 2, (hh % 2) * 64
                            vs = slice(64, 192) if po == 0 else slice(0, 128)
                            pt = nPT()
                            act(pt[:], ps[:, :], AF.Exp, [ps], [pt], scale=SCL)
                            pso = pst()
                            for kb in range(2):
                                mm(pso[:, 0:256], vaug.t[:, s * 2 + kb, hh, vs], pt[:, kb * 256:(kb + 1) * 256], kb == 0, kb == 1, [vaug, pt], [pso])
                            normalize(pso, po, 256, cm, s * 256)
                        prevc = None
                        for s in range(NSEQ):
                            for hh in range(4):
                                ps = pst()
                                for kb in range(2):
                                    kc_ = slice(s * 256 + kb * 128, s * 256 + (kb + 1) * 128)
                                    mm(ps[:, kb * 256:(kb + 1) * 256], kTh.t[0:96, hh, kc_], qh.t[0:96, hh, s * 256:(s + 1) * 256], True, True, [kTh, qh], [ps])
                                if prevc is not None:
                                    mp_fin(*prevc)
                                prevc = (ps, s, hh)
                        mp_fin(*prevc)
                    else:
                        for hh in range(4):
                            cm, po = hh // 2, (hh % 2) * 64
                            vs = slice(64, 192) if po == 0 else slice(0, 128)
                            acc = [pss[4 + 2 * (hh % 2)], pss[5 + 2 * (hh % 2)]]
                            prev = None
                            for kb in range(NBK):
                                for tti in range(2):
                                    ps = pst()
                                    mm(ps[:, :], kTh.t[0:96, hh, kb * 128:(kb + 1) * 128], qh.t[0:96, hh, tti * 512:(tti + 1) * 512], True, True, [kTh, qh], [ps])
                                    cur = (kb, tti, ps)
                                    if prev is not None:
                                        kb_, tti_, ps_ = prev
                                        pt = nPT()
                                        act(pt[:], ps_[:, :], AF.Exp, [ps_], [pt], scale=SCL)
                                        mm(acc[tti_][:, :], vaug.t[:, kb_, hh, vs], pt[:], kb_ == 0, kb_ == NBK - 1, [vaug, pt], [acc[tti_]])
                                    prev = cur
                            kb_, tti_, ps_ = prev
                            pt = nPT()
                            act(pt[:], ps_[:, :], AF.Exp, [ps_], [pt], scale=SCL)
                            mm(acc[tti_][:, :], vaug.t[:, kb_, hh, vs], pt[:], kb_ == 0, kb_ == NBK - 1, [vaug, pt], [acc[tti_]])
                            for tti in range(2):
                                normalize(acc[tti], po, 512, cm, tti * 512)
                    P.pop()

                def hyena():
                    P.push()
                    h2 = P.sb([64, 2, L], F32, name="h2")
                    hw = P.sb([64, 1152], F32, name="hw")
                    absd = P.sb([128, 512], F32, name="absd")
                    tn = P.sb([128, 2, nch], F32, name="tn")
                    P.dma("sp", hw[:], I["hyw"][l], writes=[hw])
                    P.dma("sp", absd[:], I["rowp"][l], writes=[absd])
                    P.dma("sp", tn[:], G.tn, writes=[tn])
                    act(absd[:], absd[:], AF.Abs, [absd], [absd])
                    PH = P.phase
                    P.phase = PH + ".mlp"
                    P.push()
                    zg = P.sb([17, 2, L], F32, name="zg")
                    h1 = P.sb([64, 512], F32, name="h1")
                    ri = P.sb([64, 512], I32, name="ri")
                    rf = P.sb([64, 512], F32, name="rf")
                    ra = P.sb([64, 512], F32, name="ra")
                    P.dma("sp", zg[:], G.zg, writes=[zg])
                    nt = min(L, 512)

                    def sin_layer(ps, bcol, out_ap, out_t):
                        ts("dve", ra[:, 0:nt], ps[0:64, 0:nt], colp[0:64, bcol:bcol + 1], float(PI + 16 * TWO_PI), ALU.add, ALU.add, [ps, colp], [ra])
                        ts("dve", ri[:, 0:nt], ra[:, 0:nt], float(1.0 / TWO_PI), None, ALU.mult, None, [ra], [ri])
                        cp("dve", rf[:, 0:nt], ri[:, 0:nt], [ri], [rf])
                        stt(ra[:, 0:nt], rf[:, 0:nt], float(-TWO_PI), ra[:, 0:nt], ALU.mult, ALU.add, [rf, ra], [ra])
                        ts("dve", rf[:, 0:nt], ra[:, 0:nt], 0.0, float(TWO_PI), ALU.is_lt, ALU.mult, [ra], [rf])
                        tt("dve", ra[:, 0:nt], ra[:, 0:nt], rf[:, 0:nt], ALU.add, [ra, rf], [ra])
                        ts("dve", ra[:, 0:nt], ra[:, 0:nt], float(PI), 3.1415925, ALU.subtract, ALU.min, [ra], [ra])
                        ts("dve", ra[:, 0:nt], ra[:, 0:nt], -3.1415925, None, ALU.max, None, [ra], [ra])
                        act(out_ap, ra[:, 0:nt], AF.Sin, [ra], [out_t])
                    for g in range(2):
                        for ti in range(L // nt):
                            cs = slice(ti * nt, (ti + 1) * nt)
                            ps = pst()
                            mm(ps[0:64, 0:nt], hw[0:17, 0:64], zg.t[0:17, g, cs], True, True, [hw, zg], [ps])
                            sin_layer(ps, C_HB1, h1[:, 0:nt], h1)
                            ps2 = pst()
                            mm(ps2[0:64, 0:nt], hw[0:64, 64:128], h1[:, 0:nt], True, True, [hw, h1], [ps2])
                            sin_layer(ps2, C_HB2, h2.t[:, g, cs], h2)
                    P.pop()
                    hpads = [P.sb([128, NSEQ, L + 2], F32, name="hpad%d" % i) for i in range(2)]
                    for hp_ in hpads:
                        memset("pool", hp_[:], 0.0, [hp_])
                    u = P.sb([128, 3, NTOK], F32, nsub=6, name="u")
                    z1 = P.sb([128, NTOK], F32, nsub=2, name="z1")
                    utm = P.sb([128, nch, NSEQ, 128], BF16, name="utm")
                    AB = P.sb([128, nch, 2, 2, 128], BF16, name="AB")
                    Gt = P.sb([128, nch, 2, 128], F32, name="Gt")
                    Ur = P.sb([128, nch, NSEQ, 128], F32, name="Ur")
                    Y = P.sb([128, nch, 2, NSEQ, 128], BF16, name="Y")
                    ffs = [P.sb([128, 2, 128], F32, name="ff%d" % i) for i in range(2)]
                    fbs = [P.sb([128, 2, 128], F32, name="fb%d" % i) for i in range(2)]
                    wins = [P.sb([128, 2, 128], F32, name="win%d" % i) for i in range(2)]
                    m1s = [P.sb([128, 2, 128], F32, name="m1%d" % i) for i in range(2)]
                    m2s = [P.sb([128, 2, 128], F32, name="m2%d" % i) for i in range(2)]
                    nsg = 2 if NSEQ > 1 else 1
                    sgs = [(a, a + nsg) for a in range(0, NSEQ, nsg)]
                    dft = G.dft
                    nel = nch * L

                    NH = 2 if L > 512 else 1
                    HW_ = L // NH
                    FPH = HW_ // 128

                    def dload(i):
                        res = []
                        for hv in range(NH):
                            k = wstate["h"] % (2 * NW)
                            wstate["h"] += 1
                            t, hf = wring[k // 2], k % 2
                            P.dma("sp", t[:, hf * 4096:hf * 4096 + nch * HW_], dft[i, hv], writes=[(t, hf)])
                            res.append((t, hf, t.t[:, hf * 4096:hf * 4096 + nch * HW_].rearrange("p (k n) -> p k n", k=nch)))
                        return res

                    def mcol(res, fch):
                        t, hf, v = res[fch // FPH]
                        c0 = (fch % FPH) * 128
                        return (t, hf), v, c0
                    for cc_ in range(2):
                        P.phase = PH + ".proj"
                        w = wload(I["why"][l], 6144)
                        wv = w.t[:, 0:6144].rearrange("p (k n) -> p k n", k=8)
                        for part in range(3):
                            ci = part * 2 + cc_
                            hpad = hpads[part % 2]

                            def ev_h(ps, tti):
                                if NSEQ > 1:
                                    evac(hpad.t[:, 2 * tti:2 * tti + 2, 1:L + 1], ps[:, :].rearrange("p (s t) -> p s t", s=2), [ps], [hpad])
                                else:
                                    evac(hpad.t[:, 0, 1 + tti * 512:1 + (tti + 1) * 512], ps[:, :], [ps], [hpad])
                            proj_fm(w, wv, ci * 128, 128, ev_h)
                            uv = u.t[:, part, :].rearrange("p (s t) -> p s t", s=NSEQ)
                            act(uv, hpad.t[:, :, 1:L + 1], AF.Identity, [hpad, colp], [sc(u, part)], scale=colp[:, C_HSW + 6 + ci:C_HSW + 7 + ci], bias=colp[:, C_HSB + ci:C_HSB + ci + 1])
                            stt(uv, hpad.t[:, :, 0:L], colp[:, C_HSW + ci:C_HSW + ci + 1], uv, ALU.mult, ALU.add, [hpad, colp, sc(u, part)], [sc(u, part)])
                            stt(uv, hpad.t[:, :, 2:L + 2], colp[:, C_HSW + 12 + ci:C_HSW + 13 + ci], uv, ALU.mult, ALU.add, [hpad, colp, sc(u, part)], [sc(u, part)])
                        P.phase = PH + ".filt"
                        for tch in range(nch):
                            tcs = slice(tch * 128, (tch + 1) * 128)
                            ps = pst()
                            for o_ in range(2):
                                cf = 128 + o_ * 512 + cc_ * 128
                                mm(ps[:, o_ * 128:(o_ + 1) * 128], h2.t[:, 0, tcs], hw[0:64, cf:cf + 128], True, True, [h2, hw], [ps])
                                mm(ps[:, 256 + o_ * 128:256 + (o_ + 1) * 128], h2.t[:, 1, tcs], hw[0:64, cf + 256:cf + 384], True, True, [h2, hw], [ps])
                            win, ff, fb = wins[tch % 2], ffs[tch % 2], fbs[tch % 2]
                            act(win.t[:, 0, :], absd[:, cc_ * 128:(cc_ + 1) * 128], AF.Exp, [absd, tn], [win], scale=tn.t[:, 0, tch:tch + 1])
                            act(win.t[:, 1, :], absd[:, 256 + cc_ * 128:256 + (cc_ + 1) * 128], AF.Exp, [absd, tn], [win], scale=tn.t[:, 1, tch:tch + 1])
                            tt("dve", ff[:], ps[:, 0:256].rearrange("p (o c) -> p o c", o=2), win.t[:, 0:1, :].to_broadcast([128, 2, 128]), ALU.mult, [ps, win], [ff])
                            tt("dve", fb[:], ps[:, 256:512].rearrange("p (o c) -> p o c", o=2), win.t[:, 1:2, :].to_broadcast([128, 2, 128]), ALU.mult, [ps, win], [fb])
                            if tch == 0:
                                memset("dve", fb[0:1, :, :], 0.0, [fb])
                            tt("pool", AB.t[:, tch, :, 0, :], ff[:], fb[:], ALU.add, [ff, fb], [AB])
                            tt("pool", AB.t[:, tch, :, 1, :], ff[:], fb[:], ALU.subtract, [ff, fb], [AB])
                        for o in range(2):
                            P.phase = PH + ".tr"
                            for blk in range(8):
                                s_, jc = blk // nch, blk % nch
                                ps = pst()
                                if o == 0:
                                    tr(ps[:, 0:128], u.t[:, 2, blk * 128:(blk + 1) * 128], 128, [(u, 4 + blk // 4)], [ps])
                                else:
                                    tr(ps[:, 0:128], z1[:, blk * 128:(blk + 1) * 128], 128, [(z1, blk // 4)], [ps])
                                evac(utm.t[:, jc, s_, :], ps[:, 0:128], [ps], [utm])
                            P.phase = PH + ".A"
                            mres = dload(0)
                            for fch in range(nch):
                                ps = pst()
                                mt, mv, c0 = mcol(mres, fch)
                                for tch in range(nch):
                                    mm(ps[:, 0:128], mv[:, tch, c0:c0 + 128], AB.t[:, tch, o, 0, :], tch == 0, tch == nch - 1, [mt, AB], [ps])
                                evac(Gt.t[:, fch, 0, :], ps[:, 0:128], [ps], [Gt])
                            for (sa, sb_) in sgs:
                                ns = sb_ - sa
                                for fch in range(nch):
                                    ps = pst()
                                    mt, mv, c0 = mcol(mres, fch)
                                    for jc in range(nch):
                                        mm(ps[:, 0:ns * 128], mv[:, jc, c0:c0 + 128], utm.t[:, jc, sa:sb_, :], jc == 0, jc == nch - 1, [mt, utm], [ps])
                                    evac(Ur.t[:, fch, sa:sb_, :], ps[:, 0:ns * 128].rearrange("p (s c) -> p s c", s=ns), [ps], [Ur])
                            P.phase = PH + ".B"
                            mres = dload(1)
                            for fch in range(nch):
                                ps = pst()
                                mt, mv, c0 = mcol(mres, fch)
                                for tch in range(nch):
                                    mm(ps[:, 0:128], mv[:, tch, c0:c0 + 128], AB.t[:, tch, o, 1, :], tch == 0, tch == nch - 1, [mt, AB], [ps])
                                evac(Gt.t[:, fch, 1, :], ps[:, 0:128], [ps], [Gt])
                            for (sa, sb_) in sgs:
                                ns = sb_ - sa
                                for fch in range(nch):
                                    ps = pst()
                                    mt, mv, c0 = mcol(mres, fch)
                                    for jc in range(nch):
                                        mm(ps[:, 0:ns * 128], mv[:, jc, c0:c0 + 128], utm.t[:, jc, sa:sb_, :], jc == 0, jc == nch - 1, [mt, utm], [ps])
                                    p3 = ps[:, 0:ns * 128].rearrange("p (s c) -> p s c", s=ns)
                                    gr = Gt.t[:, fch, 0:1, :].to_broadcast([128, ns, 128])
                                    gi = Gt.t[:, fch, 1:2, :].to_broadcast([128, ns, 128])
                                    ur = Ur.t[:, fch, sa:sb_, :]
                                    m1, m2 = m1s[fch % 2], m2s[fch % 2]
                                    tt("pool", m1.t[:, 0:ns, :], ur, gr, ALU.mult, [Ur, Gt], [m1])
                                    tt("dve", m2.t[:, 0:ns, :], p3, gi, ALU.mult, [ps, Gt], [m2])
                                    tt("pool", Y.t[:, fch, 0, sa:sb_, :], m1.t[:, 0:ns, :], m2.t[:, 0:ns, :], ALU.subtract, [m1, m2], [Y])
                                    tt("pool", m1.t[:, 0:ns, :], ur, gi, ALU.mult, [Ur, Gt], [m1])
                                    tt("dve", m2.t[:, 0:ns, :], p3, gr, ALU.mult, [ps, Gt], [m2])
                                    tt("pool", Y.t[:, fch, 1, sa:sb_, :], m1.t[:, 0:ns, :], m2.t[:, 0:ns, :], ALU.add, [m1, m2], [Y])
                            P.phase = PH + ".inv"
                            mrc = dload(2)
                            mrs = dload(3)
                            nt = min(L, 512)
                            tiles = [(s_, it, pst()) for s_ in range(NSEQ) for it in range(L // nt)]
                            for (s_, it, ps) in tiles:
                                t_, hf_, v_ = mrc[it]
                                for fch in range(nch):
                                    mm(ps[:, 0:nt], Y.t[:, fch, 0, s_, :], v_[:, fch, 0:nt], fch == 0, False, [Y, (t_, hf_)], [ps])
                            for (s_, it, ps) in tiles:
                                t_, hf_, v_ = mrs[it]
                                for fch in range(nch):
                                    mm(ps[:, 0:nt], Y.t[:, fch, 1, s_, :], v_[:, fch, 0:nt], False, fch == nch - 1, [Y, (t_, hf_)], [ps])
                            for (s_, it, ps) in tiles:
                                if True:
                                    t0 = s_ * L + it * nt
                                    tcs = slice(t0, t0 + nt)
                                    tix = t0 // 512
                                    t_ = tmp()
                                    bcol = colp[:, C_HBIAS + o * 2 + cc_:C_HBIAS + o * 2 + cc_ + 1]
                                    if o == 0:
                                        stt(t_[:, 0:nt], u.t[:, 2, tcs], bcol, ps[:, 0:nt], ALU.mult, ALU.add, [(u, 4 + tix), colp, ps], [t_])
                                        tt("pool", z1[:, tcs], t_[:, 0:nt], u.t[:, 0, tcs], ALU.mult, [t_, (u, 0 + tix)], [(z1, tix)])
                                    else:
                                        stt(t_[:, 0:nt], z1[:, tcs], bcol, ps[:, 0:nt], ALU.mult, ALU.add, [(z1, tix), colp, ps], [t_])
                                        tt("pool", ymix.t[:, 2 + cc_, tcs], t_[:, 0:nt], u.t[:, 1, tcs], ALU.mult, [t_, (u, 2 + tix)], [s2(ymix, 2 + cc_, tix)])
                    P.pop()

                skip = _os.environ.get("MK_SKIP", "")
                if "ret" not in skip:
                    P.phase = "ret"
                    retention()
                pstate["n"] = 4 if SAMPLE else 8
                if "gqa" not in skip:
                    P.phase = "gqa" + G.name
                    gqa()
                if "mla" not in skip:
                    P.phase = "mla" + G.name
                    mla()
                pstate["n"] = 8
                if "hy" not in skip:
                    P.phase = "hy" + G.name
                    hyena()
                P.phase = "wout"
                P.pop()
                if dbg and l == 0:
                    dump("ymix_" + G.name, ymix, ymix[:], [128, 8, NTOK], BF16)

                P.push()
                y = P.sb([128, 8, NTOK], F32, nsub=16, name="y")

                def post_norm_add(ci):
                    def fin(c, xi, cs, tti):
                        stt(x.t[:, c, cs], xr.t[:, xi, :], mods.t[:, ci, c:c + 1], x.t[:, c, cs], ALU.mult, ALU.add, [(xr, xi), mods, s2(x, c, tti)], [s2(x, c, tti)])
                    norm_seq(y, fin)
                w = wload(I["wout"][l], 8192)
                wv = w.t[:, :].rearrange("p (k n) -> p k n", k=8)
                for m in range(8):
                    for tti in range(2):
                        ps = pst()
                        for kc in range(8):
                            mm(ps[:, :], wv[:, kc, m * 128:(m + 1) * 128], ymix.t[:, kc, tti * 512:(tti + 1) * 512], kc == 0, kc == 7, [w, s2(ymix, kc, tti)], [ps])
                        evac(y.t[:, m, tti * 512:(tti + 1) * 512], ps[:, :], [ps], [s2(y, m, tti)])
                P.phase = "wout.post"
                post_norm_add(2)
                if dbg and l == 0:
                    dump("x1_" + G.name, x, x[:], [128, 8, NTOK])
                if "ffn" in skip:
                    P.pop()
                    continue
                P.phase = "ffn"
                h2_ = ymix
                norm_mod(x, 3, 4, h2_)
                P.phase = "ffn.up"
                actb = P.sb([128, NHC, NTOK], BF16, name="actb")
                gps = [P.sb([128, NSEQ, L + 2], F32, name="gp%d" % i) for i in range(2)]
                gts = [P.sb([128, NTOK], F32, name="gt%d" % i) for i in range(2)]
                for gp in gps:
                    memset("pool", gp[:], 0.0, [gp])
                psus = {}
                wcur = {}

                def ffn_tail(hc):
                    gt = gts[hc % 2]
                    act(gt[:], gt[:], AF.Silu, [gt], [gt])
                    for tti in range(2):
                        cs = slice(tti * 512, (tti + 1) * 512)
                        tt("dve", actb.t[:, hc, cs], gt[:, cs], psus[hc][tti][:, :], ALU.mult, [gt, psus[hc][tti]], [actb])
                    del psus[hc]

                for hc in range(NHC + 1):
                    if hc < NHC:
                        b_, j = hc // 2, hc % 2
                        if j == 0:
                            wcur["w"] = wload_h(I["wup"][l, b_], 4096)
                        w, whf, wo = wcur["w"]
                        wv = w.t[:, wo:wo + 4096].rearrange("p (k n) -> p k n", k=8)
                        gp, gt = gps[hc % 2], gts[hc % 2]
                        psg = []
                        for tti in range(2):
                            ps = pst()
                            for kc in range(8):
                                mm(ps[:, :], wv[:, kc, j * 128:(j + 1) * 128], h2_.t[:, kc, tti * 512:(tti + 1) * 512], kc == 0, kc == 7, [(w, whf), s2(h2_, kc, tti)], [ps])
                            psg.append(ps)
                        pu = []
                        for tti in range(2):
                            ps = pst()
                            for kc in range(8):
                                mm(ps[:, :], wv[:, kc, 256 + j * 128:256 + (j + 1) * 128], h2_.t[:, kc, tti * 512:(tti + 1) * 512], kc == 0, kc == 7, [(w, whf), s2(h2_, kc, tti)], [ps])
                            pu.append(ps)
                        psus[hc] = pu
                    if hc >= 1:
                        ffn_tail(hc - 1)
                    if hc < NHC:
                        for tti in range(2):
                            ps = psg[tti]
                            if NSEQ > 1:
                                cp("act", gp.t[:, 2 * tti:2 * tti + 2, 1:L + 1], ps[:, :].rearrange("p (s t) -> p s t", s=2), [ps], [gp])
                            else:
                                cp("act", gp.t[:, 0, 1 + tti * 512:1 + (tti + 1) * 512], ps[:, :], [ps], [gp])
                        gv = gt[:, :].rearrange("p (s t) -> p s t", s=NSEQ)
                        act(gv, gp.t[:, :, 1:L + 1], AF.Identity, [gp, colp], [gt], scale=colp[:, C_FCW + 22 + hc:C_FCW + 23 + hc], bias=colp[:, C_FCB + hc:C_FCB + hc + 1])
                        stt(gv, gp.t[:, :, 0:L], colp[:, C_FCW + hc:C_FCW + hc + 1], gv, ALU.mult, ALU.add, [gp, colp, gt], [gt])
                        stt(gv, gp.t[:, :, 2:L + 2], colp[:, C_FCW + 44 + hc:C_FCW + 45 + hc], gv, ALU.mult, ALU.add, [gp, colp, gt], [gt])
                P.phase = "ffn.dn"
                for m in range(8):
                    w, whf, wo = wload_h(I["wdn"][l, m], 22 * 128)
                    wv = w.t[:, wo:wo + 22 * 128].rearrange("p (k n) -> p k n", k=22)
                    for tti in range(2):
                        ps = pst()
                        for kc in range(NHC):
                            mm(ps[:, :], wv[:, kc, :], actb.t[:, kc, tti * 512:(tti + 1) * 512], kc == 0, kc == NHC - 1, [(w, whf), actb], [ps])
                        evac(y.t[:, m, tti * 512:(tti + 1) * 512], ps[:, :], [ps], [s2(y, m, tti)])
                P.phase = "ffn.post"
                post_norm_add(5)
                P.pop()
                if dbg and l == 0:
                    dump("x2_" + G.name, x, x[:], [128, 8, NTOK])

            P.phase = "io"
            P.push()
            ot = [P.sb([128, D], F32, name="ot%d" % i) for i in range(2)]
            for blk in range(8):
                o_ = ot[blk % 2]
                for half in range(2):
                    ps = pst()
                    for c4 in range(4):
                        c = half * 4 + c4
                        tr(ps[:, c4 * 128:(c4 + 1) * 128], x.t[:, c, blk * 128:(blk + 1) * 128], 128, [s2(x, c, blk // 4)], [ps])
                    evac(o_[:, half * 512:(half + 1) * 512], ps[:, :], [ps], [o_])
                P.dma("sp", G.y_out[blk * 128:(blk + 1) * 128, :], o_[:], reads=[o_], is_output=True)
            P.pop()
            P.pop()
            mstate["ready"] = True

        GP = Grp()
        GP.name, GP.L, GP.NSEQ, GP.sample, GP.gi = "P", 256, 4, False, 0
        GP.x_in, GP.y_out, GP.dft, GP.zg, GP.tn = I["xp"], O["yp"], I["dftP"], I["zgP"], I["tnP"]
        GS = Grp()
        GS.name, GS.L, GS.NSEQ, GS.sample, GS.gi = "S", 1024, 1, True, 1
        GS.x_in, GS.y_out, GS.dft, GS.zg, GS.tn = I["xs"], O["ys"], I["dftS"], I["zgS"], I["tnS"]
        import os as _os_mod
        which = _os_mod.environ.get("MK_GROUPS", "PS")
        if "P" in which:
            run_group(GP)
        if "S" in which:
            run_group(GS)
        if _os_mod.environ.get("MK_PHASES"):
            import json as _json
            _json.dump(P.pe_phase, open(_os_mod.environ["MK_PHASES"], "w"))
        P.emit()
    return nc, DBG


_CONST = {}


def prep_inputs(inp):
    if "c" not in _CONST:
        _CONST["c"] = host_consts()
    cst = _CONST["c"]
    w = host_weights(inp)
    shared = dict(w)
    shared.update({"cst": cst["cst"], "dftP": cst["dftP"], "dftS": cst["dftS"], "zgP": cst["zgP"], "zgS": cst["zgS"],
                   "tnP": cst["tnP"], "tnS": cst["tnS"], "rope": cst["rope"], "rmat": cst["rmat"]})
    in_maps = []
    for core in range(8):
        b = core % 2
        m = dict(shared)
        m["xp"] = np.ascontiguousarray(inp["x_prompt"][core * 4:(core + 1) * 4].reshape(NTOK, D))
        m["xs"] = np.ascontiguousarray(inp["x_sample"][b])
        m["sret"] = np.ascontiguousarray(inp["state_ret"][b])
        m["cgk"] = np.ascontiguousarray(inp["cache_gqa_k"][b])
        m["cgv"] = np.ascontiguousarray(inp["cache_gqa_v"][b])
        m["cckv"] = np.ascontiguousarray(inp["cache_mla_ckv"][b])
        m["ckr"] = np.ascontiguousarray(inp["cache_mla_krope"][b])
        cond = np.stack([col_tile(inp["c_ctx"]), col_tile(inp["c"][b])]).astype(np.float32)
        m["cond"] = np.ascontiguousarray(cond)
        in_maps.append(m)
    return in_maps


def kernel(**inputs):
    inp = {k: np.asarray(v) for k, v in inputs.items()}
    in_maps = prep_inputs(inp)
    nc, _ = build(False)
    res = run_bass_kernel_spmd(nc, in_maps, core_ids=list(range(8)))
    rs = res.results
    yp = np.concatenate([rs[c]["yp"].reshape(4, 256, D) for c in range(8)], axis=0)
    ys = np.stack([rs[0]["ys"], rs[1]["ys"]], axis=0)
    nsr = np.concatenate([rs[c]["nsr"] for c in range(8)], axis=0)
    ngk = np.concatenate([rs[c]["ngk"] for c in range(8)], axis=0)
    ngv = np.concatenate([rs[c]["ngv"] for c in range(8)], axis=0)
    nckv = np.concatenate([rs[c]["nckv"] for c in range(8)], axis=0)
    nkr = np.concatenate([rs[c]["nkr"] for c in range(8)], axis=0)
    f = lambda a: np.ascontiguousarray(a, dtype=np.float32)
    return (f(yp), f(ys), f(nsr), f(ngk), f(ngv), f(nckv), f(nkr))
```
